# Optimizing a Trainium2 kernel written in Bass

```python
import jax
import jax.numpy as jnp
from jax import lax
import numpy as np

D_MODEL = 2048
BATCH = 2
SEQ = 4096
DEPTH = 2
DEC_BATCH = 8
DEC_SEQ = 4
PAST_LEN = 16384
PAGE_SIZE = 128

N_MIXERS = 2
N_NSA_LAYERS = (DEPTH + 1) // 2
N_RET_LAYERS = DEPTH // 2
N_HEADS = 16
HEAD_DIM = D_MODEL // N_HEADS
N_KV_HEADS = 4
GQA_GROUP = N_HEADS // N_KV_HEADS
CMP_BLOCK = 32
CMP_STRIDE = 16
CMP_RATIO = CMP_BLOCK // CMP_STRIDE
CMP_HIDDEN = HEAD_DIM
SEL_BLOCK = 64
N_SEL = 16
N_LOCAL = 2
WINDOW = 512
Q_BLOCK = 128
ROPE_THETA = 10000.0
RET_HEADS = 8
RET_KDIM = D_MODEL // RET_HEADS
RET_VDIM = 2 * D_MODEL // RET_HEADS
RET_CHUNK = 128
D_FF = 4 * D_MODEL
NSA_PROJ = N_HEADS * HEAD_DIM + 6 * N_KV_HEADS * HEAD_DIM + 3 * N_HEADS
RET_PROJ = 2 * RET_HEADS * RET_KDIM + 2 * RET_HEADS * RET_VDIM
RMS_EPS = 1e-6
GN_EPS = 1e-5
NEG_INF = -1e30
FORCE_SCORE = 1e9
F32 = jnp.float32

kernel_name = 'nsa_retention_hybrid_step'


def rms_norm(x, g):
    xf = x.astype(F32)
    y = xf * lax.rsqrt(jnp.mean(xf * xf, axis=-1, keepdims=True) + RMS_EPS)
    return (y * g.astype(F32)).astype(x.dtype)


def rope(x, pos):
    half = x.shape[-1] // 2
    inv = ROPE_THETA ** (-jnp.arange(half, dtype=F32) / half)
    ang = pos.astype(F32)[:, None] * inv[None, :]
    cos = jnp.cos(ang)[None, :, None, :]
    sin = jnp.sin(ang)[None, :, None, :]
    xf = x.astype(F32)
    x1, x2 = xf[..., :half], xf[..., half:]
    return jnp.concatenate([x1 * cos - x2 * sin, x2 * cos + x1 * sin], axis=-1).astype(x.dtype)


def masked_softmax(s, mask):
    p = jax.nn.softmax(jnp.where(mask, s, NEG_INF), axis=-1)
    return jnp.where(mask, p, 0.0)


def gqa_attend(q, k, v, mask):
    b, tq, h, dh = q.shape
    g = k.shape[2]
    qg = q.reshape(b, tq, g, h // g, dh)
    s = jnp.einsum('bqgrd,bsgd->bgrqs', qg, k).astype(F32) * dh ** -0.5
    p = masked_softmax(s, mask)
    o = jnp.einsum('bgrqs,bsgd->bqgrd', p.astype(v.dtype), v)
    return o.reshape(b, tq, h, dh)


def window_mask(q_pos, k_pos):
    d = q_pos[:, None] - k_pos[None, :]
    return (d >= 0) & (d < WINDOW) & (k_pos[None, :] >= 0)


def nsa_project(x, w_in, pos):
    b, t, _ = x.shape
    nq = N_HEADS * HEAD_DIM
    nkv = 6 * N_KV_HEADS * HEAD_DIM
    z = x @ w_in
    q = rope(z[..., :nq].reshape(b, t, N_HEADS, HEAD_DIM), pos)
    kv = z[..., nq:nq + nkv].reshape(b, t, 3, 2, N_KV_HEADS, HEAD_DIM)
    k = rope(kv[:, :, :, 0].reshape(b, t, 3 * N_KV_HEADS, HEAD_DIM), pos).reshape(b, t, 3, N_KV_HEADS, HEAD_DIM)
    kv = jnp.stack([k, kv[:, :, :, 1]], axis=3)
    gates = jax.nn.sigmoid(z[..., nq + nkv:].astype(F32)).reshape(b, t, 3, N_HEADS)
    return q, kv[:, :, 0], kv[:, :, 1], kv[:, :, 2], gates


def compress_blocks(kv_raw, cmp_pos, w1, b1, w2):
    b, length = kv_raw.shape[:2]
    n_chunks = length // CMP_STRIDE
    n_cmp = n_chunks - CMP_RATIO + 1
    xc = kv_raw[:, :n_chunks * CMP_STRIDE].reshape(b, n_chunks, CMP_STRIDE, 2, N_KV_HEADS, HEAD_DIM)
    w1r = w1.reshape(2, CMP_RATIO, CMP_STRIDE, HEAD_DIM, CMP_HIDDEN)
    per = jnp.einsum('bcsegd,ersdh->rbcegh', xc, w1r)
    pos_term = jnp.einsum('ersd,ersdh->eh', cmp_pos.reshape(2, CMP_RATIO, CMP_STRIDE, HEAD_DIM), w1r)
    h = per[0, :, 0:n_cmp]
    for r in range(1, CMP_RATIO):
        h = h + per[r, :, r:r + n_cmp]
    h = jax.nn.gelu(h.astype(F32) + (pos_term + b1).astype(F32)[:, None, :])
    out = jnp.einsum('bcegh,ehd->bcegd', h.astype(w2.dtype), w2)
    return out[:, :, 0], out[:, :, 1]


def cmp_branch(q, k_c, v_c, q_pos):
    b, tq, h, dh = q.shape
    n_cmp = k_c.shape[1]
    blk_last = jnp.arange(n_cmp) * CMP_STRIDE + CMP_BLOCK - 1
    mask = blk_last[None, :] <= q_pos[:, None]
    qg = q.reshape(b, tq, N_KV_HEADS, GQA_GROUP, dh)
    s = jnp.einsum('bqgrd,bcgd->bgrqc', qg, k_c).astype(F32) * dh ** -0.5
    p = masked_softmax(s, mask)
    o = jnp.einsum('bgrqc,bcgd->bqgrd', p.astype(v_c.dtype), v_c).reshape(b, tq, h, dh)
    return o, p.sum(axis=2)


def select_blocks(imp_cmp, q_pos, n_blk):
    n_cmp = imp_cmp.shape[-1]
    c0 = jnp.arange(n_cmp) * CMP_STRIDE
    j0 = jnp.arange(n_blk) * SEL_BLOCK
    cover = ((c0[:, None] < j0[None, :] + SEL_BLOCK) & (c0[:, None] + CMP_BLOCK > j0[None, :])).astype(F32)
    imp = jnp.einsum('bgqc,cj->bgqj', imp_cmp, cover)
    jb = jnp.arange(n_blk)[None, :]
    cur = (q_pos // SEL_BLOCK)[:, None]
    visible = jb <= cur
    forced = (jb == 0) | (visible & (jb > cur - N_LOCAL))
    imp = jnp.where(forced, FORCE_SCORE, jnp.where(visible, imp, -FORCE_SCORE))
    _, idx = lax.top_k(imp, min(N_SEL, n_blk))
    return idx


def sel_branch(q, k_blk, v_blk, idx, q_pos):
    b, tq, h, dh = q.shape
    n_top = idx.shape[-1]
    take = jax.vmap(jax.vmap(lambda blocks, i: blocks[i]))
    kg = take(k_blk.transpose(0, 3, 1, 2, 4), idx).reshape(b, N_KV_HEADS, tq, n_top * SEL_BLOCK, dh)
    vg = take(v_blk.transpose(0, 3, 1, 2, 4), idx).reshape(b, N_KV_HEADS, tq, n_top * SEL_BLOCK, dh)
    k_pos = (idx[..., None] * SEL_BLOCK + jnp.arange(SEL_BLOCK)).reshape(b, N_KV_HEADS, tq, n_top * SEL_BLOCK)
    mask = (k_pos <= q_pos[None, None, :, None])[:, :, None]
    qg = q.reshape(b, tq, N_KV_HEADS, GQA_GROUP, dh)
    s = jnp.einsum('bqgrd,bgqsd->bgrqs', qg, kg).astype(F32) * dh ** -0.5
    p = masked_softmax(s, mask)
    o = jnp.einsum('bgrqs,bgqsd->bqgrd', p.astype(vg.dtype), vg)
    return o.reshape(b, tq, h, dh)


def merge_branches(gates, o_cmp, o_sel, o_win, w_out):
    g = gates.astype(o_cmp.dtype)[..., None]
    o = g[:, :, 0] * o_cmp + g[:, :, 1] * o_sel + g[:, :, 2] * o_win
    return o.reshape(o.shape[0], o.shape[1], -1) @ w_out


def nsa_prompt(x, w_in, w_out, cmp_pos, cmp_w1, cmp_b1, cmp_w2):
    b, t, _ = x.shape
    pos = jnp.arange(t)
    q, kv_cmp, kv_sel, kv_win, gates = nsa_project(x, w_in, pos)
    k_c, v_c = compress_blocks(kv_cmp, cmp_pos, cmp_w1, cmp_b1, cmp_w2)
    o_cmp, imp = cmp_branch(q, k_c, v_c, pos)
    n_blk = t // SEL_BLOCK
    idx = select_blocks(imp, pos, n_blk)
    k_blk = kv_sel[:, :, 0].reshape(b, n_blk, SEL_BLOCK, N_KV_HEADS, HEAD_DIM)
    v_blk = kv_sel[:, :, 1].reshape(b, n_blk, SEL_BLOCK, N_KV_HEADS, HEAD_DIM)
    win_pad = jnp.pad(kv_win, ((0, 0), (WINDOW, 0), (0, 0), (0, 0), (0, 0)))

    def query_block(i):
        start = i * Q_BLOCK
        qb = lax.dynamic_slice_in_dim(q, start, Q_BLOCK, axis=1)
        pb = start + jnp.arange(Q_BLOCK)
        ib = lax.dynamic_slice_in_dim(idx, start, Q_BLOCK, axis=2)
        o_s = sel_branch(qb, k_blk, v_blk, ib, pb)
        wb = lax.dynamic_slice_in_dim(win_pad, start, Q_BLOCK + WINDOW, axis=1)
        kp = start - WINDOW + jnp.arange(Q_BLOCK + WINDOW)
        o_w = gqa_attend(qb, wb[:, :, 0], wb[:, :, 1], window_mask(pb, kp))
        return o_s, o_w

    o_sel, o_win = lax.map(query_block, jnp.arange(t // Q_BLOCK))
    o_sel = o_sel.transpose(1, 0, 2, 3, 4).reshape(b, t, N_HEADS, HEAD_DIM)
    o_win = o_win.transpose(1, 0, 2, 3, 4).reshape(b, t, N_HEADS, HEAD_DIM)
    y = merge_branches(gates, o_cmp, o_sel, o_win, w_out)
    w_keep = min(WINDOW, t)
    return y, kv_cmp, kv_sel, kv_win[:, t - w_keep:]


def nsa_sample(x, cache_cmp, cache_sel, win_buf, page_table, past_len, w_in, w_out, cmp_pos, cmp_w1, cmp_b1, cmp_w2):
    b, tn, _ = x.shape
    pos = past_len + jnp.arange(tn)
    q, kv_cmp, kv_sel, kv_win, gates = nsa_project(x, w_in, pos)
    total = past_len + tn
    full_cmp = jnp.concatenate([cache_cmp[page_table].reshape(b, past_len, 2, N_KV_HEADS, HEAD_DIM), kv_cmp], axis=1)
    k_c, v_c = compress_blocks(full_cmp, cmp_pos, cmp_w1, cmp_b1, cmp_w2)
    o_cmp, imp = cmp_branch(q, k_c, v_c, pos)
    n_blk = -(-total // SEL_BLOCK)
    full_sel = jnp.concatenate([cache_sel[page_table].reshape(b, past_len, 2, N_KV_HEADS, HEAD_DIM), kv_sel], axis=1)
    full_sel = jnp.pad(full_sel, ((0, 0), (0, n_blk * SEL_BLOCK - total), (0, 0), (0, 0), (0, 0)))
    idx = select_blocks(imp, pos, n_blk)
    k_blk = full_sel[:, :, 0].reshape(b, n_blk, SEL_BLOCK, N_KV_HEADS, HEAD_DIM)
    v_blk = full_sel[:, :, 1].reshape(b, n_blk, SEL_BLOCK, N_KV_HEADS, HEAD_DIM)
    o_sel = sel_branch(q, k_blk, v_blk, idx, pos)
    w_buf = win_buf.shape[1]
    win_all = jnp.concatenate([win_buf, kv_win], axis=1)
    kp = past_len - w_buf + jnp.arange(w_buf + tn)
    o_win = gqa_attend(q, win_all[:, :, 0], win_all[:, :, 1], window_mask(pos, kp))
    y = merge_branches(gates, o_cmp, o_sel, o_win, w_out)
    return y, kv_cmp, kv_sel, win_all[:, tn:]


def ret_log_decay():
    return jnp.log(1.0 - 2.0 ** (-5.0 - jnp.arange(RET_HEADS, dtype=F32)))


def ret_project(x, w_in, pos):
    b, t, _ = x.shape
    nk = RET_HEADS * RET_KDIM
    nv = RET_HEADS * RET_VDIM
    z = x @ w_in
    q = rope(z[..., :nk].reshape(b, t, RET_HEADS, RET_KDIM), pos)
    k = rope(z[..., nk:2 * nk].reshape(b, t, RET_HEADS, RET_KDIM), pos)
    v = z[..., 2 * nk:2 * nk + nv].reshape(b, t, RET_HEADS, RET_VDIM)
    g = z[..., 2 * nk + nv:]
    return q, k, v, g


def ret_chunk(q, k, v, state, log_g):
    c = q.shape[1]
    i = jnp.arange(c, dtype=F32)
    rel = i[:, None] - i[None, :]
    decay = jnp.where(rel >= 0, jnp.exp(jnp.maximum(rel, 0.0)[None] * log_g[:, None, None]), 0.0)
    qf = q.astype(F32)
    kf = k.astype(F32) * RET_KDIM ** -0.5
    vf = v.astype(F32)
    s = jnp.einsum('bihd,bjhd->bhij', qf, kf) * decay[None]
    inner = jnp.einsum('bhij,bjhe->bihe', s, vf)
    q_dec = jnp.exp((i[:, None] + 1.0) * log_g[None, :])
    cross = jnp.einsum('bihd,bhde->bihe', qf, state) * q_dec[None, :, :, None]
    k_dec = jnp.exp((c - 1.0 - i)[:, None] * log_g[None, :])
    new_state = jnp.exp(c * log_g)[None, :, None, None] * state + jnp.einsum('bjhd,bjhe->bhde', kf * k_dec[None, :, :, None], vf)
    return inner + cross, new_state


def ret_output(o, g, gn, w_out):
    b, t = o.shape[:2]
    mu = jnp.mean(o, axis=-1, keepdims=True)
    var = jnp.mean(jnp.square(o - mu), axis=-1, keepdims=True)
    on = ((o - mu) * lax.rsqrt(var + GN_EPS)).reshape(b, t, -1) * gn.astype(F32)
    gated = (jax.nn.silu(g.astype(F32)) * on).astype(g.dtype)
    return gated @ w_out


def ret_prompt(x, w_in, gn, w_out):
    b, t, _ = x.shape
    q, k, v, g = ret_project(x, w_in, jnp.arange(t))
    log_g = ret_log_decay()
    nc = t // RET_CHUNK

    def to_chunks(a):
        return a.reshape(b, nc, RET_CHUNK, a.shape[2], a.shape[3]).swapaxes(0, 1)

    def step(state, qkv):
        o, state = ret_chunk(qkv[0], qkv[1], qkv[2], state, log_g)
        return state, o

    state0 = jnp.zeros((b, RET_HEADS, RET_KDIM, RET_VDIM), F32)
    state, o = lax.scan(step, state0, (to_chunks(q), to_chunks(k), to_chunks(v)))
    o = o.swapaxes(0, 1).reshape(b, t, RET_HEADS, RET_VDIM)
    return ret_output(o, g, gn, w_out), state


def ret_sample(x, state, past_len, w_in, gn, w_out):
    tn = x.shape[1]
    q, k, v, g = ret_project(x, w_in, past_len + jnp.arange(tn))
    o, new_state = ret_chunk(q, k, v, state.astype(F32), ret_log_decay())
    return ret_output(o, g, gn, w_out), new_state


def sq_relu_mlp(x, w1, w2):
    h = jax.nn.relu(x @ w1)
    return (h * h) @ w2


def setup_inputs(seed: int = 0) -> dict:
    key = jax.random.key(seed)
    ks = jax.random.split(key, 24)
    n_pages = PAST_LEN // PAGE_SIZE
    n_used = DEC_BATCH * n_pages
    pool = n_used + n_used // 4

    def nrm(k, shape, scale):
        return jax.random.normal(k, shape, F32) * scale

    kv_row = (2, N_KV_HEADS, HEAD_DIM)
    return {
        'x_prompt': nrm(ks[0], (BATCH, SEQ, D_MODEL), 1.0),
        'x_sample': nrm(ks[1], (DEC_BATCH, DEC_SEQ, D_MODEL), 1.0),
        'cache_cmp_kv': nrm(ks[2], (N_NSA_LAYERS, pool, PAGE_SIZE) + kv_row, 1.0),
        'cache_sel_kv': nrm(ks[3], (N_NSA_LAYERS, pool, PAGE_SIZE) + kv_row, 1.0),
        'state_win_kv': nrm(ks[4], (N_NSA_LAYERS, DEC_BATCH, min(WINDOW, PAST_LEN)) + kv_row, 1.0),
        'state_ret': nrm(ks[5], (N_RET_LAYERS, DEC_BATCH, RET_HEADS, RET_KDIM, RET_VDIM), 1.0),
        'page_table': jax.random.permutation(ks[6], pool)[:n_used].reshape(DEC_BATCH, n_pages).astype(jnp.int32),
        'norm_mix': 1.0 + nrm(ks[7], (DEPTH, D_MODEL), 0.02),
        'norm_ffn': 1.0 + nrm(ks[8], (DEPTH, D_MODEL), 0.02),
        'norm_final': 1.0 + nrm(ks[9], (D_MODEL,), 0.02),
        'nsa_w_in': nrm(ks[10], (N_NSA_LAYERS, D_MODEL, NSA_PROJ), D_MODEL ** -0.5),
        'nsa_w_out': nrm(ks[11], (N_NSA_LAYERS, N_HEADS * HEAD_DIM, D_MODEL), (N_HEADS * HEAD_DIM) ** -0.5),
        'nsa_cmp_pos': nrm(ks[12], (N_NSA_LAYERS, 2, CMP_BLOCK, HEAD_DIM), 0.1),
        'nsa_cmp_w1': nrm(ks[13], (N_NSA_LAYERS, 2, CMP_BLOCK, HEAD_DIM, CMP_HIDDEN), (CMP_BLOCK * HEAD_DIM) ** -0.5),
        'nsa_cmp_b1': nrm(ks[14], (N_NSA_LAYERS, 2, CMP_HIDDEN), 0.01),
        'nsa_cmp_w2': nrm(ks[15], (N_NSA_LAYERS, 2, CMP_HIDDEN, HEAD_DIM), CMP_HIDDEN ** -0.5),
        'ret_w_in': nrm(ks[16], (N_RET_LAYERS, D_MODEL, RET_PROJ), D_MODEL ** -0.5),
        'ret_gn': 1.0 + nrm(ks[17], (N_RET_LAYERS, RET_HEADS * RET_VDIM), 0.02),
        'ret_w_out': nrm(ks[18], (N_RET_LAYERS, RET_HEADS * RET_VDIM, D_MODEL), (RET_HEADS * RET_VDIM) ** -0.5),
        'ffn_w1': nrm(ks[19], (DEPTH, D_MODEL, D_FF), D_MODEL ** -0.5),
        'ffn_w2': nrm(ks[20], (DEPTH, D_FF, D_MODEL), D_FF ** -0.5),
    }


def reference(x_prompt, x_sample, cache_cmp_kv, cache_sel_kv, state_win_kv, state_ret, page_table,
              norm_mix, norm_ffn, norm_final, nsa_w_in, nsa_w_out, nsa_cmp_pos, nsa_cmp_w1, nsa_cmp_b1,
              nsa_cmp_w2, ret_w_in, ret_gn, ret_w_out, ffn_w1, ffn_w2):
    past_len = page_table.shape[1] * cache_sel_kv.shape[2]
    hp, hs = x_prompt, x_sample
    cmp_p, cmp_s, sel_p, sel_s, win_p, win_s, ret_p, ret_s = [], [], [], [], [], [], [], []
    for layer in range(DEPTH):
        xp = rms_norm(hp, norm_mix[layer])
        xs = rms_norm(hs, norm_mix[layer])
        if layer % N_MIXERS == 0:
            a = layer // N_MIXERS
            nsa_w = (nsa_w_in[a], nsa_w_out[a], nsa_cmp_pos[a], nsa_cmp_w1[a], nsa_cmp_b1[a], nsa_cmp_w2[a])
            yp, c_p, s_p, w_p = nsa_prompt(xp, *nsa_w)
            ys, c_s, s_s, w_s = nsa_sample(xs, cache_cmp_kv[a], cache_sel_kv[a], state_win_kv[a], page_table, past_len, *nsa_w)
            cmp_p.append(c_p)
            cmp_s.append(c_s)
            sel_p.append(s_p)
            sel_s.append(s_s)
            win_p.append(w_p)
            win_s.append(w_s)
        else:
            r = layer // N_MIXERS
            yp, st_p = ret_prompt(xp, ret_w_in[r], ret_gn[r], ret_w_out[r])
            ys, st_s = ret_sample(xs, state_ret[r], past_len, ret_w_in[r], ret_gn[r], ret_w_out[r])
            ret_p.append(st_p)
            ret_s.append(st_s)
        hp = hp + yp
        hs = hs + ys
        hp = hp + sq_relu_mlp(rms_norm(hp, norm_ffn[layer]), ffn_w1[layer], ffn_w2[layer])
        hs = hs + sq_relu_mlp(rms_norm(hs, norm_ffn[layer]), ffn_w1[layer], ffn_w2[layer])
    y_prompt = rms_norm(hp, norm_final)
    y_sample = rms_norm(hs, norm_final)
    return (y_prompt, y_sample, jnp.stack(cmp_p), jnp.stack(cmp_s), jnp.stack(sel_p), jnp.stack(sel_s),
            jnp.stack(win_p), jnp.stack(win_s), jnp.stack(ret_p), jnp.stack(ret_s))
```

```python
import contextlib
import numpy as np
import concourse.bass as bass
import concourse.mybir as mybir
from concourse.bass_utils import run_bass_kernel_spmd

F32 = mybir.dt.float32
BF16 = mybir.dt.bfloat16
I32 = mybir.dt.int32
AF = mybir.ActivationFunctionType
ALU = mybir.AluOpType
AX = mybir.AxisListType

ENGS = ("sp", "act", "dve", "pool", "pe")


class Buf:
    __slots__ = ("name", "last_w", "readers", "cnt")

    def __init__(self, name):
        self.name = name
        self.last_w = None
        self.readers = []
        self.cnt = 0


class T:
    def __init__(self, t, name):
        self.t = t
        self.b = Buf(name)

    def __getitem__(self, k):
        return self.t[k]

    def ap(self):
        return self.t.ap()


def _b(x):
    return x.b if isinstance(x, T) else x


class Op:
    __slots__ = ("eng", "fn", "deps", "is_dma", "key", "sig", "sigidx", "cnt", "inc")

    def __init__(self, eng, fn, is_dma=False, key=None, inc=16):
        self.eng = eng
        self.fn = fn
        self.deps = []
        self.is_dma = is_dma
        self.key = key
        self.sig = False
        self.sigidx = 0
        self.cnt = 0
        self.inc = inc


class Sched:
    def __init__(self, nc):
        self.nc = nc
        self.ops = []
        self.last_real = {e: None for e in ENGS}
        self.out_dmas = []

    def _add(self, op, r, w):
        r = [_b(x) for x in r]
        w = [_b(x) for x in w]
        deps = []
        for b in r:
            if b.last_w is not None:
                deps.append(b.last_w)
        for b in w:
            if b.last_w is not None:
                deps.append(b.last_w)
            deps.extend(b.readers)
        seen = set()
        for d in deps:
            if d is op or id(d) in seen:
                continue
            seen.add(id(d))
            if (not d.is_dma) and d.eng == op.eng and op.eng == "pe" and not op.is_dma:
                continue
            op.deps.append(d)
            if not d.is_dma:
                d.sig = True
        for b in r:
            b.readers.append(op)
        for b in w:
            b.last_w = op
            b.readers = []
        self.ops.append(op)
        if not op.is_dma:
            self.last_real[op.eng] = op
        return op

    def op(self, eng, fn, r=(), w=()):
        return self._add(Op(eng, fn), r, w)

    def pe(self, fn, r=(), w=()):
        return self.op("pe", fn, r, w)

    def act(self, fn, r=(), w=()):
        return self.op("act", fn, r, w)

    def dve(self, fn, r=(), w=()):
        return self.op("dve", fn, r, w)

    def pool(self, fn, r=(), w=()):
        return self.op("pool", fn, r, w)

    def dma(self, q, out, in_, r=(), w=(), key=None, final=False, **kw):
        return self.custom_dma(q, lambda e: e.dma_start(out=out, in_=in_, **kw), r, w, key, 16, final)

    def custom_dma(self, q, fn, r=(), w=(), key=None, inc=16, final=False):
        k = _b(key)
        op = Op(q, fn, is_dma=True, key=k, inc=inc)
        self._add(op, r, w)
        if final:
            self.out_dmas.append(op)
        return op

    def barrier(self):
        pend = [o for o in self.last_real.values() if o is not None]
        latest = {}
        for o in self.ops:
            if o.is_dma:
                latest[id(o.key)] = o
        for e in ENGS:
            op = Op(e, None)
            for d in pend:
                if d.eng != e:
                    op.deps.append(d)
                    d.sig = True
            op.deps.extend(latest.values())
            self.ops.append(op)

    def emit(self):
        nc = self.nc
        fin = Op("sp", None)
        latest = {}
        for o in self.out_dmas:
            latest[id(o.key)] = o
        fin.deps = list(latest.values())
        for e in ENGS:
            o = self.last_real[e]
            if e != "sp" and o is not None:
                fin.deps.append(o)
                o.sig = True
        self.ops.append(fin)

        with contextlib.ExitStack() as st:
            esem = {e: st.enter_context(nc.semaphore("s_" + e)) for e in ENGS}
            keysems = {}
            keyvals = {}
            for o in self.ops:
                if o.is_dma:
                    kid = id(o.key)
                    if kid not in keysems:
                        keysems[kid] = st.enter_context(nc.semaphore("k%d" % len(keysems)))
                        keyvals[kid] = 0
                    keyvals[kid] += o.inc
                    o.cnt = keyvals[kid]
            cnts = {e: 0 for e in ENGS}
            for o in self.ops:
                if (not o.is_dma) and o.sig:
                    assert o.fn is not None
                    cnts[o.eng] += 1
                    o.sigidx = cnts[o.eng]
            self.n_sems = len(keysems) + 5
            per = {e: [o for o in self.ops if o.eng == e] for e in ENGS}
            block = st.enter_context(nc.Block())

            def run(eng_name):
                def body(eng):
                    waited = {}
                    for o in per[eng_name]:
                        for d in o.deps:
                            if d.is_dma:
                                s, v = keysems[id(d.key)], d.cnt
                            else:
                                s, v = esem[d.eng], d.sigidx
                            if waited.get(id(s), 0) >= v:
                                continue
                            waited[id(s)] = v
                            eng.wait_ge(s, v)
                        if o.fn is None:
                            continue
                        ins = o.fn(eng)
                        if o.is_dma:
                            ins.then_inc(keysems[id(o.key)], o.inc)
                        elif o.sig:
                            ins.then_inc(esem[eng_name], 1)

                return body

            block.sync(run("sp"))
            block.scalar(run("act"))
            block.vector(run("dve"))
            block.gpsimd(run("pool"))
            block.tensor(run("pe"))


D = 2048
SEQ = 4096
NT = SEQ // 128
NTOK = 1028
NS = 16
PAST = 16384
NCOL = 1292
RMS_EPS = 1e-6
SCALE = 128 ** -0.5
NEG = -30000.0

STAGE = 1
DEBUG = False
NTQ = NT
RUN_S = True
RUN_T0 = True
RUN_E = True
RUN_T1 = True
TRACE = False


def rope_table(pos, half):
    inv = (10000.0 ** (-(np.arange(half, dtype=np.float32)) / np.float32(half))).astype(np.float32)
    ang = (pos.astype(np.float32)[:, None] * inv[None, :]).astype(np.float32)
    return np.concatenate([np.cos(ang), np.sin(ang)], axis=1).astype(np.float32)


def build(stage=STAGE):
    nc = bass.Bass("TRN2", target_bir_lowering=False)
    S = Sched(nc)

    def din(name, shape, dt=F32):
        return nc.dram_tensor(name, list(shape), dt, kind="ExternalInput")

    def dout(name, shape, dt=F32):
        return nc.dram_tensor(name, list(shape), dt, kind="ExternalOutput")

    xb = din("xb", [SEQ, D])
    xs = din("xs", [NS, D])
    w_in = din("w_in", [D, NCOL])
    gmix0 = din("gmix0", [128, D])
    cs_p = din("cs_p", [SEQ, 128])
    cs_s = din("cs_s", [NS, 128])
    ident_d = din("ident", [128, 128])

    cw1 = din("cw1", [2, 32, 128, 128])
    cw2 = din("cw2", [2, 128, 128])
    posT = din("posT", [128, 64])
    b1T = din("b1T", [128, 2])
    cover_d = din("cover", [256, 64])
    I0_d = din("I0", [128, 128])
    Jm_d = din("Jm", [128, 64])
    tri_d = din("tri", [128, 256])
    Ebig_d = din("Ebig", [64, SEQ])

    kv_p = dout("kv_p", [SEQ, 3, 2, 128])
    o_loc = nc.dram_tensor("o_loc", [SEQ, 512], BF16)
    o_locs = nc.dram_tensor("o_locs", [NS, 512], BF16)
    og = nc.dram_tensor("og", [4 * SEQ, 512], BF16)
    ogs = nc.dram_tensor("ogs", [4 * NS, 512], BF16)
    o_locT = [Buf("o_loc%d" % i) for i in range(5)]
    ogT = Buf("og")
    x_tok = din("x_tok", [NTOK, D])
    oidx_d = din("oidx", [128, 36], I32)
    w_out0 = din("w_out0", [D, D])
    ffn1_0 = din("ffn1_0", [D, 4 * D])
    ffn2_0 = din("ffn2_0", [4 * D, D])
    gffn0T = din("gffn0T", [128, 16])
    w_out1 = din("w_out1", [2 * D, D])
    ffn1_1 = din("ffn1_1", [D, 4 * D])
    ffn2_1 = din("ffn2_1", [4 * D, D])
    gmix1T = din("gmix1T", [128, 16])
    gffn1T = din("gffn1T", [128, 16])
    gfinT = din("gfinT", [128, 16])
    wr_d = din("wr", [D, 3072])
    cs_r = din("cs_r", [SEQ, 256])
    cs_rs = din("cs_rs", [NS, 256])
    decT_d = din("decT", [128, 256])
    qdecb_d = din("qdecb", [128, 256])
    kdec_d = din("kdec", [128, 2])
    decTs_d = din("decTs", [16, 32])
    kdm_d = din("kdm", [16, 8])
    colmask_d = din("colmask", [128, 64])
    gnb_d = din("gnb", [128, 1024])
    gpow_d = din("gpow", [128, 4])
    qdecs_d = din("qdecs", [128, 32])
    sret_d = din("sret", [4, 2, 256, 512])
    oidx2_d = din("oidx2", [128, 72], I32)
    ret_p = dout("ret_p", [2, 256, 512])
    ret_s = dout("ret_s", [4, 2, 256, 512])
    y_tok = dout("y_tok", [NTOK, D])
    hspill = nc.dram_tensor("hspill", [128, 16 * NTOK], F32)
    hspT = Buf("hspill")
    xg_in = nc.dram_tensor("xg_in", [8, 128, 2 * NTOK], BF16)
    xg_all = nc.dram_tensor("xg_all", [8, 512, 2 * NTOK], BF16)
    xg_inT = [Buf("xg_in%d" % i) for i in range(8)]
    xgT = Buf("xg_all")
    o_r = nc.dram_tensor("o_r", [4, 2, 1024, 512], BF16)
    o_rs = nc.dram_tensor("o_rs", [2, NS, 512], BF16)
    og2 = nc.dram_tensor("og2", [4 * 2 * 4 * 1024, 512], BF16)
    ogs2 = nc.dram_tensor("ogs2", [2 * 4 * NS, 512], BF16)
    o_rT = [[Buf("o_r%d%d" % (c, h)) for h in range(2)] for c in range(4)]
    o_rsT = [Buf("o_rs%d" % h) for h in range(2)]
    og2T = Buf("og2")
    win_s = dout("win_s", [4, 512, 2, 128])
    ccache = din("ccache", [1280, 128, 2, 128])
    scache = din("scache", [1280, 128, 2, 128])
    wstate = din("wstate", [4, 512, 2, 128])
    ptab = din("ptab", [4, 128], I32)
    smask_d = din("smask", [16, 5, 16])
    sel16_d = din("sel16", [16, 4 + 64])
    hmask_d = din("hmask", [16, 12])
    cover_s = din("cover_s", [1024, 257])
    pidx_d = din("pidx", [128, 1])
    ptabT = din("ptabT", [32, 16], I32)
    Rrep_d = din("Rrep", [32, 128])
    E2_d = din("E2", [64, 128])
    kv_s = dout("kv_s", [NS, 3, 2, 128])

    dbg_outs = {}

    def dbg(name, t, ap, shape, dt=F32):
        if not DEBUG:
            return
        d_ = nc.dram_tensor("dbg_" + name, list(shape), dt, kind="ExternalOutput")
        S.dma("sp", d_.ap(), ap, r=[t], key=Buf("dbgk_" + name), final=True)

    with contextlib.ExitStack() as top:
        def sbuf(st, name, shape, dt):
            return T(st.enter_context(nc.sbuf_tensor(name, list(shape), dt)), name)

        def psum(st, name, shape, dt):
            return T(st.enter_context(nc.psum_tensor(name, list(shape), dt)), name)

        identb = sbuf(top, "identb", [128, 128], BF16)
        identf = sbuf(top, "identf", [128, 128], F32)
        GL = sbuf(top, "GL", [128, NT + 1, 12], F32)
        QTs = sbuf(top, "QTs", [128, 4, NS], BF16)
        KTs = sbuf(top, "KTs", [128, 3, NS], BF16)
        VAs = sbuf(top, "VAs", [NS, 2, 132], BF16)
        pp = contextlib.ExitStack()
        QT = sbuf(pp, "QT", [128, 4, SEQ + NS], BF16)
        KT = sbuf(pp, "KT", [128, 3, SEQ + NS], BF16)
        VcT = sbuf(pp, "VcT", [128, SEQ + NS], BF16)
        VA = sbuf(pp, "VA", [128, NT + 1, 2, 132], BF16)
        S.dma("sp", identf[:], ident_d.ap(), w=[identf], key=identf)
        S.dma("pool", identb[:], ident_d.ap(), w=[identb], key=identb)

        def phase_a():
            with contextlib.ExitStack() as pa:
                wsb = sbuf(pa, "wsb", [128, 16, NCOL], BF16)
                gsb = sbuf(pa, "gsb", [128, D], F32)
                xt = [sbuf(pa, "xt%d" % i, [128, D], F32) for i in range(2)]
                cst = [sbuf(pa, "cst%d" % i, [128, 128], F32) for i in range(2)]
                junk = sbuf(pa, "junk", [128, D], BF16)
                ss = sbuf(pa, "ss", [128, 2], F32)
                epsb = sbuf(pa, "epsb", [128, 1], F32)
                S.pool(lambda e: e.memset(epsb[:], RMS_EPS), w=[epsb])
                xn = sbuf(pa, "xn", [128, D], BF16)
                xnT = sbuf(pa, "xnT", [128, 16, 128], BF16)
                kvf = [sbuf(pa, "kvf%d" % i, [128, 3, 2, 128], F32) for i in range(2)]
                rt = [sbuf(pa, "rt%d" % i, [128, 4, 64], F32) for i in range(4)]
                qkb = sbuf(pa, "qkb", [128, 8, 128], BF16)
                pT = [psum(pa, "pT%d" % i, [128, 1024], BF16) for i in range(2)]
                pz = [psum(pa, "pz%d" % i, [128, 512], F32) for i in range(3)]
                pq = psum(pa, "pq", [128, 1024], BF16)

                for kc in range(4):
                    S.dma("pool", wsb[:, 4 * kc:4 * kc + 4, :],
                          w_in.ap()[512 * kc:512 * kc + 512, :].rearrange("(k p) c -> p k c", p=128),
                          w=[wsb], key=wsb)
                S.dma("sp", gsb[:], gmix0.ap(), w=[gsb], key=gsb)
                S.pool(lambda e: e.memset(VA[:, :, :, 128:129], 1.0), w=[VA])

                for j in range(NT + 1):
                    n = 128 if j < NT else NS
                    c0 = j * 128
                    xsrc = xb.ap()[c0:c0 + 128, :] if j < NT else xs.ap()
                    csrc = cs_p.ap()[c0:c0 + 128, :] if j < NT else cs_s.ap()
                    X = xt[j % 2]
                    C = cst[j % 2]
                    KV = kvf[j % 2]

                    def load(jj):
                        nn = 128 if jj < NT else NS
                        cc = jj * 128
                        S.dma("sp", xt[jj % 2][:nn, :], xb.ap()[cc:cc + 128, :] if jj < NT else xs.ap(), w=[xt[jj % 2]], key=xt[jj % 2])
                        S.dma("sp", cst[jj % 2][:nn, :], cs_p.ap()[cc:cc + 128, :] if jj < NT else cs_s.ap(), w=[cst[jj % 2]], key=cst[jj % 2])
                    if j == 0:
                        load(0)
                    if j + 1 <= NT:
                        load(j + 1)
                    S.act(lambda e, X=X, n=n: e.activation(out=junk[:n, :], in_=X[:n, :], func=AF.Square,
                                                           accum_out=ss[:n, 0:1]), r=[X], w=[junk, ss])
                    S.act(lambda e, n=n: e.activation(out=ss[:n, 1:2], in_=ss[:n, 0:1], func=AF.Sqrt, scale=1.0 / D,
                                                      bias=epsb[:n, 0:1]), r=[ss, epsb], w=[ss])
                    S.dve(lambda e, n=n: e.reciprocal(out=ss[:n, 1:2], in_=ss[:n, 1:2]), r=[ss], w=[ss])
                    S.dve(lambda e, X=X, n=n: e.scalar_tensor_tensor(out=xn[:n, :], in0=X[:n, :], scalar=ss[:n, 1:2],
                                                                     in1=gsb[:n, :], op0=ALU.mult, op1=ALU.mult),
                          r=[X, ss, gsb], w=[xn])
                    for hb in range(2):
                        def tr(e, hb=hb, n=n):
                            ins = None
                            for k in range(8):
                                ins = e.transpose(pT[hb][:, k * 128:k * 128 + n], xn[:n, (hb * 8 + k) * 128:(hb * 8 + k + 1) * 128],
                                                  identb[:n, :n])
                            return ins
                        S.pe(tr, r=[xn, identb], w=[pT[hb]])
                        S.act(lambda e, hb=hb, n=n: e.copy(out=xnT[:, hb * 8:hb * 8 + 8, :n],
                                                           in_=pT[hb][:, :].rearrange("p (k t) -> p k t", k=8)[:, :, :n]),
                              r=[pT[hb]], w=[xnT])
                    cbs = [(0, 512), (512, 512), (1024, NCOL - 1024)]
                    for ci, (cb, cw) in enumerate(cbs):
                        def mm(e, ci=ci, cb=cb, cw=cw, n=n):
                            ins = None
                            for k in range(16):
                                ins = e.matmul(pz[ci][:n, :cw], xnT[:, k, :n], wsb[:, k, cb:cb + cw],
                                               start=(k == 0), stop=(k == 15))
                            return ins
                        S.pe(mm, r=[xnT, wsb], w=[pz[ci]])
                    cosb = lambda h, n=n, C=C: C[:n, 0:64].unsqueeze(1).broadcast_to([n, h, 64])
                    sinb = lambda h, n=n, C=C: C[:n, 64:128].unsqueeze(1).broadcast_to([n, h, 64])

                    def rope(src3, h, out_lo, out_hi, rd, wr, n=n, cosb=cosb, sinb=sinb):
                        a, b_, c_, d_ = rt
                        S.dve(lambda e: e.tensor_tensor(out=a[:n, :h, :], in0=src3[:, :, 0:64], in1=cosb(h), op=ALU.mult), r=rd + [C], w=[a])
                        S.dve(lambda e: e.tensor_tensor(out=b_[:n, :h, :], in0=src3[:, :, 64:128], in1=sinb(h), op=ALU.mult), r=rd + [C], w=[b_])
                        S.dve(lambda e: e.tensor_tensor(out=c_[:n, :h, :], in0=src3[:, :, 64:128], in1=cosb(h), op=ALU.mult), r=rd + [C], w=[c_])
                        S.dve(lambda e: e.tensor_tensor(out=d_[:n, :h, :], in0=src3[:, :, 0:64], in1=sinb(h), op=ALU.mult), r=rd + [C], w=[d_])
                        S.pool(lambda e: e.tensor_tensor(out=out_lo, in0=a[:n, :h, :], in1=b_[:n, :h, :], op=ALU.subtract), r=[a, b_], w=wr)
                        S.pool(lambda e: e.tensor_tensor(out=out_hi, in0=c_[:n, :h, :], in1=d_[:n, :h, :], op=ALU.add), r=[c_, d_], w=wr)

                    z0 = pz[0][:n, :].rearrange("p (h d) -> p h d", h=4)
                    rope(z0, 4, qkb[:n, 0:4, 0:64], qkb[:n, 0:4, 64:128], [pz[0]], [qkb])
                    z1 = pz[1][:n, 0:384].rearrange("p (h d) -> p h d", h=3)
                    rope(z1, 3, KV[:n, :, 0, 0:64], KV[:n, :, 0, 64:128], [pz[1]], [KV])
                    S.act(lambda e, n=n, KV=KV: e.copy(out=KV[:n, 0, 1, :], in_=pz[1][:n, 384:512]), r=[pz[1]], w=[KV])
                    S.act(lambda e, n=n, KV=KV: e.copy(out=KV[:n, 1:3, 1, :],
                                                       in_=pz[2][:n, 0:256].rearrange("p (h d) -> p h d", h=2)),
                          r=[pz[2]], w=[KV])
                    S.act(lambda e, n=n, j=j: e.copy(out=GL[:n, j, :], in_=pz[2][:n, 256:268]), r=[pz[2]], w=[GL])
                    S.pool(lambda e, n=n, KV=KV: e.tensor_copy(out=qkb[:n, 4:7, :], in_=KV[:n, :, 0, :]), r=[KV], w=[qkb])
                    S.pool(lambda e, n=n, KV=KV: e.tensor_copy(out=qkb[:n, 7, :], in_=KV[:n, 0, 1, :]), r=[KV], w=[qkb])
                    S.pool(lambda e, n=n, KV=KV, j=j: e.tensor_copy(out=VA[:n, j, :, 0:128], in_=KV[:n, 1:3, 1, :]), r=[KV], w=[VA])
                    dst = kv_p.ap()[c0:c0 + 128] if j < NT else kv_s.ap()
                    S.dma("sp", dst, KV[:n], r=[KV], key=KV, final=True)
                    if j == NT:
                        for s_ in range(4):
                            S.dma("sp", win_s.ap()[s_, 508:512], KV[4 * s_:4 * s_ + 4, 2], r=[KV], key=KV, final=True)
                    def tr2(e, n=n):
                        ins = None
                        for k in range(8):
                            ins = e.transpose(pq[:, k * 128:k * 128 + n], qkb[:n, k, :], identb[:n, :n])
                        return ins
                    S.pe(tr2, r=[qkb, identb], w=[pq])
                    pq3 = pq[:, :].rearrange("p (k t) -> p k t", k=8)
                    S.act(lambda e, n=n, c0=c0, pq3=pq3: e.copy(out=QT[:, :, c0:c0 + n], in_=pq3[:, 0:4, :n]), r=[pq], w=[QT])
                    S.act(lambda e, n=n, c0=c0, pq3=pq3: e.copy(out=KT[:, :, c0:c0 + n], in_=pq3[:, 4:7, :n]), r=[pq], w=[KT])
                    S.act(lambda e, n=n, c0=c0, pq3=pq3: e.copy(out=VcT[:, c0:c0 + n], in_=pq3[:, 7, :n]), r=[pq], w=[VcT])
                S.act(lambda e: e.copy(out=QTs[:, :, :], in_=QT[:, :, SEQ:SEQ + NS]), r=[QT], w=[QTs])
                S.act(lambda e: e.copy(out=KTs[:, :, :], in_=KT[:, :, SEQ:SEQ + NS]), r=[KT], w=[KTs])
                S.act(lambda e: e.copy(out=VAs[:, :, :], in_=VA[:NS, NT, :, :]), r=[VA], w=[VAs])
                S.barrier()

        phase_a()

        S.act(lambda e: e.activation(out=GL[:, :, :], in_=GL[:, :, :], func=AF.Sigmoid), r=[GL], w=[GL])

        def phase_c():
            with contextlib.ExitStack() as pc:
                w1sb = sbuf(pc, "w1sb", [128, 2, 32, 128], BF16)
                w2sb = sbuf(pc, "w2sb", [128, 2, 128], BF16)
                posb = sbuf(pc, "posb", [128, 64], BF16)
                b1sb = sbuf(pc, "b1sb", [128, 2], F32)
                biasb = sbuf(pc, "biasb", [128, 2], F32)
                I0 = sbuf(pc, "I0s", [128, 128], F32)
                Jm = sbuf(pc, "Jms", [128, 64], F32)
                trib = sbuf(pc, "trib", [128, 256], BF16)
                Ebig = sbuf(pc, "Ebigs", [64, SEQ], BF16)
                KcT = sbuf(pc, "KcT", [128, 256], BF16)
                VcA = sbuf(pc, "VcA", [128, 2, 196], BF16)
                hx = sbuf(pc, "hx", [128, 256], F32)
                ht = sbuf(pc, "ht", [128, 256], F32)
                hT = [sbuf(pc, "hT%d" % i, [128, 256], BF16) for i in range(2)]
                Eb = [sbuf(pc, "Eb%d" % i, [128, 4, 128], BF16) for i in range(3)]
                mk = sbuf(pc, "mk", [128, 128], BF16)
                rs = sbuf(pc, "rs", [128, 8], F32)
                ocat = sbuf(pc, "ocat", [128, 4, 128], F32)
                ocb = [sbuf(pc, "ocb%d" % i, [128, 512], BF16) for i in range(2)]
                imp = sbuf(pc, "imp", [128, 64], F32)
                vis = sbuf(pc, "vis", [128, 64], F32)
                frc = sbuf(pc, "frc", [128, 64], F32)
                col0 = sbuf(pc, "col0", [128, 64], F32)
                sc = [sbuf(pc, "sc%d" % i, [128, 64], F32) for i in range(2)]
                m8 = sbuf(pc, "m8", [128, 16], F32)
                selm = sbuf(pc, "selm", [128, 64], F32)
                nm = sbuf(pc, "nm", [128, 64], BF16)
                nmT = sbuf(pc, "nmT", [64, 4, 128], BF16)
                pS = [psum(pc, "pS%d" % i, [128, 512], F32) for i in range(3)]
                pO = [psum(pc, "pO%d" % i, [128, 512], F32) for i in range(4)]
                pM = psum(pc, "pM", [128, 1024], BF16)

                S.dma("pool", w1sb[:], cw1.ap().rearrange("e s d h -> d e s h"), w=[w1sb], key=w1sb)
                S.dma("pool", w2sb[:], cw2.ap().rearrange("e h d -> h e d"), w=[w2sb], key=w2sb)
                S.dma("pool", posb[:], posT.ap(), w=[posb], key=posb)
                S.dma("sp", b1sb[:], b1T.ap(), w=[b1sb], key=b1sb)
                S.dma("sp", I0[:], I0_d.ap(), w=[I0], key=I0)
                S.dma("sp", Jm[:], Jm_d.ap(), w=[Jm], key=Jm)
                S.dma("pool", trib[:], tri_d.ap(), w=[trib], key=trib)
                S.dma("pool", Ebig[:], Ebig_d.ap(), w=[Ebig], key=Ebig)
                S.dve(lambda e: e.memset(KcT[:], 0.0), w=[KcT])
                S.dve(lambda e: e.memset(VcA[:], 0.0), w=[VcA])
                S.dve(lambda e: e.memset(VcA[:, :, 128:129], 1.0), w=[VcA])
                S.dve(lambda e: e.memset(col0[:], 0.0), w=[col0])
                S.dve(lambda e: e.memset(col0[:, 0:1], 1.0), w=[col0])
                S.dma("pool", VcA[:, :, 129:193], cover_d.ap().rearrange("(t p) j -> p t j", p=128), w=[VcA], key=VcA)

                def bias_mm(e):
                    ins = None
                    for ee in range(2):
                        for s_ in range(32):
                            ins = e.matmul(pS[0][:, ee:ee + 1], w1sb[:, ee, s_, :], posb[:, ee * 32 + s_:ee * 32 + s_ + 1],
                                           start=(ee == 0 and s_ == 0), stop=(s_ == 31), skip_group_check=True)
                    return ins
                S.pe(bias_mm, r=[w1sb, posb], w=[pS[0]])
                S.dve(lambda e: e.tensor_tensor(out=biasb[:], in0=pS[0][:, 0:2], in1=b1sb[:], op=ALU.add), r=[pS[0], b1sb], w=[biasb])

                def compress(srcT, nblk, ncols_pad, KcT_out, Vc_out_fn):
                    for ee in range(2):
                        for c0 in range(0, nblk, 512):
                            cn = min(512, nblk - c0)
                            P = pS[(c0 // 512) % 2]

                            def hmm(e, ee=ee, c0=c0, cn=cn, P=P):
                                ins = None
                                src = srcT(ee)
                                for rs_ in range(32):
                                    lo = rs_ + 16 * c0
                                    ins = e.matmul(P[:, :cn], w1sb[:, ee, rs_, :], src[:, lo:lo + 16 * (cn - 1) + 1:16],
                                                   start=(rs_ == 0), stop=(rs_ == 31))
                                return ins
                            S.pe(hmm, r=[w1sb, KT, VcT], w=[P])
                            S.act(lambda e, ee=ee, cn=cn, P=P: e.activation(out=hx[:, :cn], in_=P[:, :cn], func=AF.Identity,
                                                                             bias=biasb[:, ee:ee + 1]), r=[P, biasb], w=[hx])
                            S.dve(lambda e, cn=cn: e.tensor_tensor(out=ht[:, :cn], in0=hx[:, :cn], in1=hx[:, :cn], op=ALU.mult), r=[hx], w=[ht])
                            S.dve(lambda e, cn=cn: e.tensor_scalar(out=ht[:, :cn], in0=ht[:, :cn], scalar1=0.044715, scalar2=1.0,
                                                                   op0=ALU.mult, op1=ALU.add), r=[ht], w=[ht])
                            S.dve(lambda e, cn=cn: e.tensor_tensor(out=ht[:, :cn], in0=ht[:, :cn], in1=hx[:, :cn], op=ALU.mult), r=[ht, hx], w=[ht])
                            S.act(lambda e, cn=cn: e.activation(out=ht[:, :cn], in_=ht[:, :cn], func=AF.Sigmoid, scale=1.5957691216),
                                  r=[ht], w=[ht])
                            H = hT[ee]
                            S.dve(lambda e, cn=cn, H=H: e.tensor_tensor(out=H[:, :cn], in0=hx[:, :cn], in1=ht[:, :cn], op=ALU.mult),
                                  r=[hx, ht], w=[H])
                            if ee == 0:
                                S.pe(lambda e, cn=cn, H=H: e.matmul(pO[0][:, :cn], w2sb[:, 0, :], H[:, :cn], start=True, stop=True),
                                     r=[w2sb, H], w=[pO[0]])
                                S.act(lambda e, cn=cn, c0=c0: e.copy(out=KcT_out[:, c0:c0 + cn], in_=pO[0][:, :cn]), r=[pO[0]], w=[KcT])
                            else:
                                for t0 in range(0, cn, 128):
                                    nb = min(128, cn - t0)
                                    S.pe(lambda e, nb=nb, t0=t0, H=H: e.matmul(pO[1][:nb, 0:128], H[:, t0:t0 + nb], w2sb[:, 1, :],
                                                                               start=True, stop=True), r=[w2sb, H], w=[pO[1]])
                                    S.act(lambda e, nb=nb, ct=(c0 + t0) // 128: e.copy(out=Vc_out_fn(ct, nb), in_=pO[1][:nb, 0:128]),
                                          r=[pO[1]], w=[VcA])

                compress(lambda ee: (KT[:, 0, :] if ee == 0 else VcT[:, :]), 255, 256, KcT, lambda ct, nb: VcA[:nb, ct, 0:128])

                HB = [(0, 0), (0, 256), (1, 0), (1, 256)]

                def pv_group(Pb, E, rhs_fn, ncol, first, r):
                    def f(e):
                        ins = None
                        for h in range(4):
                            bk, co = HB[h]
                            ins = e.matmul(Pb[bk][:, co:co + ncol], E[:, h, :], rhs_fn(), start=(first and h % 2 == 0), stop=False,
                                           skip_group_check=True)
                        return ins
                    S.pe(f, r=[E] + r, w=[Pb[0], Pb[1]])

                def finish_branch(Pb, i, br, first_branch):
                    for h in range(4):
                        bk, co = HB[h]
                        S.dve(lambda e, h=h, bk=bk, co=co: e.tensor_copy(out=rs[:, h:h + 1], in_=Pb[bk][:, co + 128:co + 129]),
                              r=[Pb[bk]], w=[rs])
                    S.dve(lambda e: e.tensor_scalar(out=rs[:, 0:4], in0=rs[:, 0:4], scalar1=1e-30, scalar2=None, op0=ALU.max), r=[rs], w=[rs])
                    S.dve(lambda e: e.reciprocal(out=rs[:, 0:4], in_=rs[:, 0:4]), r=[rs], w=[rs])
                    S.dve(lambda e, i=i, br=br: e.tensor_tensor(out=rs[:, 4:8], in0=rs[:, 0:4], in1=GL[:, i, 4 * br:4 * br + 4], op=ALU.mult),
                          r=[rs, GL], w=[rs])
                    for h in range(4):
                        bk, co = HB[h]
                        if first_branch:
                            S.dve(lambda e, h=h, bk=bk, co=co: e.tensor_scalar(out=ocat[:, h, :], in0=Pb[bk][:, co:co + 128],
                                                                               scalar1=rs[:, 4 + h:5 + h], scalar2=None, op0=ALU.mult),
                                  r=[Pb[bk], rs], w=[ocat])
                        else:
                            S.dve(lambda e, h=h, bk=bk, co=co: e.scalar_tensor_tensor(out=ocat[:, h, :], in0=Pb[bk][:, co:co + 128],
                                                                                      scalar=rs[:, 4 + h:5 + h], in1=ocat[:, h, :],
                                                                                      op0=ALU.mult, op1=ALU.add),
                                  r=[Pb[bk], rs, ocat], w=[ocat])

                pair_ctr = [0]

                def qk_exp(i, lhsT_fn, lr, with_mask):
                    x = pair_ctr[0] % 3
                    pair_ctr[0] += 1
                    P, E = pS[x], Eb[x]

                    def f(e):
                        ins = e.matmul(P[:, :], lhsT_fn(), QT[:, :, i * 128:i * 128 + 128], start=True, stop=not with_mask)
                        if with_mask is not False:
                            ins = e.matmul(P[:, :], Ebig[:, with_mask * 128:with_mask * 128 + 128], nmT[:, :, :], start=False, stop=True)
                        return ins
                    S.pe(f, r=[QT, Ebig, nmT] + lr, w=[P])
                    S.act(lambda e: e.activation(out=E[:, :, :], in_=P[:, :].rearrange("p (h q) -> p h q", h=4), func=AF.Exp, scale=SCALE),
                          r=[P], w=[E])
                    return E

                def mul_mask(E, m_ap, r):
                    S.dve(lambda e: e.tensor_tensor(out=E[:, :, :], in0=E[:, :, :], in1=m_ap.unsqueeze(1).broadcast_to([128, 4, 128]),
                                                    op=ALU.mult), r=[E] + r, w=[E])

                for i in range(NTQ):
                    Pb = pO[0:2]
                    n_ct = 1 if i < 16 else 2
                    for ct in range(n_ct):
                        E = qk_exp(i, lambda ct=ct: KcT[:, ct * 128:ct * 128 + 128], [KcT], False)
                        cval = float(128 * i - 2048 * ct - 31)
                        S.dve(lambda e, cval=cval: e.tensor_scalar(out=mk[:, :], in0=I0[:, :], scalar1=cval, scalar2=0.0,
                                                                   op0=ALU.add, op1=ALU.is_ge), r=[I0], w=[mk])
                        mul_mask(E, mk[:, :], [mk])
                        pv_group(Pb, E, lambda ct=ct: VcA[:, ct, 0:193], 193, ct == 0, [VcA])
                    for h in range(4):
                        bk, co = HB[h]
                        S.dve(lambda e, h=h, bk=bk, co=co: e.tensor_copy(out=rs[:, h:h + 1], in_=Pb[bk][:, co + 128:co + 129]),
                              r=[Pb[bk]], w=[rs])
                    S.dve(lambda e: e.tensor_scalar(out=rs[:, 0:4], in0=rs[:, 0:4], scalar1=1e-30, scalar2=None, op0=ALU.max), r=[rs], w=[rs])
                    S.dve(lambda e: e.reciprocal(out=rs[:, 0:4], in_=rs[:, 0:4]), r=[rs], w=[rs])
                    for h in range(4):
                        bk, co = HB[h]
                        if h == 0:
                            S.dve(lambda e, bk=bk, co=co: e.tensor_scalar(out=imp[:, :], in0=Pb[bk][:, co + 129:co + 193], scalar1=rs[:, 0:1],
                                                                          scalar2=None, op0=ALU.mult), r=[Pb[bk], rs], w=[imp])
                        else:
                            S.dve(lambda e, h=h, bk=bk, co=co: e.scalar_tensor_tensor(out=imp[:, :], in0=Pb[bk][:, co + 129:co + 193],
                                                                                      scalar=rs[:, h:h + 1], in1=imp[:, :],
                                                                                      op0=ALU.mult, op1=ALU.add),
                                  r=[Pb[bk], rs, imp], w=[imp])
                    finish_branch(Pb, i, 0, True)
                    if i == 0:
                        dbg("ocat_cmp", ocat, ocat[:, :, :], [128, 4, 128])
                        dbg("rs_cmp", rs, rs[:, :], [128, 8])
                        dbg("imp", imp, imp[:, :], [128, 64])
                        dbg("GL", GL, GL[:, :, :], [128, NT + 1, 12])
                    S.dve(lambda e, i=i: e.tensor_scalar(out=vis[:, :], in0=Jm[:, :], scalar1=float(2 * i), scalar2=None, op0=ALU.is_le),
                          r=[Jm], w=[vis])
                    if i >= 8:
                        S.dve(lambda e, i=i: e.tensor_scalar(out=frc[:, :], in0=Jm[:, :], scalar1=float(2 * i - 1), scalar2=None, op0=ALU.is_ge),
                              r=[Jm], w=[frc])
                        S.dve(lambda e: e.tensor_tensor(out=frc[:, :], in0=frc[:, :], in1=vis[:, :], op=ALU.mult), r=[frc, vis], w=[frc])
                        S.dve(lambda e: e.tensor_tensor(out=frc[:, :], in0=frc[:, :], in1=col0[:, :], op=ALU.max), r=[frc, col0], w=[frc])
                        S.dve(lambda e: e.tensor_tensor(out=sc[0][:, :], in0=imp[:, :], in1=vis[:, :], op=ALU.mult), r=[imp, vis], w=[sc[0]])
                        S.dve(lambda e: e.tensor_scalar(out=sc[1][:, :], in0=vis[:, :], scalar1=-1.0, scalar2=1e9, op0=ALU.add, op1=ALU.mult),
                              r=[vis], w=[sc[1]])
                        S.dve(lambda e: e.tensor_tensor(out=sc[0][:, :], in0=sc[0][:, :], in1=sc[1][:, :], op=ALU.add), r=[sc[0], sc[1]], w=[sc[0]])
                        S.dve(lambda e: e.scalar_tensor_tensor(out=sc[0][:, :], in0=frc[:, :], scalar=2e9, in1=sc[0][:, :],
                                                               op0=ALU.mult, op1=ALU.add), r=[frc, sc[0]], w=[sc[0]])
                        S.dve(lambda e: e.max(out=m8[:, 0:8], in_=sc[0][:, :]), r=[sc[0]], w=[m8])
                        S.dve(lambda e: e.match_replace(out=sc[1][:, :], in_to_replace=m8[:, 0:8], in_values=sc[0][:, :], imm_value=-3e38),
                              r=[sc[0], m8], w=[sc[1]])
                        S.dve(lambda e: e.max(out=m8[:, 8:16], in_=sc[1][:, :]), r=[sc[1]], w=[m8])
                        S.dve(lambda e: e.tensor_scalar(out=selm[:, :], in0=sc[0][:, :], scalar1=m8[:, 15:16], scalar2=None, op0=ALU.is_ge),
                              r=[sc[0], m8], w=[selm])
                        S.dve(lambda e: e.tensor_tensor(out=selm[:, :], in0=selm[:, :], in1=vis[:, :], op=ALU.mult), r=[selm, vis], w=[selm])
                        SM = selm
                    else:
                        SM = vis
                    S.dve(lambda e, SM=SM: e.tensor_scalar(out=nm[:, :], in0=SM[:, :], scalar1=-NEG, scalar2=NEG, op0=ALU.mult, op1=ALU.add),
                          r=[SM], w=[nm])
                    S.pe(lambda e: e.transpose(pM[:64, 0:128], nm[:, :], identb[:, :]), r=[nm, identb], w=[pM])
                    S.act(lambda e: e.copy(out=nmT[:, :, :], in_=pM[:64, 0:128].unsqueeze(1).broadcast_to([64, 4, 128])), r=[pM], w=[nmT])
                    Pb = pO[2:4]
                    for kt in range(i + 1):
                        E = qk_exp(i, lambda kt=kt: KT[:, 1, kt * 128:kt * 128 + 128], [KT], kt)
                        if kt == i:
                            mul_mask(E, trib[:, 0:128], [trib])
                        pv_group(Pb, E, lambda kt=kt: VA[:, kt, 0, 0:129], 129, kt == 0, [VA])
                    finish_branch(Pb, i, 1, False)
                    if i == 0:
                        dbg("ocat_sel", ocat, ocat[:, :, :], [128, 4, 128])
                        dbg("rs_sel", rs, rs[:, :], [128, 8])
                        dbg("nmT", nmT, nmT[:, :, :], [64, 4, 128], BF16)
                        dbg("Eb", Eb[0], Eb[0][:, :, :], [128, 4, 128], BF16)
                        dbg("Eb1", Eb[1], Eb[1][:, :, :], [128, 4, 128], BF16)
                    Pb = pO[0:2]
                    kts = [kt for kt in range(i - 4, i + 1) if kt >= 0]
                    for kt in kts:
                        E = qk_exp(i, lambda kt=kt: KT[:, 2, kt * 128:kt * 128 + 128], [KT], False)
                        if kt == i:
                            mul_mask(E, trib[:, 0:128], [trib])
                        elif kt == i - 4:
                            mul_mask(E, trib[:, 128:256], [trib])
                        pv_group(Pb, E, lambda kt=kt: VA[:, kt, 1, 0:129], 129, kt == kts[0], [VA])
                    finish_branch(Pb, i, 2, False)
                    OB = ocb[i % 2]
                    S.act(lambda e, OB=OB: e.copy(out=OB[:, :], in_=ocat[:, :, :].rearrange("p h d -> p (h d)")), r=[ocat], w=[OB])
                    S.dma("sp", o_loc.ap()[i * 128:i * 128 + 128, :], OB[:, :], r=[OB], w=[o_locT[i // 8]], key=OB)
                S.barrier()
        phase_c()

        pp.close()

        S.dma("sp", win_s.ap()[:, 0:508], wstate.ap()[:, 4:512], key=Buf("wcopy"), final=True)
        def phase_s():
            with contextlib.ExitStack() as ps_:
                NP = PAST // 128
                NB = 257
                w1sb = sbuf(ps_, "w1sb_s", [128, 2, 32, 128], BF16)
                w2sb = sbuf(ps_, "w2sb_s", [128, 2, 128], BF16)
                posb = sbuf(ps_, "posb_s", [128, 64], BF16)
                b1sb = sbuf(ps_, "b1sb_s", [128, 2], F32)
                biasb = sbuf(ps_, "biasb_s", [128, 2], F32)
                Ebig = sbuf(ps_, "Ebig_s", [64, SEQ], BF16)
                smask = sbuf(ps_, "smask_s", [16, 5, 16], BF16)
                sel16 = sbuf(ps_, "sel16_s", [16, 68], F32)
                sel16b = sbuf(ps_, "sel16b_s", [16, 4], BF16)
                hmask = sbuf(ps_, "hmask_s", [16, 12], F32)
                big = sbuf(ps_, "bigS", [128, 2, 16512], BF16)
                XcT = big
                KsT = big
                Vs = big
                Vs3 = big[:, 1, :].rearrange("p (g c) -> p g c", c=129)
                E2 = sbuf(ps_, "E2s", [64, 128], BF16)
                Rrep = sbuf(ps_, "Rrep_s", [32, 128], F32)
                KwT = sbuf(ps_, "KwT", [128, 512], BF16)
                Vw = sbuf(ps_, "Vw", [128, 4, 130], BF16)
                stg = [sbuf(ps_, "stg%d" % i, [128, 8192], F32) for i in range(2)]
                wst = sbuf(ps_, "wst", [128, 4, 2, 128], F32)
                KcT = sbuf(ps_, "KcT_s", [128, 1024], BF16)
                Vc = sbuf(ps_, "Vc_s", [128, 8, 392], BF16)
                hx = sbuf(ps_, "hx_s", [128, 512], F32)
                ht = sbuf(ps_, "ht_s", [128, 512], F32)
                hT = [sbuf(ps_, "hT_s%d" % i, [128, 512], BF16) for i in range(2)]
                Es = [sbuf(ps_, "Es%d" % i, [128, 16], BF16) for i in range(2)]
                rs = sbuf(ps_, "rs_s", [16, 8], F32)
                U = sbuf(ps_, "U_s", [16, 260], BF16)
                gt = sbuf(ps_, "gt_s", [16, 12], F32)
                gs = sbuf(ps_, "gs_s", [16, 4], F32)
                ocat = sbuf(ps_, "ocat_s", [16, 128], F32)
                ocb = sbuf(ps_, "ocb_s", [16, 128], BF16)
                imp = sbuf(ps_, "imp_s", [4, 260], F32)
                sc = [sbuf(ps_, "sc_s%d" % i, [4, 260], F32) for i in range(2)]
                m8 = sbuf(ps_, "m8_s", [4, 16], F32)
                nm = sbuf(ps_, "nm_s", [4, 320], BF16)
                nmT = sbuf(ps_, "nmT_s", [64, 5, 4, 4], BF16)
                pS = [psum(ps_, "qS%d" % i, [128, 512], F32) for i in range(2)]
                pO = [psum(ps_, "qO%d" % i, [128, 512], F32) for i in range(3)]
                pTf = [psum(ps_, "qT%d" % i, [128, 512], F32) for i in range(2)]
                pM = psum(ps_, "qM", [128, 1024], BF16)

                S.dma("pool", w1sb[:], cw1.ap().rearrange("e s d h -> d e s h"), w=[w1sb], key=w1sb)
                S.dma("pool", w2sb[:], cw2.ap().rearrange("e h d -> h e d"), w=[w2sb], key=w2sb)
                S.dma("pool", posb[:], posT.ap(), w=[posb], key=posb)
                S.dma("sp", b1sb[:], b1T.ap(), w=[b1sb], key=b1sb)
                S.dma("pool", Ebig[:], Ebig_d.ap(), w=[Ebig], key=Ebig)
                S.dma("pool", smask[:], smask_d.ap(), w=[smask], key=smask)
                S.dma("sp", sel16[:], sel16_d.ap(), w=[sel16], key=sel16)
                S.dma("pool", sel16b[:], sel16_d.ap()[:, 0:4], w=[sel16b], key=sel16b)
                S.dma("sp", hmask[:], hmask_d.ap(), w=[hmask], key=hmask)
                S.dve(lambda e: e.memset(Vw[:, :, 128:129], 1.0), w=[Vw])
                S.dve(lambda e: e.memset(KcT[:], 0.0), w=[KcT])
                S.dve(lambda e: e.memset(Vc[:], 0.0), w=[Vc])
                S.dve(lambda e: e.memset(Vc[:, 0:7, 128:129], 1.0), w=[Vc])
                S.dve(lambda e: e.memset(Vc[:127, 7, 128:129], 1.0), w=[Vc])
                S.dma("pool", Vc[:, :, 129:386], cover_s.ap().rearrange("(t p) j -> p t j", p=128), w=[Vc], key=Vc)

                def bias_mm(e):
                    ins = None
                    for ee in range(2):
                        for s_ in range(32):
                            ins = e.matmul(pS[0][:, ee:ee + 1], w1sb[:, ee, s_, :], posb[:, ee * 32 + s_:ee * 32 + s_ + 1],
                                           start=(ee == 0 and s_ == 0), stop=(s_ == 31), skip_group_check=True)
                    return ins
                S.pe(bias_mm, r=[w1sb, posb], w=[pS[0]])
                S.dve(lambda e: e.tensor_tensor(out=biasb[:], in0=pS[0][:, 0:2], in1=b1sb[:], op=ALU.add), r=[pS[0], b1sb], w=[biasb])

                pti = sbuf(ps_, "pti", [32, 16], I32)
                ptf = sbuf(ps_, "ptf", [32, 16], F32)
                idf = sbuf(ps_, "idf", [128, 16], F32)
                idi = sbuf(ps_, "idi", [128, 16], I32)
                pix = sbuf(ps_, "pix", [128, 1], F32)
                S.dma("sp", pti[:], ptabT.ap(), w=[pti], key=pti)
                S.dma("sp", pix[:], pidx_d.ap(), w=[pix], key=pix)
                S.dma("sp", Rrep[:], Rrep_d.ap(), w=[Rrep], key=Rrep)
                S.dma("pool", E2[:], E2_d.ap(), w=[E2], key=E2)
                S.dve(lambda e: e.tensor_copy(out=ptf[:], in_=pti[:]), r=[pti], w=[ptf])
                S.pe(lambda e: e.matmul(pS[1][:, 0:16], Rrep[:, :], ptf[:, :], start=True, stop=True), r=[Rrep, ptf], w=[pS[1]])
                S.dve(lambda e: e.tensor_scalar(out=idf[:], in0=pS[1][:, 0:16], scalar1=4.0, scalar2=pix[:, 0:1], op0=ALU.mult, op1=ALU.add),
                      r=[pS[1], pix], w=[idf])
                S.dve(lambda e: e.tensor_copy(out=idi[:], in_=idf[:]), r=[idf], w=[idi])
                gctr = [0]

                def qgather(cache, s_, q):
                    G = stg[gctr[0] % 2]
                    gctr[0] += 1
                    rows = cache.ap().rearrange("g (u t) e d -> (g u) (t e d)", u=4)
                    k = s_ * 4 + q
                    S.custom_dma("pool", lambda e: e.indirect_dma_start(
                        out=G[:, :], out_offset=None, in_=rows,
                        in_offset=bass.IndirectOffsetOnAxis(ap=idi[:, k:k + 1], axis=0)), r=[idi], w=[G], key=G)
                    return G

                for s_ in range(4):
                    ev = 0
                    for q in range(4):
                        G = qgather(ccache, s_, q)
                        G3 = G[:, :].rearrange("p (t e d) -> p t e d", t=32, e=2)
                        for ee in range(2):
                            for t0 in range(0, 32, 4):
                                P = pTf[ev % 2]

                                def trp(e, G3=G3, ee=ee, t0=t0, P=P):
                                    ins = None
                                    for j in range(4):
                                        ins = e.transpose(P[:, j * 128:j * 128 + 128], G3[:, t0 + j, ee, :], identf[:, :])
                                    return ins
                                S.pe(trp, r=[G, identf], w=[P])
                                dst = XcT[:, ee, 4096 * q:4096 * q + 4096].rearrange("d (p t) -> d t p", t=32)[:, t0:t0 + 4, :]
                                src = P[:, 0:512].rearrange("d (j p) -> d j p", j=4)
                                if ev % 2 == 0:
                                    S.act(lambda e, dst=dst, src=src: e.copy(out=dst, in_=src), r=[P], w=[XcT])
                                else:
                                    S.dve(lambda e, dst=dst, src=src: e.tensor_copy(out=dst, in_=src), r=[P], w=[XcT])
                                ev += 1
                    for ee in range(2):
                        for c0 in (0, 512):
                            cn = 512 if c0 == 0 else 511
                            P = pS[(c0 // 512) % 2]

                            def hmm(e, ee=ee, c0=c0, cn=cn, P=P):
                                ins = None
                                for rs_ in range(32):
                                    lo = rs_ + 16 * c0
                                    ins = e.matmul(P[:, :cn], w1sb[:, ee, rs_, :], XcT[:, ee, lo:lo + 16 * (cn - 1) + 1:16],
                                                   start=(rs_ == 0), stop=(rs_ == 31))
                                return ins
                            S.pe(hmm, r=[w1sb, XcT], w=[P])
                            S.act(lambda e, ee=ee, cn=cn, P=P: e.activation(out=hx[:, :cn], in_=P[:, :cn], func=AF.Identity,
                                                                             bias=biasb[:, ee:ee + 1]), r=[P, biasb], w=[hx])
                            S.dve(lambda e, cn=cn: e.tensor_tensor(out=ht[:, :cn], in0=hx[:, :cn], in1=hx[:, :cn], op=ALU.mult), r=[hx], w=[ht])
                            S.dve(lambda e, cn=cn: e.tensor_scalar(out=ht[:, :cn], in0=ht[:, :cn], scalar1=0.044715, scalar2=1.0,
                                                                   op0=ALU.mult, op1=ALU.add), r=[ht], w=[ht])
                            S.dve(lambda e, cn=cn: e.tensor_tensor(out=ht[:, :cn], in0=ht[:, :cn], in1=hx[:, :cn], op=ALU.mult), r=[ht, hx], w=[ht])
                            S.act(lambda e, cn=cn: e.activation(out=ht[:, :cn], in_=ht[:, :cn], func=AF.Sigmoid, scale=1.5957691216),
                                  r=[ht], w=[ht])
                            H = hT[ee]
                            S.dve(lambda e, cn=cn, H=H: e.tensor_tensor(out=H[:, :cn], in0=hx[:, :cn], in1=ht[:, :cn], op=ALU.mult),
                                  r=[hx, ht], w=[H])
                            if ee == 0:
                                S.pe(lambda e, cn=cn, H=H: e.matmul(pO[0][:, :cn], w2sb[:, 0, :], H[:, :cn], start=True, stop=True),
                                     r=[w2sb, H], w=[pO[0]])
                                S.act(lambda e, cn=cn, c0=c0: e.copy(out=KcT[:, c0:c0 + cn], in_=pO[0][:, :cn]), r=[pO[0]], w=[KcT])
                            else:
                                for t0 in range(0, cn, 128):
                                    nb = min(128, cn - t0)
                                    S.pe(lambda e, nb=nb, t0=t0, H=H: e.matmul(pO[1][:nb, 0:128], H[:, t0:t0 + nb], w2sb[:, 1, :],
                                                                               start=True, stop=True), r=[w2sb, H], w=[pO[1]])
                                    S.act(lambda e, nb=nb, ct=(c0 + t0) // 128: e.copy(out=Vc[:nb, ct, 0:128], in_=pO[1][:nb, 0:128]),
                                          r=[pO[1]], w=[Vc])

                    pc_ = [0]
                    Qs = QTs[:, :, 4 * s_:4 * s_ + 4]

                    def qk16(lhsT, nk, lr, maskchunk=None, Qs=Qs):
                        x = pc_[0] % 2
                        pc_[0] += 1
                        P, E = pS[x], Es[x]

                        def f(e):
                            ins = e.matmul(P[:nk, 0:16], lhsT, Qs, start=True, stop=(maskchunk is None))
                            if maskchunk is not None:
                                ch, kt = maskchunk
                                ins = e.matmul(P[:nk, 0:16], E2[:, :], nmT[:, ch, :, :], start=False, stop=True)
                            return ins
                        S.pe(f, r=[QTs, E2, nmT] + lr, w=[P])
                        S.act(lambda e: e.activation(out=E[:nk, :], in_=P[:nk, 0:16], func=AF.Exp, scale=SCALE), r=[P], w=[E])
                        return E

                    def pv16(Pacc, E, nk, rhs, ncol, first, last, r):
                        S.pe(lambda e: e.matmul(Pacc[:16, 0:ncol], E[:nk, :], rhs, start=first, stop=last), r=[E] + r, w=[Pacc])

                    def fin16(Pacc, br, first_branch):
                        S.dve(lambda e: e.tensor_scalar(out=rs[:, 0:1], in0=Pacc[:16, 128:129], scalar1=1e-30, scalar2=None, op0=ALU.max),
                              r=[Pacc], w=[rs])
                        S.dve(lambda e: e.reciprocal(out=rs[:, 0:1], in_=rs[:, 0:1]), r=[rs], w=[rs])
                        S.dve(lambda e: e.tensor_tensor(out=rs[:, 1:2], in0=rs[:, 0:1], in1=gs[:, br:br + 1], op=ALU.mult), r=[rs, gs], w=[rs])
                        if first_branch:
                            S.dve(lambda e: e.tensor_scalar(out=ocat[:, :], in0=Pacc[:16, 0:128], scalar1=rs[:, 1:2], scalar2=None, op0=ALU.mult),
                                  r=[Pacc, rs], w=[ocat])
                        else:
                            S.dve(lambda e: e.scalar_tensor_tensor(out=ocat[:, :], in0=Pacc[:16, 0:128], scalar=rs[:, 1:2], in1=ocat[:, :],
                                                                   op0=ALU.mult, op1=ALU.add), r=[Pacc, rs, ocat], w=[ocat])

                    S.pe(lambda e, s_=s_: e.matmul(pO[2][:16, 0:12], sel16[:, 4 + 16 * s_:4 + 16 * s_ + 16], GL[:16, NT, :], start=True, stop=True),
                         r=[sel16, GL], w=[pO[2]])
                    S.dve(lambda e: e.tensor_tensor(out=gt[:, :], in0=pO[2][:16, 0:12], in1=hmask[:, :], op=ALU.mult), r=[pO[2], hmask], w=[gt])
                    S.dve(lambda e: e.tensor_reduce(out=gs[:, 0:3], in_=gt[:, :].rearrange("p (b h) -> p b h", b=3), axis=AX.X, op=ALU.add),
                          r=[gt], w=[gs])

                    for ct in range(8):
                        E = qk16(KcT[:, ct * 128:ct * 128 + 128], 128, [KcT])
                        pv16(pO[0], E, 128, Vc[:, ct, 0:386], 386, ct == 0, ct == 7, [Vc])
                    fin16(pO[0], 0, True)
                    S.dve(lambda e: e.tensor_scalar(out=U[:, 0:257], in0=pO[0][:16, 129:386], scalar1=rs[:, 0:1], scalar2=None, op0=ALU.mult),
                          r=[pO[0], rs], w=[U])
                    S.pe(lambda e: e.matmul(pO[2][:4, 0:257], sel16b[:, 0:4], U[:, 0:257], start=True, stop=True), r=[sel16b, U], w=[pO[2]])
                    S.dve(lambda e: e.tensor_copy(out=sc[0][:, 0:257], in_=pO[2][:4, 0:257]), r=[pO[2]], w=[sc[0]])
                    S.dve(lambda e: e.memset(sc[0][:, 0:1], 2e9), w=[sc[0]])
                    S.dve(lambda e: e.memset(sc[0][:, 255:257], 2e9), w=[sc[0]])
                    S.dve(lambda e: e.max(out=m8[:, 0:8], in_=sc[0][:, 0:257]), r=[sc[0]], w=[m8])
                    S.dve(lambda e: e.match_replace(out=sc[1][:, 0:257], in_to_replace=m8[:, 0:8], in_values=sc[0][:, 0:257], imm_value=-3e38),
                          r=[sc[0], m8], w=[sc[1]])
                    S.dve(lambda e: e.max(out=m8[:, 8:16], in_=sc[1][:, 0:257]), r=[sc[1]], w=[m8])
                    S.dve(lambda e: e.memset(nm[:, :], 0.0), w=[nm])
                    S.dve(lambda e: e.tensor_scalar(out=sc[1][:, 0:257], in0=sc[0][:, 0:257], scalar1=m8[:, 15:16], scalar2=None, op0=ALU.is_ge),
                          r=[sc[0], m8], w=[sc[1]])
                    S.dve(lambda e: e.tensor_scalar(out=nm[:, 0:257], in0=sc[1][:, 0:257], scalar1=-NEG, scalar2=NEG, op0=ALU.mult, op1=ALU.add),
                          r=[sc[1]], w=[nm])

                    def trn(e):
                        ins = None
                        for ch in range(5):
                            ins = e.transpose(pM[:64, ch * 4:ch * 4 + 4], nm[:, ch * 64:ch * 64 + 64], identb[:4, :4])
                        return ins
                    S.pe(trn, r=[nm, identb], w=[pM])
                    S.act(lambda e: e.copy(out=nmT[:, :, :, :], in_=pM[:64, 0:20].rearrange("p (c q) -> p c q", c=5).unsqueeze(2)
                                           .broadcast_to([64, 5, 4, 4])), r=[pM], w=[nmT])

                    S.dve(lambda e: e.memset(Vs3[:, :, 128:129], 1.0), w=[Vs])
                    ev = 0
                    for q in range(4):
                        G = qgather(scache, s_, q)
                        G3 = G[:, :].rearrange("p (t e d) -> p t e d", t=32, e=2)
                        for t0 in range(0, 32, 4):
                            P = pTf[ev % 2]
                            ev += 1
                            kt0 = q * 32 + t0

                            def trk(e, G3=G3, t0=t0, P=P):
                                ins = None
                                for j in range(4):
                                    ins = e.transpose(P[:, j * 128:j * 128 + 128], G3[:, t0 + j, 0, :], identf[:, :])
                                return ins
                            S.pe(trk, r=[G, identf], w=[P])
                            S.act(lambda e, P=P, kt0=kt0: e.copy(out=KsT[:, 0, kt0 * 128:kt0 * 128 + 512], in_=P[:, 0:512]), r=[P], w=[KsT])
                            S.dve(lambda e, G3=G3, t0=t0, kt0=kt0: e.tensor_copy(out=Vs3[:, kt0:kt0 + 4, 0:128], in_=G3[:, t0:t0 + 4, 1, :]),
                                  r=[G], w=[Vs])
                    for kt in range(NP):
                        E = qk16(KsT[:, 0, kt * 128:kt * 128 + 128], 128, [KsT], maskchunk=(kt // 32, kt))
                        pv16(pO[1], E, 128, Vs3[:, kt, 0:129], 129, kt == 0, False, [Vs])
                    E = qk16(KTs[:, 1, :], 16, [KTs])
                    S.dve(lambda e, E=E, s_=s_: e.tensor_tensor(out=E[:16, :], in0=E[:16, :], in1=smask[:, s_, :], op=ALU.mult), r=[E, smask], w=[E])
                    pv16(pO[1], E, 16, VAs[:, 0, 0:129], 129, False, True, [VAs])
                    fin16(pO[1], 1, False)
                    S.dma("sp", wst[:, :, :, :], wstate.ap()[s_].rearrange("(t p) e d -> p t e d", p=128), w=[wst], key=wst)
                    for t_ in range(4):
                        P = pTf[t_ % 2]
                        S.pe(lambda e, t_=t_, P=P: e.transpose(P[:, 0:128], wst[:, t_, 0, :], identf[:, :]), r=[wst, identf], w=[P])
                        S.act(lambda e, t_=t_, P=P: e.copy(out=KwT[:, t_ * 128:t_ * 128 + 128], in_=P[:, 0:128]), r=[P], w=[KwT])
                    S.dve(lambda e: e.tensor_copy(out=Vw[:, :, 0:128], in_=wst[:, :, 1, :]), r=[wst], w=[Vw])
                    for t_ in range(4):
                        E = qk16(KwT[:, t_ * 128:t_ * 128 + 128], 128, [KwT])
                        if t_ == 0:
                            S.dve(lambda e, E=E: e.tensor_tensor(out=E[:16, :], in0=E[:16, :], in1=smask[:, 4, :], op=ALU.mult), r=[E, smask], w=[E])
                        pv16(pO[0], E, 128, Vw[:, t_, 0:129], 129, t_ == 0, False, [Vw])
                    E = qk16(KTs[:, 2, :], 16, [KTs])
                    S.dve(lambda e, E=E, s_=s_: e.tensor_tensor(out=E[:16, :], in0=E[:16, :], in1=smask[:, s_, :], op=ALU.mult), r=[E, smask], w=[E])
                    pv16(pO[0], E, 16, VAs[:, 1, 0:129], 129, False, True, [VAs])
                    fin16(pO[0], 2, False)
                    S.act(lambda e: e.copy(out=ocb[:, :], in_=ocat[:, :]), r=[ocat], w=[ocb])
                    for h in range(4):
                        S.dma("sp", o_locs.ap()[4 * s_:4 * s_ + 4, 128 * h:128 * h + 128], ocb[4 * h:4 * h + 4, :], r=[ocb], w=[o_locT[4]], key=ocb)
                S.barrier()

        if RUN_S:
            phase_s()

        RG = [[0, 1, 2, 3], [4, 5, 6, 7]]
        def allgather(src_ap, dst_ap, rbuf, wbuf_):
            S.custom_dma("pool", lambda e: e.collective_compute("AllGather", ALU.bypass, replica_groups=RG, ins=[src_ap], outs=[dst_ap]),
                         r=[rbuf], w=[wbuf_], key=wbuf_, inc=1)
        for c in range(4):
            allgather(o_loc.ap()[c * 1024:c * 1024 + 1024, :].rearrange("(p a) f -> p (a f)", a=8),
                      og.ap()[c * 4096:c * 4096 + 4096, :].rearrange("(q a) f -> q (a f)", a=8), o_locT[c], ogT)
        allgather(o_locs.ap().rearrange("r (b x) -> (r b) x", b=8), ogs.ap().rearrange("r (b x) -> (r b) x", b=8), o_locT[4], ogT)

        TBS = [(0, 512), (512, 512), (1024, 4)]

        def token_phase(layer):
            with contextlib.ExitStack() as pd:
                hT = sbuf(pd, "hres%d" % layer, [128, 16, NTOK], F32)
                actT = sbuf(pd, "actT%d" % layer, [128, 8, NTOK], BF16)
                xnT = sbuf(pd, "xnT_d%d" % layer, [128, 16, NTOK], BF16)
                wbuf = [sbuf(pd, "wbuf%d_%d" % (i, layer), [128, 16, 512], BF16) for i in range(2)]
                xst = sbuf(pd, "xst%d" % layer, [128, D], F32)
                ost = [sbuf(pd, "ost%d_%d" % (i, layer), [128, 2, 512], BF16) for i in range(2)]
                oidx = sbuf(pd, "oidx_s%d" % layer, [128, 72], I32)
                rstd = sbuf(pd, "rstd_d%d" % layer, [128, 512], F32)
                sq = sbuf(pd, "sq_d%d" % layer, [128, 512], F32)
                rl = [sbuf(pd, "rl%d_%d" % (i, layer), [128, 512], F32) for i in range(2)]
                gT = sbuf(pd, "gT_d%d" % layer, [128, 16], F32)
                onesf = sbuf(pd, "onesf%d" % layer, [128, 128], F32)
                epsb = sbuf(pd, "epsb_d%d" % layer, [128, 1], F32)
                pz = [psum(pd, "dz%d_%d" % (i, layer), [128, 512], F32) for i in range(4)]
                pM = psum(pd, "dM%d" % layer, [128, 1024], BF16)
                pT = psum(pd, "dT%d" % layer, [128, 512], F32)
                pn = psum(pd, "dn%d" % layer, [128, 512], F32)
                ctr = {"w": 0, "z": 0, "o": 0, "r": 0}

                if layer == 0:
                    S.dma("sp", oidx[:, 0:36], oidx_d.ap(), w=[oidx], key=oidx)
                else:
                    S.dma("sp", oidx[:, :], oidx2_d.ap(), w=[oidx], key=oidx)
                S.dve(lambda e: e.memset(onesf[:], 1.0 / D), w=[onesf])
                S.dve(lambda e: e.memset(epsb[:], RMS_EPS), w=[epsb])

                if layer == 0:
                    for t in range(9):
                        n = 128 if t < 8 else 4
                        S.dma("sp", xst[:n, :], x_tok.ap()[t * 128:t * 128 + n, :], w=[xst], key=xst)
                        for q in range(4):
                            def trx(e, q=q, n=n):
                                ins = None
                                for kk in range(4):
                                    k = q * 4 + kk
                                    ins = e.transpose(pT[:, kk * 128:kk * 128 + n], xst[:n, k * 128:k * 128 + 128], identf[:n, :n])
                                return ins
                            S.pe(trx, r=[xst, identf], w=[pT])
                            S.act(lambda e, q=q, n=n, t=t: e.copy(out=hT[:, q * 4:q * 4 + 4, t * 128:t * 128 + n],
                                                                 in_=pT[:, :].rearrange("p (k t) -> p k t", k=4)[:, :, :n]), r=[pT], w=[hT])
                else:
                    S.dma("sp", hT[:, :, :], hspill.ap().rearrange("p (k t) -> p k t", k=16), r=[hspT], w=[hT], key=hT)

                def load_act(gsrc, gsrc_s, gbuf, colfn):
                    for t in range(9):
                        n = 128 if t < 8 else 4
                        O_ = ost[ctr["o"] % 2]
                        ctr["o"] += 1
                        for jj in range(2):
                            S.custom_dma("pool", lambda e, O_=O_, jj=jj, t=t, col=colfn(t, jj): e.indirect_dma_start(
                                out=O_[:, jj, :], out_offset=None, in_=(gsrc if t < 8 else gsrc_s).ap(),
                                in_offset=bass.IndirectOffsetOnAxis(ap=oidx[:, col:col + 1], axis=0)),
                                r=[oidx, gbuf], w=[O_], key=O_)

                        def tro(e, O_=O_, n=n):
                            ins = None
                            for kk in range(8):
                                jj, q = kk // 4, kk % 4
                                ins = e.transpose(pM[:, kk * 128:kk * 128 + n], O_[:n, jj, q * 128:q * 128 + 128], identb[:n, :n])
                            return ins
                        S.pe(tro, r=[O_, identb], w=[pM])
                        S.act(lambda e, n=n, t=t: e.copy(out=actT[:, :, t * 128:t * 128 + n],
                                                         in_=pM[:, :].rearrange("p (k t) -> p k t", k=8)[:, :, :n]), r=[pM], w=[actT])

                def proj_blocks(Wd, row0):
                    blks = []
                    for cb in range(4):
                        def dma(wb, cb=cb):
                            S.dma("pool", wb[:, 0:8, :], Wd.ap()[row0:row0 + 1024, cb * 512:cb * 512 + 512].rearrange("(k p) c -> p k c", p=128),
                                  w=[wb], key=wb)

                        def comp(wb, cb=cb):
                            for cc in range(4):
                                for (t0, nt) in TBS:
                                    P = pz[ctr["z"] % 4]
                                    ctr["z"] += 1

                                    def mm(e, wb=wb, cc=cc, t0=t0, nt=nt, P=P):
                                        ins = None
                                        for k in range(8):
                                            ins = e.matmul(P[:, :nt], wb[:, k, cc * 128:cc * 128 + 128], actT[:, k, t0:t0 + nt],
                                                           start=(k == 0), stop=(k == 7))
                                        return ins
                                    S.pe(mm, r=[wb, actT], w=[P])
                                    c = cb * 4 + cc
                                    S.dve(lambda e, P=P, c=c, t0=t0, nt=nt: e.tensor_tensor(out=hT[:, c, t0:t0 + nt], in0=P[:, :nt],
                                                                                             in1=hT[:, c, t0:t0 + nt], op=ALU.add),
                                          r=[P, hT], w=[hT])
                        blks.append((dma, comp))
                    return blks

                def run_blocks(blks):
                    bufs = []
                    for i, (dma, comp) in enumerate(blks):
                        if i == 0:
                            wb0 = wbuf[ctr["w"] % 2]
                            ctr["w"] += 1
                            dma(wb0)
                            bufs.append(wb0)
                        if i + 1 < len(blks):
                            wbn = wbuf[ctr["w"] % 2]
                            ctr["w"] += 1
                            blks[i + 1][0](wbn)
                            bufs.append(wbn)
                        comp(bufs[i])

                def proj_accum(Wd, row0):
                    run_blocks(proj_blocks(Wd, row0))

                def rmsnorm_T(gain_d, out_fn, wlist):
                    S.dma("sp", gT[:], gain_d.ap(), w=[gT], key=gT)
                    for (t0, nt) in TBS:
                        for k in range(16):
                            S.act(lambda e, k=k, t0=t0, nt=nt: e.activation(out=sq[:, :nt], in_=hT[:, k, t0:t0 + nt], func=AF.Square),
                                  r=[hT], w=[sq])
                            S.pe(lambda e, k=k, nt=nt: e.matmul(pn[:, :nt], onesf[:, :], sq[:, :nt], start=(k == 0), stop=(k == 15)),
                                 r=[onesf, sq], w=[pn])
                        S.act(lambda e, nt=nt: e.activation(out=rstd[:, :nt], in_=pn[:, :nt], func=AF.Sqrt, bias=epsb[:, 0:1]),
                              r=[pn, epsb], w=[rstd])
                        S.dve(lambda e, nt=nt: e.reciprocal(out=rstd[:, :nt], in_=rstd[:, :nt]), r=[rstd], w=[rstd])
                        for k in range(16):
                            S.dve(lambda e, k=k, t0=t0, nt=nt: e.scalar_tensor_tensor(out=out_fn(k, t0, nt), in0=hT[:, k, t0:t0 + nt],
                                                                                      scalar=gT[:, k:k + 1], in1=rstd[:, :nt],
                                                                                      op0=ALU.mult, op1=ALU.mult),
                                  r=[hT, gT, rstd], w=wlist)

                def ffn(W1, W2):
                    blks = []
                    for hg in range(8):
                        for sub in range(2):
                            c0 = hg * 1024 + sub * 512

                            def dma(wb, c0=c0):
                                S.dma("pool", wb[:, :, :], W1.ap()[:, c0:c0 + 512].rearrange("(k p) c -> p k c", p=128), w=[wb], key=wb)

                            def comp(wb, sub=sub):
                                for fc in range(4):
                                    for (t0, nt) in TBS:
                                        P = pz[ctr["z"] % 4]
                                        ctr["z"] += 1

                                        def mm(e, wb=wb, fc=fc, t0=t0, nt=nt, P=P):
                                            ins = None
                                            for k in range(16):
                                                ins = e.matmul(P[:, :nt], wb[:, k, fc * 128:fc * 128 + 128], xnT[:, k, t0:t0 + nt],
                                                               start=(k == 0), stop=(k == 15))
                                            return ins
                                        S.pe(mm, r=[wb, xnT], w=[P])
                                        R_ = rl[ctr["r"] % 2]
                                        ctr["r"] += 1
                                        S.act(lambda e, P=P, R_=R_, nt=nt: e.activation(out=R_[:, :nt], in_=P[:, :nt], func=AF.Relu), r=[P], w=[R_])
                                        eng = S.pool if ctr["r"] % 2 == 0 else S.dve
                                        eng(lambda e, R_=R_, nt=nt, t0=t0, kk=sub * 4 + fc: e.tensor_tensor(out=actT[:, kk, t0:t0 + nt], in0=R_[:, :nt],
                                                                                                          in1=R_[:, :nt], op=ALU.mult),
                                            r=[R_], w=[actT])
                            blks.append((dma, comp))
                        blks.extend(proj_blocks(W2, hg * 1024))
                    run_blocks(blks)

                xn_out = lambda k, t0, nt: xnT[:, k, t0:t0 + nt]
                if layer == 0:
                    for g in range(2):
                        load_act(og, ogs, ogT, lambda t, jj, g=g: t * 4 + 2 * g + jj)
                        proj_accum(w_out0, g * 1024)
                    rmsnorm_T(gffn0T, xn_out, [xnT])
                    ffn(ffn1_0, ffn2_0)
                    rmsnorm_T(gmix1T, xn_out, [xnT])
                    for cp in range(8):
                        S.dma("sp", xg_in.ap()[cp].rearrange("p (k t) -> p k t", k=2), xnT[:, 2 * cp:2 * cp + 2, :], r=[xnT],
                              w=[xg_inT[cp]], key=Buf("xgk%d" % cp))
                        allgather(xg_in.ap()[cp], xg_all.ap()[cp], xg_inT[cp], xgT)
                    S.dma("sp", hspill.ap().rearrange("p (k t) -> p k t", k=16), hT[:, :, :], r=[hT], w=[hspT], key=hT)
                else:
                    for g in range(4):
                        load_act(og2, ogs2, og2T, lambda t, jj, g=g: t * 8 + 2 * g + jj)
                        proj_accum(w_out1, g * 1024)
                    rmsnorm_T(gffn1T, xn_out, [xnT])
                    ffn(ffn1_1, ffn2_1)
                    S.dma("sp", gT[:], gfinT.ap(), w=[gT], key=gT)
                    for (t0, nt) in TBS:
                        for k in range(16):
                            S.act(lambda e, k=k, t0=t0, nt=nt: e.activation(out=sq[:, :nt], in_=hT[:, k, t0:t0 + nt], func=AF.Square),
                                  r=[hT], w=[sq])
                            S.pe(lambda e, k=k, nt=nt: e.matmul(pn[:, :nt], onesf[:, :], sq[:, :nt], start=(k == 0), stop=(k == 15)),
                                 r=[onesf, sq], w=[pn])
                        S.act(lambda e, nt=nt: e.activation(out=rstd[:, :nt], in_=pn[:, :nt], func=AF.Sqrt, bias=epsb[:, 0:1]),
                              r=[pn, epsb], w=[rstd])
                        S.dve(lambda e, nt=nt: e.reciprocal(out=rstd[:, :nt], in_=rstd[:, :nt]), r=[rstd], w=[rstd])
                        for k in range(16):
                            S.dve(lambda e, k=k, t0=t0, nt=nt: e.scalar_tensor_tensor(out=hT[:, k, t0:t0 + nt], in0=hT[:, k, t0:t0 + nt],
                                                                                      scalar=gT[:, k:k + 1], in1=rstd[:, :nt],
                                                                                      op0=ALU.mult, op1=ALU.mult),
                                  r=[hT, gT, rstd], w=[hT])
                    for t in range(9):
                        n = 128 if t < 8 else 4
                        for q in range(4):
                            def trh(e, q=q, n=n, t=t):
                                ins = None
                                for kk in range(4):
                                    ins = e.transpose(pT[:n, kk * 128:kk * 128 + 128], hT[:, q * 4 + kk, t * 128:t * 128 + n], identf[:, :])
                                return ins
                            S.pe(trh, r=[hT, identf], w=[pT])
                            S.act(lambda e, q=q, n=n: e.copy(out=xst[:n, q * 512:q * 512 + 512], in_=pT[:n, :]), r=[pT], w=[xst])
                        S.dma("sp", y_tok.ap()[t * 128:t * 128 + n, :], xst[:n, :], r=[xst], key=xst, final=True)
                S.barrier()

        def phase_e():
            with contextlib.ExitStack() as pe_:
                wr = sbuf(pe_, "wr_s", [128, 16, 3072], BF16)
                xt = [sbuf(pe_, "ext%d" % i, [128, 16, 128], BF16) for i in range(2)]
                cst = [sbuf(pe_, "ecs%d" % i, [128, 256], F32) for i in range(2)]
                St = sbuf(pe_, "St", [128, 4, 512], F32)
                Sb = sbuf(pe_, "Sb", [128, 4, 512], BF16)
                Ss = [sbuf(pe_, "Ss%d" % i, [128, 4, 512], F32) for i in range(2)]
                Ssb = [sbuf(pe_, "Ssb%d" % i, [128, 4, 512], BF16) for i in range(4)]
                qkr = sbuf(pe_, "qkr", [128, 4, 256], BF16)
                Vb = sbuf(pe_, "Vb", [128, 2, 512], BF16)
                sg = sbuf(pe_, "sg", [128, 1024], F32)
                ob = [sbuf(pe_, "ob%d" % i, [128, 512], F32) for i in range(2)]
                jk = sbuf(pe_, "jk", [128, 512], BF16)
                rt = [sbuf(pe_, "ert%d" % i, [128, 4, 128], F32) for i in range(4)]
                qkT = sbuf(pe_, "qkT", [128, 8, 128], BF16)
                QdT = sbuf(pe_, "QdT", [128, 2, 2, 128], BF16)
                QdTm = sbuf(pe_, "QdTm", [128, 4, 2, 2, 16], BF16)
                Kd = sbuf(pe_, "Kd", [128, 2, 256], BF16)
                Kdm = sbuf(pe_, "Kdm", [16, 4, 2, 256], BF16)
                AT = sbuf(pe_, "AT", [128, 2, 128], BF16)
                decT = sbuf(pe_, "decT_s", [128, 2, 128], F32)
                qdecb = sbuf(pe_, "qdecb_s", [128, 2, 128], F32)
                kdec = sbuf(pe_, "kdec_s", [128, 2], F32)
                decTs = sbuf(pe_, "decTs_s", [16, 2, 16], F32)
                kdm = sbuf(pe_, "kdm_s", [16, 4, 2], F32)
                colmask = sbuf(pe_, "colmask_s", [128, 4, 16], F32)
                gnb = sbuf(pe_, "gnb_s", [128, 2, 512], F32)
                gpow = sbuf(pe_, "gpow_s", [128, 4], F32)
                qdecs = sbuf(pe_, "qdecs_s", [128, 2, 16], F32)
                st8 = sbuf(pe_, "st8", [128, 16], F32)
                og_ = [sbuf(pe_, "og_%d" % i, [128, 2, 512], BF16) for i in range(2)]
                pz = [psum(pe_, "ez%d" % i, [128, 512], F32) for i in range(2)]
                pq = psum(pe_, "eq", [128, 1024], BF16)
                pst = psum(pe_, "est", [128, 512], F32)
                po = [psum(pe_, "eo%d" % i, [128, 512], F32) for i in range(2)]
                pu = [psum(pe_, "eu%d" % i, [128, 512], F32) for i in range(2)]

                for kc in range(4):
                    for cq in range(3):
                        S.dma("pool", wr[:, 4 * kc:4 * kc + 4, 1024 * cq:1024 * cq + 1024],
                              wr_d.ap()[512 * kc:512 * kc + 512, 1024 * cq:1024 * cq + 1024].rearrange("(k p) c -> p k c", p=128),
                              w=[wr], key=wr)
                S.dma("sp", decT[:], decT_d.ap().rearrange("p (h i) -> p h i", h=2), w=[decT], key=decT)
                S.dma("sp", qdecb[:], qdecb_d.ap().rearrange("p (h i) -> p h i", h=2), w=[qdecb], key=qdecb)
                S.dma("sp", decTs[:], decTs_d.ap().rearrange("p (h i) -> p h i", h=2), w=[decTs], key=decTs)
                S.dma("sp", kdm[:], kdm_d.ap().rearrange("p (s h) -> p s h", s=4), w=[kdm], key=kdm)
                S.dma("sp", colmask[:], colmask_d.ap().rearrange("p (s i) -> p s i", s=4), w=[colmask], key=colmask)
                S.dma("sp", gnb[:], gnb_d.ap().rearrange("p (h e) -> p h e", h=2), w=[gnb], key=gnb)
                S.dma("sp", qdecs[:], qdecs_d.ap().rearrange("p (h i) -> p h i", h=2), w=[qdecs], key=qdecs)
                S.dma("sp", kdec[:], kdec_d.ap(), w=[kdec], key=kdec)
                S.dma("sp", gpow[:], gpow_d.ap(), w=[gpow], key=gpow)
                S.dve(lambda e: e.memset(St[:], 0.0), w=[St])
                S.dve(lambda e: e.memset(Sb[:], 0.0), w=[Sb])

                xg5 = xg_all.ap().rearrange("c (j p) (k t) -> c j p k t", j=4, k=2)

                def load_tile(T_):
                    X, C = xt[T_ % 2], cst[T_ % 2]
                    if T_ < NT:
                        j, tl = T_ // 8, T_ % 8
                        for cp in range(8):
                            S.dma("sp", X[:, 2 * cp:2 * cp + 2, :], xg5[cp, j, :, :, tl * 128:tl * 128 + 128], r=[xgT], w=[X], key=X)
                        S.dma("sp", C[:, :], cs_r.ap()[T_ * 128:T_ * 128 + 128, :], w=[C], key=C)
                    else:
                        for cp in range(8):
                            for j in range(4):
                                S.dma("sp", X[:, 2 * cp:2 * cp + 2, 4 * j:4 * j + 4], xg5[cp, j, :, :, 1024:1028], r=[xgT], w=[X], key=X)
                        S.dma("sp", C[:NS, :], cs_rs.ap(), w=[C], key=C)

                load_tile(0)
                zc = [0]
                for T_ in range(NT + 1):
                    n = 128 if T_ < NT else NS
                    X, C = xt[T_ % 2], cst[T_ % 2]
                    if T_ + 1 <= NT:
                        load_tile(T_ + 1)

                    def proj(cb, n=n, X=X):
                        P = pz[zc[0] % 2]
                        zc[0] += 1

                        def mm(e):
                            ins = None
                            for k in range(16):
                                ins = e.matmul(P[:n, :], X[:, k, :n], wr[:, k, cb * 512:cb * 512 + 512], start=(k == 0), stop=(k == 15))
                            return ins
                        S.pe(mm, r=[X, wr], w=[P])
                        return P

                    cosb = lambda n=n, C=C: C[:n, 0:128].unsqueeze(1).broadcast_to([n, 2, 128])
                    sinb = lambda n=n, C=C: C[:n, 128:256].unsqueeze(1).broadcast_to([n, 2, 128])
                    for half_ in range(2):
                        P = proj(half_)
                        z3 = P[:n, :].rearrange("p (h d) -> p h d", h=2)
                        a, b_, c_, d_ = rt

                        def emit_rope(z3=z3, P=P, half_=half_, n=n, cosb=cosb, sinb=sinb, C=C):
                            S.dve(lambda e: e.tensor_tensor(out=a[:n, :2, :], in0=z3[:, :, 0:128], in1=cosb(), op=ALU.mult), r=[P, C], w=[a])
                            S.dve(lambda e: e.tensor_tensor(out=b_[:n, :2, :], in0=z3[:, :, 128:256], in1=sinb(), op=ALU.mult), r=[P, C], w=[b_])
                            S.dve(lambda e: e.tensor_tensor(out=c_[:n, :2, :], in0=z3[:, :, 128:256], in1=cosb(), op=ALU.mult), r=[P, C], w=[c_])
                            S.dve(lambda e: e.tensor_tensor(out=d_[:n, :2, :], in0=z3[:, :, 0:128], in1=sinb(), op=ALU.mult), r=[P, C], w=[d_])
                            S.pool(lambda e: e.tensor_tensor(out=qkr[:n, 2 * half_:2 * half_ + 2, 0:128], in0=a[:n, :2, :], in1=b_[:n, :2, :],
                                                             op=ALU.subtract), r=[a, b_], w=[qkr])
                            S.pool(lambda e: e.tensor_tensor(out=qkr[:n, 2 * half_:2 * half_ + 2, 128:256], in0=c_[:n, :2, :], in1=d_[:n, :2, :],
                                                             op=ALU.add), r=[c_, d_], w=[qkr])
                        emit_rope()
                    for hh in range(2):
                        P = proj(2 + hh)
                        S.act(lambda e, P=P, hh=hh, n=n: e.copy(out=Vb[:n, hh, :], in_=P[:n, :]), r=[P], w=[Vb])
                    for hh in range(2):
                        P = proj(4 + hh)
                        S.act(lambda e, P=P, hh=hh, n=n: e.activation(out=sg[:n, hh * 512:hh * 512 + 512], in_=P[:n, :], func=AF.Silu), r=[P], w=[sg])
                    def trqk(e, n=n):
                        ins = None
                        for a_ in range(4):
                            for c in range(2):
                                ins = e.transpose(pq[:, (a_ * 2 + c) * 128:(a_ * 2 + c) * 128 + n], qkr[:n, a_, c * 128:c * 128 + 128], identb[:n, :n])
                        return ins
                    S.pe(trqk, r=[qkr, identb], w=[pq])
                    S.act(lambda e, n=n: e.copy(out=qkT[:, :, :n], in_=pq[:, :].rearrange("p (a t) -> p a t", a=8)[:, :, :n]), r=[pq], w=[qkT])
                    qT4 = qkT[:, 0:4, :].rearrange("p (h c) t -> p h c t", h=2)
                    kT4 = qkT[:, 4:8, :].rearrange("p (h c) t -> p h c t", h=2)

                    if T_ < NT:
                        S.dve(lambda e: e.tensor_tensor(out=QdT[:, :, :, :], in0=qT4, in1=qdecb[:, :, :].unsqueeze(2).broadcast_to([128, 2, 2, 128]),
                                                        op=ALU.mult), r=[qkT, qdecb], w=[QdT])
                        S.dve(lambda e: e.tensor_tensor(out=Kd[:, :, :], in0=qkr[:, 2:4, :], in1=kdec[:, :].unsqueeze(2).broadcast_to([128, 2, 256]),
                                                        op=ALU.mult), r=[qkr, kdec], w=[Kd])
                        def smm(e):
                            ins = None
                            for hh in range(2):
                                for c in range(2):
                                    ins = e.matmul(pst[:, hh * 128:hh * 128 + 128], kT4[:, hh, c, :], qT4[:, hh, c, :], start=(hh == 0 and c == 0),
                                                   stop=(c == 1), skip_group_check=True)
                            return ins
                        S.pe(smm, r=[qkT], w=[pst])
                        S.dve(lambda e: e.tensor_tensor(out=AT[:, :, :], in0=pst[:, 0:256].rearrange("p (h i) -> p h i", h=2), in1=decT[:, :, :],
                                                        op=ALU.mult), r=[pst, decT], w=[AT])
                        for hh in range(2):
                            def omm(e, hh=hh):
                                e.matmul(po[hh][:, :], AT[:, hh, :], Vb[:, hh, :], start=True, stop=False)
                                e.matmul(po[hh][:, :], QdT[:, hh, 0, :], Sb[:, hh * 2, :], start=False, stop=False)
                                return e.matmul(po[hh][:, :], QdT[:, hh, 1, :], Sb[:, hh * 2 + 1, :], start=False, stop=True)
                            S.pe(omm, r=[AT, Vb, QdT, Sb], w=[po[hh]])
                        for hh in range(2):
                            for c in range(2):
                                U_ = pu[c]
                                S.pe(lambda e, hh=hh, c=c, U_=U_: e.matmul(U_[:, :], Kd[:, hh, c * 128:c * 128 + 128], Vb[:, hh, :], start=True, stop=True),
                                     r=[Kd, Vb], w=[U_])
                                S.dve(lambda e, hh=hh, c=c, U_=U_: e.scalar_tensor_tensor(out=St[:, hh * 2 + c, :], in0=St[:, hh * 2 + c, :],
                                                                                         scalar=gpow[:, hh:hh + 1], in1=U_[:, :], op0=ALU.mult, op1=ALU.add),
                                      r=[St, gpow, U_], w=[St])
                                S.act(lambda e, hh=hh, c=c: e.copy(out=Sb[:, hh * 2 + c, :], in_=St[:, hh * 2 + c, :]), r=[St], w=[Sb])
                    else:
                        S.dve(lambda e: e.tensor_tensor(out=QdT[:, :, :, 0:NS], in0=qT4[:, :, :, 0:NS],
                                                        in1=qdecs[:, :, :].unsqueeze(2).broadcast_to([128, 2, 2, NS]), op=ALU.mult),
                              r=[qkT, qdecs], w=[QdT])
                        for s_ in range(4):
                            S.dve(lambda e, s_=s_: e.tensor_tensor(out=QdTm[:, s_, :, :, :], in0=QdT[:, :, :, 0:NS],
                                                                   in1=colmask[:, s_, :].unsqueeze(1).unsqueeze(2).broadcast_to([128, 2, 2, 16]),
                                                                   op=ALU.mult), r=[QdT, colmask], w=[QdTm])
                            S.dve(lambda e, s_=s_: e.tensor_tensor(out=Kdm[:, s_, :, :], in0=qkr[:NS, 2:4, :],
                                                                   in1=kdm[:, s_, :].unsqueeze(2).broadcast_to([NS, 2, 256]), op=ALU.mult),
                                  r=[qkr, kdm], w=[Kdm])

                        def smm_s(e):
                            ins = None
                            for hh in range(2):
                                for c in range(2):
                                    ins = e.matmul(pst[:NS, hh * 16:hh * 16 + 16], kT4[:, hh, c, 0:NS], qT4[:, hh, c, 0:NS], start=(hh == 0 and c == 0),
                                                   stop=(c == 1), skip_group_check=True)
                            return ins
                        S.pe(smm_s, r=[qkT], w=[pst])
                        S.dve(lambda e: e.tensor_tensor(out=AT[:NS, :, 0:NS], in0=pst[:NS, 0:32].rearrange("p (h i) -> p h i", h=2), in1=decTs[:, :, :],
                                                        op=ALU.mult), r=[pst, decTs], w=[AT])
                        for s_ in range(4):
                            SS = Ss[s_ % 2]
                            S.dma("sp", SS[:, :, :], sret_d.ap()[s_].rearrange("h (c p) e -> p (h c) e", p=128), w=[SS], key=SS)
                            S.act(lambda e, s_=s_, SS=SS: e.copy(out=Ssb[s_][:, :, :], in_=SS[:, :, :]), r=[SS], w=[Ssb[s_]])
                            for hh in range(2):
                                for c in range(2):
                                    U_ = pu[c]
                                    S.pe(lambda e, hh=hh, c=c, U_=U_, s_=s_: e.matmul(U_[:, :], Kdm[:, s_, hh, c * 128:c * 128 + 128], Vb[:NS, hh, :],
                                                                                      start=True, stop=True), r=[Kdm, Vb], w=[U_])
                                    S.dve(lambda e, hh=hh, c=c, U_=U_, SS=SS: e.scalar_tensor_tensor(out=SS[:, hh * 2 + c, :], in0=SS[:, hh * 2 + c, :],
                                                                                                    scalar=gpow[:, 2 + hh:3 + hh], in1=U_[:, :],
                                                                                                    op0=ALU.mult, op1=ALU.add),
                                          r=[SS, gpow, U_], w=[SS])
                            S.dma("sp", ret_s.ap()[s_].rearrange("h (c p) e -> p (h c) e", p=128), SS[:, :, :], r=[SS], key=SS, final=True)
                        for hh in range(2):
                            def omm_s(e, hh=hh):
                                ins = e.matmul(po[hh][:NS, :], AT[:NS, hh, 0:NS], Vb[:NS, hh, :], start=True, stop=False)
                                for s_ in range(4):
                                    for c in range(2):
                                        ins = e.matmul(po[hh][:NS, :], QdTm[:, s_, hh, c, :], Ssb[s_][:, hh * 2 + c, :], start=False,
                                                       stop=(s_ == 3 and c == 1))
                                return ins
                            S.pe(omm_s, r=[AT, Vb, QdTm] + Ssb, w=[po[hh]])

                    OG = og_[T_ % 2]
                    for hh in range(2):
                        OB = ob[hh]
                        S.act(lambda e, hh=hh, OB=OB, n=n: e.activation(out=OB[:n, :], in_=po[hh][:n, :], func=AF.Identity, accum_out=st8[:n, hh * 8:hh * 8 + 1]),
                              r=[po[hh]], w=[OB, st8])
                        S.act(lambda e, hh=hh, OB=OB, n=n: e.activation(out=jk[:n, :], in_=OB[:n, :], func=AF.Square, accum_out=st8[:n, hh * 8 + 1:hh * 8 + 2]),
                              r=[OB], w=[jk, st8])
                        o8 = hh * 8
                        S.dve(lambda e, o8=o8, n=n: e.tensor_scalar(out=st8[:n, o8 + 2:o8 + 4], in0=st8[:n, o8:o8 + 2], scalar1=1.0 / 512, scalar2=None,
                                                                    op0=ALU.mult), r=[st8], w=[st8])
                        S.dve(lambda e, o8=o8, n=n: e.tensor_tensor(out=st8[:n, o8 + 4:o8 + 5], in0=st8[:n, o8 + 2:o8 + 3], in1=st8[:n, o8 + 2:o8 + 3],
                                                                    op=ALU.mult), r=[st8], w=[st8])
                        S.dve(lambda e, o8=o8, n=n: e.tensor_tensor(out=st8[:n, o8 + 5:o8 + 6], in0=st8[:n, o8 + 3:o8 + 4], in1=st8[:n, o8 + 4:o8 + 5],
                                                                    op=ALU.subtract), r=[st8], w=[st8])
                        S.dve(lambda e, o8=o8, n=n: e.tensor_scalar(out=st8[:n, o8 + 5:o8 + 6], in0=st8[:n, o8 + 5:o8 + 6], scalar1=0.0, scalar2=1e-5,
                                                                    op0=ALU.max, op1=ALU.add), r=[st8], w=[st8])
                        S.act(lambda e, o8=o8, n=n: e.activation(out=st8[:n, o8 + 6:o8 + 7], in_=st8[:n, o8 + 5:o8 + 6], func=AF.Sqrt), r=[st8], w=[st8])
                        S.dve(lambda e, o8=o8, n=n: e.reciprocal(out=st8[:n, o8 + 6:o8 + 7], in_=st8[:n, o8 + 6:o8 + 7]), r=[st8], w=[st8])
                        S.dve(lambda e, o8=o8, OB=OB, n=n: e.tensor_scalar(out=OB[:n, :], in0=OB[:n, :], scalar1=st8[:n, o8 + 2:o8 + 3],
                                                                           scalar2=st8[:n, o8 + 6:o8 + 7], op0=ALU.subtract, op1=ALU.mult),
                              r=[OB, st8], w=[OB])
                        S.pool(lambda e, hh=hh, OB=OB, n=n: e.tensor_tensor(out=OB[:n, :], in0=OB[:n, :], in1=gnb[:n, hh, :], op=ALU.mult), r=[OB, gnb], w=[OB])
                        S.pool(lambda e, hh=hh, OB=OB, OG=OG, n=n: e.tensor_tensor(out=OG[:n, hh, :], in0=OB[:n, :], in1=sg[:n, hh * 512:hh * 512 + 512],
                                                                                   op=ALU.mult), r=[OB, sg], w=[OG])
                        if T_ < NT:
                            c_, tl = T_ // 8, T_ % 8
                            S.dma("sp", o_r.ap()[c_, hh, tl * 128:tl * 128 + 128, :], OG[:, hh, :], r=[OG], w=[o_rT[c_][hh]], key=OG)
                        else:
                            S.dma("sp", o_rs.ap()[hh], OG[:NS, hh, :], r=[OG], w=[o_rsT[hh]], key=OG)
                    if T_ < NT and T_ % 8 == 7:
                        c_ = T_ // 8
                        for hh in range(2):
                            base = ((c_ * 2 + hh) * 4) * 1024
                            allgather(o_r.ap()[c_, hh].rearrange("(p a) f -> p (a f)", a=8),
                                      og2.ap()[base:base + 4096, :].rearrange("(q a) f -> q (a f)", a=8), o_rT[c_][hh], og2T)
                    if T_ == NT - 1:
                        S.dma("sp", ret_p.ap().rearrange("h (c p) e -> p (h c) e", p=128), St[:, :, :], r=[St], key=St, final=True)
                for hh in range(2):
                    allgather(o_rs.ap()[hh].rearrange("r (b x) -> (r b) x", b=8),
                              ogs2.ap()[hh * 64:hh * 64 + 64, :].rearrange("r (b x) -> (r b) x", b=8), o_rsT[hh], og2T)
                S.barrier()

        if RUN_T0:
            token_phase(0)
        if RUN_E:
            phase_e()
        if RUN_T1:
            token_phase(1)

        S.emit()
    return nc


def _col_slice(g):
    cols = list(range(512 * g, 512 * g + 512))
    for e in (0, 1):
        for br in range(3):
            base = 2048 + ((br * 2 + e) * 4 + g) * 128
            cols += list(range(base, base + 128))
    for br in range(3):
        base = 2048 + 3072 + br * 16 + 4 * g
        cols += list(range(base, base + 4))
    return np.array(cols)


def _consts():
    c0 = np.arange(256)[:, None] * 16
    j0 = np.arange(64)[None, :] * 64
    cover = ((c0 < j0 + 64) & (c0 + 32 > j0)).astype(np.float32)
    cover[255] = 0.0
    p = np.arange(128)[:, None]
    col = np.arange(128)[None, :]
    I0 = (col - 16 * p).astype(np.float32)
    Jm = (np.arange(64)[None, :] - (p >= 64)).astype(np.float32)
    tri = np.concatenate([(p <= col), (col < p)], axis=1).astype(np.float32)
    Ebig = (np.arange(SEQ)[None, :] // 64 == np.arange(64)[:, None]).astype(np.float32)
    key = np.arange(16)
    hq = np.arange(16)
    smask = np.zeros((16, 5, 16), np.float32)
    for s_ in range(4):
        smask[:, s_, :] = ((key[:, None] // 4 == s_) & (key[:, None] % 4 <= hq[None, :] % 4))
    smask[:, 4, :] = (key[:, None] > hq[None, :] % 4)
    sel16 = np.zeros((16, 68), np.float32)
    sel16[:, 0:4] = (hq[:, None] % 4 == np.arange(4)[None, :])
    for s_ in range(4):
        sel16[:, 4 + 16 * s_:20 + 16 * s_] = (key[:, None] == 4 * s_ + hq[None, :] % 4)
    hmask = (np.arange(12)[None, :] % 4 == hq[:, None] // 4).astype(np.float32)
    c0s = np.arange(1024)[:, None] * 16
    j0s = np.arange(257)[None, :] * 64
    cover_s = ((c0s < j0s + 64) & (c0s + 32 > j0s)).astype(np.float32)
    cover_s[1023] = 0.0
    return {"cover": cover, "I0": I0, "Jm": Jm, "tri": tri, "Ebig": Ebig, "smask": smask, "sel16": sel16,
            "hmask": hmask, "cover_s": cover_s, "cs_r": rope_table(np.arange(SEQ), 128),
            "cs_rs": rope_table(np.tile(PAST + np.arange(4), 4), 128)}


CONST = _consts()


def _ret_cols(r):
    cols = []
    for base, w in ((0, 256), (2048, 256), (4096, 512), (8192, 512)):
        for hh in range(2):
            h = 2 * r + hh
            cols += list(range(base + w * h, base + w * h + w))
    return np.array(cols)


def _ret_consts(r):
    out = {}
    i = np.arange(128, dtype=np.float64)
    decT = np.zeros((128, 2, 128), np.float64)
    qdecb = np.zeros((128, 2, 128), np.float64)
    kdec = np.zeros((128, 2), np.float64)
    decTs = np.zeros((16, 2, 16), np.float64)
    kdm = np.zeros((16, 4, 2), np.float64)
    gpow = np.zeros((128, 4), np.float64)
    qdecs = np.zeros((128, 2, 16), np.float64)
    t16 = np.arange(16)
    for hh in range(2):
        h = 2 * r + hh
        lg = np.log(np.float32(1.0) - np.float32(2.0) ** np.float32(-5.0 - h)).astype(np.float64)
        rel = i[None, :] - i[:, None]
        decT[:, hh, :] = np.where(rel >= 0, np.exp(np.maximum(rel, 0) * lg), 0.0) / 16.0
        qdecb[:, hh, :] = np.exp((i + 1.0) * lg)[None, :]
        kdec[:, hh] = np.exp((127.0 - i) * lg) / 16.0
        rel4 = (t16[None, :] % 4) - (t16[:, None] % 4)
        same = (t16[None, :] // 4) == (t16[:, None] // 4)
        decTs[:, hh, :] = np.where(same & (rel4 >= 0), np.exp(np.maximum(rel4, 0) * lg), 0.0) / 16.0
        for s_ in range(4):
            kdm[:, s_, hh] = np.where(t16 // 4 == s_, np.exp((3.0 - t16 % 4) * lg), 0.0) / 16.0
        gpow[:, hh] = np.exp(128.0 * lg)
        gpow[:, 2 + hh] = np.exp(4.0 * lg)
        qdecs[:, hh, :] = np.exp((t16 % 4 + 1.0) * lg)[None, :]
    colmask = np.zeros((128, 4, 16), np.float32)
    for s_ in range(4):
        colmask[:, s_, :] = (t16 // 4 == s_)[None, :]
    out["decT"] = decT.reshape(128, 256).astype(np.float32)
    out["qdecb"] = qdecb.reshape(128, 256).astype(np.float32)
    out["kdec"] = kdec.astype(np.float32)
    out["decTs"] = decTs.reshape(16, 32).astype(np.float32)
    out["kdm"] = kdm.reshape(16, 8).astype(np.float32)
    out["gpow"] = gpow.astype(np.float32)
    out["qdecs"] = qdecs.reshape(128, 32).astype(np.float32)
    out["colmask"] = colmask.reshape(128, 64)
    return out


def _oidx2(r):
    p = np.arange(128)
    idx = np.zeros((128, 72), np.int32)
    for t in range(9):
        for j in range(4):
            for hh in range(2):
                if t < 8:
                    idx[:, t * 8 + 2 * j + hh] = ((r * 2 + hh) * 4 + j) * 1024 + t * 128 + p
                else:
                    idx[:, t * 8 + 2 * j + hh] = (hh * 4 + j) * NS + 4 * r + np.minimum(p, 3)
    return idx


def _oidx(r):
    p = np.arange(128)
    idx = np.zeros((128, 36), np.int32)
    for t in range(9):
        for j in range(4):
            if t < 8:
                idx[:, t * 4 + j] = r * 4096 + j * 1024 + t * 128 + p
            else:
                idx[:, t * 4 + j] = j * NS + 4 * r + np.minimum(p, 3)
    return idx


def make_in_maps(inp):
    maps = []
    ident = np.eye(128, dtype=np.float32)
    cs_p = rope_table(np.arange(SEQ), 64)
    cs_s = rope_table(np.tile(PAST + np.arange(4), 4), 64)
    for c in range(8):
        b, r = c // 4, c % 4
        m = {
            "xb": np.ascontiguousarray(inp["x_prompt"][b]),
            "xs": np.ascontiguousarray(inp["x_sample"][4 * b:4 * b + 4].reshape(NS, D)),
            "w_in": np.ascontiguousarray(inp["nsa_w_in"][0][:, _col_slice(r)]),
            "gmix0": np.ascontiguousarray(np.broadcast_to(inp["norm_mix"][0][None, :], (128, D))),
            "cs_p": cs_p, "cs_s": cs_s, "ident": ident,
            "cw1": inp["nsa_cmp_w1"][0], "cw2": inp["nsa_cmp_w2"][0],
            "posT": np.ascontiguousarray(inp["nsa_cmp_pos"][0].reshape(64, 128).T),
            "b1T": np.ascontiguousarray(inp["nsa_cmp_b1"][0].T),
            "x_tok": np.ascontiguousarray(np.concatenate([inp["x_prompt"][b, 1024 * r:1024 * r + 1024], inp["x_sample"][c]], axis=0)),
            "oidx": _oidx(r),
            "w_out0": inp["nsa_w_out"][0], "ffn1_0": inp["ffn_w1"][0], "ffn2_0": inp["ffn_w2"][0],
            "gffn0T": np.ascontiguousarray(inp["norm_ffn"][0].reshape(16, 128).T),
            "w_out1": inp["ret_w_out"][0], "ffn1_1": inp["ffn_w1"][1], "ffn2_1": inp["ffn_w2"][1],
            "gmix1T": np.ascontiguousarray(inp["norm_mix"][1].reshape(16, 128).T),
            "gffn1T": np.ascontiguousarray(inp["norm_ffn"][1].reshape(16, 128).T),
            "gfinT": np.ascontiguousarray(inp["norm_final"].reshape(16, 128).T),
            "wr": np.ascontiguousarray(inp["ret_w_in"][0][:, _ret_cols(r)]),
            "cs_r": CONST["cs_r"], "cs_rs": CONST["cs_rs"],
            "gnb": np.ascontiguousarray(np.broadcast_to(inp["ret_gn"][0][1024 * r:1024 * r + 1024][None, :], (128, 1024))),
            "sret": np.ascontiguousarray(inp["state_ret"][0][4 * b:4 * b + 4, 2 * r:2 * r + 2]),
            "oidx2": _oidx2(r),
            **_ret_consts(r),
            "ccache": np.ascontiguousarray(inp["cache_cmp_kv"][0][:, :, :, r, :]),
            "scache": np.ascontiguousarray(inp["cache_sel_kv"][0][:, :, :, r, :]),
            "wstate": np.ascontiguousarray(inp["state_win_kv"][0][4 * b:4 * b + 4, :, :, r, :]),
            "ptab": np.ascontiguousarray(inp["page_table"][4 * b:4 * b + 4].astype(np.int32)),
            "smask": CONST["smask"], "sel16": CONST["sel16"], "hmask": CONST["hmask"], "cover_s": CONST["cover_s"],
            "pidx": (np.arange(128) % 4).astype(np.float32)[:, None].copy(),
            "ptabT": np.ascontiguousarray(inp["page_table"][4 * b:4 * b + 4].astype(np.int32).reshape(4, 4, 32).transpose(2, 0, 1).reshape(32, 16)),
            "Rrep": (np.arange(128)[None, :] // 4 == np.arange(32)[:, None]).astype(np.float32),
            "E2": (np.arange(128)[None, :] // 2 == np.arange(64)[:, None]).astype(np.float32),
            "cover": CONST["cover"], "I0": CONST["I0"], "Jm": CONST["Jm"], "tri": CONST["tri"], "Ebig": CONST["Ebig"],
        }
        maps.append(m)
    return maps


_NC = None


def kernel(**inputs):
    global _NC
    inp = {k: np.asarray(v) for k, v in inputs.items()}
    if _NC is None:
        _NC = build()
    maps = make_in_maps(inp)
    res = run_bass_kernel_spmd(_NC, maps, core_ids=list(range(8)), **({'trace': True} if TRACE else {}))
    global LAST_RES
    LAST_RES = res
    R = res.results
    global LAST
    LAST = R
    cmp_p = np.zeros((1, 2, SEQ, 2, 4, 128), np.float32)
    sel_p = np.zeros_like(cmp_p)
    win_full = np.zeros_like(cmp_p)
    cmp_s = np.zeros((1, 8, 4, 2, 4, 128), np.float32)
    sel_s = np.zeros_like(cmp_s)
    win_new = np.zeros_like(cmp_s)
    for c in range(8):
        b, g = c // 4, c % 4
        kp = R[c]["kv_p"]
        cmp_p[0, b, :, :, g, :] = kp[:, 0]
        sel_p[0, b, :, :, g, :] = kp[:, 1]
        win_full[0, b, :, :, g, :] = kp[:, 2]
        ks = R[c]["kv_s"].reshape(4, 4, 3, 2, 128)
        cmp_s[0, 4 * b:4 * b + 4, :, :, g, :] = ks[:, :, 0]
        sel_s[0, 4 * b:4 * b + 4, :, :, g, :] = ks[:, :, 1]
        win_new[0, 4 * b:4 * b + 4, :, :, g, :] = ks[:, :, 2]
    win_p = np.ascontiguousarray(win_full[:, :, SEQ - 512:])
    y_p = np.zeros((2, SEQ, D), np.float32)
    y_s = np.zeros((8, 4, D), np.float32)
    win_s = np.zeros((1, 8, 512, 2, 4, 128), np.float32)
    ret_p = np.zeros((1, 2, 8, 256, 512), np.float32)
    ret_s = np.zeros((1, 8, 8, 256, 512), np.float32)
    for c in range(8):
        b, g = c // 4, c % 4
        win_s[0, 4 * b:4 * b + 4, :, :, g, :] = R[c]["win_s"]
        y_p[b, 1024 * g:1024 * g + 1024] = R[c]["y_tok"][:1024]
        y_s[c] = R[c]["y_tok"][1024:]
        ret_p[0, b, 2 * g:2 * g + 2] = R[c]["ret_p"]
        ret_s[0, 4 * b:4 * b + 4, 2 * g:2 * g + 2] = R[c]["ret_s"]
    return (y_p, y_s, cmp_p, cmp_s, sel_p, sel_s, win_p, win_s, ret_p, ret_s)
```

```python
import contextlib
import numpy as np
import concourse.bass as bass
import concourse.mybir as mybir
from concourse.bass_utils import run_bass_kernel_spmd

F32 = mybir.dt.float32
BF16 = mybir.dt.bfloat16
I32 = mybir.dt.int32
AF = mybir.ActivationFunctionType
ALU = mybir.AluOpType
AX = mybir.AxisListType

ENGS = ("sp", "act", "dve", "pool", "pe")


class Buf:
    __slots__ = ("name", "last_w", "readers", "cnt")

    def __init__(self, name):
        self.name = name
        self.last_w = None
        self.readers = []
        self.cnt = 0


class T:
    def __init__(self, t, name):
        self.t = t
        self.b = Buf(name)

    def __getitem__(self, k):
        return self.t[k]

    def ap(self):
        return self.t.ap()


def _b(x):
    return x.b if isinstance(x, T) else x


class Op:
    __slots__ = ("eng", "fn", "deps", "is_dma", "key", "sig", "sigidx", "cnt", "inc")

    def __init__(self, eng, fn, is_dma=False, key=None, inc=16):
        self.eng = eng
        self.fn = fn
        self.deps = []
        self.is_dma = is_dma
        self.key = key
        self.sig = False
        self.sigidx = 0
        self.cnt = 0
        self.inc = inc


class Sched:
    def __init__(self, nc):
        self.nc = nc
        self.ops = []
        self.last_real = {e: None for e in ENGS}
        self.out_dmas = []

    def _add(self, op, r, w):
        r = [_b(x) for x in r]
        w = [_b(x) for x in w]
        deps = []
        for b in r:
            if b.last_w is not None:
                deps.append(b.last_w)
        for b in w:
            if b.last_w is not None:
                deps.append(b.last_w)
            deps.extend(b.readers)
        seen = set()
        for d in deps:
            if d is op or id(d) in seen:
                continue
            seen.add(id(d))
            if (not d.is_dma) and d.eng == op.eng and op.eng == "pe" and not op.is_dma:
                continue
            op.deps.append(d)
            if not d.is_dma:
                d.sig = True
        for b in r:
            b.readers.append(op)
        for b in w:
            b.last_w = op
            b.readers = []
        self.ops.append(op)
        if not op.is_dma:
            self.last_real[op.eng] = op
        return op

    def op(self, eng, fn, r=(), w=()):
        return self._add(Op(eng, fn), r, w)

    def pe(self, fn, r=(), w=()):
        return self.op("pe", fn, r, w)

    def act(self, fn, r=(), w=()):
        return self.op("act", fn, r, w)

    def dve(self, fn, r=(), w=()):
        return self.op("dve", fn, r, w)

    def pool(self, fn, r=(), w=()):
        return self.op("pool", fn, r, w)

    def dma(self, q, out, in_, r=(), w=(), key=None, final=False, **kw):
        return self.custom_dma(q, lambda e: e.dma_start(out=out, in_=in_, **kw), r, w, key, 16, final)

    def custom_dma(self, q, fn, r=(), w=(), key=None, inc=16, final=False):
        k = _b(key)
        op = Op(q, fn, is_dma=True, key=k, inc=inc)
        self._add(op, r, w)
        if final:
            self.out_dmas.append(op)
        return op

    def barrier(self):
        pend = [o for o in self.last_real.values() if o is not None]
        latest = {}
        for o in self.ops:
            if o.is_dma:
                latest[id(o.key)] = o
        for e in ENGS:
            op = Op(e, None)
            for d in pend:
                if d.eng != e:
                    op.deps.append(d)
                    d.sig = True
            op.deps.extend(latest.values())
            self.ops.append(op)

    def emit(self):
        nc = self.nc
        fin = Op("sp", None)
        latest = {}
        for o in self.out_dmas:
            latest[id(o.key)] = o
        fin.deps = list(latest.values())
        for e in ENGS:
            o = self.last_real[e]
            if e != "sp" and o is not None:
                fin.deps.append(o)
                o.sig = True
        self.ops.append(fin)

        with contextlib.ExitStack() as st:
            esem = {e: st.enter_context(nc.semaphore("s_" + e)) for e in ENGS}
            keysems = {}
            keyvals = {}
            for o in self.ops:
                if o.is_dma:
                    kid = id(o.key)
                    if kid not in keysems:
                        keysems[kid] = st.enter_context(nc.semaphore("k%d" % len(keysems)))
                        keyvals[kid] = 0
                    keyvals[kid] += o.inc
                    o.cnt = keyvals[kid]
            cnts = {e: 0 for e in ENGS}
            for o in self.ops:
                if (not o.is_dma) and o.sig:
                    assert o.fn is not None
                    cnts[o.eng] += 1
                    o.sigidx = cnts[o.eng]
            self.n_sems = len(keysems) + 5
            per = {e: [o for o in self.ops if o.eng == e] for e in ENGS}
            block = st.enter_context(nc.Block())

            def run(eng_name):
                def body(eng):
                    waited = {}
                    for o in per[eng_name]:
                        for d in o.deps:
                            if d.is_dma:
                                s, v = keysems[id(d.key)], d.cnt
                            else:
                                s, v = esem[d.eng], d.sigidx
                            if waited.get(id(s), 0) >= v:
                                continue
                            waited[id(s)] = v
                            eng.wait_ge(s, v)
                        if o.fn is None:
                            continue
                        ins = o.fn(eng)
                        if o.is_dma:
                            ins.then_inc(keysems[id(o.key)], o.inc)
                        elif o.sig:
                            ins.then_inc(esem[eng_name], 1)

                return body

            block.sync(run("sp"))
            block.scalar(run("act"))
            block.vector(run("dve"))
            block.gpsimd(run("pool"))
            block.tensor(run("pe"))


D = 2048
SEQ = 4096
NT = SEQ // 128
NTOK = 1028
NS = 16
PAST = 16384
NCOL = 1292
RMS_EPS = 1e-6
SCALE = 128 ** -0.5
NEG = -30000.0

STAGE = 1
DEBUG = False
NTQ = NT
RUN_S = True
RUN_T0 = True
RUN_E = True
RUN_T1 = True
TRACE = False


def rope_table(pos, half):
    inv = (10000.0 ** (-(np.arange(half, dtype=np.float32)) / np.float32(half))).astype(np.float32)
    ang = (pos.astype(np.float32)[:, None] * inv[None, :]).astype(np.float32)
    return np.concatenate([np.cos(ang), np.sin(ang)], axis=1).astype(np.float32)


def build(stage=STAGE):
    nc = bass.Bass("TRN2", target_bir_lowering=False)
    S = Sched(nc)

    def din(name, shape, dt=F32):
        return nc.dram_tensor(name, list(shape), dt, kind="ExternalInput")

    def dout(name, shape, dt=F32):
        return nc.dram_tensor(name, list(shape), dt, kind="ExternalOutput")

    xb = din("xb", [SEQ, D])
    xs = din("xs", [NS, D])
    w_in = din("w_in", [D, NCOL])
    gmix0 = din("gmix0", [128, D])
    cs_p = din("cs_p", [SEQ, 128])
    cs_s = din("cs_s", [NS, 128])
    ident_d = din("ident", [128, 128])

    cw1 = din("cw1", [2, 32, 128, 128])
    cw2 = din("cw2", [2, 128, 128])
    posT = din("posT", [128, 64])
    b1T = din("b1T", [128, 2])
    cover_d = din("cover", [256, 64])
    I0_d = din("I0", [128, 128])
    Jm_d = din("Jm", [128, 64])
    tri_d = din("tri", [128, 256])
    Ebig_d = din("Ebig", [64, SEQ])

    kv_p = dout("kv_p", [SEQ, 3, 2, 128])
    o_loc = nc.dram_tensor("o_loc", [SEQ, 512], BF16)
    o_locs = nc.dram_tensor("o_locs", [NS, 512], BF16)
    og = nc.dram_tensor("og", [4 * SEQ, 512], BF16)
    ogs = nc.dram_tensor("ogs", [4 * NS, 512], BF16)
    o_locT = [Buf("o_loc%d" % i) for i in range(5)]
    ogT = Buf("og")
    x_tok = din("x_tok", [NTOK, D])
    oidx_d = din("oidx", [128, 36], I32)
    w_out0 = din("w_out0", [D, D])
    ffn1_0 = din("ffn1_0", [D, 4 * D])
    ffn2_0 = din("ffn2_0", [4 * D, D])
    gffn0T = din("gffn0T", [128, 16])
    w_out1 = din("w_out1", [2 * D, D])
    ffn1_1 = din("ffn1_1", [D, 4 * D])
    ffn2_1 = din("ffn2_1", [4 * D, D])
    gmix1T = din("gmix1T", [128, 16])
    gffn1T = din("gffn1T", [128, 16])
    gfinT = din("gfinT", [128, 16])
    wr_d = din("wr", [D, 3072])
    cs_r = din("cs_r", [SEQ, 256])
    cs_rs = din("cs_rs", [NS, 256])
    decT_d = din("decT", [128, 256])
    qdecb_d = din("qdecb", [128, 256])
    kdec_d = din("kdec", [128, 2])
    decTs_d = din("decTs", [16, 32])
    kdm_d = din("kdm", [16, 8])
    colmask_d = din("colmask", [128, 64])
    gnb_d = din("gnb", [128, 1024])
    gpow_d = din("gpow", [128, 4])
    qdecs_d = din("qdecs", [128, 32])
    sret_d = din("sret", [4, 2, 256, 512])
    oidx2_d = din("oidx2", [128, 72], I32)
    ret_p = dout("ret_p", [2, 256, 512])
    ret_s = dout("ret_s", [4, 2, 256, 512])
    y_tok = dout("y_tok", [NTOK, D])
    hspill = nc.dram_tensor("hspill", [128, 16 * NTOK], F32)
    hspT = Buf("hspill")
    xg_in = nc.dram_tensor("xg_in", [8, 128, 2 * NTOK], BF16)
    xg_all = nc.dram_tensor("xg_all", [8, 512, 2 * NTOK], BF16)
    xg_inT = [Buf("xg_in%d" % i) for i in range(8)]
    xgT = Buf("xg_all")
    o_r = nc.dram_tensor("o_r", [4, 2, 1024, 512], BF16)
    o_rs = nc.dram_tensor("o_rs", [2, NS, 512], BF16)
    og2 = nc.dram_tensor("og2", [4 * 2 * 4 * 1024, 512], BF16)
    ogs2 = nc.dram_tensor("ogs2", [2 * 4 * NS, 512], BF16)
    o_rT = [[Buf("o_r%d%d" % (c, h)) for h in range(2)] for c in range(4)]
    o_rsT = [Buf("o_rs%d" % h) for h in range(2)]
    og2T = Buf("og2")
    win_s = dout("win_s", [4, 512, 2, 128])
    ccache = din("ccache", [1280, 128, 2, 128])
    scache = din("scache", [1280, 128, 2, 128])
    wstate = din("wstate", [4, 512, 2, 128])
    ptab = din("ptab", [4, 128], I32)
    smask_d = din("smask", [16, 5, 16])
    sel16_d = din("sel16", [16, 4 + 64])
    hmask_d = din("hmask", [16, 12])
    cover_s = din("cover_s", [1024, 257])
    pidx_d = din("pidx", [128, 1])
    ptabT = din("ptabT", [32, 16], I32)
    Rrep_d = din("Rrep", [32, 128])
    E2_d = din("E2", [64, 128])
    kv_s = dout("kv_s", [NS, 3, 2, 128])

    dbg_outs = {}

    def dbg(name, t, ap, shape, dt=F32):
        if not DEBUG:
            return
        d_ = nc.dram_tensor("dbg_" + name, list(shape), dt, kind="ExternalOutput")
        S.dma("sp", d_.ap(), ap, r=[t], key=Buf("dbgk_" + name), final=True)

    with contextlib.ExitStack() as top:
        def sbuf(st, name, shape, dt):
            return T(st.enter_context(nc.sbuf_tensor(name, list(shape), dt)), name)

        def psum(st, name, shape, dt):
            return T(st.enter_context(nc.psum_tensor(name, list(shape), dt)), name)

        identb = sbuf(top, "identb", [128, 128], BF16)
        identf = sbuf(top, "identf", [128, 128], F32)
        GL = sbuf(top, "GL", [128, NT + 1, 12], F32)
        QTs = sbuf(top, "QTs", [128, 4, NS], BF16)
        KTs = sbuf(top, "KTs", [128, 3, NS], BF16)
        VAs = sbuf(top, "VAs", [NS, 2, 132], BF16)
        pp = contextlib.ExitStack()
        QT = sbuf(pp, "QT", [128, 4, SEQ + NS], BF16)
        KT = sbuf(pp, "KT", [128, 3, SEQ + NS], BF16)
        VcT = sbuf(pp, "VcT", [128, SEQ + NS], BF16)
        VA = sbuf(pp, "VA", [128, NT + 1, 2, 132], BF16)
        S.dma("sp", identf[:], ident_d.ap(), w=[identf], key=identf)
        S.dma("pool", identb[:], ident_d.ap(), w=[identb], key=identb)

        def phase_a():
            with contextlib.ExitStack() as pa:
                wsb = sbuf(pa, "wsb", [128, 16, NCOL], BF16)
                gsb = sbuf(pa, "gsb", [128, D], F32)
                xt = [sbuf(pa, "xt%d" % i, [128, D], F32) for i in range(2)]
                cst = [sbuf(pa, "cst%d" % i, [128, 128], F32) for i in range(2)]
                junk = sbuf(pa, "junk", [128, D], BF16)
                ss = sbuf(pa, "ss", [128, 2], F32)
                epsb = sbuf(pa, "epsb", [128, 1], F32)
                S.pool(lambda e: e.memset(epsb[:], RMS_EPS), w=[epsb])
                xn = sbuf(pa, "xn", [128, D], BF16)
                xnT = sbuf(pa, "xnT", [128, 16, 128], BF16)
                kvf = [sbuf(pa, "kvf%d" % i, [128, 3, 2, 128], F32) for i in range(2)]
                rt = [sbuf(pa, "rt%d" % i, [128, 4, 64], F32) for i in range(4)]
                qkb = sbuf(pa, "qkb", [128, 8, 128], BF16)
                pT = [psum(pa, "pT%d" % i, [128, 1024], BF16) for i in range(2)]
                pz = [psum(pa, "pz%d" % i, [128, 512], F32) for i in range(3)]
                pq = psum(pa, "pq", [128, 1024], BF16)

                for kc in range(4):
                    S.dma("pool", wsb[:, 4 * kc:4 * kc + 4, :],
                          w_in.ap()[512 * kc:512 * kc + 512, :].rearrange("(k p) c -> p k c", p=128),
                          w=[wsb], key=wsb)
                S.dma("sp", gsb[:], gmix0.ap(), w=[gsb], key=gsb)
                S.pool(lambda e: e.memset(VA[:, :, :, 128:129], 1.0), w=[VA])

                for j in range(NT + 1):
                    n = 128 if j < NT else NS
                    c0 = j * 128
                    xsrc = xb.ap()[c0:c0 + 128, :] if j < NT else xs.ap()
                    csrc = cs_p.ap()[c0:c0 + 128, :] if j < NT else cs_s.ap()
                    X = xt[j % 2]
                    C = cst[j % 2]
                    KV = kvf[j % 2]

                    def load(jj):
                        nn = 128 if jj < NT else NS
                        cc = jj * 128
                        S.dma("sp", xt[jj % 2][:nn, :], xb.ap()[cc:cc + 128, :] if jj < NT else xs.ap(), w=[xt[jj % 2]], key=xt[jj % 2])
                        S.dma("sp", cst[jj % 2][:nn, :], cs_p.ap()[cc:cc + 128, :] if jj < NT else cs_s.ap(), w=[cst[jj % 2]], key=cst[jj % 2])
                    if j == 0:
                        load(0)
                    if j + 1 <= NT:
                        load(j + 1)
                    S.act(lambda e, X=X, n=n: e.activation(out=junk[:n, :], in_=X[:n, :], func=AF.Square,
                                                           accum_out=ss[:n, 0:1]), r=[X], w=[junk, ss])
                    S.act(lambda e, n=n: e.activation(out=ss[:n, 1:2], in_=ss[:n, 0:1], func=AF.Sqrt, scale=1.0 / D,
                                                      bias=epsb[:n, 0:1]), r=[ss, epsb], w=[ss])
                    S.dve(lambda e, n=n: e.reciprocal(out=ss[:n, 1:2], in_=ss[:n, 1:2]), r=[ss], w=[ss])
                    S.dve(lambda e, X=X, n=n: e.scalar_tensor_tensor(out=xn[:n, :], in0=X[:n, :], scalar=ss[:n, 1:2],
                                                                     in1=gsb[:n, :], op0=ALU.mult, op1=ALU.mult),
                          r=[X, ss, gsb], w=[xn])
                    for hb in range(2):
                        def tr(e, hb=hb, n=n):
                            ins = None
                            for k in range(8):
                                ins = e.transpose(pT[hb][:, k * 128:k * 128 + n], xn[:n, (hb * 8 + k) * 128:(hb * 8 + k + 1) * 128],
                                                  identb[:n, :n])
                            return ins
                        S.pe(tr, r=[xn, identb], w=[pT[hb]])
                        S.act(lambda e, hb=hb, n=n: e.copy(out=xnT[:, hb * 8:hb * 8 + 8, :n],
                                                           in_=pT[hb][:, :].rearrange("p (k t) -> p k t", k=8)[:, :, :n]),
                              r=[pT[hb]], w=[xnT])
                    cbs = [(0, 512), (512, 512), (1024, NCOL - 1024)]
                    for ci, (cb, cw) in enumerate(cbs):
                        def mm(e, ci=ci, cb=cb, cw=cw, n=n):
                            ins = None
                            for k in range(16):
                                ins = e.matmul(pz[ci][:n, :cw], xnT[:, k, :n], wsb[:, k, cb:cb + cw],
                                               start=(k == 0), stop=(k == 15))
                            return ins
                        S.pe(mm, r=[xnT, wsb], w=[pz[ci]])
                    cosb = lambda h, n=n, C=C: C[:n, 0:64].unsqueeze(1).broadcast_to([n, h, 64])
                    sinb = lambda h, n=n, C=C: C[:n, 64:128].unsqueeze(1).broadcast_to([n, h, 64])

                    def rope(src3, h, out_lo, out_hi, rd, wr, n=n, cosb=cosb, sinb=sinb):
                        a, b_, c_, d_ = rt
                        S.dve(lambda e: e.tensor_tensor(out=a[:n, :h, :], in0=src3[:, :, 0:64], in1=cosb(h), op=ALU.mult), r=rd + [C], w=[a])
                        S.dve(lambda e: e.tensor_tensor(out=b_[:n, :h, :], in0=src3[:, :, 64:128], in1=sinb(h), op=ALU.mult), r=rd + [C], w=[b_])
                        S.dve(lambda e: e.tensor_tensor(out=c_[:n, :h, :], in0=src3[:, :, 64:128], in1=cosb(h), op=ALU.mult), r=rd + [C], w=[c_])
                        S.dve(lambda e: e.tensor_tensor(out=d_[:n, :h, :], in0=src3[:, :, 0:64], in1=sinb(h), op=ALU.mult), r=rd + [C], w=[d_])
                        S.pool(lambda e: e.tensor_tensor(out=out_lo, in0=a[:n, :h, :], in1=b_[:n, :h, :], op=ALU.subtract), r=[a, b_], w=wr)
                        S.pool(lambda e: e.tensor_tensor(out=out_hi, in0=c_[:n, :h, :], in1=d_[:n, :h, :], op=ALU.add), r=[c_, d_], w=wr)

                    z0 = pz[0][:n, :].rearrange("p (h d) -> p h d", h=4)
                    rope(z0, 4, qkb[:n, 0:4, 0:64], qkb[:n, 0:4, 64:128], [pz[0]], [qkb])
                    z1 = pz[1][:n, 0:384].rearrange("p (h d) -> p h d", h=3)
                    rope(z1, 3, KV[:n, :, 0, 0:64], KV[:n, :, 0, 64:128], [pz[1]], [KV])
                    S.act(lambda e, n=n, KV=KV: e.copy(out=KV[:n, 0, 1, :], in_=pz[1][:n, 384:512]), r=[pz[1]], w=[KV])
                    S.act(lambda e, n=n, KV=KV: e.copy(out=KV[:n, 1:3, 1, :],
                                                       in_=pz[2][:n, 0:256].rearrange("p (h d) -> p h d", h=2)),
                          r=[pz[2]], w=[KV])
                    S.act(lambda e, n=n, j=j: e.copy(out=GL[:n, j, :], in_=pz[2][:n, 256:268]), r=[pz[2]], w=[GL])
                    S.pool(lambda e, n=n, KV=KV: e.tensor_copy(out=qkb[:n, 4:7, :], in_=KV[:n, :, 0, :]), r=[KV], w=[qkb])
                    S.pool(lambda e, n=n, KV=KV: e.tensor_copy(out=qkb[:n, 7, :], in_=KV[:n, 0, 1, :]), r=[KV], w=[qkb])
                    S.pool(lambda e, n=n, KV=KV, j=j: e.tensor_copy(out=VA[:n, j, :, 0:128], in_=KV[:n, 1:3, 1, :]), r=[KV], w=[VA])
                    dst = kv_p.ap()[c0:c0 + 128] if j < NT else kv_s.ap()
                    S.dma("sp", dst, KV[:n], r=[KV], key=KV, final=True)
                    if j == NT:
                        for s_ in range(4):
                            S.dma("sp", win_s.ap()[s_, 508:512], KV[4 * s_:4 * s_ + 4, 2], r=[KV], key=KV, final=True)
                    def tr2(e, n=n):
                        ins = None
                        for k in range(8):
                            ins = e.transpose(pq[:, k * 128:k * 128 + n], qkb[:n, k, :], identb[:n, :n])
                        return ins
                    S.pe(tr2, r=[qkb, identb], w=[pq])
                    pq3 = pq[:, :].rearrange("p (k t) -> p k t", k=8)
                    S.act(lambda e, n=n, c0=c0, pq3=pq3: e.copy(out=QT[:, :, c0:c0 + n], in_=pq3[:, 0:4, :n]), r=[pq], w=[QT])
                    S.act(lambda e, n=n, c0=c0, pq3=pq3: e.copy(out=KT[:, :, c0:c0 + n], in_=pq3[:, 4:7, :n]), r=[pq], w=[KT])
                    S.act(lambda e, n=n, c0=c0, pq3=pq3: e.copy(out=VcT[:, c0:c0 + n], in_=pq3[:, 7, :n]), r=[pq], w=[VcT])
                S.act(lambda e: e.copy(out=QTs[:, :, :], in_=QT[:, :, SEQ:SEQ + NS]), r=[QT], w=[QTs])
                S.act(lambda e: e.copy(out=KTs[:, :, :], in_=KT[:, :, SEQ:SEQ + NS]), r=[KT], w=[KTs])
                S.act(lambda e: e.copy(out=VAs[:, :, :], in_=VA[:NS, NT, :, :]), r=[VA], w=[VAs])
                S.barrier()

        phase_a()

        S.act(lambda e: e.activation(out=GL[:, :, :], in_=GL[:, :, :], func=AF.Sigmoid), r=[GL], w=[GL])

        def phase_c():
            with contextlib.ExitStack() as pc:
                w1sb = sbuf(pc, "w1sb", [128, 2, 32, 128], BF16)
                w2sb = sbuf(pc, "w2sb", [128, 2, 128], BF16)
                posb = sbuf(pc, "posb", [128, 64], BF16)
                b1sb = sbuf(pc, "b1sb", [128, 2], F32)
                biasb = sbuf(pc, "biasb", [128, 2], F32)
                I0 = sbuf(pc, "I0s", [128, 128], F32)
                Jm = sbuf(pc, "Jms", [128, 64], F32)
                trib = sbuf(pc, "trib", [128, 256], BF16)
                Ebig = sbuf(pc, "Ebigs", [64, SEQ], BF16)
                KcT = sbuf(pc, "KcT", [128, 256], BF16)
                VcA = sbuf(pc, "VcA", [128, 2, 196], BF16)
                hx = sbuf(pc, "hx", [128, 256], F32)
                ht = sbuf(pc, "ht", [128, 256], F32)
                hT = [sbuf(pc, "hT%d" % i, [128, 256], BF16) for i in range(2)]
                Eb = [sbuf(pc, "Eb%d" % i, [128, 4, 128], BF16) for i in range(3)]
                mk = sbuf(pc, "mk", [128, 128], BF16)
                rs = sbuf(pc, "rs", [128, 8], F32)
                ocats = [sbuf(pc, "ocat%d" % i, [128, 4, 128], F32) for i in range(2)]
                ocb = [sbuf(pc, "ocb%d" % i, [128, 512], BF16) for i in range(2)]
                imp = sbuf(pc, "imp", [128, 64], F32)
                vis = sbuf(pc, "vis", [128, 64], F32)
                frc = sbuf(pc, "frc", [128, 64], F32)
                col0 = sbuf(pc, "col0", [128, 64], F32)
                sc = [sbuf(pc, "sc%d" % i, [128, 64], F32) for i in range(2)]
                m8 = sbuf(pc, "m8", [128, 16], F32)
                selm = sbuf(pc, "selm", [128, 64], F32)
                nm = sbuf(pc, "nm", [128, 64], BF16)
                nmTs = [sbuf(pc, "nmT%d" % i, [64, 4, 128], BF16) for i in range(2)]
                pS = [psum(pc, "pS%d" % i, [128, 512], F32) for i in range(3)]
                pO = [psum(pc, "pO%d" % i, [128, 512], F32) for i in range(4)]
                pM = psum(pc, "pM", [128, 1024], BF16)

                S.dma("pool", w1sb[:], cw1.ap().rearrange("e s d h -> d e s h"), w=[w1sb], key=w1sb)
                S.dma("pool", w2sb[:], cw2.ap().rearrange("e h d -> h e d"), w=[w2sb], key=w2sb)
                S.dma("pool", posb[:], posT.ap(), w=[posb], key=posb)
                S.dma("sp", b1sb[:], b1T.ap(), w=[b1sb], key=b1sb)
                S.dma("sp", I0[:], I0_d.ap(), w=[I0], key=I0)
                S.dma("sp", Jm[:], Jm_d.ap(), w=[Jm], key=Jm)
                S.dma("pool", trib[:], tri_d.ap(), w=[trib], key=trib)
                S.dma("pool", Ebig[:], Ebig_d.ap(), w=[Ebig], key=Ebig)
                S.dve(lambda e: e.memset(KcT[:], 0.0), w=[KcT])
                S.dve(lambda e: e.memset(VcA[:], 0.0), w=[VcA])
                S.dve(lambda e: e.memset(VcA[:, :, 128:129], 1.0), w=[VcA])
                S.dve(lambda e: e.memset(col0[:], 0.0), w=[col0])
                S.dve(lambda e: e.memset(col0[:, 0:1], 1.0), w=[col0])
                S.dma("pool", VcA[:, :, 129:193], cover_d.ap().rearrange("(t p) j -> p t j", p=128), w=[VcA], key=VcA)

                def bias_mm(e):
                    ins = None
                    for ee in range(2):
                        for s_ in range(32):
                            ins = e.matmul(pS[0][:, ee:ee + 1], w1sb[:, ee, s_, :], posb[:, ee * 32 + s_:ee * 32 + s_ + 1],
                                           start=(ee == 0 and s_ == 0), stop=(s_ == 31), skip_group_check=True)
                    return ins
                S.pe(bias_mm, r=[w1sb, posb], w=[pS[0]])
                S.dve(lambda e: e.tensor_tensor(out=biasb[:], in0=pS[0][:, 0:2], in1=b1sb[:], op=ALU.add), r=[pS[0], b1sb], w=[biasb])

                def compress(srcT, nblk, ncols_pad, KcT_out, Vc_out_fn):
                    for ee in range(2):
                        for c0 in range(0, nblk, 512):
                            cn = min(512, nblk - c0)
                            P = pS[(c0 // 512) % 2]

                            def hmm(e, ee=ee, c0=c0, cn=cn, P=P):
                                ins = None
                                src = srcT(ee)
                                for rs_ in range(32):
                                    lo = rs_ + 16 * c0
                                    ins = e.matmul(P[:, :cn], w1sb[:, ee, rs_, :], src[:, lo:lo + 16 * (cn - 1) + 1:16],
                                                   start=(rs_ == 0), stop=(rs_ == 31))
                                return ins
                            S.pe(hmm, r=[w1sb, KT, VcT], w=[P])
                            S.act(lambda e, ee=ee, cn=cn, P=P: e.activation(out=hx[:, :cn], in_=P[:, :cn], func=AF.Identity,
                                                                             bias=biasb[:, ee:ee + 1]), r=[P, biasb], w=[hx])
                            S.dve(lambda e, cn=cn: e.tensor_tensor(out=ht[:, :cn], in0=hx[:, :cn], in1=hx[:, :cn], op=ALU.mult), r=[hx], w=[ht])
                            S.dve(lambda e, cn=cn: e.tensor_scalar(out=ht[:, :cn], in0=ht[:, :cn], scalar1=0.044715, scalar2=1.0,
                                                                   op0=ALU.mult, op1=ALU.add), r=[ht], w=[ht])
                            S.dve(lambda e, cn=cn: e.tensor_tensor(out=ht[:, :cn], in0=ht[:, :cn], in1=hx[:, :cn], op=ALU.mult), r=[ht, hx], w=[ht])
                            S.act(lambda e, cn=cn: e.activation(out=ht[:, :cn], in_=ht[:, :cn], func=AF.Sigmoid, scale=1.5957691216),
                                  r=[ht], w=[ht])
                            H = hT[ee]
                            S.dve(lambda e, cn=cn, H=H: e.tensor_tensor(out=H[:, :cn], in0=hx[:, :cn], in1=ht[:, :cn], op=ALU.mult),
                                  r=[hx, ht], w=[H])
                            if ee == 0:
                                S.pe(lambda e, cn=cn, H=H: e.matmul(pO[0][:, :cn], w2sb[:, 0, :], H[:, :cn], start=True, stop=True),
                                     r=[w2sb, H], w=[pO[0]])
                                S.act(lambda e, cn=cn, c0=c0: e.copy(out=KcT_out[:, c0:c0 + cn], in_=pO[0][:, :cn]), r=[pO[0]], w=[KcT])
                            else:
                                for t0 in range(0, cn, 128):
                                    nb = min(128, cn - t0)
                                    S.pe(lambda e, nb=nb, t0=t0, H=H: e.matmul(pO[1][:nb, 0:128], H[:, t0:t0 + nb], w2sb[:, 1, :],
                                                                               start=True, stop=True), r=[w2sb, H], w=[pO[1]])
                                    S.act(lambda e, nb=nb, ct=(c0 + t0) // 128: e.copy(out=Vc_out_fn(ct, nb), in_=pO[1][:nb, 0:128]),
                                          r=[pO[1]], w=[VcA])

                compress(lambda ee: (KT[:, 0, :] if ee == 0 else VcT[:, :]), 255, 256, KcT, lambda ct, nb: VcA[:nb, ct, 0:128])

                HB = [(0, 0), (0, 256), (1, 0), (1, 256)]

                def pv_group(Pb, E, rhs_fn, ncol, first, r):
                    def f(e):
                        ins = None
                        for h in range(4):
                            bk, co = HB[h]
                            ins = e.matmul(Pb[bk][:, co:co + ncol], E[:, h, :], rhs_fn(), start=(first and h % 2 == 0), stop=False,
                                           skip_group_check=True)
                        return ins
                    S.pe(f, r=[E] + r, w=[Pb[0], Pb[1]])

                def finish_branch(Pb, i, br, first_branch, ocat):
                    for h in range(4):
                        bk, co = HB[h]
                        S.dve(lambda e, h=h, bk=bk, co=co: e.tensor_copy(out=rs[:, h:h + 1], in_=Pb[bk][:, co + 128:co + 129]),
                              r=[Pb[bk]], w=[rs])
                    S.dve(lambda e: e.tensor_scalar(out=rs[:, 0:4], in0=rs[:, 0:4], scalar1=1e-30, scalar2=None, op0=ALU.max), r=[rs], w=[rs])
                    S.dve(lambda e: e.reciprocal(out=rs[:, 0:4], in_=rs[:, 0:4]), r=[rs], w=[rs])
                    S.dve(lambda e, i=i, br=br: e.tensor_tensor(out=rs[:, 4:8], in0=rs[:, 0:4], in1=GL[:, i, 4 * br:4 * br + 4], op=ALU.mult),
                          r=[rs, GL], w=[rs])
                    for h in range(4):
                        bk, co = HB[h]
                        if first_branch:
                            S.dve(lambda e, h=h, bk=bk, co=co: e.tensor_scalar(out=ocat[:, h, :], in0=Pb[bk][:, co:co + 128],
                                                                               scalar1=rs[:, 4 + h:5 + h], scalar2=None, op0=ALU.mult),
                                  r=[Pb[bk], rs], w=[ocat])
                        else:
                            S.dve(lambda e, h=h, bk=bk, co=co: e.scalar_tensor_tensor(out=ocat[:, h, :], in0=Pb[bk][:, co:co + 128],
                                                                                      scalar=rs[:, 4 + h:5 + h], in1=ocat[:, h, :],
                                                                                      op0=ALU.mult, op1=ALU.add),
                                  r=[Pb[bk], rs, ocat], w=[ocat])

                pair_ctr = [0]

                def qk_exp(i, lhsT_fn, lr, with_mask, nmT=None):
                    nmT = nmT if nmT is not None else nmTs[0]
                    x = pair_ctr[0] % 3
                    pair_ctr[0] += 1
                    P, E = pS[x], Eb[x]

                    def f(e):
                        ins = e.matmul(P[:, :], lhsT_fn(), QT[:, :, i * 128:i * 128 + 128], start=True, stop=not with_mask)
                        if with_mask is not False:
                            ins = e.matmul(P[:, :], Ebig[:, with_mask * 128:with_mask * 128 + 128], nmT[:, :, :], start=False, stop=True)
                        return ins
                    S.pe(f, r=[QT, Ebig, nmT] + lr, w=[P])
                    S.act(lambda e: e.activation(out=E[:, :, :], in_=P[:, :].rearrange("p (h q) -> p h q", h=4), func=AF.Exp, scale=SCALE),
                          r=[P], w=[E])
                    return E

                def mul_mask(E, m_ap, r):
                    S.dve(lambda e: e.tensor_tensor(out=E[:, :, :], in0=E[:, :, :], in1=m_ap.unsqueeze(1).broadcast_to([128, 4, 128]),
                                                    op=ALU.mult), r=[E] + r, w=[E])

                def stage1(i):
                    ocat = ocats[i % 2]
                    nmT = nmTs[i % 2]
                    Pb = pO[0:2]
                    n_ct = 1 if i < 16 else 2
                    for ct in range(n_ct):
                        E = qk_exp(i, lambda ct=ct: KcT[:, ct * 128:ct * 128 + 128], [KcT], False)
                        cval = float(128 * i - 2048 * ct - 31)
                        S.dve(lambda e, cval=cval: e.tensor_scalar(out=mk[:, :], in0=I0[:, :], scalar1=cval, scalar2=0.0,
                                                                   op0=ALU.add, op1=ALU.is_ge), r=[I0], w=[mk])
                        mul_mask(E, mk[:, :], [mk])
                        pv_group(Pb, E, lambda ct=ct: VcA[:, ct, 0:193], 193, ct == 0, [VcA])
                    for h in range(4):
                        bk, co = HB[h]
                        S.dve(lambda e, h=h, bk=bk, co=co: e.tensor_copy(out=rs[:, h:h + 1], in_=Pb[bk][:, co + 128:co + 129]),
                              r=[Pb[bk]], w=[rs])
                    S.dve(lambda e: e.tensor_scalar(out=rs[:, 0:4], in0=rs[:, 0:4], scalar1=1e-30, scalar2=None, op0=ALU.max), r=[rs], w=[rs])
                    S.dve(lambda e: e.reciprocal(out=rs[:, 0:4], in_=rs[:, 0:4]), r=[rs], w=[rs])
                    for h in range(4):
                        bk, co = HB[h]
                        if h == 0:
                            S.dve(lambda e, bk=bk, co=co: e.tensor_scalar(out=imp[:, :], in0=Pb[bk][:, co + 129:co + 193], scalar1=rs[:, 0:1],
                                                                          scalar2=None, op0=ALU.mult), r=[Pb[bk], rs], w=[imp])
                        else:
                            S.dve(lambda e, h=h, bk=bk, co=co: e.scalar_tensor_tensor(out=imp[:, :], in0=Pb[bk][:, co + 129:co + 193],
                                                                                      scalar=rs[:, h:h + 1], in1=imp[:, :],
                                                                                      op0=ALU.mult, op1=ALU.add),
                                  r=[Pb[bk], rs, imp], w=[imp])
                    finish_branch(Pb, i, 0, True, ocat)
                    if i == 0:
                        dbg("ocat_cmp", ocat, ocat[:, :, :], [128, 4, 128])
                        dbg("rs_cmp", rs, rs[:, :], [128, 8])
                        dbg("imp", imp, imp[:, :], [128, 64])
                        dbg("GL", GL, GL[:, :, :], [128, NT + 1, 12])
                    S.dve(lambda e, i=i: e.tensor_scalar(out=vis[:, :], in0=Jm[:, :], scalar1=float(2 * i), scalar2=None, op0=ALU.is_le),
                          r=[Jm], w=[vis])
                    if i >= 8:
                        S.dve(lambda e, i=i: e.tensor_scalar(out=frc[:, :], in0=Jm[:, :], scalar1=float(2 * i - 1), scalar2=None, op0=ALU.is_ge),
                              r=[Jm], w=[frc])
                        S.dve(lambda e: e.tensor_tensor(out=frc[:, :], in0=frc[:, :], in1=vis[:, :], op=ALU.mult), r=[frc, vis], w=[frc])
                        S.dve(lambda e: e.tensor_tensor(out=frc[:, :], in0=frc[:, :], in1=col0[:, :], op=ALU.max), r=[frc, col0], w=[frc])
                        S.dve(lambda e: e.tensor_tensor(out=sc[0][:, :], in0=imp[:, :], in1=vis[:, :], op=ALU.mult), r=[imp, vis], w=[sc[0]])
                        S.dve(lambda e: e.tensor_scalar(out=sc[1][:, :], in0=vis[:, :], scalar1=-1.0, scalar2=1e9, op0=ALU.add, op1=ALU.mult),
                              r=[vis], w=[sc[1]])
                        S.dve(lambda e: e.tensor_tensor(out=sc[0][:, :], in0=sc[0][:, :], in1=sc[1][:, :], op=ALU.add), r=[sc[0], sc[1]], w=[sc[0]])
                        S.dve(lambda e: e.scalar_tensor_tensor(out=sc[0][:, :], in0=frc[:, :], scalar=2e9, in1=sc[0][:, :],
                                                               op0=ALU.mult, op1=ALU.add), r=[frc, sc[0]], w=[sc[0]])
                        S.dve(lambda e: e.max(out=m8[:, 0:8], in_=sc[0][:, :]), r=[sc[0]], w=[m8])
                        S.dve(lambda e: e.match_replace(out=sc[1][:, :], in_to_replace=m8[:, 0:8], in_values=sc[0][:, :], imm_value=-3e38),
                              r=[sc[0], m8], w=[sc[1]])
                        S.dve(lambda e: e.max(out=m8[:, 8:16], in_=sc[1][:, :]), r=[sc[1]], w=[m8])
                        S.dve(lambda e: e.tensor_scalar(out=selm[:, :], in0=sc[0][:, :], scalar1=m8[:, 15:16], scalar2=None, op0=ALU.is_ge),
                              r=[sc[0], m8], w=[selm])
                        S.dve(lambda e: e.tensor_tensor(out=selm[:, :], in0=selm[:, :], in1=vis[:, :], op=ALU.mult), r=[selm, vis], w=[selm])
                        SM = selm
                    else:
                        SM = vis
                    S.dve(lambda e, SM=SM: e.tensor_scalar(out=nm[:, :], in0=SM[:, :], scalar1=-NEG, scalar2=NEG, op0=ALU.mult, op1=ALU.add),
                          r=[SM], w=[nm])
                    S.pe(lambda e: e.transpose(pM[:64, 0:128], nm[:, :], identb[:, :]), r=[nm, identb], w=[pM])
                    S.act(lambda e: e.copy(out=nmT[:, :, :], in_=pM[:64, 0:128].unsqueeze(1).broadcast_to([64, 4, 128])), r=[pM], w=[nmT])
                def stage2(i):
                    ocat = ocats[i % 2]
                    nmT = nmTs[i % 2]
                    Pb = pO[2:4]
                    for kt in range(i + 1):
                        E = qk_exp(i, lambda kt=kt: KT[:, 1, kt * 128:kt * 128 + 128], [KT], kt, nmT)
                        if kt == i:
                            mul_mask(E, trib[:, 0:128], [trib])
                        pv_group(Pb, E, lambda kt=kt: VA[:, kt, 0, 0:129], 129, kt == 0, [VA])
                    finish_branch(Pb, i, 1, False, ocat)
                    if i == 0:
                        dbg("ocat_sel", ocat, ocat[:, :, :], [128, 4, 128])
                        dbg("rs_sel", rs, rs[:, :], [128, 8])
                        dbg("nmT", nmT, nmT[:, :, :], [64, 4, 128], BF16)
                        dbg("Eb", Eb[0], Eb[0][:, :, :], [128, 4, 128], BF16)
                        dbg("Eb1", Eb[1], Eb[1][:, :, :], [128, 4, 128], BF16)
                    Pb = pO[0:2]
                    kts = [kt for kt in range(i - 4, i + 1) if kt >= 0]
                    for kt in kts:
                        E = qk_exp(i, lambda kt=kt: KT[:, 2, kt * 128:kt * 128 + 128], [KT], False)
                        if kt == i:
                            mul_mask(E, trib[:, 0:128], [trib])
                        elif kt == i - 4:
                            mul_mask(E, trib[:, 128:256], [trib])
                        pv_group(Pb, E, lambda kt=kt: VA[:, kt, 1, 0:129], 129, kt == kts[0], [VA])
                    finish_branch(Pb, i, 2, False, ocat)
                    OB = ocb[i % 2]
                    S.act(lambda e, OB=OB: e.copy(out=OB[:, :], in_=ocat[:, :, :].rearrange("p h d -> p (h d)")), r=[ocat], w=[OB])
                    S.dma("sp", o_loc.ap()[i * 128:i * 128 + 128, :], OB[:, :], r=[OB], w=[o_locT[i // 8]], key=OB)
                if NTQ > 0:
                    stage1(0)
                for i in range(NTQ):
                    if i + 1 < NTQ:
                        stage1(i + 1)
                    stage2(i)
                S.barrier()
        phase_c()

        pp.close()

        S.dma("sp", win_s.ap()[:, 0:508], wstate.ap()[:, 4:512], key=Buf("wcopy"), final=True)
        def phase_s():
            with contextlib.ExitStack() as ps_:
                NP = PAST // 128
                NB = 257
                w1sb = sbuf(ps_, "w1sb_s", [128, 2, 32, 128], BF16)
                w2sb = sbuf(ps_, "w2sb_s", [128, 2, 128], BF16)
                posb = sbuf(ps_, "posb_s", [128, 64], BF16)
                b1sb = sbuf(ps_, "b1sb_s", [128, 2], F32)
                biasb = sbuf(ps_, "biasb_s", [128, 2], F32)
                Ebig = sbuf(ps_, "Ebig_s", [64, SEQ], BF16)
                smask = sbuf(ps_, "smask_s", [16, 5, 16], BF16)
                sel16 = sbuf(ps_, "sel16_s", [16, 68], F32)
                sel16b = sbuf(ps_, "sel16b_s", [16, 4], BF16)
                hmask = sbuf(ps_, "hmask_s", [16, 12], F32)
                big = sbuf(ps_, "bigS", [128, 2, 16512], BF16)
                XcT = big
                KsT = big
                Vs = big
                Vs3 = big[:, 1, :].rearrange("p (g c) -> p g c", c=129)
                E2 = sbuf(ps_, "E2s", [64, 128], BF16)
                Rrep = sbuf(ps_, "Rrep_s", [32, 128], F32)
                KwT = sbuf(ps_, "KwT", [128, 512], BF16)
                Vw = sbuf(ps_, "Vw", [128, 4, 130], BF16)
                stg = [sbuf(ps_, "stg%d" % i, [128, 8192], F32) for i in range(2)]
                wst = sbuf(ps_, "wst", [128, 4, 2, 128], F32)
                KcT = sbuf(ps_, "KcT_s", [128, 1024], BF16)
                Vc = sbuf(ps_, "Vc_s", [128, 8, 392], BF16)
                hx = sbuf(ps_, "hx_s", [128, 512], F32)
                ht = sbuf(ps_, "ht_s", [128, 512], F32)
                hT = [sbuf(ps_, "hT_s%d" % i, [128, 512], BF16) for i in range(2)]
                Es = [sbuf(ps_, "Es%d" % i, [128, 16], BF16) for i in range(2)]
                rs = sbuf(ps_, "rs_s", [16, 8], F32)
                U = sbuf(ps_, "U_s", [16, 260], BF16)
                gt = sbuf(ps_, "gt_s", [16, 12], F32)
                gs = sbuf(ps_, "gs_s", [16, 4], F32)
                ocat = sbuf(ps_, "ocat_s", [16, 128], F32)
                ocb = sbuf(ps_, "ocb_s", [16, 128], BF16)
                imp = sbuf(ps_, "imp_s", [4, 260], F32)
                sc = [sbuf(ps_, "sc_s%d" % i, [4, 260], F32) for i in range(2)]
                m8 = sbuf(ps_, "m8_s", [4, 16], F32)
                nm = sbuf(ps_, "nm_s", [4, 320], BF16)
                nmT = sbuf(ps_, "nmT_s", [64, 5, 4, 4], BF16)
                pS = [psum(ps_, "qS%d" % i, [128, 512], F32) for i in range(2)]
                pO = [psum(ps_, "qO%d" % i, [128, 512], F32) for i in range(3)]
                pTf = [psum(ps_, "qT%d" % i, [128, 512], F32) for i in range(2)]
                pM = psum(ps_, "qM", [128, 1024], BF16)

                S.dma("pool", w1sb[:], cw1.ap().rearrange("e s d h -> d e s h"), w=[w1sb], key=w1sb)
                S.dma("pool", w2sb[:], cw2.ap().rearrange("e h d -> h e d"), w=[w2sb], key=w2sb)
                S.dma("pool", posb[:], posT.ap(), w=[posb], key=posb)
                S.dma("sp", b1sb[:], b1T.ap(), w=[b1sb], key=b1sb)
                S.dma("pool", Ebig[:], Ebig_d.ap(), w=[Ebig], key=Ebig)
                S.dma("pool", smask[:], smask_d.ap(), w=[smask], key=smask)
                S.dma("sp", sel16[:], sel16_d.ap(), w=[sel16], key=sel16)
                S.dma("pool", sel16b[:], sel16_d.ap()[:, 0:4], w=[sel16b], key=sel16b)
                S.dma("sp", hmask[:], hmask_d.ap(), w=[hmask], key=hmask)
                S.dve(lambda e: e.memset(Vw[:, :, 128:129], 1.0), w=[Vw])
                S.dve(lambda e: e.memset(KcT[:], 0.0), w=[KcT])
                S.dve(lambda e: e.memset(Vc[:], 0.0), w=[Vc])
                S.dve(lambda e: e.memset(Vc[:, 0:7, 128:129], 1.0), w=[Vc])
                S.dve(lambda e: e.memset(Vc[:127, 7, 128:129], 1.0), w=[Vc])
                S.dma("pool", Vc[:, :, 129:386], cover_s.ap().rearrange("(t p) j -> p t j", p=128), w=[Vc], key=Vc)

                def bias_mm(e):
                    ins = None
                    for ee in range(2):
                        for s_ in range(32):
                            ins = e.matmul(pS[0][:, ee:ee + 1], w1sb[:, ee, s_, :], posb[:, ee * 32 + s_:ee * 32 + s_ + 1],
                                           start=(ee == 0 and s_ == 0), stop=(s_ == 31), skip_group_check=True)
                    return ins
                S.pe(bias_mm, r=[w1sb, posb], w=[pS[0]])
                S.dve(lambda e: e.tensor_tensor(out=biasb[:], in0=pS[0][:, 0:2], in1=b1sb[:], op=ALU.add), r=[pS[0], b1sb], w=[biasb])

                pti = sbuf(ps_, "pti", [32, 16], I32)
                ptf = sbuf(ps_, "ptf", [32, 16], F32)
                idf = sbuf(ps_, "idf", [128, 16], F32)
                idi = sbuf(ps_, "idi", [128, 16], I32)
                pix = sbuf(ps_, "pix", [128, 1], F32)
                S.dma("sp", pti[:], ptabT.ap(), w=[pti], key=pti)
                S.dma("sp", pix[:], pidx_d.ap(), w=[pix], key=pix)
                S.dma("sp", Rrep[:], Rrep_d.ap(), w=[Rrep], key=Rrep)
                S.dma("pool", E2[:], E2_d.ap(), w=[E2], key=E2)
                S.dve(lambda e: e.tensor_copy(out=ptf[:], in_=pti[:]), r=[pti], w=[ptf])
                S.pe(lambda e: e.matmul(pS[1][:, 0:16], Rrep[:, :], ptf[:, :], start=True, stop=True), r=[Rrep, ptf], w=[pS[1]])
                S.dve(lambda e: e.tensor_scalar(out=idf[:], in0=pS[1][:, 0:16], scalar1=4.0, scalar2=pix[:, 0:1], op0=ALU.mult, op1=ALU.add),
                      r=[pS[1], pix], w=[idf])
                S.dve(lambda e: e.tensor_copy(out=idi[:], in_=idf[:]), r=[idf], w=[idi])
                gctr = [0]

                def qgather(cache, s_, q):
                    G = stg[gctr[0] % 2]
                    gctr[0] += 1
                    rows = cache.ap().rearrange("g (u t) e d -> (g u) (t e d)", u=4)
                    k = s_ * 4 + q
                    S.custom_dma("pool", lambda e: e.indirect_dma_start(
                        out=G[:, :], out_offset=None, in_=rows,
                        in_offset=bass.IndirectOffsetOnAxis(ap=idi[:, k:k + 1], axis=0)), r=[idi], w=[G], key=G)
                    return G

                for s_ in range(4):
                    ev = 0
                    for q in range(4):
                        G = qgather(ccache, s_, q)
                        G3 = G[:, :].rearrange("p (t e d) -> p t e d", t=32, e=2)
                        for ee in range(2):
                            for t0 in range(0, 32, 4):
                                P = pTf[ev % 2]

                                def trp(e, G3=G3, ee=ee, t0=t0, P=P):
                                    ins = None
                                    for j in range(4):
                                        ins = e.transpose(P[:, j * 128:j * 128 + 128], G3[:, t0 + j, ee, :], identf[:, :])
                                    return ins
                                S.pe(trp, r=[G, identf], w=[P])
                                dst = XcT[:, ee, 4096 * q:4096 * q + 4096].rearrange("d (p t) -> d t p", t=32)[:, t0:t0 + 4, :]
                                src = P[:, 0:512].rearrange("d (j p) -> d j p", j=4)
                                if ev % 2 == 0:
                                    S.act(lambda e, dst=dst, src=src: e.copy(out=dst, in_=src), r=[P], w=[XcT])
                                else:
                                    S.dve(lambda e, dst=dst, src=src: e.tensor_copy(out=dst, in_=src), r=[P], w=[XcT])
                                ev += 1
                    for ee in range(2):
                        for c0 in (0, 512):
                            cn = 512 if c0 == 0 else 511
                            P = pS[(c0 // 512) % 2]

                            def hmm(e, ee=ee, c0=c0, cn=cn, P=P):
                                ins = None
                                for rs_ in range(32):
                                    lo = rs_ + 16 * c0
                                    ins = e.matmul(P[:, :cn], w1sb[:, ee, rs_, :], XcT[:, ee, lo:lo + 16 * (cn - 1) + 1:16],
                                                   start=(rs_ == 0), stop=(rs_ == 31))
                                return ins
                            S.pe(hmm, r=[w1sb, XcT], w=[P])
                            S.act(lambda e, ee=ee, cn=cn, P=P: e.activation(out=hx[:, :cn], in_=P[:, :cn], func=AF.Identity,
                                                                             bias=biasb[:, ee:ee + 1]), r=[P, biasb], w=[hx])
                            S.dve(lambda e, cn=cn: e.tensor_tensor(out=ht[:, :cn], in0=hx[:, :cn], in1=hx[:, :cn], op=ALU.mult), r=[hx], w=[ht])
                            S.dve(lambda e, cn=cn: e.tensor_scalar(out=ht[:, :cn], in0=ht[:, :cn], scalar1=0.044715, scalar2=1.0,
                                                                   op0=ALU.mult, op1=ALU.add), r=[ht], w=[ht])
                            S.dve(lambda e, cn=cn: e.tensor_tensor(out=ht[:, :cn], in0=ht[:, :cn], in1=hx[:, :cn], op=ALU.mult), r=[ht, hx], w=[ht])
                            S.act(lambda e, cn=cn: e.activation(out=ht[:, :cn], in_=ht[:, :cn], func=AF.Sigmoid, scale=1.5957691216),
                                  r=[ht], w=[ht])
                            H = hT[ee]
                            S.dve(lambda e, cn=cn, H=H: e.tensor_tensor(out=H[:, :cn], in0=hx[:, :cn], in1=ht[:, :cn], op=ALU.mult),
                                  r=[hx, ht], w=[H])
                            if ee == 0:
                                S.pe(lambda e, cn=cn, H=H: e.matmul(pO[0][:, :cn], w2sb[:, 0, :], H[:, :cn], start=True, stop=True),
                                     r=[w2sb, H], w=[pO[0]])
                                S.act(lambda e, cn=cn, c0=c0: e.copy(out=KcT[:, c0:c0 + cn], in_=pO[0][:, :cn]), r=[pO[0]], w=[KcT])
                            else:
                                for t0 in range(0, cn, 128):
                                    nb = min(128, cn - t0)
                                    S.pe(lambda e, nb=nb, t0=t0, H=H: e.matmul(pO[1][:nb, 0:128], H[:, t0:t0 + nb], w2sb[:, 1, :],
                                                                               start=True, stop=True), r=[w2sb, H], w=[pO[1]])
                                    S.act(lambda e, nb=nb, ct=(c0 + t0) // 128: e.copy(out=Vc[:nb, ct, 0:128], in_=pO[1][:nb, 0:128]),
                                          r=[pO[1]], w=[Vc])

                    pc_ = [0]
                    Qs = QTs[:, :, 4 * s_:4 * s_ + 4]

                    def qk16(lhsT, nk, lr, maskchunk=None, Qs=Qs):
                        x = pc_[0] % 2
                        pc_[0] += 1
                        P, E = pS[x], Es[x]

                        def f(e):
                            ins = e.matmul(P[:nk, 0:16], lhsT, Qs, start=True, stop=(maskchunk is None))
                            if maskchunk is not None:
                                ch, kt = maskchunk
                                ins = e.matmul(P[:nk, 0:16], E2[:, :], nmT[:, ch, :, :], start=False, stop=True)
                            return ins
                        S.pe(f, r=[QTs, E2, nmT] + lr, w=[P])
                        S.act(lambda e: e.activation(out=E[:nk, :], in_=P[:nk, 0:16], func=AF.Exp, scale=SCALE), r=[P], w=[E])
                        return E

                    def pv16(Pacc, E, nk, rhs, ncol, first, last, r):
                        S.pe(lambda e: e.matmul(Pacc[:16, 0:ncol], E[:nk, :], rhs, start=first, stop=last), r=[E] + r, w=[Pacc])

                    def fin16(Pacc, br, first_branch):
                        S.dve(lambda e: e.tensor_scalar(out=rs[:, 0:1], in0=Pacc[:16, 128:129], scalar1=1e-30, scalar2=None, op0=ALU.max),
                              r=[Pacc], w=[rs])
                        S.dve(lambda e: e.reciprocal(out=rs[:, 0:1], in_=rs[:, 0:1]), r=[rs], w=[rs])
                        S.dve(lambda e: e.tensor_tensor(out=rs[:, 1:2], in0=rs[:, 0:1], in1=gs[:, br:br + 1], op=ALU.mult), r=[rs, gs], w=[rs])
                        if first_branch:
                            S.dve(lambda e: e.tensor_scalar(out=ocat[:, :], in0=Pacc[:16, 0:128], scalar1=rs[:, 1:2], scalar2=None, op0=ALU.mult),
                                  r=[Pacc, rs], w=[ocat])
                        else:
                            S.dve(lambda e: e.scalar_tensor_tensor(out=ocat[:, :], in0=Pacc[:16, 0:128], scalar=rs[:, 1:2], in1=ocat[:, :],
                                                                   op0=ALU.mult, op1=ALU.add), r=[Pacc, rs, ocat], w=[ocat])

                    S.pe(lambda e, s_=s_: e.matmul(pO[2][:16, 0:12], sel16[:, 4 + 16 * s_:4 + 16 * s_ + 16], GL[:16, NT, :], start=True, stop=True),
                         r=[sel16, GL], w=[pO[2]])
                    S.dve(lambda e: e.tensor_tensor(out=gt[:, :], in0=pO[2][:16, 0:12], in1=hmask[:, :], op=ALU.mult), r=[pO[2], hmask], w=[gt])
                    S.dve(lambda e: e.tensor_reduce(out=gs[:, 0:3], in_=gt[:, :].rearrange("p (b h) -> p b h", b=3), axis=AX.X, op=ALU.add),
                          r=[gt], w=[gs])

                    for ct in range(8):
                        E = qk16(KcT[:, ct * 128:ct * 128 + 128], 128, [KcT])
                        pv16(pO[0], E, 128, Vc[:, ct, 0:386], 386, ct == 0, ct == 7, [Vc])
                    fin16(pO[0], 0, True)
                    S.dve(lambda e: e.tensor_scalar(out=U[:, 0:257], in0=pO[0][:16, 129:386], scalar1=rs[:, 0:1], scalar2=None, op0=ALU.mult),
                          r=[pO[0], rs], w=[U])
                    S.pe(lambda e: e.matmul(pO[2][:4, 0:257], sel16b[:, 0:4], U[:, 0:257], start=True, stop=True), r=[sel16b, U], w=[pO[2]])
                    S.dve(lambda e: e.tensor_copy(out=sc[0][:, 0:257], in_=pO[2][:4, 0:257]), r=[pO[2]], w=[sc[0]])
                    S.dve(lambda e: e.memset(sc[0][:, 0:1], 2e9), w=[sc[0]])
                    S.dve(lambda e: e.memset(sc[0][:, 255:257], 2e9), w=[sc[0]])
                    S.dve(lambda e: e.max(out=m8[:, 0:8], in_=sc[0][:, 0:257]), r=[sc[0]], w=[m8])
                    S.dve(lambda e: e.match_replace(out=sc[1][:, 0:257], in_to_replace=m8[:, 0:8], in_values=sc[0][:, 0:257], imm_value=-3e38),
                          r=[sc[0], m8], w=[sc[1]])
                    S.dve(lambda e: e.max(out=m8[:, 8:16], in_=sc[1][:, 0:257]), r=[sc[1]], w=[m8])
                    S.dve(lambda e: e.memset(nm[:, :], 0.0), w=[nm])
                    S.dve(lambda e: e.tensor_scalar(out=sc[1][:, 0:257], in0=sc[0][:, 0:257], scalar1=m8[:, 15:16], scalar2=None, op0=ALU.is_ge),
                          r=[sc[0], m8], w=[sc[1]])
                    S.dve(lambda e: e.tensor_scalar(out=nm[:, 0:257], in0=sc[1][:, 0:257], scalar1=-NEG, scalar2=NEG, op0=ALU.mult, op1=ALU.add),
                          r=[sc[1]], w=[nm])

                    def trn(e):
                        ins = None
                        for ch in range(5):
                            ins = e.transpose(pM[:64, ch * 4:ch * 4 + 4], nm[:, ch * 64:ch * 64 + 64], identb[:4, :4])
                        return ins
                    S.pe(trn, r=[nm, identb], w=[pM])
                    S.act(lambda e: e.copy(out=nmT[:, :, :, :], in_=pM[:64, 0:20].rearrange("p (c q) -> p c q", c=5).unsqueeze(2)
                                           .broadcast_to([64, 5, 4, 4])), r=[pM], w=[nmT])

                    S.dve(lambda e: e.memset(Vs3[:, :, 128:129], 1.0), w=[Vs])
                    ev = 0
                    for q in range(4):
                        G = qgather(scache, s_, q)
                        G3 = G[:, :].rearrange("p (t e d) -> p t e d", t=32, e=2)
                        for t0 in range(0, 32, 4):
                            P = pTf[ev % 2]
                            ev += 1
                            kt0 = q * 32 + t0

                            def trk(e, G3=G3, t0=t0, P=P):
                                ins = None
                                for j in range(4):
                                    ins = e.transpose(P[:, j * 128:j * 128 + 128], G3[:, t0 + j, 0, :], identf[:, :])
                                return ins
                            S.pe(trk, r=[G, identf], w=[P])
                            S.act(lambda e, P=P, kt0=kt0: e.copy(out=KsT[:, 0, kt0 * 128:kt0 * 128 + 512], in_=P[:, 0:512]), r=[P], w=[KsT])
                            S.dve(lambda e, G3=G3, t0=t0, kt0=kt0: e.tensor_copy(out=Vs3[:, kt0:kt0 + 4, 0:128], in_=G3[:, t0:t0 + 4, 1, :]),
                                  r=[G], w=[Vs])
                    for kt in range(NP):
                        E = qk16(KsT[:, 0, kt * 128:kt * 128 + 128], 128, [KsT], maskchunk=(kt // 32, kt))
                        pv16(pO[1], E, 128, Vs3[:, kt, 0:129], 129, kt == 0, False, [Vs])
                    E = qk16(KTs[:, 1, :], 16, [KTs])
                    S.dve(lambda e, E=E, s_=s_: e.tensor_tensor(out=E[:16, :], in0=E[:16, :], in1=smask[:, s_, :], op=ALU.mult), r=[E, smask], w=[E])
                    pv16(pO[1], E, 16, VAs[:, 0, 0:129], 129, False, True, [VAs])
                    fin16(pO[1], 1, False)
                    S.dma("sp", wst[:, :, :, :], wstate.ap()[s_].rearrange("(t p) e d -> p t e d", p=128), w=[wst], key=wst)
                    for t_ in range(4):
                        P = pTf[t_ % 2]
                        S.pe(lambda e, t_=t_, P=P: e.transpose(P[:, 0:128], wst[:, t_, 0, :], identf[:, :]), r=[wst, identf], w=[P])
                        S.act(lambda e, t_=t_, P=P: e.copy(out=KwT[:, t_ * 128:t_ * 128 + 128], in_=P[:, 0:128]), r=[P], w=[KwT])
                    S.dve(lambda e: e.tensor_copy(out=Vw[:, :, 0:128], in_=wst[:, :, 1, :]), r=[wst], w=[Vw])
                    for t_ in range(4):
                        E = qk16(KwT[:, t_ * 128:t_ * 128 + 128], 128, [KwT])
                        if t_ == 0:
                            S.dve(lambda e, E=E: e.tensor_tensor(out=E[:16, :], in0=E[:16, :], in1=smask[:, 4, :], op=ALU.mult), r=[E, smask], w=[E])
                        pv16(pO[0], E, 128, Vw[:, t_, 0:129], 129, t_ == 0, False, [Vw])
                    E = qk16(KTs[:, 2, :], 16, [KTs])
                    S.dve(lambda e, E=E, s_=s_: e.tensor_tensor(out=E[:16, :], in0=E[:16, :], in1=smask[:, s_, :], op=ALU.mult), r=[E, smask], w=[E])
                    pv16(pO[0], E, 16, VAs[:, 1, 0:129], 129, False, True, [VAs])
                    fin16(pO[0], 2, False)
                    S.act(lambda e: e.copy(out=ocb[:, :], in_=ocat[:, :]), r=[ocat], w=[ocb])
                    for h in range(4):
                        S.dma("sp", o_locs.ap()[4 * s_:4 * s_ + 4, 128 * h:128 * h + 128], ocb[4 * h:4 * h + 4, :], r=[ocb], w=[o_locT[4]], key=ocb)
                S.barrier()

        if RUN_S:
            phase_s()

        RG = [[0, 1, 2, 3], [4, 5, 6, 7]]
        def allgather(src_ap, dst_ap, rbuf, wbuf_):
            S.custom_dma("pool", lambda e: e.collective_compute("AllGather", ALU.bypass, replica_groups=RG, ins=[src_ap], outs=[dst_ap]),
                         r=[rbuf], w=[wbuf_], key=wbuf_, inc=1)
        for c in range(4):
            allgather(o_loc.ap()[c * 1024:c * 1024 + 1024, :].rearrange("(p a) f -> p (a f)", a=8),
                      og.ap()[c * 4096:c * 4096 + 4096, :].rearrange("(q a) f -> q (a f)", a=8), o_locT[c], ogT)
        allgather(o_locs.ap().rearrange("r (b x) -> (r b) x", b=8), ogs.ap().rearrange("r (b x) -> (r b) x", b=8), o_locT[4], ogT)

        TBS = [(0, 512), (512, 512), (1024, 4)]

        def token_phase(layer):
            with contextlib.ExitStack() as pd:
                hT = sbuf(pd, "hres%d" % layer, [128, 16, NTOK], F32)
                actT = sbuf(pd, "actT%d" % layer, [128, 8, NTOK], BF16)
                xnT = sbuf(pd, "xnT_d%d" % layer, [128, 16, NTOK], BF16)
                wbuf = [sbuf(pd, "wbuf%d_%d" % (i, layer), [128, 16, 512], BF16) for i in range(2)]
                xst = sbuf(pd, "xst%d" % layer, [128, D], F32)
                ost = [sbuf(pd, "ost%d_%d" % (i, layer), [128, 2, 512], BF16) for i in range(2)]
                oidx = sbuf(pd, "oidx_s%d" % layer, [128, 72], I32)
                rstd = sbuf(pd, "rstd_d%d" % layer, [128, 512], F32)
                sq = sbuf(pd, "sq_d%d" % layer, [128, 512], F32)
                rl = [sbuf(pd, "rl%d_%d" % (i, layer), [128, 512], F32) for i in range(2)]
                gT = sbuf(pd, "gT_d%d" % layer, [128, 16], F32)
                onesf = sbuf(pd, "onesf%d" % layer, [128, 128], F32)
                epsb = sbuf(pd, "epsb_d%d" % layer, [128, 1], F32)
                pz = [psum(pd, "dz%d_%d" % (i, layer), [128, 512], F32) for i in range(4)]
                pM = psum(pd, "dM%d" % layer, [128, 1024], BF16)
                pT = psum(pd, "dT%d" % layer, [128, 512], F32)
                pn = psum(pd, "dn%d" % layer, [128, 512], F32)
                ctr = {"w": 0, "z": 0, "o": 0, "r": 0}

                if layer == 0:
                    S.dma("sp", oidx[:, 0:36], oidx_d.ap(), w=[oidx], key=oidx)
                else:
                    S.dma("sp", oidx[:, :], oidx2_d.ap(), w=[oidx], key=oidx)
                S.dve(lambda e: e.memset(onesf[:], 1.0 / D), w=[onesf])
                S.dve(lambda e: e.memset(epsb[:], RMS_EPS), w=[epsb])

                if layer == 0:
                    for t in range(9):
                        n = 128 if t < 8 else 4
                        S.dma("sp", xst[:n, :], x_tok.ap()[t * 128:t * 128 + n, :], w=[xst], key=xst)
                        for q in range(4):
                            def trx(e, q=q, n=n):
                                ins = None
                                for kk in range(4):
                                    k = q * 4 + kk
                                    ins = e.transpose(pT[:, kk * 128:kk * 128 + n], xst[:n, k * 128:k * 128 + 128], identf[:n, :n])
                                return ins
                            S.pe(trx, r=[xst, identf], w=[pT])
                            S.act(lambda e, q=q, n=n, t=t: e.copy(out=hT[:, q * 4:q * 4 + 4, t * 128:t * 128 + n],
                                                                 in_=pT[:, :].rearrange("p (k t) -> p k t", k=4)[:, :, :n]), r=[pT], w=[hT])
                else:
                    S.dma("sp", hT[:, :, :], hspill.ap().rearrange("p (k t) -> p k t", k=16), r=[hspT], w=[hT], key=hT)

                def load_act(gsrc, gsrc_s, gbuf, colfn):
                    for t in range(9):
                        n = 128 if t < 8 else 4
                        O_ = ost[ctr["o"] % 2]
                        ctr["o"] += 1
                        for jj in range(2):
                            S.custom_dma("pool", lambda e, O_=O_, jj=jj, t=t, col=colfn(t, jj): e.indirect_dma_start(
                                out=O_[:, jj, :], out_offset=None, in_=(gsrc if t < 8 else gsrc_s).ap(),
                                in_offset=bass.IndirectOffsetOnAxis(ap=oidx[:, col:col + 1], axis=0)),
                                r=[oidx, gbuf], w=[O_], key=O_)

                        def tro(e, O_=O_, n=n):
                            ins = None
                            for kk in range(8):
                                jj, q = kk // 4, kk % 4
                                ins = e.transpose(pM[:, kk * 128:kk * 128 + n], O_[:n, jj, q * 128:q * 128 + 128], identb[:n, :n])
                            return ins
                        S.pe(tro, r=[O_, identb], w=[pM])
                        S.act(lambda e, n=n, t=t: e.copy(out=actT[:, :, t * 128:t * 128 + n],
                                                         in_=pM[:, :].rearrange("p (k t) -> p k t", k=8)[:, :, :n]), r=[pM], w=[actT])

                def proj_blocks(Wd, row0):
                    blks = []
                    for cb in range(4):
                        def dma(wb, cb=cb):
                            S.dma("pool", wb[:, 0:8, :], Wd.ap()[row0:row0 + 1024, cb * 512:cb * 512 + 512].rearrange("(k p) c -> p k c", p=128),
                                  w=[wb], key=wb)

                        def comp(wb, cb=cb):
                            for cc in range(4):
                                for (t0, nt) in TBS:
                                    P = pz[ctr["z"] % 4]
                                    ctr["z"] += 1

                                    def mm(e, wb=wb, cc=cc, t0=t0, nt=nt, P=P):
                                        ins = None
                                        for k in range(8):
                                            ins = e.matmul(P[:, :nt], wb[:, k, cc * 128:cc * 128 + 128], actT[:, k, t0:t0 + nt],
                                                           start=(k == 0), stop=(k == 7))
                                        return ins
                                    S.pe(mm, r=[wb, actT], w=[P])
                                    c = cb * 4 + cc
                                    S.dve(lambda e, P=P, c=c, t0=t0, nt=nt: e.tensor_tensor(out=hT[:, c, t0:t0 + nt], in0=P[:, :nt],
                                                                                             in1=hT[:, c, t0:t0 + nt], op=ALU.add),
                                          r=[P, hT], w=[hT])
                        blks.append((dma, comp))
                    return blks

                def run_blocks(blks):
                    bufs = []
                    for i, (dma, comp) in enumerate(blks):
                        if i == 0:
                            wb0 = wbuf[ctr["w"] % 2]
                            ctr["w"] += 1
                            dma(wb0)
                            bufs.append(wb0)
                        if i + 1 < len(blks):
                            wbn = wbuf[ctr["w"] % 2]
                            ctr["w"] += 1
                            blks[i + 1][0](wbn)
                            bufs.append(wbn)
                        comp(bufs[i])

                def proj_accum(Wd, row0):
                    run_blocks(proj_blocks(Wd, row0))

                def rmsnorm_T(gain_d, out_fn, wlist):
                    S.dma("sp", gT[:], gain_d.ap(), w=[gT], key=gT)
                    for (t0, nt) in TBS:
                        for k in range(16):
                            S.act(lambda e, k=k, t0=t0, nt=nt: e.activation(out=sq[:, :nt], in_=hT[:, k, t0:t0 + nt], func=AF.Square),
                                  r=[hT], w=[sq])
                            S.pe(lambda e, k=k, nt=nt: e.matmul(pn[:, :nt], onesf[:, :], sq[:, :nt], start=(k == 0), stop=(k == 15)),
                                 r=[onesf, sq], w=[pn])
                        S.act(lambda e, nt=nt: e.activation(out=rstd[:, :nt], in_=pn[:, :nt], func=AF.Sqrt, bias=epsb[:, 0:1]),
                              r=[pn, epsb], w=[rstd])
                        S.dve(lambda e, nt=nt: e.reciprocal(out=rstd[:, :nt], in_=rstd[:, :nt]), r=[rstd], w=[rstd])
                        for k in range(16):
                            S.dve(lambda e, k=k, t0=t0, nt=nt: e.scalar_tensor_tensor(out=out_fn(k, t0, nt), in0=hT[:, k, t0:t0 + nt],
                                                                                      scalar=gT[:, k:k + 1], in1=rstd[:, :nt],
                                                                                      op0=ALU.mult, op1=ALU.mult),
                                  r=[hT, gT, rstd], w=wlist)

                def ffn(W1, W2):
                    blks = []
                    for hg in range(8):
                        for sub in range(2):
                            c0 = hg * 1024 + sub * 512

                            def dma(wb, c0=c0):
                                S.dma("pool", wb[:, :, :], W1.ap()[:, c0:c0 + 512].rearrange("(k p) c -> p k c", p=128), w=[wb], key=wb)

                            def comp(wb, sub=sub):
                                for fc in range(4):
                                    for (t0, nt) in TBS:
                                        P = pz[ctr["z"] % 4]
                                        ctr["z"] += 1

                                        def mm(e, wb=wb, fc=fc, t0=t0, nt=nt, P=P):
                                            ins = None
                                            for k in range(16):
                                                ins = e.matmul(P[:, :nt], wb[:, k, fc * 128:fc * 128 + 128], xnT[:, k, t0:t0 + nt],
                                                               start=(k == 0), stop=(k == 15))
                                            return ins
                                        S.pe(mm, r=[wb, xnT], w=[P])
                                        R_ = rl[ctr["r"] % 2]
                                        ctr["r"] += 1
                                        S.act(lambda e, P=P, R_=R_, nt=nt: e.activation(out=R_[:, :nt], in_=P[:, :nt], func=AF.Relu), r=[P], w=[R_])
                                        eng = S.pool if ctr["r"] % 2 == 0 else S.dve
                                        eng(lambda e, R_=R_, nt=nt, t0=t0, kk=sub * 4 + fc: e.tensor_tensor(out=actT[:, kk, t0:t0 + nt], in0=R_[:, :nt],
                                                                                                          in1=R_[:, :nt], op=ALU.mult),
                                            r=[R_], w=[actT])
                            blks.append((dma, comp))
                        blks.extend(proj_blocks(W2, hg * 1024))
                    run_blocks(blks)

                xn_out = lambda k, t0, nt: xnT[:, k, t0:t0 + nt]
                if layer == 0:
                    for g in range(2):
                        load_act(og, ogs, ogT, lambda t, jj, g=g: t * 4 + 2 * g + jj)
                        proj_accum(w_out0, g * 1024)
                    rmsnorm_T(gffn0T, xn_out, [xnT])
                    ffn(ffn1_0, ffn2_0)
                    rmsnorm_T(gmix1T, xn_out, [xnT])
                    for cp in range(8):
                        S.dma("sp", xg_in.ap()[cp].rearrange("p (k t) -> p k t", k=2), xnT[:, 2 * cp:2 * cp + 2, :], r=[xnT],
                              w=[xg_inT[cp]], key=Buf("xgk%d" % cp))
                        allgather(xg_in.ap()[cp], xg_all.ap()[cp], xg_inT[cp], xgT)
                    S.dma("sp", hspill.ap().rearrange("p (k t) -> p k t", k=16), hT[:, :, :], r=[hT], w=[hspT], key=hT)
                else:
                    for g in range(4):
                        load_act(og2, ogs2, og2T, lambda t, jj, g=g: t * 8 + 2 * g + jj)
                        proj_accum(w_out1, g * 1024)
                    rmsnorm_T(gffn1T, xn_out, [xnT])
                    ffn(ffn1_1, ffn2_1)
                    S.dma("sp", gT[:], gfinT.ap(), w=[gT], key=gT)
                    for (t0, nt) in TBS:
                        for k in range(16):
                            S.act(lambda e, k=k, t0=t0, nt=nt: e.activation(out=sq[:, :nt], in_=hT[:, k, t0:t0 + nt], func=AF.Square),
                                  r=[hT], w=[sq])
                            S.pe(lambda e, k=k, nt=nt: e.matmul(pn[:, :nt], onesf[:, :], sq[:, :nt], start=(k == 0), stop=(k == 15)),
                                 r=[onesf, sq], w=[pn])
                        S.act(lambda e, nt=nt: e.activation(out=rstd[:, :nt], in_=pn[:, :nt], func=AF.Sqrt, bias=epsb[:, 0:1]),
                              r=[pn, epsb], w=[rstd])
                        S.dve(lambda e, nt=nt: e.reciprocal(out=rstd[:, :nt], in_=rstd[:, :nt]), r=[rstd], w=[rstd])
                        for k in range(16):
                            S.dve(lambda e, k=k, t0=t0, nt=nt: e.scalar_tensor_tensor(out=hT[:, k, t0:t0 + nt], in0=hT[:, k, t0:t0 + nt],
                                                                                      scalar=gT[:, k:k + 1], in1=rstd[:, :nt],
                                                                                      op0=ALU.mult, op1=ALU.mult),
                                  r=[hT, gT, rstd], w=[hT])
                    for t in range(9):
                        n = 128 if t < 8 else 4
                        for q in range(4):
                            def trh(e, q=q, n=n, t=t):
                                ins = None
                                for kk in range(4):
                                    ins = e.transpose(pT[:n, kk * 128:kk * 128 + 128], hT[:, q * 4 + kk, t * 128:t * 128 + n], identf[:, :])
                                return ins
                            S.pe(trh, r=[hT, identf], w=[pT])
                            S.act(lambda e, q=q, n=n: e.copy(out=xst[:n, q * 512:q * 512 + 512], in_=pT[:n, :]), r=[pT], w=[xst])
                        S.dma("sp", y_tok.ap()[t * 128:t * 128 + n, :], xst[:n, :], r=[xst], key=xst, final=True)
                S.barrier()

        def phase_e():
            with contextlib.ExitStack() as pe_:
                wr = sbuf(pe_, "wr_s", [128, 16, 3072], BF16)
                xt = [sbuf(pe_, "ext%d" % i, [128, 16, 128], BF16) for i in range(2)]
                cst = [sbuf(pe_, "ecs%d" % i, [128, 256], F32) for i in range(2)]
                St = sbuf(pe_, "St", [128, 4, 512], F32)
                Sb = sbuf(pe_, "Sb", [128, 4, 512], BF16)
                Ss = [sbuf(pe_, "Ss%d" % i, [128, 4, 512], F32) for i in range(2)]
                Ssb = [sbuf(pe_, "Ssb%d" % i, [128, 4, 512], BF16) for i in range(4)]
                qkr = sbuf(pe_, "qkr", [128, 4, 256], BF16)
                Vb = sbuf(pe_, "Vb", [128, 2, 512], BF16)
                sg = sbuf(pe_, "sg", [128, 1024], F32)
                ob = [sbuf(pe_, "ob%d" % i, [128, 512], F32) for i in range(2)]
                jk = sbuf(pe_, "jk", [128, 512], BF16)
                rt = [sbuf(pe_, "ert%d" % i, [128, 4, 128], F32) for i in range(4)]
                qkT = sbuf(pe_, "qkT", [128, 8, 128], BF16)
                QdT = sbuf(pe_, "QdT", [128, 2, 2, 128], BF16)
                QdTm = sbuf(pe_, "QdTm", [128, 4, 2, 2, 16], BF16)
                Kd = sbuf(pe_, "Kd", [128, 2, 256], BF16)
                Kdm = sbuf(pe_, "Kdm", [16, 4, 2, 256], BF16)
                AT = sbuf(pe_, "AT", [128, 2, 128], BF16)
                decT = sbuf(pe_, "decT_s", [128, 2, 128], F32)
                qdecb = sbuf(pe_, "qdecb_s", [128, 2, 128], F32)
                kdec = sbuf(pe_, "kdec_s", [128, 2], F32)
                decTs = sbuf(pe_, "decTs_s", [16, 2, 16], F32)
                kdm = sbuf(pe_, "kdm_s", [16, 4, 2], F32)
                colmask = sbuf(pe_, "colmask_s", [128, 4, 16], F32)
                gnb = sbuf(pe_, "gnb_s", [128, 2, 512], F32)
                gpow = sbuf(pe_, "gpow_s", [128, 4], F32)
                qdecs = sbuf(pe_, "qdecs_s", [128, 2, 16], F32)
                st8 = sbuf(pe_, "st8", [128, 16], F32)
                og_ = [sbuf(pe_, "og_%d" % i, [128, 2, 512], BF16) for i in range(2)]
                pz = [psum(pe_, "ez%d" % i, [128, 512], F32) for i in range(2)]
                pq = psum(pe_, "eq", [128, 1024], BF16)
                pst = psum(pe_, "est", [128, 512], F32)
                po = [psum(pe_, "eo%d" % i, [128, 512], F32) for i in range(2)]
                pu = [psum(pe_, "eu%d" % i, [128, 512], F32) for i in range(2)]

                for kc in range(4):
                    for cq in range(3):
                        S.dma("pool", wr[:, 4 * kc:4 * kc + 4, 1024 * cq:1024 * cq + 1024],
                              wr_d.ap()[512 * kc:512 * kc + 512, 1024 * cq:1024 * cq + 1024].rearrange("(k p) c -> p k c", p=128),
                              w=[wr], key=wr)
                S.dma("sp", decT[:], decT_d.ap().rearrange("p (h i) -> p h i", h=2), w=[decT], key=decT)
                S.dma("sp", qdecb[:], qdecb_d.ap().rearrange("p (h i) -> p h i", h=2), w=[qdecb], key=qdecb)
                S.dma("sp", decTs[:], decTs_d.ap().rearrange("p (h i) -> p h i", h=2), w=[decTs], key=decTs)
                S.dma("sp", kdm[:], kdm_d.ap().rearrange("p (s h) -> p s h", s=4), w=[kdm], key=kdm)
                S.dma("sp", colmask[:], colmask_d.ap().rearrange("p (s i) -> p s i", s=4), w=[colmask], key=colmask)
                S.dma("sp", gnb[:], gnb_d.ap().rearrange("p (h e) -> p h e", h=2), w=[gnb], key=gnb)
                S.dma("sp", qdecs[:], qdecs_d.ap().rearrange("p (h i) -> p h i", h=2), w=[qdecs], key=qdecs)
                S.dma("sp", kdec[:], kdec_d.ap(), w=[kdec], key=kdec)
                S.dma("sp", gpow[:], gpow_d.ap(), w=[gpow], key=gpow)
                S.dve(lambda e: e.memset(St[:], 0.0), w=[St])
                S.dve(lambda e: e.memset(Sb[:], 0.0), w=[Sb])

                xg5 = xg_all.ap().rearrange("c (j p) (k t) -> c j p k t", j=4, k=2)

                def load_tile(T_):
                    X, C = xt[T_ % 2], cst[T_ % 2]
                    if T_ < NT:
                        j, tl = T_ // 8, T_ % 8
                        for cp in range(8):
                            S.dma("sp", X[:, 2 * cp:2 * cp + 2, :], xg5[cp, j, :, :, tl * 128:tl * 128 + 128], r=[xgT], w=[X], key=X)
                        S.dma("sp", C[:, :], cs_r.ap()[T_ * 128:T_ * 128 + 128, :], w=[C], key=C)
                    else:
                        for cp in range(8):
                            for j in range(4):
                                S.dma("sp", X[:, 2 * cp:2 * cp + 2, 4 * j:4 * j + 4], xg5[cp, j, :, :, 1024:1028], r=[xgT], w=[X], key=X)
                        S.dma("sp", C[:NS, :], cs_rs.ap(), w=[C], key=C)

                load_tile(0)
                zc = [0]
                for T_ in range(NT + 1):
                    n = 128 if T_ < NT else NS
                    X, C = xt[T_ % 2], cst[T_ % 2]
                    if T_ + 1 <= NT:
                        load_tile(T_ + 1)

                    def proj(cb, n=n, X=X):
                        P = pz[zc[0] % 2]
                        zc[0] += 1

                        def mm(e):
                            ins = None
                            for k in range(16):
                                ins = e.matmul(P[:n, :], X[:, k, :n], wr[:, k, cb * 512:cb * 512 + 512], start=(k == 0), stop=(k == 15))
                            return ins
                        S.pe(mm, r=[X, wr], w=[P])
                        return P

                    cosb = lambda n=n, C=C: C[:n, 0:128].unsqueeze(1).broadcast_to([n, 2, 128])
                    sinb = lambda n=n, C=C: C[:n, 128:256].unsqueeze(1).broadcast_to([n, 2, 128])
                    for half_ in range(2):
                        P = proj(half_)
                        z3 = P[:n, :].rearrange("p (h d) -> p h d", h=2)
                        a, b_, c_, d_ = rt

                        def emit_rope(z3=z3, P=P, half_=half_, n=n, cosb=cosb, sinb=sinb, C=C):
                            S.dve(lambda e: e.tensor_tensor(out=a[:n, :2, :], in0=z3[:, :, 0:128], in1=cosb(), op=ALU.mult), r=[P, C], w=[a])
                            S.dve(lambda e: e.tensor_tensor(out=b_[:n, :2, :], in0=z3[:, :, 128:256], in1=sinb(), op=ALU.mult), r=[P, C], w=[b_])
                            S.dve(lambda e: e.tensor_tensor(out=c_[:n, :2, :], in0=z3[:, :, 128:256], in1=cosb(), op=ALU.mult), r=[P, C], w=[c_])
                            S.dve(lambda e: e.tensor_tensor(out=d_[:n, :2, :], in0=z3[:, :, 0:128], in1=sinb(), op=ALU.mult), r=[P, C], w=[d_])
                            S.pool(lambda e: e.tensor_tensor(out=qkr[:n, 2 * half_:2 * half_ + 2, 0:128], in0=a[:n, :2, :], in1=b_[:n, :2, :],
                                                             op=ALU.subtract), r=[a, b_], w=[qkr])
                            S.pool(lambda e: e.tensor_tensor(out=qkr[:n, 2 * half_:2 * half_ + 2, 128:256], in0=c_[:n, :2, :], in1=d_[:n, :2, :],
                                                             op=ALU.add), r=[c_, d_], w=[qkr])
                        emit_rope()
                    for hh in range(2):
                        P = proj(2 + hh)
                        S.act(lambda e, P=P, hh=hh, n=n: e.copy(out=Vb[:n, hh, :], in_=P[:n, :]), r=[P], w=[Vb])
                    for hh in range(2):
                        P = proj(4 + hh)
                        S.act(lambda e, P=P, hh=hh, n=n: e.activation(out=sg[:n, hh * 512:hh * 512 + 512], in_=P[:n, :], func=AF.Silu), r=[P], w=[sg])
                    def trqk(e, n=n):
                        ins = None
                        for a_ in range(4):
                            for c in range(2):
                                ins = e.transpose(pq[:, (a_ * 2 + c) * 128:(a_ * 2 + c) * 128 + n], qkr[:n, a_, c * 128:c * 128 + 128], identb[:n, :n])
                        return ins
                    S.pe(trqk, r=[qkr, identb], w=[pq])
                    S.act(lambda e, n=n: e.copy(out=qkT[:, :, :n], in_=pq[:, :].rearrange("p (a t) -> p a t", a=8)[:, :, :n]), r=[pq], w=[qkT])
                    qT4 = qkT[:, 0:4, :].rearrange("p (h c) t -> p h c t", h=2)
                    kT4 = qkT[:, 4:8, :].rearrange("p (h c) t -> p h c t", h=2)

                    if T_ < NT:
                        S.dve(lambda e: e.tensor_tensor(out=QdT[:, :, :, :], in0=qT4, in1=qdecb[:, :, :].unsqueeze(2).broadcast_to([128, 2, 2, 128]),
                                                        op=ALU.mult), r=[qkT, qdecb], w=[QdT])
                        S.dve(lambda e: e.tensor_tensor(out=Kd[:, :, :], in0=qkr[:, 2:4, :], in1=kdec[:, :].unsqueeze(2).broadcast_to([128, 2, 256]),
                                                        op=ALU.mult), r=[qkr, kdec], w=[Kd])
                        def smm(e):
                            ins = None
                            for hh in range(2):
                                for c in range(2):
                                    ins = e.matmul(pst[:, hh * 128:hh * 128 + 128], kT4[:, hh, c, :], qT4[:, hh, c, :], start=(hh == 0 and c == 0),
                                                   stop=(c == 1), skip_group_check=True)
                            return ins
                        S.pe(smm, r=[qkT], w=[pst])
                        S.dve(lambda e: e.tensor_tensor(out=AT[:, :, :], in0=pst[:, 0:256].rearrange("p (h i) -> p h i", h=2), in1=decT[:, :, :],
                                                        op=ALU.mult), r=[pst, decT], w=[AT])
                        for hh in range(2):
                            def omm(e, hh=hh):
                                e.matmul(po[hh][:, :], AT[:, hh, :], Vb[:, hh, :], start=True, stop=False)
                                e.matmul(po[hh][:, :], QdT[:, hh, 0, :], Sb[:, hh * 2, :], start=False, stop=False)
                                return e.matmul(po[hh][:, :], QdT[:, hh, 1, :], Sb[:, hh * 2 + 1, :], start=False, stop=True)
                            S.pe(omm, r=[AT, Vb, QdT, Sb], w=[po[hh]])
                        for hh in range(2):
                            for c in range(2):
                                U_ = pu[c]
                                S.pe(lambda e, hh=hh, c=c, U_=U_: e.matmul(U_[:, :], Kd[:, hh, c * 128:c * 128 + 128], Vb[:, hh, :], start=True, stop=True),
                                     r=[Kd, Vb], w=[U_])
                                S.dve(lambda e, hh=hh, c=c, U_=U_: e.scalar_tensor_tensor(out=St[:, hh * 2 + c, :], in0=St[:, hh * 2 + c, :],
                                                                                         scalar=gpow[:, hh:hh + 1], in1=U_[:, :], op0=ALU.mult, op1=ALU.add),
                                      r=[St, gpow, U_], w=[St])
                                S.act(lambda e, hh=hh, c=c: e.copy(out=Sb[:, hh * 2 + c, :], in_=St[:, hh * 2 + c, :]), r=[St], w=[Sb])
                    else:
                        S.dve(lambda e: e.tensor_tensor(out=QdT[:, :, :, 0:NS], in0=qT4[:, :, :, 0:NS],
                                                        in1=qdecs[:, :, :].unsqueeze(2).broadcast_to([128, 2, 2, NS]), op=ALU.mult),
                              r=[qkT, qdecs], w=[QdT])
                        for s_ in range(4):
                            S.dve(lambda e, s_=s_: e.tensor_tensor(out=QdTm[:, s_, :, :, :], in0=QdT[:, :, :, 0:NS],
                                                                   in1=colmask[:, s_, :].unsqueeze(1).unsqueeze(2).broadcast_to([128, 2, 2, 16]),
                                                                   op=ALU.mult), r=[QdT, colmask], w=[QdTm])
                            S.dve(lambda e, s_=s_: e.tensor_tensor(out=Kdm[:, s_, :, :], in0=qkr[:NS, 2:4, :],
                                                                   in1=kdm[:, s_, :].unsqueeze(2).broadcast_to([NS, 2, 256]), op=ALU.mult),
                                  r=[qkr, kdm], w=[Kdm])

                        def smm_s(e):
                            ins = None
                            for hh in range(2):
                                for c in range(2):
                                    ins = e.matmul(pst[:NS, hh * 16:hh * 16 + 16], kT4[:, hh, c, 0:NS], qT4[:, hh, c, 0:NS], start=(hh == 0 and c == 0),
                                                   stop=(c == 1), skip_group_check=True)
                            return ins
                        S.pe(smm_s, r=[qkT], w=[pst])
                        S.dve(lambda e: e.tensor_tensor(out=AT[:NS, :, 0:NS], in0=pst[:NS, 0:32].rearrange("p (h i) -> p h i", h=2), in1=decTs[:, :, :],
                                                        op=ALU.mult), r=[pst, decTs], w=[AT])
                        for s_ in range(4):
                            SS = Ss[s_ % 2]
                            S.dma("sp", SS[:, :, :], sret_d.ap()[s_].rearrange("h (c p) e -> p (h c) e", p=128), w=[SS], key=SS)
                            S.act(lambda e, s_=s_, SS=SS: e.copy(out=Ssb[s_][:, :, :], in_=SS[:, :, :]), r=[SS], w=[Ssb[s_]])
                            for hh in range(2):
                                for c in range(2):
                                    U_ = pu[c]
                                    S.pe(lambda e, hh=hh, c=c, U_=U_, s_=s_: e.matmul(U_[:, :], Kdm[:, s_, hh, c * 128:c * 128 + 128], Vb[:NS, hh, :],
                                                                                      start=True, stop=True), r=[Kdm, Vb], w=[U_])
                                    S.dve(lambda e, hh=hh, c=c, U_=U_, SS=SS: e.scalar_tensor_tensor(out=SS[:, hh * 2 + c, :], in0=SS[:, hh * 2 + c, :],
                                                                                                    scalar=gpow[:, 2 + hh:3 + hh], in1=U_[:, :],
                                                                                                    op0=ALU.mult, op1=ALU.add),
                                          r=[SS, gpow, U_], w=[SS])
                            S.dma("sp", ret_s.ap()[s_].rearrange("h (c p) e -> p (h c) e", p=128), SS[:, :, :], r=[SS], key=SS, final=True)
                        for hh in range(2):
                            def omm_s(e, hh=hh):
                                ins = e.matmul(po[hh][:NS, :], AT[:NS, hh, 0:NS], Vb[:NS, hh, :], start=True, stop=False)
                                for s_ in range(4):
                                    for c in range(2):
                                        ins = e.matmul(po[hh][:NS, :], QdTm[:, s_, hh, c, :], Ssb[s_][:, hh * 2 + c, :], start=False,
                                                       stop=(s_ == 3 and c == 1))
                                return ins
                            S.pe(omm_s, r=[AT, Vb, QdTm] + Ssb, w=[po[hh]])

                    OG = og_[T_ % 2]
                    for hh in range(2):
                        OB = ob[hh]
                        S.act(lambda e, hh=hh, OB=OB, n=n: e.activation(out=OB[:n, :], in_=po[hh][:n, :], func=AF.Identity, accum_out=st8[:n, hh * 8:hh * 8 + 1]),
                              r=[po[hh]], w=[OB, st8])
                        S.act(lambda e, hh=hh, OB=OB, n=n: e.activation(out=jk[:n, :], in_=OB[:n, :], func=AF.Square, accum_out=st8[:n, hh * 8 + 1:hh * 8 + 2]),
                              r=[OB], w=[jk, st8])
                        o8 = hh * 8
                        S.dve(lambda e, o8=o8, n=n: e.tensor_scalar(out=st8[:n, o8 + 2:o8 + 4], in0=st8[:n, o8:o8 + 2], scalar1=1.0 / 512, scalar2=None,
                                                                    op0=ALU.mult), r=[st8], w=[st8])
                        S.dve(lambda e, o8=o8, n=n: e.tensor_tensor(out=st8[:n, o8 + 4:o8 + 5], in0=st8[:n, o8 + 2:o8 + 3], in1=st8[:n, o8 + 2:o8 + 3],
                                                                    op=ALU.mult), r=[st8], w=[st8])
                        S.dve(lambda e, o8=o8, n=n: e.tensor_tensor(out=st8[:n, o8 + 5:o8 + 6], in0=st8[:n, o8 + 3:o8 + 4], in1=st8[:n, o8 + 4:o8 + 5],
                                                                    op=ALU.subtract), r=[st8], w=[st8])
                        S.dve(lambda e, o8=o8, n=n: e.tensor_scalar(out=st8[:n, o8 + 5:o8 + 6], in0=st8[:n, o8 + 5:o8 + 6], scalar1=0.0, scalar2=1e-5,
                                                                    op0=ALU.max, op1=ALU.add), r=[st8], w=[st8])
                        S.act(lambda e, o8=o8, n=n: e.activation(out=st8[:n, o8 + 6:o8 + 7], in_=st8[:n, o8 + 5:o8 + 6], func=AF.Sqrt), r=[st8], w=[st8])
                        S.dve(lambda e, o8=o8, n=n: e.reciprocal(out=st8[:n, o8 + 6:o8 + 7], in_=st8[:n, o8 + 6:o8 + 7]), r=[st8], w=[st8])
                        S.dve(lambda e, o8=o8, OB=OB, n=n: e.tensor_scalar(out=OB[:n, :], in0=OB[:n, :], scalar1=st8[:n, o8 + 2:o8 + 3],
                                                                           scalar2=st8[:n, o8 + 6:o8 + 7], op0=ALU.subtract, op1=ALU.mult),
                              r=[OB, st8], w=[OB])
                        S.pool(lambda e, hh=hh, OB=OB, n=n: e.tensor_tensor(out=OB[:n, :], in0=OB[:n, :], in1=gnb[:n, hh, :], op=ALU.mult), r=[OB, gnb], w=[OB])
                        S.pool(lambda e, hh=hh, OB=OB, OG=OG, n=n: e.tensor_tensor(out=OG[:n, hh, :], in0=OB[:n, :], in1=sg[:n, hh * 512:hh * 512 + 512],
                                                                                   op=ALU.mult), r=[OB, sg], w=[OG])
                        if T_ < NT:
                            c_, tl = T_ // 8, T_ % 8
                            S.dma("sp", o_r.ap()[c_, hh, tl * 128:tl * 128 + 128, :], OG[:, hh, :], r=[OG], w=[o_rT[c_][hh]], key=OG)
                        else:
                            S.dma("sp", o_rs.ap()[hh], OG[:NS, hh, :], r=[OG], w=[o_rsT[hh]], key=OG)
                    if T_ < NT and T_ % 8 == 7:
                        c_ = T_ // 8
                        for hh in range(2):
                            base = ((c_ * 2 + hh) * 4) * 1024
                            allgather(o_r.ap()[c_, hh].rearrange("(p a) f -> p (a f)", a=8),
                                      og2.ap()[base:base + 4096, :].rearrange("(q a) f -> q (a f)", a=8), o_rT[c_][hh], og2T)
                    if T_ == NT - 1:
                        S.dma("sp", ret_p.ap().rearrange("h (c p) e -> p (h c) e", p=128), St[:, :, :], r=[St], key=St, final=True)
                for hh in range(2):
                    allgather(o_rs.ap()[hh].rearrange("r (b x) -> (r b) x", b=8),
                              ogs2.ap()[hh * 64:hh * 64 + 64, :].rearrange("r (b x) -> (r b) x", b=8), o_rsT[hh], og2T)
                S.barrier()

        if RUN_T0:
            token_phase(0)
        if RUN_E:
            phase_e()
        if RUN_T1:
            token_phase(1)

        S.emit()
    return nc


def _col_slice(g):
    cols = list(range(512 * g, 512 * g + 512))
    for e in (0, 1):
        for br in range(3):
            base = 2048 + ((br * 2 + e) * 4 + g) * 128
            cols += list(range(base, base + 128))
    for br in range(3):
        base = 2048 + 3072 + br * 16 + 4 * g
        cols += list(range(base, base + 4))
    return np.array(cols)


def _consts():
    c0 = np.arange(256)[:, None] * 16
    j0 = np.arange(64)[None, :] * 64
    cover = ((c0 < j0 + 64) & (c0 + 32 > j0)).astype(np.float32)
    cover[255] = 0.0
    p = np.arange(128)[:, None]
    col = np.arange(128)[None, :]
    I0 = (col - 16 * p).astype(np.float32)
    Jm = (np.arange(64)[None, :] - (p >= 64)).astype(np.float32)
    tri = np.concatenate([(p <= col), (col < p)], axis=1).astype(np.float32)
    Ebig = (np.arange(SEQ)[None, :] // 64 == np.arange(64)[:, None]).astype(np.float32)
    key = np.arange(16)
    hq = np.arange(16)
    smask = np.zeros((16, 5, 16), np.float32)
    for s_ in range(4):
        smask[:, s_, :] = ((key[:, None] // 4 == s_) & (key[:, None] % 4 <= hq[None, :] % 4))
    smask[:, 4, :] = (key[:, None] > hq[None, :] % 4)
    sel16 = np.zeros((16, 68), np.float32)
    sel16[:, 0:4] = (hq[:, None] % 4 == np.arange(4)[None, :])
    for s_ in range(4):
        sel16[:, 4 + 16 * s_:20 + 16 * s_] = (key[:, None] == 4 * s_ + hq[None, :] % 4)
    hmask = (np.arange(12)[None, :] % 4 == hq[:, None] // 4).astype(np.float32)
    c0s = np.arange(1024)[:, None] * 16
    j0s = np.arange(257)[None, :] * 64
    cover_s = ((c0s < j0s + 64) & (c0s + 32 > j0s)).astype(np.float32)
    cover_s[1023] = 0.0
    return {"cover": cover, "I0": I0, "Jm": Jm, "tri": tri, "Ebig": Ebig, "smask": smask, "sel16": sel16,
            "hmask": hmask, "cover_s": cover_s, "cs_r": rope_table(np.arange(SEQ), 128),
            "cs_rs": rope_table(np.tile(PAST + np.arange(4), 4), 128)}


CONST = _consts()


def _ret_cols(r):
    cols = []
    for base, w in ((0, 256), (2048, 256), (4096, 512), (8192, 512)):
        for hh in range(2):
            h = 2 * r + hh
            cols += list(range(base + w * h, base + w * h + w))
    return np.array(cols)


def _ret_consts(r):
    out = {}
    i = np.arange(128, dtype=np.float64)
    decT = np.zeros((128, 2, 128), np.float64)
    qdecb = np.zeros((128, 2, 128), np.float64)
    kdec = np.zeros((128, 2), np.float64)
    decTs = np.zeros((16, 2, 16), np.float64)
    kdm = np.zeros((16, 4, 2), np.float64)
    gpow = np.zeros((128, 4), np.float64)
    qdecs = np.zeros((128, 2, 16), np.float64)
    t16 = np.arange(16)
    for hh in range(2):
        h = 2 * r + hh
        lg = np.log(np.float32(1.0) - np.float32(2.0) ** np.float32(-5.0 - h)).astype(np.float64)
        rel = i[None, :] - i[:, None]
        decT[:, hh, :] = np.where(rel >= 0, np.exp(np.maximum(rel, 0) * lg), 0.0) / 16.0
        qdecb[:, hh, :] = np.exp((i + 1.0) * lg)[None, :]
        kdec[:, hh] = np.exp((127.0 - i) * lg) / 16.0
        rel4 = (t16[None, :] % 4) - (t16[:, None] % 4)
        same = (t16[None, :] // 4) == (t16[:, None] // 4)
        decTs[:, hh, :] = np.where(same & (rel4 >= 0), np.exp(np.maximum(rel4, 0) * lg), 0.0) / 16.0
        for s_ in range(4):
            kdm[:, s_, hh] = np.where(t16 // 4 == s_, np.exp((3.0 - t16 % 4) * lg), 0.0) / 16.0
        gpow[:, hh] = np.exp(128.0 * lg)
        gpow[:, 2 + hh] = np.exp(4.0 * lg)
        qdecs[:, hh, :] = np.exp((t16 % 4 + 1.0) * lg)[None, :]
    colmask = np.zeros((128, 4, 16), np.float32)
    for s_ in range(4):
        colmask[:, s_, :] = (t16 // 4 == s_)[None, :]
    out["decT"] = decT.reshape(128, 256).astype(np.float32)
    out["qdecb"] = qdecb.reshape(128, 256).astype(np.float32)
    out["kdec"] = kdec.astype(np.float32)
    out["decTs"] = decTs.reshape(16, 32).astype(np.float32)
    out["kdm"] = kdm.reshape(16, 8).astype(np.float32)
    out["gpow"] = gpow.astype(np.float32)
    out["qdecs"] = qdecs.reshape(128, 32).astype(np.float32)
    out["colmask"] = colmask.reshape(128, 64)
    return out


def _oidx2(r):
    p = np.arange(128)
    idx = np.zeros((128, 72), np.int32)
    for t in range(9):
        for j in range(4):
            for hh in range(2):
                if t < 8:
                    idx[:, t * 8 + 2 * j + hh] = ((r * 2 + hh) * 4 + j) * 1024 + t * 128 + p
                else:
                    idx[:, t * 8 + 2 * j + hh] = (hh * 4 + j) * NS + 4 * r + np.minimum(p, 3)
    return idx


def _oidx(r):
    p = np.arange(128)
    idx = np.zeros((128, 36), np.int32)
    for t in range(9):
        for j in range(4):
            if t < 8:
                idx[:, t * 4 + j] = r * 4096 + j * 1024 + t * 128 + p
            else:
                idx[:, t * 4 + j] = j * NS + 4 * r + np.minimum(p, 3)
    return idx


def make_in_maps(inp):
    maps = []
    ident = np.eye(128, dtype=np.float32)
    cs_p = rope_table(np.arange(SEQ), 64)
    cs_s = rope_table(np.tile(PAST + np.arange(4), 4), 64)
    for c in range(8):
        b, r = c // 4, c % 4
        m = {
            "xb": np.ascontiguousarray(inp["x_prompt"][b]),
            "xs": np.ascontiguousarray(inp["x_sample"][4 * b:4 * b + 4].reshape(NS, D)),
            "w_in": np.ascontiguousarray(inp["nsa_w_in"][0][:, _col_slice(r)]),
            "gmix0": np.ascontiguousarray(np.broadcast_to(inp["norm_mix"][0][None, :], (128, D))),
            "cs_p": cs_p, "cs_s": cs_s, "ident": ident,
            "cw1": inp["nsa_cmp_w1"][0], "cw2": inp["nsa_cmp_w2"][0],
            "posT": np.ascontiguousarray(inp["nsa_cmp_pos"][0].reshape(64, 128).T),
            "b1T": np.ascontiguousarray(inp["nsa_cmp_b1"][0].T),
            "x_tok": np.ascontiguousarray(np.concatenate([inp["x_prompt"][b, 1024 * r:1024 * r + 1024], inp["x_sample"][c]], axis=0)),
            "oidx": _oidx(r),
            "w_out0": inp["nsa_w_out"][0], "ffn1_0": inp["ffn_w1"][0], "ffn2_0": inp["ffn_w2"][0],
            "gffn0T": np.ascontiguousarray(inp["norm_ffn"][0].reshape(16, 128).T),
            "w_out1": inp["ret_w_out"][0], "ffn1_1": inp["ffn_w1"][1], "ffn2_1": inp["ffn_w2"][1],
            "gmix1T": np.ascontiguousarray(inp["norm_mix"][1].reshape(16, 128).T),
            "gffn1T": np.ascontiguousarray(inp["norm_ffn"][1].reshape(16, 128).T),
            "gfinT": np.ascontiguousarray(inp["norm_final"].reshape(16, 128).T),
            "wr": np.ascontiguousarray(inp["ret_w_in"][0][:, _ret_cols(r)]),
            "cs_r": CONST["cs_r"], "cs_rs": CONST["cs_rs"],
            "gnb": np.ascontiguousarray(np.broadcast_to(inp["ret_gn"][0][1024 * r:1024 * r + 1024][None, :], (128, 1024))),
            "sret": np.ascontiguousarray(inp["state_ret"][0][4 * b:4 * b + 4, 2 * r:2 * r + 2]),
            "oidx2": _oidx2(r),
            **_ret_consts(r),
            "ccache": np.ascontiguousarray(inp["cache_cmp_kv"][0][:, :, :, r, :]),
            "scache": np.ascontiguousarray(inp["cache_sel_kv"][0][:, :, :, r, :]),
            "wstate": np.ascontiguousarray(inp["state_win_kv"][0][4 * b:4 * b + 4, :, :, r, :]),
            "ptab": np.ascontiguousarray(inp["page_table"][4 * b:4 * b + 4].astype(np.int32)),
            "smask": CONST["smask"], "sel16": CONST["sel16"], "hmask": CONST["hmask"], "cover_s": CONST["cover_s"],
            "pidx": (np.arange(128) % 4).astype(np.float32)[:, None].copy(),
            "ptabT": np.ascontiguousarray(inp["page_table"][4 * b:4 * b + 4].astype(np.int32).reshape(4, 4, 32).transpose(2, 0, 1).reshape(32, 16)),
            "Rrep": (np.arange(128)[None, :] // 4 == np.arange(32)[:, None]).astype(np.float32),
            "E2": (np.arange(128)[None, :] // 2 == np.arange(64)[:, None]).astype(np.float32),
            "cover": CONST["cover"], "I0": CONST["I0"], "Jm": CONST["Jm"], "tri": CONST["tri"], "Ebig": CONST["Ebig"],
        }
        maps.append(m)
    return maps


_NC = None


def kernel(**inputs):
    global _NC
    inp = {k: np.asarray(v) for k, v in inputs.items()}
    if _NC is None:
        _NC = build()
    maps = make_in_maps(inp)
    res = run_bass_kernel_spmd(_NC, maps, core_ids=list(range(8)), **({'trace': True} if TRACE else {}))
    global LAST_RES
    LAST_RES = res
    R = res.results
    global LAST
    LAST = R
    cmp_p = np.zeros((1, 2, SEQ, 2, 4, 128), np.float32)
    sel_p = np.zeros_like(cmp_p)
    win_full = np.zeros_like(cmp_p)
    cmp_s = np.zeros((1, 8, 4, 2, 4, 128), np.float32)
    sel_s = np.zeros_like(cmp_s)
    win_new = np.zeros_like(cmp_s)
    for c in range(8):
        b, g = c // 4, c % 4
        kp = R[c]["kv_p"]
        cmp_p[0, b, :, :, g, :] = kp[:, 0]
        sel_p[0, b, :, :, g, :] = kp[:, 1]
        win_full[0, b, :, :, g, :] = kp[:, 2]
        ks = R[c]["kv_s"].reshape(4, 4, 3, 2, 128)
        cmp_s[0, 4 * b:4 * b + 4, :, :, g, :] = ks[:, :, 0]
        sel_s[0, 4 * b:4 * b + 4, :, :, g, :] = ks[:, :, 1]
        win_new[0, 4 * b:4 * b + 4, :, :, g, :] = ks[:, :, 2]
    win_p = np.ascontiguousarray(win_full[:, :, SEQ - 512:])
    y_p = np.zeros((2, SEQ, D), np.float32)
    y_s = np.zeros((8, 4, D), np.float32)
    win_s = np.zeros((1, 8, 512, 2, 4, 128), np.float32)
    ret_p = np.zeros((1, 2, 8, 256, 512), np.float32)
    ret_s = np.zeros((1, 8, 8, 256, 512), np.float32)
    for c in range(8):
        b, g = c // 4, c % 4
        win_s[0, 4 * b:4 * b + 4, :, :, g, :] = R[c]["win_s"]
        y_p[b, 1024 * g:1024 * g + 1024] = R[c]["y_tok"][:1024]
        y_s[c] = R[c]["y_tok"][1024:]
        ret_p[0, b, 2 * g:2 * g + 2] = R[c]["ret_p"]
        ret_s[0, 4 * b:4 * b + 4, 2 * g:2 * g + 2] = R[c]["ret_s"]
    return (y_p, y_s, cmp_p, cmp_s, sel_p, sel_s, win_p, win_s, ret_p, ret_s)
```

```python
import contextlib
import numpy as np
import concourse.bass as bass
import concourse.mybir as mybir
from concourse.bass_utils import run_bass_kernel_spmd

F32 = mybir.dt.float32
BF16 = mybir.dt.bfloat16
I32 = mybir.dt.int32
AF = mybir.ActivationFunctionType
ALU = mybir.AluOpType
AX = mybir.AxisListType

ENGS = ("sp", "act", "dve", "pool", "pe")


class Buf:
    __slots__ = ("name", "last_w", "readers", "cnt")

    def __init__(self, name):
        self.name = name
        self.last_w = None
        self.readers = []
        self.cnt = 0


class T:
    def __init__(self, t, name):
        self.t = t
        self.b = Buf(name)

    def __getitem__(self, k):
        return self.t[k]

    def ap(self):
        return self.t.ap()


def _b(x):
    return x.b if isinstance(x, T) else x


class Op:
    __slots__ = ("eng", "fn", "deps", "is_dma", "key", "sig", "sigidx", "cnt", "inc")

    def __init__(self, eng, fn, is_dma=False, key=None, inc=16):
        self.eng = eng
        self.fn = fn
        self.deps = []
        self.is_dma = is_dma
        self.key = key
        self.sig = False
        self.sigidx = 0
        self.cnt = 0
        self.inc = inc


class Sched:
    def __init__(self, nc):
        self.nc = nc
        self.ops = []
        self.last_real = {e: None for e in ENGS}
        self.out_dmas = []

    def _add(self, op, r, w):
        r = [_b(x) for x in r]
        w = [_b(x) for x in w]
        deps = []
        for b in r:
            if b.last_w is not None:
                deps.append(b.last_w)
        for b in w:
            if b.last_w is not None:
                deps.append(b.last_w)
            deps.extend(b.readers)
        seen = set()
        for d in deps:
            if d is op or id(d) in seen:
                continue
            seen.add(id(d))
            if (not d.is_dma) and d.eng == op.eng and op.eng == "pe" and not op.is_dma:
                continue
            op.deps.append(d)
            if not d.is_dma:
                d.sig = True
        for b in r:
            b.readers.append(op)
        for b in w:
            b.last_w = op
            b.readers = []
        self.ops.append(op)
        if not op.is_dma:
            self.last_real[op.eng] = op
        return op

    def op(self, eng, fn, r=(), w=()):
        return self._add(Op(eng, fn), r, w)

    def pe(self, fn, r=(), w=()):
        return self.op("pe", fn, r, w)

    def act(self, fn, r=(), w=()):
        return self.op("act", fn, r, w)

    def dve(self, fn, r=(), w=()):
        return self.op("dve", fn, r, w)

    def pool(self, fn, r=(), w=()):
        return self.op("pool", fn, r, w)

    def dma(self, q, out, in_, r=(), w=(), key=None, final=False, **kw):
        return self.custom_dma(q, lambda e: e.dma_start(out=out, in_=in_, **kw), r, w, key, 16, final)

    def custom_dma(self, q, fn, r=(), w=(), key=None, inc=16, final=False):
        k = _b(key)
        op = Op(q, fn, is_dma=True, key=k, inc=inc)
        self._add(op, r, w)
        if final:
            self.out_dmas.append(op)
        return op

    def barrier(self):
        pend = [o for o in self.last_real.values() if o is not None]
        latest = {}
        for o in self.ops:
            if o.is_dma:
                latest[id(o.key)] = o
        for e in ENGS:
            op = Op(e, None)
            for d in pend:
                if d.eng != e:
                    op.deps.append(d)
                    d.sig = True
            op.deps.extend(latest.values())
            self.ops.append(op)

    def emit(self):
        nc = self.nc
        fin = Op("sp", None)
        latest = {}
        for o in self.out_dmas:
            latest[id(o.key)] = o
        fin.deps = list(latest.values())
        for e in ENGS:
            o = self.last_real[e]
            if e != "sp" and o is not None:
                fin.deps.append(o)
                o.sig = True
        self.ops.append(fin)

        with contextlib.ExitStack() as st:
            esem = {e: st.enter_context(nc.semaphore("s_" + e)) for e in ENGS}
            keysems = {}
            keyvals = {}
            for o in self.ops:
                if o.is_dma:
                    kid = id(o.key)
                    if kid not in keysems:
                        keysems[kid] = st.enter_context(nc.semaphore("k%d" % len(keysems)))
                        keyvals[kid] = 0
                    keyvals[kid] += o.inc
                    o.cnt = keyvals[kid]
            cnts = {e: 0 for e in ENGS}
            for o in self.ops:
                if (not o.is_dma) and o.sig:
                    assert o.fn is not None
                    cnts[o.eng] += 1
                    o.sigidx = cnts[o.eng]
            self.n_sems = len(keysems) + 5
            per = {e: [o for o in self.ops if o.eng == e] for e in ENGS}
            block = st.enter_context(nc.Block())

            def run(eng_name):
                def body(eng):
                    waited = {}
                    for o in per[eng_name]:
                        for d in o.deps:
                            if d.is_dma:
                                s, v = keysems[id(d.key)], d.cnt
                            else:
                                s, v = esem[d.eng], d.sigidx
                            if waited.get(id(s), 0) >= v:
                                continue
                            waited[id(s)] = v
                            eng.wait_ge(s, v)
                        if o.fn is None:
                            continue
                        ins = o.fn(eng)
                        if o.is_dma:
                            ins.then_inc(keysems[id(o.key)], o.inc)
                        elif o.sig:
                            ins.then_inc(esem[eng_name], 1)

                return body

            block.sync(run("sp"))
            block.scalar(run("act"))
            block.vector(run("dve"))
            block.gpsimd(run("pool"))
            block.tensor(run("pe"))


D = 2048
SEQ = 4096
NT = SEQ // 128
NTOK = 1028
NS = 16
PAST = 16384
NCOL = 1292
RMS_EPS = 1e-6
SCALE = 128 ** -0.5
NEG = -30000.0

STAGE = 1
DEBUG = False
NTQ = NT
RUN_S = True
RUN_T0 = True
RUN_E = True
RUN_T1 = True
TRACE = False


def rope_table(pos, half):
    inv = (10000.0 ** (-(np.arange(half, dtype=np.float32)) / np.float32(half))).astype(np.float32)
    ang = (pos.astype(np.float32)[:, None] * inv[None, :]).astype(np.float32)
    return np.concatenate([np.cos(ang), np.sin(ang)], axis=1).astype(np.float32)


def build(stage=STAGE):
    nc = bass.Bass("TRN2", target_bir_lowering=False)
    S = Sched(nc)

    def din(name, shape, dt=F32):
        return nc.dram_tensor(name, list(shape), dt, kind="ExternalInput")

    def dout(name, shape, dt=F32):
        return nc.dram_tensor(name, list(shape), dt, kind="ExternalOutput")

    xb = din("xb", [SEQ, D])
    xs = din("xs", [NS, D])
    w_in = din("w_in", [D, NCOL])
    gmix0 = din("gmix0", [128, D])
    cs_p = din("cs_p", [SEQ, 128])
    cs_s = din("cs_s", [NS, 128])
    ident_d = din("ident", [128, 128])

    cw1 = din("cw1", [2, 32, 128, 128])
    cw2 = din("cw2", [2, 128, 128])
    posT = din("posT", [128, 64])
    b1T = din("b1T", [128, 2])
    cover_d = din("cover", [256, 64])
    I0_d = din("I0", [128, 128])
    Jm_d = din("Jm", [128, 64])
    tri_d = din("tri", [128, 256])
    Ebig_d = din("Ebig", [64, SEQ])

    kv_p = dout("kv_p", [SEQ, 3, 2, 128])
    o_loc = nc.dram_tensor("o_loc", [SEQ, 512], BF16)
    o_locs = nc.dram_tensor("o_locs", [NS, 512], BF16)
    og = nc.dram_tensor("og", [4 * SEQ, 512], BF16)
    ogs = nc.dram_tensor("ogs", [4 * NS, 512], BF16)
    o_locT = [Buf("o_loc%d" % i) for i in range(5)]
    ogT = Buf("og")
    x_tok = din("x_tok", [NTOK, D])
    oidx_d = din("oidx", [128, 36], I32)
    w_out0 = din("w_out0", [D, D])
    ffn1_0 = din("ffn1_0", [D, 4 * D])
    ffn2_0 = din("ffn2_0", [4 * D, D])
    gffn0T = din("gffn0T", [128, 16])
    w_out1 = din("w_out1", [2 * D, D])
    ffn1_1 = din("ffn1_1", [D, 4 * D])
    ffn2_1 = din("ffn2_1", [4 * D, D])
    gmix1T = din("gmix1T", [128, 16])
    gffn1T = din("gffn1T", [128, 16])
    gfinT = din("gfinT", [128, 16])
    wr_d = din("wr", [D, 3072])
    cs_r = din("cs_r", [SEQ, 256])
    cs_rs = din("cs_rs", [NS, 256])
    decT_d = din("decT", [128, 256])
    qdecb_d = din("qdecb", [128, 256])
    kdec_d = din("kdec", [128, 2])
    decTs_d = din("decTs", [16, 32])
    kdm_d = din("kdm", [16, 8])
    colmask_d = din("colmask", [128, 64])
    gnb_d = din("gnb", [128, 1024])
    gpow_d = din("gpow", [128, 4])
    qdecs_d = din("qdecs", [128, 32])
    sret_d = din("sret", [4, 2, 256, 512])
    oidx2_d = din("oidx2", [128, 72], I32)
    ret_p = dout("ret_p", [2, 256, 512])
    ret_s = dout("ret_s", [4, 2, 256, 512])
    y_tok = dout("y_tok", [NTOK, D])
    hspill = nc.dram_tensor("hspill", [128, 16 * NTOK], F32)
    hspT = Buf("hspill")
    xg_in = nc.dram_tensor("xg_in", [8, 128, 2 * NTOK], BF16)
    xg_all = nc.dram_tensor("xg_all", [8, 512, 2 * NTOK], BF16)
    xg_inT = [Buf("xg_in%d" % i) for i in range(8)]
    xgT = Buf("xg_all")
    o_r = nc.dram_tensor("o_r", [4, 2, 1024, 512], BF16)
    o_rs = nc.dram_tensor("o_rs", [2, NS, 512], BF16)
    og2 = nc.dram_tensor("og2", [4 * 2 * 4 * 1024, 512], BF16)
    ogs2 = nc.dram_tensor("ogs2", [2 * 4 * NS, 512], BF16)
    o_rT = [[Buf("o_r%d%d" % (c, h)) for h in range(2)] for c in range(4)]
    o_rsT = [Buf("o_rs%d" % h) for h in range(2)]
    og2T = Buf("og2")
    win_s = dout("win_s", [4, 512, 2, 128])
    ccache = din("ccache", [1280, 128, 2, 128])
    scache = din("scache", [1280, 128, 2, 128])
    wstate = din("wstate", [4, 512, 2, 128])
    ptab = din("ptab", [4, 128], I32)
    smask_d = din("smask", [16, 5, 16])
    sel16_d = din("sel16", [16, 4 + 64])
    hmask_d = din("hmask", [16, 12])
    cover_s = din("cover_s", [1024, 257])
    pidx_d = din("pidx", [128, 1])
    ptabT = din("ptabT", [32, 16], I32)
    Rrep_d = din("Rrep", [32, 128])
    E2_d = din("E2", [64, 128])
    kv_s = dout("kv_s", [NS, 3, 2, 128])

    dbg_outs = {}

    def dbg(name, t, ap, shape, dt=F32):
        if not DEBUG:
            return
        d_ = nc.dram_tensor("dbg_" + name, list(shape), dt, kind="ExternalOutput")
        S.dma("sp", d_.ap(), ap, r=[t], key=Buf("dbgk_" + name), final=True)

    with contextlib.ExitStack() as top:
        def sbuf(st, name, shape, dt):
            return T(st.enter_context(nc.sbuf_tensor(name, list(shape), dt)), name)

        def psum(st, name, shape, dt):
            return T(st.enter_context(nc.psum_tensor(name, list(shape), dt)), name)

        identb = sbuf(top, "identb", [128, 128], BF16)
        identf = sbuf(top, "identf", [128, 128], F32)
        GL = sbuf(top, "GL", [128, NT + 1, 12], F32)
        QTs = sbuf(top, "QTs", [128, 4, NS], BF16)
        KTs = sbuf(top, "KTs", [128, 3, NS], BF16)
        VAs = sbuf(top, "VAs", [NS, 2, 132], BF16)
        pp = contextlib.ExitStack()
        QT = sbuf(pp, "QT", [128, 4, SEQ + NS], BF16)
        KT = sbuf(pp, "KT", [128, 3, SEQ + NS], BF16)
        VcT = sbuf(pp, "VcT", [128, SEQ + NS], BF16)
        VA = sbuf(pp, "VA", [128, NT + 1, 2, 132], BF16)
        S.dma("sp", identf[:], ident_d.ap(), w=[identf], key=identf)
        S.dma("pool", identb[:], ident_d.ap(), w=[identb], key=identb)

        def phase_a():
            with contextlib.ExitStack() as pa:
                wsb = sbuf(pa, "wsb", [128, 16, NCOL], BF16)
                gsb = sbuf(pa, "gsb", [128, D], F32)
                xt = [sbuf(pa, "xt%d" % i, [128, D], F32) for i in range(2)]
                cst = [sbuf(pa, "cst%d" % i, [128, 128], F32) for i in range(2)]
                junk = sbuf(pa, "junk", [128, D], BF16)
                ss = sbuf(pa, "ss", [128, 2], F32)
                epsb = sbuf(pa, "epsb", [128, 1], F32)
                S.pool(lambda e: e.memset(epsb[:], RMS_EPS), w=[epsb])
                xn = sbuf(pa, "xn", [128, D], BF16)
                xnT = sbuf(pa, "xnT", [128, 16, 128], BF16)
                kvf = [sbuf(pa, "kvf%d" % i, [128, 3, 2, 128], F32) for i in range(2)]
                rt = [sbuf(pa, "rt%d" % i, [128, 4, 64], F32) for i in range(4)]
                qkb = sbuf(pa, "qkb", [128, 8, 128], BF16)
                pT = [psum(pa, "pT%d" % i, [128, 1024], BF16) for i in range(2)]
                pz = [psum(pa, "pz%d" % i, [128, 512], F32) for i in range(3)]
                pq = psum(pa, "pq", [128, 1024], BF16)

                for kc in range(4):
                    S.dma("pool", wsb[:, 4 * kc:4 * kc + 4, :],
                          w_in.ap()[512 * kc:512 * kc + 512, :].rearrange("(k p) c -> p k c", p=128),
                          w=[wsb], key=wsb)
                S.dma("sp", gsb[:], gmix0.ap(), w=[gsb], key=gsb)
                S.pool(lambda e: e.memset(VA[:, :, :, 128:129], 1.0), w=[VA])

                for j in range(NT + 1):
                    n = 128 if j < NT else NS
                    c0 = j * 128
                    xsrc = xb.ap()[c0:c0 + 128, :] if j < NT else xs.ap()
                    csrc = cs_p.ap()[c0:c0 + 128, :] if j < NT else cs_s.ap()
                    X = xt[j % 2]
                    C = cst[j % 2]
                    KV = kvf[j % 2]

                    def load(jj):
                        nn = 128 if jj < NT else NS
                        cc = jj * 128
                        S.dma("sp", xt[jj % 2][:nn, :], xb.ap()[cc:cc + 128, :] if jj < NT else xs.ap(), w=[xt[jj % 2]], key=xt[jj % 2])
                        S.dma("sp", cst[jj % 2][:nn, :], cs_p.ap()[cc:cc + 128, :] if jj < NT else cs_s.ap(), w=[cst[jj % 2]], key=cst[jj % 2])
                    if j == 0:
                        load(0)
                    if j + 1 <= NT:
                        load(j + 1)
                    S.act(lambda e, X=X, n=n: e.activation(out=junk[:n, :], in_=X[:n, :], func=AF.Square,
                                                           accum_out=ss[:n, 0:1]), r=[X], w=[junk, ss])
                    S.act(lambda e, n=n: e.activation(out=ss[:n, 1:2], in_=ss[:n, 0:1], func=AF.Sqrt, scale=1.0 / D,
                                                      bias=epsb[:n, 0:1]), r=[ss, epsb], w=[ss])
                    S.dve(lambda e, n=n: e.reciprocal(out=ss[:n, 1:2], in_=ss[:n, 1:2]), r=[ss], w=[ss])
                    S.dve(lambda e, X=X, n=n: e.scalar_tensor_tensor(out=xn[:n, :], in0=X[:n, :], scalar=ss[:n, 1:2],
                                                                     in1=gsb[:n, :], op0=ALU.mult, op1=ALU.mult),
                          r=[X, ss, gsb], w=[xn])
                    for hb in range(2):
                        def tr(e, hb=hb, n=n):
                            ins = None
                            for k in range(8):
                                ins = e.transpose(pT[hb][:, k * 128:k * 128 + n], xn[:n, (hb * 8 + k) * 128:(hb * 8 + k + 1) * 128],
                                                  identb[:n, :n])
                            return ins
                        S.pe(tr, r=[xn, identb], w=[pT[hb]])
                        S.act(lambda e, hb=hb, n=n: e.copy(out=xnT[:, hb * 8:hb * 8 + 8, :n],
                                                           in_=pT[hb][:, :].rearrange("p (k t) -> p k t", k=8)[:, :, :n]),
                              r=[pT[hb]], w=[xnT])
                    cbs = [(0, 512), (512, 512), (1024, NCOL - 1024)]
                    for ci, (cb, cw) in enumerate(cbs):
                        def mm(e, ci=ci, cb=cb, cw=cw, n=n):
                            ins = None
                            for k in range(16):
                                ins = e.matmul(pz[ci][:n, :cw], xnT[:, k, :n], wsb[:, k, cb:cb + cw],
                                               start=(k == 0), stop=(k == 15))
                            return ins
                        S.pe(mm, r=[xnT, wsb], w=[pz[ci]])
                    cosb = lambda h, n=n, C=C: C[:n, 0:64].unsqueeze(1).broadcast_to([n, h, 64])
                    sinb = lambda h, n=n, C=C: C[:n, 64:128].unsqueeze(1).broadcast_to([n, h, 64])

                    def rope(src3, h, out_lo, out_hi, rd, wr, n=n, cosb=cosb, sinb=sinb):
                        a, b_, c_, d_ = rt
                        S.dve(lambda e: e.tensor_tensor(out=a[:n, :h, :], in0=src3[:, :, 0:64], in1=cosb(h), op=ALU.mult), r=rd + [C], w=[a])
                        S.dve(lambda e: e.tensor_tensor(out=b_[:n, :h, :], in0=src3[:, :, 64:128], in1=sinb(h), op=ALU.mult), r=rd + [C], w=[b_])
                        S.dve(lambda e: e.tensor_tensor(out=c_[:n, :h, :], in0=src3[:, :, 64:128], in1=cosb(h), op=ALU.mult), r=rd + [C], w=[c_])
                        S.dve(lambda e: e.tensor_tensor(out=d_[:n, :h, :], in0=src3[:, :, 0:64], in1=sinb(h), op=ALU.mult), r=rd + [C], w=[d_])
                        S.pool(lambda e: e.tensor_tensor(out=out_lo, in0=a[:n, :h, :], in1=b_[:n, :h, :], op=ALU.subtract), r=[a, b_], w=wr)
                        S.pool(lambda e: e.tensor_tensor(out=out_hi, in0=c_[:n, :h, :], in1=d_[:n, :h, :], op=ALU.add), r=[c_, d_], w=wr)

                    z0 = pz[0][:n, :].rearrange("p (h d) -> p h d", h=4)
                    rope(z0, 4, qkb[:n, 0:4, 0:64], qkb[:n, 0:4, 64:128], [pz[0]], [qkb])
                    z1 = pz[1][:n, 0:384].rearrange("p (h d) -> p h d", h=3)
                    rope(z1, 3, KV[:n, :, 0, 0:64], KV[:n, :, 0, 64:128], [pz[1]], [KV])
                    S.act(lambda e, n=n, KV=KV: e.copy(out=KV[:n, 0, 1, :], in_=pz[1][:n, 384:512]), r=[pz[1]], w=[KV])
                    S.act(lambda e, n=n, KV=KV: e.copy(out=KV[:n, 1:3, 1, :],
                                                       in_=pz[2][:n, 0:256].rearrange("p (h d) -> p h d", h=2)),
                          r=[pz[2]], w=[KV])
                    S.act(lambda e, n=n, j=j: e.copy(out=GL[:n, j, :], in_=pz[2][:n, 256:268]), r=[pz[2]], w=[GL])
                    S.pool(lambda e, n=n, KV=KV: e.tensor_copy(out=qkb[:n, 4:7, :], in_=KV[:n, :, 0, :]), r=[KV], w=[qkb])
                    S.pool(lambda e, n=n, KV=KV: e.tensor_copy(out=qkb[:n, 7, :], in_=KV[:n, 0, 1, :]), r=[KV], w=[qkb])
                    S.pool(lambda e, n=n, KV=KV, j=j: e.tensor_copy(out=VA[:n, j, :, 0:128], in_=KV[:n, 1:3, 1, :]), r=[KV], w=[VA])
                    dst = kv_p.ap()[c0:c0 + 128] if j < NT else kv_s.ap()
                    S.dma("sp", dst, KV[:n], r=[KV], key=KV, final=True)
                    if j == NT:
                        for s_ in range(4):
                            S.dma("sp", win_s.ap()[s_, 508:512], KV[4 * s_:4 * s_ + 4, 2], r=[KV], key=KV, final=True)
                    def tr2(e, n=n):
                        ins = None
                        for k in range(8):
                            ins = e.transpose(pq[:, k * 128:k * 128 + n], qkb[:n, k, :], identb[:n, :n])
                        return ins
                    S.pe(tr2, r=[qkb, identb], w=[pq])
                    pq3 = pq[:, :].rearrange("p (k t) -> p k t", k=8)
                    S.act(lambda e, n=n, c0=c0, pq3=pq3: e.copy(out=QT[:, :, c0:c0 + n], in_=pq3[:, 0:4, :n]), r=[pq], w=[QT])
                    S.act(lambda e, n=n, c0=c0, pq3=pq3: e.copy(out=KT[:, :, c0:c0 + n], in_=pq3[:, 4:7, :n]), r=[pq], w=[KT])
                    S.act(lambda e, n=n, c0=c0, pq3=pq3: e.copy(out=VcT[:, c0:c0 + n], in_=pq3[:, 7, :n]), r=[pq], w=[VcT])
                S.act(lambda e: e.copy(out=QTs[:, :, :], in_=QT[:, :, SEQ:SEQ + NS]), r=[QT], w=[QTs])
                S.act(lambda e: e.copy(out=KTs[:, :, :], in_=KT[:, :, SEQ:SEQ + NS]), r=[KT], w=[KTs])
                S.act(lambda e: e.copy(out=VAs[:, :, :], in_=VA[:NS, NT, :, :]), r=[VA], w=[VAs])
                S.barrier()

        phase_a()

        S.act(lambda e: e.activation(out=GL[:, :, :], in_=GL[:, :, :], func=AF.Sigmoid), r=[GL], w=[GL])

        def phase_c():
            with contextlib.ExitStack() as pc:
                w1sb = sbuf(pc, "w1sb", [128, 2, 32, 128], BF16)
                w2sb = sbuf(pc, "w2sb", [128, 2, 128], BF16)
                posb = sbuf(pc, "posb", [128, 64], BF16)
                b1sb = sbuf(pc, "b1sb", [128, 2], F32)
                biasb = sbuf(pc, "biasb", [128, 2], F32)
                I0 = sbuf(pc, "I0s", [128, 128], F32)
                Jm = sbuf(pc, "Jms", [128, 64], F32)
                trib = sbuf(pc, "trib", [128, 256], BF16)
                Ebig = sbuf(pc, "Ebigs", [64, SEQ], BF16)
                KcT = sbuf(pc, "KcT", [128, 256], BF16)
                VcA = sbuf(pc, "VcA", [128, 2, 196], BF16)
                hx = sbuf(pc, "hx", [128, 256], F32)
                ht = sbuf(pc, "ht", [128, 256], F32)
                hT = [sbuf(pc, "hT%d" % i, [128, 256], BF16) for i in range(2)]
                Eb = [sbuf(pc, "Eb%d" % i, [128, 4, 128], BF16) for i in range(3)]
                mk = sbuf(pc, "mk", [128, 128], BF16)
                rs = sbuf(pc, "rs", [128, 8], F32)
                ocats = [sbuf(pc, "ocat%d" % i, [128, 4, 128], F32) for i in range(2)]
                ocb = [sbuf(pc, "ocb%d" % i, [128, 512], BF16) for i in range(2)]
                imp = sbuf(pc, "imp", [128, 64], F32)
                vis = sbuf(pc, "vis", [128, 64], F32)
                frc = sbuf(pc, "frc", [128, 64], F32)
                col0 = sbuf(pc, "col0", [128, 64], F32)
                sc = [sbuf(pc, "sc%d" % i, [128, 64], F32) for i in range(2)]
                m8 = sbuf(pc, "m8", [128, 16], F32)
                selm = sbuf(pc, "selm", [128, 64], F32)
                nm = sbuf(pc, "nm", [128, 64], BF16)
                nmTs = [sbuf(pc, "nmT%d" % i, [64, 4, 128], BF16) for i in range(2)]
                pS = [psum(pc, "pS%d" % i, [128, 512], F32) for i in range(3)]
                pO = [psum(pc, "pO%d" % i, [128, 512], F32) for i in range(4)]
                pM = psum(pc, "pM", [128, 1024], BF16)

                S.dma("pool", w1sb[:], cw1.ap().rearrange("e s d h -> d e s h"), w=[w1sb], key=w1sb)
                S.dma("pool", w2sb[:], cw2.ap().rearrange("e h d -> h e d"), w=[w2sb], key=w2sb)
                S.dma("pool", posb[:], posT.ap(), w=[posb], key=posb)
                S.dma("sp", b1sb[:], b1T.ap(), w=[b1sb], key=b1sb)
                S.dma("sp", I0[:], I0_d.ap(), w=[I0], key=I0)
                S.dma("sp", Jm[:], Jm_d.ap(), w=[Jm], key=Jm)
                S.dma("pool", trib[:], tri_d.ap(), w=[trib], key=trib)
                S.dma("pool", Ebig[:], Ebig_d.ap(), w=[Ebig], key=Ebig)
                S.dve(lambda e: e.memset(KcT[:], 0.0), w=[KcT])
                S.dve(lambda e: e.memset(VcA[:], 0.0), w=[VcA])
                S.dve(lambda e: e.memset(VcA[:, :, 128:129], 1.0), w=[VcA])
                S.dve(lambda e: e.memset(col0[:], 0.0), w=[col0])
                S.dve(lambda e: e.memset(col0[:, 0:1], 1.0), w=[col0])
                S.dma("pool", VcA[:, :, 129:193], cover_d.ap().rearrange("(t p) j -> p t j", p=128), w=[VcA], key=VcA)

                def bias_mm(e):
                    ins = None
                    for ee in range(2):
                        for s_ in range(32):
                            ins = e.matmul(pS[0][:, ee:ee + 1], w1sb[:, ee, s_, :], posb[:, ee * 32 + s_:ee * 32 + s_ + 1],
                                           start=(ee == 0 and s_ == 0), stop=(s_ == 31), skip_group_check=True)
                    return ins
                S.pe(bias_mm, r=[w1sb, posb], w=[pS[0]])
                S.dve(lambda e: e.tensor_tensor(out=biasb[:], in0=pS[0][:, 0:2], in1=b1sb[:], op=ALU.add), r=[pS[0], b1sb], w=[biasb])

                def compress(srcT, nblk, ncols_pad, KcT_out, Vc_out_fn):
                    for ee in range(2):
                        for c0 in range(0, nblk, 512):
                            cn = min(512, nblk - c0)
                            P = pS[(c0 // 512) % 2]

                            def hmm(e, ee=ee, c0=c0, cn=cn, P=P):
                                ins = None
                                src = srcT(ee)
                                for rs_ in range(32):
                                    lo = rs_ + 16 * c0
                                    ins = e.matmul(P[:, :cn], w1sb[:, ee, rs_, :], src[:, lo:lo + 16 * (cn - 1) + 1:16],
                                                   start=(rs_ == 0), stop=(rs_ == 31))
                                return ins
                            S.pe(hmm, r=[w1sb, KT, VcT], w=[P])
                            S.act(lambda e, ee=ee, cn=cn, P=P: e.activation(out=hx[:, :cn], in_=P[:, :cn], func=AF.Identity,
                                                                             bias=biasb[:, ee:ee + 1]), r=[P, biasb], w=[hx])
                            S.dve(lambda e, cn=cn: e.tensor_tensor(out=ht[:, :cn], in0=hx[:, :cn], in1=hx[:, :cn], op=ALU.mult), r=[hx], w=[ht])
                            S.dve(lambda e, cn=cn: e.tensor_scalar(out=ht[:, :cn], in0=ht[:, :cn], scalar1=0.044715, scalar2=1.0,
                                                                   op0=ALU.mult, op1=ALU.add), r=[ht], w=[ht])
                            S.dve(lambda e, cn=cn: e.tensor_tensor(out=ht[:, :cn], in0=ht[:, :cn], in1=hx[:, :cn], op=ALU.mult), r=[ht, hx], w=[ht])
                            S.act(lambda e, cn=cn: e.activation(out=ht[:, :cn], in_=ht[:, :cn], func=AF.Sigmoid, scale=1.5957691216),
                                  r=[ht], w=[ht])
                            H = hT[ee]
                            S.dve(lambda e, cn=cn, H=H: e.tensor_tensor(out=H[:, :cn], in0=hx[:, :cn], in1=ht[:, :cn], op=ALU.mult),
                                  r=[hx, ht], w=[H])
                            if ee == 0:
                                S.pe(lambda e, cn=cn, H=H: e.matmul(pO[0][:, :cn], w2sb[:, 0, :], H[:, :cn], start=True, stop=True),
                                     r=[w2sb, H], w=[pO[0]])
                                S.act(lambda e, cn=cn, c0=c0: e.copy(out=KcT_out[:, c0:c0 + cn], in_=pO[0][:, :cn]), r=[pO[0]], w=[KcT])
                            else:
                                for t0 in range(0, cn, 128):
                                    nb = min(128, cn - t0)
                                    S.pe(lambda e, nb=nb, t0=t0, H=H: e.matmul(pO[1][:nb, 0:128], H[:, t0:t0 + nb], w2sb[:, 1, :],
                                                                               start=True, stop=True), r=[w2sb, H], w=[pO[1]])
                                    S.act(lambda e, nb=nb, ct=(c0 + t0) // 128: e.copy(out=Vc_out_fn(ct, nb), in_=pO[1][:nb, 0:128]),
                                          r=[pO[1]], w=[VcA])

                compress(lambda ee: (KT[:, 0, :] if ee == 0 else VcT[:, :]), 255, 256, KcT, lambda ct, nb: VcA[:nb, ct, 0:128])

                HB = [(0, 0), (0, 256), (1, 0), (1, 256)]

                def pv_group(Pb, E, rhs_fn, ncol, first, r):
                    def f(e):
                        ins = None
                        for h in range(4):
                            bk, co = HB[h]
                            ins = e.matmul(Pb[bk][:, co:co + ncol], E[:, h, :], rhs_fn(), start=(first and h % 2 == 0), stop=False,
                                           skip_group_check=True)
                        return ins
                    S.pe(f, r=[E] + r, w=[Pb[0], Pb[1]])

                def finish_branch(Pb, i, br, first_branch, ocat):
                    for h in range(4):
                        bk, co = HB[h]
                        S.dve(lambda e, h=h, bk=bk, co=co: e.tensor_copy(out=rs[:, h:h + 1], in_=Pb[bk][:, co + 128:co + 129]),
                              r=[Pb[bk]], w=[rs])
                    S.dve(lambda e: e.tensor_scalar(out=rs[:, 0:4], in0=rs[:, 0:4], scalar1=1e-30, scalar2=None, op0=ALU.max), r=[rs], w=[rs])
                    S.dve(lambda e: e.reciprocal(out=rs[:, 0:4], in_=rs[:, 0:4]), r=[rs], w=[rs])
                    S.dve(lambda e, i=i, br=br: e.tensor_tensor(out=rs[:, 4:8], in0=rs[:, 0:4], in1=GL[:, i, 4 * br:4 * br + 4], op=ALU.mult),
                          r=[rs, GL], w=[rs])
                    for h in range(4):
                        bk, co = HB[h]
                        if first_branch:
                            S.dve(lambda e, h=h, bk=bk, co=co: e.tensor_scalar(out=ocat[:, h, :], in0=Pb[bk][:, co:co + 128],
                                                                               scalar1=rs[:, 4 + h:5 + h], scalar2=None, op0=ALU.mult),
                                  r=[Pb[bk], rs], w=[ocat])
                        else:
                            S.dve(lambda e, h=h, bk=bk, co=co: e.scalar_tensor_tensor(out=ocat[:, h, :], in0=Pb[bk][:, co:co + 128],
                                                                                      scalar=rs[:, 4 + h:5 + h], in1=ocat[:, h, :],
                                                                                      op0=ALU.mult, op1=ALU.add),
                                  r=[Pb[bk], rs, ocat], w=[ocat])

                pair_ctr = [0]

                def qk_exp(i, lhsT_fn, lr, with_mask, nmT=None):
                    nmT = nmT if nmT is not None else nmTs[0]
                    x = pair_ctr[0] % 3
                    pair_ctr[0] += 1
                    P, E = pS[x], Eb[x]

                    def f(e):
                        ins = e.matmul(P[:, :], lhsT_fn(), QT[:, :, i * 128:i * 128 + 128], start=True, stop=not with_mask)
                        if with_mask is not False:
                            ins = e.matmul(P[:, :], Ebig[:, with_mask * 128:with_mask * 128 + 128], nmT[:, :, :], start=False, stop=True)
                        return ins
                    S.pe(f, r=[QT, Ebig, nmT] + lr, w=[P])
                    S.act(lambda e: e.activation(out=E[:, :, :], in_=P[:, :].rearrange("p (h q) -> p h q", h=4), func=AF.Exp, scale=SCALE),
                          r=[P], w=[E])
                    return E

                def mul_mask(E, m_ap, r):
                    S.dve(lambda e: e.tensor_tensor(out=E[:, :, :], in0=E[:, :, :], in1=m_ap.unsqueeze(1).broadcast_to([128, 4, 128]),
                                                    op=ALU.mult), r=[E] + r, w=[E])

                def stage1(i):
                    ocat = ocats[i % 2]
                    nmT = nmTs[i % 2]
                    Pb = pO[0:2]
                    n_ct = 1 if i < 16 else 2
                    for ct in range(n_ct):
                        E = qk_exp(i, lambda ct=ct: KcT[:, ct * 128:ct * 128 + 128], [KcT], False)
                        cval = float(128 * i - 2048 * ct - 31)
                        S.dve(lambda e, cval=cval: e.tensor_scalar(out=mk[:, :], in0=I0[:, :], scalar1=cval, scalar2=0.0,
                                                                   op0=ALU.add, op1=ALU.is_ge), r=[I0], w=[mk])
                        mul_mask(E, mk[:, :], [mk])
                        pv_group(Pb, E, lambda ct=ct: VcA[:, ct, 0:193], 193, ct == 0, [VcA])
                    for h in range(4):
                        bk, co = HB[h]
                        S.dve(lambda e, h=h, bk=bk, co=co: e.tensor_copy(out=rs[:, h:h + 1], in_=Pb[bk][:, co + 128:co + 129]),
                              r=[Pb[bk]], w=[rs])
                    S.dve(lambda e: e.tensor_scalar(out=rs[:, 0:4], in0=rs[:, 0:4], scalar1=1e-30, scalar2=None, op0=ALU.max), r=[rs], w=[rs])
                    S.dve(lambda e: e.reciprocal(out=rs[:, 0:4], in_=rs[:, 0:4]), r=[rs], w=[rs])
                    for h in range(4):
                        bk, co = HB[h]
                        if h == 0:
                            S.dve(lambda e, bk=bk, co=co: e.tensor_scalar(out=imp[:, :], in0=Pb[bk][:, co + 129:co + 193], scalar1=rs[:, 0:1],
                                                                          scalar2=None, op0=ALU.mult), r=[Pb[bk], rs], w=[imp])
                        else:
                            S.dve(lambda e, h=h, bk=bk, co=co: e.scalar_tensor_tensor(out=imp[:, :], in0=Pb[bk][:, co + 129:co + 193],
                                                                                      scalar=rs[:, h:h + 1], in1=imp[:, :],
                                                                                      op0=ALU.mult, op1=ALU.add),
                                  r=[Pb[bk], rs, imp], w=[imp])
                    finish_branch(Pb, i, 0, True, ocat)
                    if i == 0:
                        dbg("ocat_cmp", ocat, ocat[:, :, :], [128, 4, 128])
                        dbg("rs_cmp", rs, rs[:, :], [128, 8])
                        dbg("imp", imp, imp[:, :], [128, 64])
                        dbg("GL", GL, GL[:, :, :], [128, NT + 1, 12])
                    S.dve(lambda e, i=i: e.tensor_scalar(out=vis[:, :], in0=Jm[:, :], scalar1=float(2 * i), scalar2=None, op0=ALU.is_le),
                          r=[Jm], w=[vis])
                    if i >= 8:
                        S.dve(lambda e, i=i: e.tensor_scalar(out=frc[:, :], in0=Jm[:, :], scalar1=float(2 * i - 1), scalar2=None, op0=ALU.is_ge),
                              r=[Jm], w=[frc])
                        S.dve(lambda e: e.tensor_tensor(out=frc[:, :], in0=frc[:, :], in1=vis[:, :], op=ALU.mult), r=[frc, vis], w=[frc])
                        S.dve(lambda e: e.tensor_tensor(out=frc[:, :], in0=frc[:, :], in1=col0[:, :], op=ALU.max), r=[frc, col0], w=[frc])
                        S.dve(lambda e: e.tensor_tensor(out=sc[0][:, :], in0=imp[:, :], in1=vis[:, :], op=ALU.mult), r=[imp, vis], w=[sc[0]])
                        S.dve(lambda e: e.tensor_scalar(out=sc[1][:, :], in0=vis[:, :], scalar1=-1.0, scalar2=1e9, op0=ALU.add, op1=ALU.mult),
                              r=[vis], w=[sc[1]])
                        S.dve(lambda e: e.tensor_tensor(out=sc[0][:, :], in0=sc[0][:, :], in1=sc[1][:, :], op=ALU.add), r=[sc[0], sc[1]], w=[sc[0]])
                        S.dve(lambda e: e.scalar_tensor_tensor(out=sc[0][:, :], in0=frc[:, :], scalar=2e9, in1=sc[0][:, :],
                                                               op0=ALU.mult, op1=ALU.add), r=[frc, sc[0]], w=[sc[0]])
                        S.dve(lambda e: e.max(out=m8[:, 0:8], in_=sc[0][:, :]), r=[sc[0]], w=[m8])
                        S.dve(lambda e: e.match_replace(out=sc[1][:, :], in_to_replace=m8[:, 0:8], in_values=sc[0][:, :], imm_value=-3e38),
                              r=[sc[0], m8], w=[sc[1]])
                        S.dve(lambda e: e.max(out=m8[:, 8:16], in_=sc[1][:, :]), r=[sc[1]], w=[m8])
                        S.dve(lambda e: e.tensor_scalar(out=selm[:, :], in0=sc[0][:, :], scalar1=m8[:, 15:16], scalar2=None, op0=ALU.is_ge),
                              r=[sc[0], m8], w=[selm])
                        S.dve(lambda e: e.tensor_tensor(out=selm[:, :], in0=selm[:, :], in1=vis[:, :], op=ALU.mult), r=[selm, vis], w=[selm])
                        SM = selm
                    else:
                        SM = vis
                    S.dve(lambda e, SM=SM: e.tensor_scalar(out=nm[:, :], in0=SM[:, :], scalar1=-NEG, scalar2=NEG, op0=ALU.mult, op1=ALU.add),
                          r=[SM], w=[nm])

                def stage1b(i):
                    nmT = nmTs[i % 2]
                    S.pe(lambda e: e.transpose(pM[:64, 0:128], nm[:, :], identb[:, :]), r=[nm, identb], w=[pM])
                    S.act(lambda e: e.copy(out=nmT[:, :, :], in_=pM[:64, 0:128].unsqueeze(1).broadcast_to([64, 4, 128])), r=[pM], w=[nmT])

                def stage2(i):
                    ocat = ocats[i % 2]
                    nmT = nmTs[i % 2]
                    pairs = []
                    PbS = pO[2:4]
                    for kt in range(i + 1):
                        pairs.append((lambda kt=kt: KT[:, 1, kt * 128:kt * 128 + 128], kt, (trib[:, 0:128] if kt == i else None),
                                      PbS, lambda kt=kt: VA[:, kt, 0, 0:129], kt == 0, 1 if kt == i else None))
                    PbW = pO[0:2]
                    kts = [kt for kt in range(i - 4, i + 1) if kt >= 0]
                    for kt in kts:
                        m_ = trib[:, 0:128] if kt == i else (trib[:, 128:256] if kt == i - 4 else None)
                        pairs.append((lambda kt=kt: KT[:, 2, kt * 128:kt * 128 + 128], False, m_,
                                      PbW, lambda kt=kt: VA[:, kt, 1, 0:129], kt == kts[0], 2 if kt == kts[-1] else None))
                    En = qk_exp(i, pairs[0][0], [KT], pairs[0][1], nmT)
                    for k_, (lf, wm, m_, Pb_, rf, first_, fin_) in enumerate(pairs):
                        E = En
                        if k_ + 1 < len(pairs):
                            En = qk_exp(i, pairs[k_ + 1][0], [KT], pairs[k_ + 1][1], nmT)
                        if m_ is not None:
                            mul_mask(E, m_, [trib])
                        pv_group(Pb_, E, rf, 129, first_, [VA])
                        if fin_ is not None:
                            finish_branch(Pb_, i, fin_, False, ocat)
                    if i == 0:
                        pass
                    OB = ocb[i % 2]
                    S.act(lambda e, OB=OB: e.copy(out=OB[:, :], in_=ocat[:, :, :].rearrange("p h d -> p (h d)")), r=[ocat], w=[OB])
                    S.dma("sp", o_loc.ap()[i * 128:i * 128 + 128, :], OB[:, :], r=[OB], w=[o_locT[i // 8]], key=OB)
                if NTQ > 0:
                    stage1(0)
                    stage1b(0)
                for i in range(NTQ):
                    if i + 1 < NTQ:
                        stage1(i + 1)
                    stage2(i)
                    if i + 1 < NTQ:
                        stage1b(i + 1)
                S.barrier()
        phase_c()

        pp.close()

        S.dma("sp", win_s.ap()[:, 0:508], wstate.ap()[:, 4:512], key=Buf("wcopy"), final=True)
        def phase_s():
            with contextlib.ExitStack() as ps_:
                NP = PAST // 128
                NB = 257
                w1sb = sbuf(ps_, "w1sb_s", [128, 2, 32, 128], BF16)
                w2sb = sbuf(ps_, "w2sb_s", [128, 2, 128], BF16)
                posb = sbuf(ps_, "posb_s", [128, 64], BF16)
                b1sb = sbuf(ps_, "b1sb_s", [128, 2], F32)
                biasb = sbuf(ps_, "biasb_s", [128, 2], F32)
                Ebig = sbuf(ps_, "Ebig_s", [64, SEQ], BF16)
                smask = sbuf(ps_, "smask_s", [16, 5, 16], BF16)
                sel16 = sbuf(ps_, "sel16_s", [16, 68], F32)
                sel16b = sbuf(ps_, "sel16b_s", [16, 4], BF16)
                hmask = sbuf(ps_, "hmask_s", [16, 12], F32)
                big = sbuf(ps_, "bigS", [128, 2, 16512], BF16)
                XcT = big
                KsT = big
                Vs = big
                Vs3 = big[:, 1, :].rearrange("p (g c) -> p g c", c=129)
                E2 = sbuf(ps_, "E2s", [64, 128], BF16)
                Rrep = sbuf(ps_, "Rrep_s", [32, 128], F32)
                KwT = sbuf(ps_, "KwT", [128, 512], BF16)
                Vw = sbuf(ps_, "Vw", [128, 4, 130], BF16)
                stg = [sbuf(ps_, "stg%d" % i, [128, 8192], F32) for i in range(2)]
                wst = sbuf(ps_, "wst", [128, 4, 2, 128], F32)
                KcT = sbuf(ps_, "KcT_s", [128, 1024], BF16)
                Vc = sbuf(ps_, "Vc_s", [128, 8, 392], BF16)
                hx = sbuf(ps_, "hx_s", [128, 512], F32)
                ht = sbuf(ps_, "ht_s", [128, 512], F32)
                hT = [sbuf(ps_, "hT_s%d" % i, [128, 512], BF16) for i in range(2)]
                Es = [sbuf(ps_, "Es%d" % i, [128, 16], BF16) for i in range(2)]
                rs = sbuf(ps_, "rs_s", [16, 8], F32)
                U = sbuf(ps_, "U_s", [16, 260], BF16)
                gt = sbuf(ps_, "gt_s", [16, 12], F32)
                gs = sbuf(ps_, "gs_s", [16, 4], F32)
                ocat = sbuf(ps_, "ocat_s", [16, 128], F32)
                ocb = sbuf(ps_, "ocb_s", [16, 128], BF16)
                imp = sbuf(ps_, "imp_s", [4, 260], F32)
                sc = [sbuf(ps_, "sc_s%d" % i, [4, 260], F32) for i in range(2)]
                m8 = sbuf(ps_, "m8_s", [4, 16], F32)
                nm = sbuf(ps_, "nm_s", [4, 320], BF16)
                nmT = sbuf(ps_, "nmT_s", [64, 5, 4, 4], BF16)
                pS = [psum(ps_, "qS%d" % i, [128, 512], F32) for i in range(2)]
                pO = [psum(ps_, "qO%d" % i, [128, 512], F32) for i in range(3)]
                pTf = [psum(ps_, "qT%d" % i, [128, 512], F32) for i in range(2)]
                pM = psum(ps_, "qM", [128, 1024], BF16)

                S.dma("pool", w1sb[:], cw1.ap().rearrange("e s d h -> d e s h"), w=[w1sb], key=w1sb)
                S.dma("pool", w2sb[:], cw2.ap().rearrange("e h d -> h e d"), w=[w2sb], key=w2sb)
                S.dma("pool", posb[:], posT.ap(), w=[posb], key=posb)
                S.dma("sp", b1sb[:], b1T.ap(), w=[b1sb], key=b1sb)
                S.dma("pool", Ebig[:], Ebig_d.ap(), w=[Ebig], key=Ebig)
                S.dma("pool", smask[:], smask_d.ap(), w=[smask], key=smask)
                S.dma("sp", sel16[:], sel16_d.ap(), w=[sel16], key=sel16)
                S.dma("pool", sel16b[:], sel16_d.ap()[:, 0:4], w=[sel16b], key=sel16b)
                S.dma("sp", hmask[:], hmask_d.ap(), w=[hmask], key=hmask)
                S.dve(lambda e: e.memset(Vw[:, :, 128:129], 1.0), w=[Vw])
                S.dve(lambda e: e.memset(KcT[:], 0.0), w=[KcT])
                S.dve(lambda e: e.memset(Vc[:], 0.0), w=[Vc])
                S.dve(lambda e: e.memset(Vc[:, 0:7, 128:129], 1.0), w=[Vc])
                S.dve(lambda e: e.memset(Vc[:127, 7, 128:129], 1.0), w=[Vc])
                S.dma("pool", Vc[:, :, 129:386], cover_s.ap().rearrange("(t p) j -> p t j", p=128), w=[Vc], key=Vc)

                def bias_mm(e):
                    ins = None
                    for ee in range(2):
                        for s_ in range(32):
                            ins = e.matmul(pS[0][:, ee:ee + 1], w1sb[:, ee, s_, :], posb[:, ee * 32 + s_:ee * 32 + s_ + 1],
                                           start=(ee == 0 and s_ == 0), stop=(s_ == 31), skip_group_check=True)
                    return ins
                S.pe(bias_mm, r=[w1sb, posb], w=[pS[0]])
                S.dve(lambda e: e.tensor_tensor(out=biasb[:], in0=pS[0][:, 0:2], in1=b1sb[:], op=ALU.add), r=[pS[0], b1sb], w=[biasb])

                pti = sbuf(ps_, "pti", [32, 16], I32)
                ptf = sbuf(ps_, "ptf", [32, 16], F32)
                idf = sbuf(ps_, "idf", [128, 16], F32)
                idi = sbuf(ps_, "idi", [128, 16], I32)
                pix = sbuf(ps_, "pix", [128, 1], F32)
                S.dma("sp", pti[:], ptabT.ap(), w=[pti], key=pti)
                S.dma("sp", pix[:], pidx_d.ap(), w=[pix], key=pix)
                S.dma("sp", Rrep[:], Rrep_d.ap(), w=[Rrep], key=Rrep)
                S.dma("pool", E2[:], E2_d.ap(), w=[E2], key=E2)
                S.dve(lambda e: e.tensor_copy(out=ptf[:], in_=pti[:]), r=[pti], w=[ptf])
                S.pe(lambda e: e.matmul(pS[1][:, 0:16], Rrep[:, :], ptf[:, :], start=True, stop=True), r=[Rrep, ptf], w=[pS[1]])
                S.dve(lambda e: e.tensor_scalar(out=idf[:], in0=pS[1][:, 0:16], scalar1=4.0, scalar2=pix[:, 0:1], op0=ALU.mult, op1=ALU.add),
                      r=[pS[1], pix], w=[idf])
                S.dve(lambda e: e.tensor_copy(out=idi[:], in_=idf[:]), r=[idf], w=[idi])
                gctr = [0]

                def qgather(cache, s_, q):
                    G = stg[gctr[0] % 2]
                    gctr[0] += 1
                    rows = cache.ap().rearrange("g (u t) e d -> (g u) (t e d)", u=4)
                    k = s_ * 4 + q
                    S.custom_dma("pool", lambda e: e.indirect_dma_start(
                        out=G[:, :], out_offset=None, in_=rows,
                        in_offset=bass.IndirectOffsetOnAxis(ap=idi[:, k:k + 1], axis=0)), r=[idi], w=[G], key=G)
                    return G

                for s_ in range(4):
                    ev = 0
                    for q in range(4):
                        G = qgather(ccache, s_, q)
                        G3 = G[:, :].rearrange("p (t e d) -> p t e d", t=32, e=2)
                        for ee in range(2):
                            for t0 in range(0, 32, 4):
                                P = pTf[ev % 2]

                                def trp(e, G3=G3, ee=ee, t0=t0, P=P):
                                    ins = None
                                    for j in range(4):
                                        ins = e.transpose(P[:, j * 128:j * 128 + 128], G3[:, t0 + j, ee, :], identf[:, :])
                                    return ins
                                S.pe(trp, r=[G, identf], w=[P])
                                dst = XcT[:, ee, 4096 * q:4096 * q + 4096].rearrange("d (p t) -> d t p", t=32)[:, t0:t0 + 4, :]
                                src = P[:, 0:512].rearrange("d (j p) -> d j p", j=4)
                                if ev % 2 == 0:
                                    S.act(lambda e, dst=dst, src=src: e.copy(out=dst, in_=src), r=[P], w=[XcT])
                                else:
                                    S.dve(lambda e, dst=dst, src=src: e.tensor_copy(out=dst, in_=src), r=[P], w=[XcT])
                                ev += 1
                    for ee in range(2):
                        for c0 in (0, 512):
                            cn = 512 if c0 == 0 else 511
                            P = pS[(c0 // 512) % 2]

                            def hmm(e, ee=ee, c0=c0, cn=cn, P=P):
                                ins = None
                                for rs_ in range(32):
                                    lo = rs_ + 16 * c0
                                    ins = e.matmul(P[:, :cn], w1sb[:, ee, rs_, :], XcT[:, ee, lo:lo + 16 * (cn - 1) + 1:16],
                                                   start=(rs_ == 0), stop=(rs_ == 31))
                                return ins
                            S.pe(hmm, r=[w1sb, XcT], w=[P])
                            S.act(lambda e, ee=ee, cn=cn, P=P: e.activation(out=hx[:, :cn], in_=P[:, :cn], func=AF.Identity,
                                                                             bias=biasb[:, ee:ee + 1]), r=[P, biasb], w=[hx])
                            S.dve(lambda e, cn=cn: e.tensor_tensor(out=ht[:, :cn], in0=hx[:, :cn], in1=hx[:, :cn], op=ALU.mult), r=[hx], w=[ht])
                            S.dve(lambda e, cn=cn: e.tensor_scalar(out=ht[:, :cn], in0=ht[:, :cn], scalar1=0.044715, scalar2=1.0,
                                                                   op0=ALU.mult, op1=ALU.add), r=[ht], w=[ht])
                            S.dve(lambda e, cn=cn: e.tensor_tensor(out=ht[:, :cn], in0=ht[:, :cn], in1=hx[:, :cn], op=ALU.mult), r=[ht, hx], w=[ht])
                            S.act(lambda e, cn=cn: e.activation(out=ht[:, :cn], in_=ht[:, :cn], func=AF.Sigmoid, scale=1.5957691216),
                                  r=[ht], w=[ht])
                            H = hT[ee]
                            S.dve(lambda e, cn=cn, H=H: e.tensor_tensor(out=H[:, :cn], in0=hx[:, :cn], in1=ht[:, :cn], op=ALU.mult),
                                  r=[hx, ht], w=[H])
                            if ee == 0:
                                S.pe(lambda e, cn=cn, H=H: e.matmul(pO[0][:, :cn], w2sb[:, 0, :], H[:, :cn], start=True, stop=True),
                                     r=[w2sb, H], w=[pO[0]])
                                S.act(lambda e, cn=cn, c0=c0: e.copy(out=KcT[:, c0:c0 + cn], in_=pO[0][:, :cn]), r=[pO[0]], w=[KcT])
                            else:
                                for t0 in range(0, cn, 128):
                                    nb = min(128, cn - t0)
                                    S.pe(lambda e, nb=nb, t0=t0, H=H: e.matmul(pO[1][:nb, 0:128], H[:, t0:t0 + nb], w2sb[:, 1, :],
                                                                               start=True, stop=True), r=[w2sb, H], w=[pO[1]])
                                    S.act(lambda e, nb=nb, ct=(c0 + t0) // 128: e.copy(out=Vc[:nb, ct, 0:128], in_=pO[1][:nb, 0:128]),
                                          r=[pO[1]], w=[Vc])

                    pc_ = [0]
                    Qs = QTs[:, :, 4 * s_:4 * s_ + 4]

                    def qk16(lhsT, nk, lr, maskchunk=None, Qs=Qs):
                        x = pc_[0] % 2
                        pc_[0] += 1
                        P, E = pS[x], Es[x]

                        def f(e):
                            ins = e.matmul(P[:nk, 0:16], lhsT, Qs, start=True, stop=(maskchunk is None))
                            if maskchunk is not None:
                                ch, kt = maskchunk
                                ins = e.matmul(P[:nk, 0:16], E2[:, :], nmT[:, ch, :, :], start=False, stop=True)
                            return ins
                        S.pe(f, r=[QTs, E2, nmT] + lr, w=[P])
                        S.act(lambda e: e.activation(out=E[:nk, :], in_=P[:nk, 0:16], func=AF.Exp, scale=SCALE), r=[P], w=[E])
                        return E

                    def pv16(Pacc, E, nk, rhs, ncol, first, last, r):
                        S.pe(lambda e: e.matmul(Pacc[:16, 0:ncol], E[:nk, :], rhs, start=first, stop=last), r=[E] + r, w=[Pacc])

                    def fin16(Pacc, br, first_branch):
                        S.dve(lambda e: e.tensor_scalar(out=rs[:, 0:1], in0=Pacc[:16, 128:129], scalar1=1e-30, scalar2=None, op0=ALU.max),
                              r=[Pacc], w=[rs])
                        S.dve(lambda e: e.reciprocal(out=rs[:, 0:1], in_=rs[:, 0:1]), r=[rs], w=[rs])
                        S.dve(lambda e: e.tensor_tensor(out=rs[:, 1:2], in0=rs[:, 0:1], in1=gs[:, br:br + 1], op=ALU.mult), r=[rs, gs], w=[rs])
                        if first_branch:
                            S.dve(lambda e: e.tensor_scalar(out=ocat[:, :], in0=Pacc[:16, 0:128], scalar1=rs[:, 1:2], scalar2=None, op0=ALU.mult),
                                  r=[Pacc, rs], w=[ocat])
                        else:
                            S.dve(lambda e: e.scalar_tensor_tensor(out=ocat[:, :], in0=Pacc[:16, 0:128], scalar=rs[:, 1:2], in1=ocat[:, :],
                                                                   op0=ALU.mult, op1=ALU.add), r=[Pacc, rs, ocat], w=[ocat])

                    S.pe(lambda e, s_=s_: e.matmul(pO[2][:16, 0:12], sel16[:, 4 + 16 * s_:4 + 16 * s_ + 16], GL[:16, NT, :], start=True, stop=True),
                         r=[sel16, GL], w=[pO[2]])
                    S.dve(lambda e: e.tensor_tensor(out=gt[:, :], in0=pO[2][:16, 0:12], in1=hmask[:, :], op=ALU.mult), r=[pO[2], hmask], w=[gt])
                    S.dve(lambda e: e.tensor_reduce(out=gs[:, 0:3], in_=gt[:, :].rearrange("p (b h) -> p b h", b=3), axis=AX.X, op=ALU.add),
                          r=[gt], w=[gs])

                    def run_pairs(plist):
                        En = qk16(*plist[0][0][:3], **plist[0][0][3])
                        for k_, (qa, post, pa) in enumerate(plist):
                            E = En
                            if k_ + 1 < len(plist):
                                nq = plist[k_ + 1][0]
                                En = qk16(*nq[:3], **nq[3])
                            if post is not None:
                                post(E)
                            pv16(pa[0], E, *pa[1:])

                    run_pairs([((KcT[:, ct * 128:ct * 128 + 128], 128, [KcT], {}), None,
                                (pO[0], 128, Vc[:, ct, 0:386], 386, ct == 0, ct == 7, [Vc])) for ct in range(8)])
                    fin16(pO[0], 0, True)
                    S.dve(lambda e: e.tensor_scalar(out=U[:, 0:257], in0=pO[0][:16, 129:386], scalar1=rs[:, 0:1], scalar2=None, op0=ALU.mult),
                          r=[pO[0], rs], w=[U])
                    S.pe(lambda e: e.matmul(pO[2][:4, 0:257], sel16b[:, 0:4], U[:, 0:257], start=True, stop=True), r=[sel16b, U], w=[pO[2]])
                    S.dve(lambda e: e.tensor_copy(out=sc[0][:, 0:257], in_=pO[2][:4, 0:257]), r=[pO[2]], w=[sc[0]])
                    S.dve(lambda e: e.memset(sc[0][:, 0:1], 2e9), w=[sc[0]])
                    S.dve(lambda e: e.memset(sc[0][:, 255:257], 2e9), w=[sc[0]])
                    S.dve(lambda e: e.max(out=m8[:, 0:8], in_=sc[0][:, 0:257]), r=[sc[0]], w=[m8])
                    S.dve(lambda e: e.match_replace(out=sc[1][:, 0:257], in_to_replace=m8[:, 0:8], in_values=sc[0][:, 0:257], imm_value=-3e38),
                          r=[sc[0], m8], w=[sc[1]])
                    S.dve(lambda e: e.max(out=m8[:, 8:16], in_=sc[1][:, 0:257]), r=[sc[1]], w=[m8])
                    S.dve(lambda e: e.memset(nm[:, :], 0.0), w=[nm])
                    S.dve(lambda e: e.tensor_scalar(out=sc[1][:, 0:257], in0=sc[0][:, 0:257], scalar1=m8[:, 15:16], scalar2=None, op0=ALU.is_ge),
                          r=[sc[0], m8], w=[sc[1]])
                    S.dve(lambda e: e.tensor_scalar(out=nm[:, 0:257], in0=sc[1][:, 0:257], scalar1=-NEG, scalar2=NEG, op0=ALU.mult, op1=ALU.add),
                          r=[sc[1]], w=[nm])

                    def trn(e):
                        ins = None
                        for ch in range(5):
                            ins = e.transpose(pM[:64, ch * 4:ch * 4 + 4], nm[:, ch * 64:ch * 64 + 64], identb[:4, :4])
                        return ins
                    S.pe(trn, r=[nm, identb], w=[pM])
                    S.act(lambda e: e.copy(out=nmT[:, :, :, :], in_=pM[:64, 0:20].rearrange("p (c q) -> p c q", c=5).unsqueeze(2)
                                           .broadcast_to([64, 5, 4, 4])), r=[pM], w=[nmT])

                    S.dve(lambda e: e.memset(Vs3[:, :, 128:129], 1.0), w=[Vs])
                    ev = 0
                    for q in range(4):
                        G = qgather(scache, s_, q)
                        G3 = G[:, :].rearrange("p (t e d) -> p t e d", t=32, e=2)
                        for t0 in range(0, 32, 4):
                            P = pTf[ev % 2]
                            ev += 1
                            kt0 = q * 32 + t0

                            def trk(e, G3=G3, t0=t0, P=P):
                                ins = None
                                for j in range(4):
                                    ins = e.transpose(P[:, j * 128:j * 128 + 128], G3[:, t0 + j, 0, :], identf[:, :])
                                return ins
                            S.pe(trk, r=[G, identf], w=[P])
                            S.act(lambda e, P=P, kt0=kt0: e.copy(out=KsT[:, 0, kt0 * 128:kt0 * 128 + 512], in_=P[:, 0:512]), r=[P], w=[KsT])
                            S.dve(lambda e, G3=G3, t0=t0, kt0=kt0: e.tensor_copy(out=Vs3[:, kt0:kt0 + 4, 0:128], in_=G3[:, t0:t0 + 4, 1, :]),
                                  r=[G], w=[Vs])
                    def newmask(E, s_=s_):
                        S.dve(lambda e: e.tensor_tensor(out=E[:16, :], in0=E[:16, :], in1=smask[:, s_, :], op=ALU.mult), r=[E, smask], w=[E])

                    def oldmask(E):
                        S.dve(lambda e: e.tensor_tensor(out=E[:16, :], in0=E[:16, :], in1=smask[:, 4, :], op=ALU.mult), r=[E, smask], w=[E])

                    pl = [((KsT[:, 0, kt * 128:kt * 128 + 128], 128, [KsT], {"maskchunk": (kt // 32, kt)}), None,
                           (pO[1], 128, Vs3[:, kt, 0:129], 129, kt == 0, False, [Vs])) for kt in range(NP)]
                    pl.append(((KTs[:, 1, :], 16, [KTs], {}), newmask, (pO[1], 16, VAs[:, 0, 0:129], 129, False, True, [VAs])))
                    run_pairs(pl)
                    fin16(pO[1], 1, False)
                    S.dma("sp", wst[:, :, :, :], wstate.ap()[s_].rearrange("(t p) e d -> p t e d", p=128), w=[wst], key=wst)
                    for t_ in range(4):
                        P = pTf[t_ % 2]
                        S.pe(lambda e, t_=t_, P=P: e.transpose(P[:, 0:128], wst[:, t_, 0, :], identf[:, :]), r=[wst, identf], w=[P])
                        S.act(lambda e, t_=t_, P=P: e.copy(out=KwT[:, t_ * 128:t_ * 128 + 128], in_=P[:, 0:128]), r=[P], w=[KwT])
                    S.dve(lambda e: e.tensor_copy(out=Vw[:, :, 0:128], in_=wst[:, :, 1, :]), r=[wst], w=[Vw])
                    pl = [((KwT[:, t_ * 128:t_ * 128 + 128], 128, [KwT], {}), (oldmask if t_ == 0 else None),
                           (pO[0], 128, Vw[:, t_, 0:129], 129, t_ == 0, False, [Vw])) for t_ in range(4)]
                    pl.append(((KTs[:, 2, :], 16, [KTs], {}), newmask, (pO[0], 16, VAs[:, 1, 0:129], 129, False, True, [VAs])))
                    run_pairs(pl)
                    fin16(pO[0], 2, False)
                    S.act(lambda e: e.copy(out=ocb[:, :], in_=ocat[:, :]), r=[ocat], w=[ocb])
                    for h in range(4):
                        S.dma("sp", o_locs.ap()[4 * s_:4 * s_ + 4, 128 * h:128 * h + 128], ocb[4 * h:4 * h + 4, :], r=[ocb], w=[o_locT[4]], key=ocb)
                S.barrier()

        if RUN_S:
            phase_s()

        RG = [[0, 1, 2, 3], [4, 5, 6, 7]]
        def allgather(src_ap, dst_ap, rbuf, wbuf_):
            S.custom_dma("pool", lambda e: e.collective_compute("AllGather", ALU.bypass, replica_groups=RG, ins=[src_ap], outs=[dst_ap]),
                         r=[rbuf], w=[wbuf_], key=wbuf_, inc=1)
        for c in range(4):
            allgather(o_loc.ap()[c * 1024:c * 1024 + 1024, :].rearrange("(p a) f -> p (a f)", a=8),
                      og.ap()[c * 4096:c * 4096 + 4096, :].rearrange("(q a) f -> q (a f)", a=8), o_locT[c], ogT)
        allgather(o_locs.ap().rearrange("r (b x) -> (r b) x", b=8), ogs.ap().rearrange("r (b x) -> (r b) x", b=8), o_locT[4], ogT)

        TBS = [(0, 512), (512, 512), (1024, 4)]

        def token_phase(layer):
            with contextlib.ExitStack() as pd:
                hT = sbuf(pd, "hres%d" % layer, [128, 16, NTOK], F32)
                actT = sbuf(pd, "actT%d" % layer, [128, 8, NTOK], BF16)
                xnT = sbuf(pd, "xnT_d%d" % layer, [128, 16, NTOK], BF16)
                wbuf = [sbuf(pd, "wbuf%d_%d" % (i, layer), [128, 16, 512], BF16) for i in range(2)]
                xst = sbuf(pd, "xst%d" % layer, [128, D], F32)
                ost = [sbuf(pd, "ost%d_%d" % (i, layer), [128, 2, 512], BF16) for i in range(2)]
                oidx = sbuf(pd, "oidx_s%d" % layer, [128, 72], I32)
                rstd = sbuf(pd, "rstd_d%d" % layer, [128, 512], F32)
                sq = sbuf(pd, "sq_d%d" % layer, [128, 512], F32)
                rl = [sbuf(pd, "rl%d_%d" % (i, layer), [128, 512], F32) for i in range(2)]
                gT = sbuf(pd, "gT_d%d" % layer, [128, 16], F32)
                onesf = sbuf(pd, "onesf%d" % layer, [128, 128], F32)
                epsb = sbuf(pd, "epsb_d%d" % layer, [128, 1], F32)
                pz = [psum(pd, "dz%d_%d" % (i, layer), [128, 512], F32) for i in range(4)]
                pM = psum(pd, "dM%d" % layer, [128, 1024], BF16)
                pT = psum(pd, "dT%d" % layer, [128, 512], F32)
                pn = psum(pd, "dn%d" % layer, [128, 512], F32)
                ctr = {"w": 0, "z": 0, "o": 0, "r": 0}

                if layer == 0:
                    S.dma("sp", oidx[:, 0:36], oidx_d.ap(), w=[oidx], key=oidx)
                else:
                    S.dma("sp", oidx[:, :], oidx2_d.ap(), w=[oidx], key=oidx)
                S.dve(lambda e: e.memset(onesf[:], 1.0 / D), w=[onesf])
                S.dve(lambda e: e.memset(epsb[:], RMS_EPS), w=[epsb])

                if layer == 0:
                    for t in range(9):
                        n = 128 if t < 8 else 4
                        S.dma("sp", xst[:n, :], x_tok.ap()[t * 128:t * 128 + n, :], w=[xst], key=xst)
                        for q in range(4):
                            def trx(e, q=q, n=n):
                                ins = None
                                for kk in range(4):
                                    k = q * 4 + kk
                                    ins = e.transpose(pT[:, kk * 128:kk * 128 + n], xst[:n, k * 128:k * 128 + 128], identf[:n, :n])
                                return ins
                            S.pe(trx, r=[xst, identf], w=[pT])
                            S.act(lambda e, q=q, n=n, t=t: e.copy(out=hT[:, q * 4:q * 4 + 4, t * 128:t * 128 + n],
                                                                 in_=pT[:, :].rearrange("p (k t) -> p k t", k=4)[:, :, :n]), r=[pT], w=[hT])
                else:
                    S.dma("sp", hT[:, :, :], hspill.ap().rearrange("p (k t) -> p k t", k=16), r=[hspT], w=[hT], key=hT)

                def load_act(gsrc, gsrc_s, gbuf, colfn):
                    for t in range(9):
                        n = 128 if t < 8 else 4
                        O_ = ost[ctr["o"] % 2]
                        ctr["o"] += 1
                        for jj in range(2):
                            S.custom_dma("pool", lambda e, O_=O_, jj=jj, t=t, col=colfn(t, jj): e.indirect_dma_start(
                                out=O_[:, jj, :], out_offset=None, in_=(gsrc if t < 8 else gsrc_s).ap(),
                                in_offset=bass.IndirectOffsetOnAxis(ap=oidx[:, col:col + 1], axis=0)),
                                r=[oidx, gbuf], w=[O_], key=O_)

                        def tro(e, O_=O_, n=n):
                            ins = None
                            for kk in range(8):
                                jj, q = kk // 4, kk % 4
                                ins = e.transpose(pM[:, kk * 128:kk * 128 + n], O_[:n, jj, q * 128:q * 128 + 128], identb[:n, :n])
                            return ins
                        S.pe(tro, r=[O_, identb], w=[pM])
                        S.act(lambda e, n=n, t=t: e.copy(out=actT[:, :, t * 128:t * 128 + n],
                                                         in_=pM[:, :].rearrange("p (k t) -> p k t", k=8)[:, :, :n]), r=[pM], w=[actT])

                def proj_blocks(Wd, row0):
                    blks = []
                    for cb in range(4):
                        def dma(wb, cb=cb):
                            S.dma("pool", wb[:, 0:8, :], Wd.ap()[row0:row0 + 1024, cb * 512:cb * 512 + 512].rearrange("(k p) c -> p k c", p=128),
                                  w=[wb], key=wb)

                        def comp(wb, cb=cb):
                            for cc in range(4):
                                for (t0, nt) in TBS:
                                    P = pz[ctr["z"] % 4]
                                    ctr["z"] += 1

                                    def mm(e, wb=wb, cc=cc, t0=t0, nt=nt, P=P):
                                        ins = None
                                        for k in range(8):
                                            ins = e.matmul(P[:, :nt], wb[:, k, cc * 128:cc * 128 + 128], actT[:, k, t0:t0 + nt],
                                                           start=(k == 0), stop=(k == 7))
                                        return ins
                                    S.pe(mm, r=[wb, actT], w=[P])
                                    c = cb * 4 + cc
                                    S.dve(lambda e, P=P, c=c, t0=t0, nt=nt: e.tensor_tensor(out=hT[:, c, t0:t0 + nt], in0=P[:, :nt],
                                                                                             in1=hT[:, c, t0:t0 + nt], op=ALU.add),
                                          r=[P, hT], w=[hT])
                        blks.append((dma, comp))
                    return blks

                def run_blocks(blks):
                    bufs = []
                    for i, (dma, comp) in enumerate(blks):
                        if i == 0:
                            wb0 = wbuf[ctr["w"] % 2]
                            ctr["w"] += 1
                            dma(wb0)
                            bufs.append(wb0)
                        if i + 1 < len(blks):
                            wbn = wbuf[ctr["w"] % 2]
                            ctr["w"] += 1
                            blks[i + 1][0](wbn)
                            bufs.append(wbn)
                        comp(bufs[i])

                def proj_accum(Wd, row0):
                    run_blocks(proj_blocks(Wd, row0))

                def rmsnorm_T(gain_d, out_fn, wlist):
                    S.dma("sp", gT[:], gain_d.ap(), w=[gT], key=gT)
                    for (t0, nt) in TBS:
                        for k in range(16):
                            S.act(lambda e, k=k, t0=t0, nt=nt: e.activation(out=sq[:, :nt], in_=hT[:, k, t0:t0 + nt], func=AF.Square),
                                  r=[hT], w=[sq])
                            S.pe(lambda e, k=k, nt=nt: e.matmul(pn[:, :nt], onesf[:, :], sq[:, :nt], start=(k == 0), stop=(k == 15)),
                                 r=[onesf, sq], w=[pn])
                        S.act(lambda e, nt=nt: e.activation(out=rstd[:, :nt], in_=pn[:, :nt], func=AF.Sqrt, bias=epsb[:, 0:1]),
                              r=[pn, epsb], w=[rstd])
                        S.dve(lambda e, nt=nt: e.reciprocal(out=rstd[:, :nt], in_=rstd[:, :nt]), r=[rstd], w=[rstd])
                        for k in range(16):
                            S.dve(lambda e, k=k, t0=t0, nt=nt: e.scalar_tensor_tensor(out=out_fn(k, t0, nt), in0=hT[:, k, t0:t0 + nt],
                                                                                      scalar=gT[:, k:k + 1], in1=rstd[:, :nt],
                                                                                      op0=ALU.mult, op1=ALU.mult),
                                  r=[hT, gT, rstd], w=wlist)

                def ffn(W1, W2):
                    blks = []
                    for hg in range(8):
                        for sub in range(2):
                            c0 = hg * 1024 + sub * 512

                            def dma(wb, c0=c0):
                                S.dma("pool", wb[:, :, :], W1.ap()[:, c0:c0 + 512].rearrange("(k p) c -> p k c", p=128), w=[wb], key=wb)

                            def comp(wb, sub=sub):
                                for fc in range(4):
                                    for (t0, nt) in TBS:
                                        P = pz[ctr["z"] % 4]
                                        ctr["z"] += 1

                                        def mm(e, wb=wb, fc=fc, t0=t0, nt=nt, P=P):
                                            ins = None
                                            for k in range(16):
                                                ins = e.matmul(P[:, :nt], wb[:, k, fc * 128:fc * 128 + 128], xnT[:, k, t0:t0 + nt],
                                                               start=(k == 0), stop=(k == 15))
                                            return ins
                                        S.pe(mm, r=[wb, xnT], w=[P])
                                        R_ = rl[ctr["r"] % 2]
                                        ctr["r"] += 1
                                        S.act(lambda e, P=P, R_=R_, nt=nt: e.activation(out=R_[:, :nt], in_=P[:, :nt], func=AF.Relu), r=[P], w=[R_])
                                        eng = S.pool if ctr["r"] % 2 == 0 else S.dve
                                        eng(lambda e, R_=R_, nt=nt, t0=t0, kk=sub * 4 + fc: e.tensor_tensor(out=actT[:, kk, t0:t0 + nt], in0=R_[:, :nt],
                                                                                                          in1=R_[:, :nt], op=ALU.mult),
                                            r=[R_], w=[actT])
                            blks.append((dma, comp))
                        blks.extend(proj_blocks(W2, hg * 1024))
                    run_blocks(blks)

                xn_out = lambda k, t0, nt: xnT[:, k, t0:t0 + nt]
                if layer == 0:
                    for g in range(2):
                        load_act(og, ogs, ogT, lambda t, jj, g=g: t * 4 + 2 * g + jj)
                        proj_accum(w_out0, g * 1024)
                    rmsnorm_T(gffn0T, xn_out, [xnT])
                    ffn(ffn1_0, ffn2_0)
                    rmsnorm_T(gmix1T, xn_out, [xnT])
                    for cp in range(8):
                        S.dma("sp", xg_in.ap()[cp].rearrange("p (k t) -> p k t", k=2), xnT[:, 2 * cp:2 * cp + 2, :], r=[xnT],
                              w=[xg_inT[cp]], key=Buf("xgk%d" % cp))
                        allgather(xg_in.ap()[cp], xg_all.ap()[cp], xg_inT[cp], xgT)
                    S.dma("sp", hspill.ap().rearrange("p (k t) -> p k t", k=16), hT[:, :, :], r=[hT], w=[hspT], key=hT)
                else:
                    for g in range(4):
                        load_act(og2, ogs2, og2T, lambda t, jj, g=g: t * 8 + 2 * g + jj)
                        proj_accum(w_out1, g * 1024)
                    rmsnorm_T(gffn1T, xn_out, [xnT])
                    ffn(ffn1_1, ffn2_1)
                    S.dma("sp", gT[:], gfinT.ap(), w=[gT], key=gT)
                    for (t0, nt) in TBS:
                        for k in range(16):
                            S.act(lambda e, k=k, t0=t0, nt=nt: e.activation(out=sq[:, :nt], in_=hT[:, k, t0:t0 + nt], func=AF.Square),
                                  r=[hT], w=[sq])
                            S.pe(lambda e, k=k, nt=nt: e.matmul(pn[:, :nt], onesf[:, :], sq[:, :nt], start=(k == 0), stop=(k == 15)),
                                 r=[onesf, sq], w=[pn])
                        S.act(lambda e, nt=nt: e.activation(out=rstd[:, :nt], in_=pn[:, :nt], func=AF.Sqrt, bias=epsb[:, 0:1]),
                              r=[pn, epsb], w=[rstd])
                        S.dve(lambda e, nt=nt: e.reciprocal(out=rstd[:, :nt], in_=rstd[:, :nt]), r=[rstd], w=[rstd])
                        for k in range(16):
                            S.dve(lambda e, k=k, t0=t0, nt=nt: e.scalar_tensor_tensor(out=hT[:, k, t0:t0 + nt], in0=hT[:, k, t0:t0 + nt],
                                                                                      scalar=gT[:, k:k + 1], in1=rstd[:, :nt],
                                                                                      op0=ALU.mult, op1=ALU.mult),
                                  r=[hT, gT, rstd], w=[hT])
                    for t in range(9):
                        n = 128 if t < 8 else 4
                        for q in range(4):
                            def trh(e, q=q, n=n, t=t):
                                ins = None
                                for kk in range(4):
                                    ins = e.transpose(pT[:n, kk * 128:kk * 128 + 128], hT[:, q * 4 + kk, t * 128:t * 128 + n], identf[:, :])
                                return ins
                            S.pe(trh, r=[hT, identf], w=[pT])
                            S.act(lambda e, q=q, n=n: e.copy(out=xst[:n, q * 512:q * 512 + 512], in_=pT[:n, :]), r=[pT], w=[xst])
                        S.dma("sp", y_tok.ap()[t * 128:t * 128 + n, :], xst[:n, :], r=[xst], key=xst, final=True)
                S.barrier()

        def phase_e():
            with contextlib.ExitStack() as pe_:
                wr = sbuf(pe_, "wr_s", [128, 16, 3072], BF16)
                xt = [sbuf(pe_, "ext%d" % i, [128, 16, 128], BF16) for i in range(2)]
                cst = [sbuf(pe_, "ecs%d" % i, [128, 256], F32) for i in range(2)]
                St = sbuf(pe_, "St", [128, 4, 512], F32)
                Sb = sbuf(pe_, "Sb", [128, 4, 512], BF16)
                Ss = [sbuf(pe_, "Ss%d" % i, [128, 4, 512], F32) for i in range(2)]
                Ssb = [sbuf(pe_, "Ssb%d" % i, [128, 4, 512], BF16) for i in range(4)]
                qkr = sbuf(pe_, "qkr", [128, 4, 256], BF16)
                Vb = sbuf(pe_, "Vb", [128, 2, 512], BF16)
                sg = sbuf(pe_, "sg", [128, 1024], F32)
                ob = [sbuf(pe_, "ob%d" % i, [128, 512], F32) for i in range(2)]
                jk = sbuf(pe_, "jk", [128, 512], BF16)
                rt = [sbuf(pe_, "ert%d" % i, [128, 4, 128], F32) for i in range(4)]
                qkT = sbuf(pe_, "qkT", [128, 8, 128], BF16)
                QdT = sbuf(pe_, "QdT", [128, 2, 2, 128], BF16)
                QdTm = sbuf(pe_, "QdTm", [128, 4, 2, 2, 16], BF16)
                Kd = sbuf(pe_, "Kd", [128, 2, 256], BF16)
                Kdm = sbuf(pe_, "Kdm", [16, 4, 2, 256], BF16)
                AT = sbuf(pe_, "AT", [128, 2, 128], BF16)
                decT = sbuf(pe_, "decT_s", [128, 2, 128], F32)
                qdecb = sbuf(pe_, "qdecb_s", [128, 2, 128], F32)
                kdec = sbuf(pe_, "kdec_s", [128, 2], F32)
                decTs = sbuf(pe_, "decTs_s", [16, 2, 16], F32)
                kdm = sbuf(pe_, "kdm_s", [16, 4, 2], F32)
                colmask = sbuf(pe_, "colmask_s", [128, 4, 16], F32)
                gnb = sbuf(pe_, "gnb_s", [128, 2, 512], F32)
                gpow = sbuf(pe_, "gpow_s", [128, 4], F32)
                qdecs = sbuf(pe_, "qdecs_s", [128, 2, 16], F32)
                st8 = sbuf(pe_, "st8", [128, 16], F32)
                og_ = [sbuf(pe_, "og_%d" % i, [128, 2, 512], BF16) for i in range(2)]
                pz = [psum(pe_, "ez%d" % i, [128, 512], F32) for i in range(2)]
                pq = psum(pe_, "eq", [128, 1024], BF16)
                pst = psum(pe_, "est", [128, 512], F32)
                po = [psum(pe_, "eo%d" % i, [128, 512], F32) for i in range(2)]
                pu = [psum(pe_, "eu%d" % i, [128, 512], F32) for i in range(2)]

                for kc in range(4):
                    for cq in range(3):
                        S.dma("pool", wr[:, 4 * kc:4 * kc + 4, 1024 * cq:1024 * cq + 1024],
                              wr_d.ap()[512 * kc:512 * kc + 512, 1024 * cq:1024 * cq + 1024].rearrange("(k p) c -> p k c", p=128),
                              w=[wr], key=wr)
                S.dma("sp", decT[:], decT_d.ap().rearrange("p (h i) -> p h i", h=2), w=[decT], key=decT)
                S.dma("sp", qdecb[:], qdecb_d.ap().rearrange("p (h i) -> p h i", h=2), w=[qdecb], key=qdecb)
                S.dma("sp", decTs[:], decTs_d.ap().rearrange("p (h i) -> p h i", h=2), w=[decTs], key=decTs)
                S.dma("sp", kdm[:], kdm_d.ap().rearrange("p (s h) -> p s h", s=4), w=[kdm], key=kdm)
                S.dma("sp", colmask[:], colmask_d.ap().rearrange("p (s i) -> p s i", s=4), w=[colmask], key=colmask)
                S.dma("sp", gnb[:], gnb_d.ap().rearrange("p (h e) -> p h e", h=2), w=[gnb], key=gnb)
                S.dma("sp", qdecs[:], qdecs_d.ap().rearrange("p (h i) -> p h i", h=2), w=[qdecs], key=qdecs)
                S.dma("sp", kdec[:], kdec_d.ap(), w=[kdec], key=kdec)
                S.dma("sp", gpow[:], gpow_d.ap(), w=[gpow], key=gpow)
                S.dve(lambda e: e.memset(St[:], 0.0), w=[St])
                S.dve(lambda e: e.memset(Sb[:], 0.0), w=[Sb])

                xg5 = xg_all.ap().rearrange("c (j p) (k t) -> c j p k t", j=4, k=2)

                def load_tile(T_):
                    X, C = xt[T_ % 2], cst[T_ % 2]
                    if T_ < NT:
                        j, tl = T_ // 8, T_ % 8
                        for cp in range(8):
                            S.dma("sp", X[:, 2 * cp:2 * cp + 2, :], xg5[cp, j, :, :, tl * 128:tl * 128 + 128], r=[xgT], w=[X], key=X)
                        S.dma("sp", C[:, :], cs_r.ap()[T_ * 128:T_ * 128 + 128, :], w=[C], key=C)
                    else:
                        for cp in range(8):
                            for j in range(4):
                                S.dma("sp", X[:, 2 * cp:2 * cp + 2, 4 * j:4 * j + 4], xg5[cp, j, :, :, 1024:1028], r=[xgT], w=[X], key=X)
                        S.dma("sp", C[:NS, :], cs_rs.ap(), w=[C], key=C)

                load_tile(0)
                zc = [0]
                for T_ in range(NT + 1):
                    n = 128 if T_ < NT else NS
                    X, C = xt[T_ % 2], cst[T_ % 2]
                    if T_ + 1 <= NT:
                        load_tile(T_ + 1)

                    def proj(cb, n=n, X=X):
                        P = pz[zc[0] % 2]
                        zc[0] += 1

                        def mm(e):
                            ins = None
                            for k in range(16):
                                ins = e.matmul(P[:n, :], X[:, k, :n], wr[:, k, cb * 512:cb * 512 + 512], start=(k == 0), stop=(k == 15))
                            return ins
                        S.pe(mm, r=[X, wr], w=[P])
                        return P

                    cosb = lambda n=n, C=C: C[:n, 0:128].unsqueeze(1).broadcast_to([n, 2, 128])
                    sinb = lambda n=n, C=C: C[:n, 128:256].unsqueeze(1).broadcast_to([n, 2, 128])
                    for half_ in range(2):
                        P = proj(half_)
                        z3 = P[:n, :].rearrange("p (h d) -> p h d", h=2)
                        a, b_, c_, d_ = rt

                        def emit_rope(z3=z3, P=P, half_=half_, n=n, cosb=cosb, sinb=sinb, C=C):
                            S.dve(lambda e: e.tensor_tensor(out=a[:n, :2, :], in0=z3[:, :, 0:128], in1=cosb(), op=ALU.mult), r=[P, C], w=[a])
                            S.dve(lambda e: e.tensor_tensor(out=b_[:n, :2, :], in0=z3[:, :, 128:256], in1=sinb(), op=ALU.mult), r=[P, C], w=[b_])
                            S.dve(lambda e: e.tensor_tensor(out=c_[:n, :2, :], in0=z3[:, :, 128:256], in1=cosb(), op=ALU.mult), r=[P, C], w=[c_])
                            S.dve(lambda e: e.tensor_tensor(out=d_[:n, :2, :], in0=z3[:, :, 0:128], in1=sinb(), op=ALU.mult), r=[P, C], w=[d_])
                            S.pool(lambda e: e.tensor_tensor(out=qkr[:n, 2 * half_:2 * half_ + 2, 0:128], in0=a[:n, :2, :], in1=b_[:n, :2, :],
                                                             op=ALU.subtract), r=[a, b_], w=[qkr])
                            S.pool(lambda e: e.tensor_tensor(out=qkr[:n, 2 * half_:2 * half_ + 2, 128:256], in0=c_[:n, :2, :], in1=d_[:n, :2, :],
                                                             op=ALU.add), r=[c_, d_], w=[qkr])
                        emit_rope()
                    for hh in range(2):
                        P = proj(2 + hh)
                        S.act(lambda e, P=P, hh=hh, n=n: e.copy(out=Vb[:n, hh, :], in_=P[:n, :]), r=[P], w=[Vb])
                    for hh in range(2):
                        P = proj(4 + hh)
                        S.act(lambda e, P=P, hh=hh, n=n: e.activation(out=sg[:n, hh * 512:hh * 512 + 512], in_=P[:n, :], func=AF.Silu), r=[P], w=[sg])
                    def trqk(e, n=n):
                        ins = None
                        for a_ in range(4):
                            for c in range(2):
                                ins = e.transpose(pq[:, (a_ * 2 + c) * 128:(a_ * 2 + c) * 128 + n], qkr[:n, a_, c * 128:c * 128 + 128], identb[:n, :n])
                        return ins
                    S.pe(trqk, r=[qkr, identb], w=[pq])
                    S.act(lambda e, n=n: e.copy(out=qkT[:, :, :n], in_=pq[:, :].rearrange("p (a t) -> p a t", a=8)[:, :, :n]), r=[pq], w=[qkT])
                    qT4 = qkT[:, 0:4, :].rearrange("p (h c) t -> p h c t", h=2)
                    kT4 = qkT[:, 4:8, :].rearrange("p (h c) t -> p h c t", h=2)

                    if T_ < NT:
                        S.dve(lambda e: e.tensor_tensor(out=QdT[:, :, :, :], in0=qT4, in1=qdecb[:, :, :].unsqueeze(2).broadcast_to([128, 2, 2, 128]),
                                                        op=ALU.mult), r=[qkT, qdecb], w=[QdT])
                        S.dve(lambda e: e.tensor_tensor(out=Kd[:, :, :], in0=qkr[:, 2:4, :], in1=kdec[:, :].unsqueeze(2).broadcast_to([128, 2, 256]),
                                                        op=ALU.mult), r=[qkr, kdec], w=[Kd])
                        def smm(e):
                            ins = None
                            for hh in range(2):
                                for c in range(2):
                                    ins = e.matmul(pst[:, hh * 128:hh * 128 + 128], kT4[:, hh, c, :], qT4[:, hh, c, :], start=(hh == 0 and c == 0),
                                                   stop=(c == 1), skip_group_check=True)
                            return ins
                        S.pe(smm, r=[qkT], w=[pst])
                        S.dve(lambda e: e.tensor_tensor(out=AT[:, :, :], in0=pst[:, 0:256].rearrange("p (h i) -> p h i", h=2), in1=decT[:, :, :],
                                                        op=ALU.mult), r=[pst, decT], w=[AT])
                        for hh in range(2):
                            def omm(e, hh=hh):
                                e.matmul(po[hh][:, :], AT[:, hh, :], Vb[:, hh, :], start=True, stop=False)
                                e.matmul(po[hh][:, :], QdT[:, hh, 0, :], Sb[:, hh * 2, :], start=False, stop=False)
                                return e.matmul(po[hh][:, :], QdT[:, hh, 1, :], Sb[:, hh * 2 + 1, :], start=False, stop=True)
                            S.pe(omm, r=[AT, Vb, QdT, Sb], w=[po[hh]])
                        for hh in range(2):
                            for c in range(2):
                                U_ = pu[c]
                                S.pe(lambda e, hh=hh, c=c, U_=U_: e.matmul(U_[:, :], Kd[:, hh, c * 128:c * 128 + 128], Vb[:, hh, :], start=True, stop=True),
                                     r=[Kd, Vb], w=[U_])
                                S.dve(lambda e, hh=hh, c=c, U_=U_: e.scalar_tensor_tensor(out=St[:, hh * 2 + c, :], in0=St[:, hh * 2 + c, :],
                                                                                         scalar=gpow[:, hh:hh + 1], in1=U_[:, :], op0=ALU.mult, op1=ALU.add),
                                      r=[St, gpow, U_], w=[St])
                                S.act(lambda e, hh=hh, c=c: e.copy(out=Sb[:, hh * 2 + c, :], in_=St[:, hh * 2 + c, :]), r=[St], w=[Sb])
                    else:
                        S.dve(lambda e: e.tensor_tensor(out=QdT[:, :, :, 0:NS], in0=qT4[:, :, :, 0:NS],
                                                        in1=qdecs[:, :, :].unsqueeze(2).broadcast_to([128, 2, 2, NS]), op=ALU.mult),
                              r=[qkT, qdecs], w=[QdT])
                        for s_ in range(4):
                            S.dve(lambda e, s_=s_: e.tensor_tensor(out=QdTm[:, s_, :, :, :], in0=QdT[:, :, :, 0:NS],
                                                                   in1=colmask[:, s_, :].unsqueeze(1).unsqueeze(2).broadcast_to([128, 2, 2, 16]),
                                                                   op=ALU.mult), r=[QdT, colmask], w=[QdTm])
                            S.dve(lambda e, s_=s_: e.tensor_tensor(out=Kdm[:, s_, :, :], in0=qkr[:NS, 2:4, :],
                                                                   in1=kdm[:, s_, :].unsqueeze(2).broadcast_to([NS, 2, 256]), op=ALU.mult),
                                  r=[qkr, kdm], w=[Kdm])

                        def smm_s(e):
                            ins = None
                            for hh in range(2):
                                for c in range(2):
                                    ins = e.matmul(pst[:NS, hh * 16:hh * 16 + 16], kT4[:, hh, c, 0:NS], qT4[:, hh, c, 0:NS], start=(hh == 0 and c == 0),
                                                   stop=(c == 1), skip_group_check=True)
                            return ins
                        S.pe(smm_s, r=[qkT], w=[pst])
                        S.dve(lambda e: e.tensor_tensor(out=AT[:NS, :, 0:NS], in0=pst[:NS, 0:32].rearrange("p (h i) -> p h i", h=2), in1=decTs[:, :, :],
                                                        op=ALU.mult), r=[pst, decTs], w=[AT])
                        for s_ in range(4):
                            SS = Ss[s_ % 2]
                            S.dma("sp", SS[:, :, :], sret_d.ap()[s_].rearrange("h (c p) e -> p (h c) e", p=128), w=[SS], key=SS)
                            S.act(lambda e, s_=s_, SS=SS: e.copy(out=Ssb[s_][:, :, :], in_=SS[:, :, :]), r=[SS], w=[Ssb[s_]])
                            for hh in range(2):
                                for c in range(2):
                                    U_ = pu[c]
                                    S.pe(lambda e, hh=hh, c=c, U_=U_, s_=s_: e.matmul(U_[:, :], Kdm[:, s_, hh, c * 128:c * 128 + 128], Vb[:NS, hh, :],
                                                                                      start=True, stop=True), r=[Kdm, Vb], w=[U_])
                                    S.dve(lambda e, hh=hh, c=c, U_=U_, SS=SS: e.scalar_tensor_tensor(out=SS[:, hh * 2 + c, :], in0=SS[:, hh * 2 + c, :],
                                                                                                    scalar=gpow[:, 2 + hh:3 + hh], in1=U_[:, :],
                                                                                                    op0=ALU.mult, op1=ALU.add),
                                          r=[SS, gpow, U_], w=[SS])
                            S.dma("sp", ret_s.ap()[s_].rearrange("h (c p) e -> p (h c) e", p=128), SS[:, :, :], r=[SS], key=SS, final=True)
                        for hh in range(2):
                            def omm_s(e, hh=hh):
                                ins = e.matmul(po[hh][:NS, :], AT[:NS, hh, 0:NS], Vb[:NS, hh, :], start=True, stop=False)
                                for s_ in range(4):
                                    for c in range(2):
                                        ins = e.matmul(po[hh][:NS, :], QdTm[:, s_, hh, c, :], Ssb[s_][:, hh * 2 + c, :], start=False,
                                                       stop=(s_ == 3 and c == 1))
                                return ins
                            S.pe(omm_s, r=[AT, Vb, QdTm] + Ssb, w=[po[hh]])

                    OG = og_[T_ % 2]
                    for hh in range(2):
                        OB = ob[hh]
                        S.act(lambda e, hh=hh, OB=OB, n=n: e.activation(out=OB[:n, :], in_=po[hh][:n, :], func=AF.Identity, accum_out=st8[:n, hh * 8:hh * 8 + 1]),
                              r=[po[hh]], w=[OB, st8])
                        S.act(lambda e, hh=hh, OB=OB, n=n: e.activation(out=jk[:n, :], in_=OB[:n, :], func=AF.Square, accum_out=st8[:n, hh * 8 + 1:hh * 8 + 2]),
                              r=[OB], w=[jk, st8])
                        o8 = hh * 8
                        S.dve(lambda e, o8=o8, n=n: e.tensor_scalar(out=st8[:n, o8 + 2:o8 + 4], in0=st8[:n, o8:o8 + 2], scalar1=1.0 / 512, scalar2=None,
                                                                    op0=ALU.mult), r=[st8], w=[st8])
                        S.dve(lambda e, o8=o8, n=n: e.tensor_tensor(out=st8[:n, o8 + 4:o8 + 5], in0=st8[:n, o8 + 2:o8 + 3], in1=st8[:n, o8 + 2:o8 + 3],
                                                                    op=ALU.mult), r=[st8], w=[st8])
                        S.dve(lambda e, o8=o8, n=n: e.tensor_tensor(out=st8[:n, o8 + 5:o8 + 6], in0=st8[:n, o8 + 3:o8 + 4], in1=st8[:n, o8 + 4:o8 + 5],
                                                                    op=ALU.subtract), r=[st8], w=[st8])
                        S.dve(lambda e, o8=o8, n=n: e.tensor_scalar(out=st8[:n, o8 + 5:o8 + 6], in0=st8[:n, o8 + 5:o8 + 6], scalar1=0.0, scalar2=1e-5,
                                                                    op0=ALU.max, op1=ALU.add), r=[st8], w=[st8])
                        S.act(lambda e, o8=o8, n=n: e.activation(out=st8[:n, o8 + 6:o8 + 7], in_=st8[:n, o8 + 5:o8 + 6], func=AF.Sqrt), r=[st8], w=[st8])
                        S.dve(lambda e, o8=o8, n=n: e.reciprocal(out=st8[:n, o8 + 6:o8 + 7], in_=st8[:n, o8 + 6:o8 + 7]), r=[st8], w=[st8])
                        S.dve(lambda e, o8=o8, OB=OB, n=n: e.tensor_scalar(out=OB[:n, :], in0=OB[:n, :], scalar1=st8[:n, o8 + 2:o8 + 3],
                                                                           scalar2=st8[:n, o8 + 6:o8 + 7], op0=ALU.subtract, op1=ALU.mult),
                              r=[OB, st8], w=[OB])
                        S.pool(lambda e, hh=hh, OB=OB, n=n: e.tensor_tensor(out=OB[:n, :], in0=OB[:n, :], in1=gnb[:n, hh, :], op=ALU.mult), r=[OB, gnb], w=[OB])
                        S.pool(lambda e, hh=hh, OB=OB, OG=OG, n=n: e.tensor_tensor(out=OG[:n, hh, :], in0=OB[:n, :], in1=sg[:n, hh * 512:hh * 512 + 512],
                                                                                   op=ALU.mult), r=[OB, sg], w=[OG])
                        if T_ < NT:
                            c_, tl = T_ // 8, T_ % 8
                            S.dma("sp", o_r.ap()[c_, hh, tl * 128:tl * 128 + 128, :], OG[:, hh, :], r=[OG], w=[o_rT[c_][hh]], key=OG)
                        else:
                            S.dma("sp", o_rs.ap()[hh], OG[:NS, hh, :], r=[OG], w=[o_rsT[hh]], key=OG)
                    if T_ < NT and T_ % 8 == 7:
                        c_ = T_ // 8
                        for hh in range(2):
                            base = ((c_ * 2 + hh) * 4) * 1024
                            allgather(o_r.ap()[c_, hh].rearrange("(p a) f -> p (a f)", a=8),
                                      og2.ap()[base:base + 4096, :].rearrange("(q a) f -> q (a f)", a=8), o_rT[c_][hh], og2T)
                    if T_ == NT - 1:
                        S.dma("sp", ret_p.ap().rearrange("h (c p) e -> p (h c) e", p=128), St[:, :, :], r=[St], key=St, final=True)
                for hh in range(2):
                    allgather(o_rs.ap()[hh].rearrange("r (b x) -> (r b) x", b=8),
                              ogs2.ap()[hh * 64:hh * 64 + 64, :].rearrange("r (b x) -> (r b) x", b=8), o_rsT[hh], og2T)
                S.barrier()

        if RUN_T0:
            token_phase(0)
        if RUN_E:
            phase_e()
        if RUN_T1:
            token_phase(1)

        S.emit()
    return nc


def _col_slice(g):
    cols = list(range(512 * g, 512 * g + 512))
    for e in (0, 1):
        for br in range(3):
            base = 2048 + ((br * 2 + e) * 4 + g) * 128
            cols += list(range(base, base + 128))
    for br in range(3):
        base = 2048 + 3072 + br * 16 + 4 * g
        cols += list(range(base, base + 4))
    return np.array(cols)


def _consts():
    c0 = np.arange(256)[:, None] * 16
    j0 = np.arange(64)[None, :] * 64
    cover = ((c0 < j0 + 64) & (c0 + 32 > j0)).astype(np.float32)
    cover[255] = 0.0
    p = np.arange(128)[:, None]
    col = np.arange(128)[None, :]
    I0 = (col - 16 * p).astype(np.float32)
    Jm = (np.arange(64)[None, :] - (p >= 64)).astype(np.float32)
    tri = np.concatenate([(p <= col), (col < p)], axis=1).astype(np.float32)
    Ebig = (np.arange(SEQ)[None, :] // 64 == np.arange(64)[:, None]).astype(np.float32)
    key = np.arange(16)
    hq = np.arange(16)
    smask = np.zeros((16, 5, 16), np.float32)
    for s_ in range(4):
        smask[:, s_, :] = ((key[:, None] // 4 == s_) & (key[:, None] % 4 <= hq[None, :] % 4))
    smask[:, 4, :] = (key[:, None] > hq[None, :] % 4)
    sel16 = np.zeros((16, 68), np.float32)
    sel16[:, 0:4] = (hq[:, None] % 4 == np.arange(4)[None, :])
    for s_ in range(4):
        sel16[:, 4 + 16 * s_:20 + 16 * s_] = (key[:, None] == 4 * s_ + hq[None, :] % 4)
    hmask = (np.arange(12)[None, :] % 4 == hq[:, None] // 4).astype(np.float32)
    c0s = np.arange(1024)[:, None] * 16
    j0s = np.arange(257)[None, :] * 64
    cover_s = ((c0s < j0s + 64) & (c0s + 32 > j0s)).astype(np.float32)
    cover_s[1023] = 0.0
    return {"cover": cover, "I0": I0, "Jm": Jm, "tri": tri, "Ebig": Ebig, "smask": smask, "sel16": sel16,
            "hmask": hmask, "cover_s": cover_s, "cs_r": rope_table(np.arange(SEQ), 128),
            "cs_rs": rope_table(np.tile(PAST + np.arange(4), 4), 128)}


CONST = _consts()


def _ret_cols(r):
    cols = []
    for base, w in ((0, 256), (2048, 256), (4096, 512), (8192, 512)):
        for hh in range(2):
            h = 2 * r + hh
            cols += list(range(base + w * h, base + w * h + w))
    return np.array(cols)


def _ret_consts(r):
    out = {}
    i = np.arange(128, dtype=np.float64)
    decT = np.zeros((128, 2, 128), np.float64)
    qdecb = np.zeros((128, 2, 128), np.float64)
    kdec = np.zeros((128, 2), np.float64)
    decTs = np.zeros((16, 2, 16), np.float64)
    kdm = np.zeros((16, 4, 2), np.float64)
    gpow = np.zeros((128, 4), np.float64)
    qdecs = np.zeros((128, 2, 16), np.float64)
    t16 = np.arange(16)
    for hh in range(2):
        h = 2 * r + hh
        lg = np.log(np.float32(1.0) - np.float32(2.0) ** np.float32(-5.0 - h)).astype(np.float64)
        rel = i[None, :] - i[:, None]
        decT[:, hh, :] = np.where(rel >= 0, np.exp(np.maximum(rel, 0) * lg), 0.0) / 16.0
        qdecb[:, hh, :] = np.exp((i + 1.0) * lg)[None, :]
        kdec[:, hh] = np.exp((127.0 - i) * lg) / 16.0
        rel4 = (t16[None, :] % 4) - (t16[:, None] % 4)
        same = (t16[None, :] // 4) == (t16[:, None] // 4)
        decTs[:, hh, :] = np.where(same & (rel4 >= 0), np.exp(np.maximum(rel4, 0) * lg), 0.0) / 16.0
        for s_ in range(4):
            kdm[:, s_, hh] = np.where(t16 // 4 == s_, np.exp((3.0 - t16 % 4) * lg), 0.0) / 16.0
        gpow[:, hh] = np.exp(128.0 * lg)
        gpow[:, 2 + hh] = np.exp(4.0 * lg)
        qdecs[:, hh, :] = np.exp((t16 % 4 + 1.0) * lg)[None, :]
    colmask = np.zeros((128, 4, 16), np.float32)
    for s_ in range(4):
        colmask[:, s_, :] = (t16 // 4 == s_)[None, :]
    out["decT"] = decT.reshape(128, 256).astype(np.float32)
    out["qdecb"] = qdecb.reshape(128, 256).astype(np.float32)
    out["kdec"] = kdec.astype(np.float32)
    out["decTs"] = decTs.reshape(16, 32).astype(np.float32)
    out["kdm"] = kdm.reshape(16, 8).astype(np.float32)
    out["gpow"] = gpow.astype(np.float32)
    out["qdecs"] = qdecs.reshape(128, 32).astype(np.float32)
    out["colmask"] = colmask.reshape(128, 64)
    return out


def _oidx2(r):
    p = np.arange(128)
    idx = np.zeros((128, 72), np.int32)
    for t in range(9):
        for j in range(4):
            for hh in range(2):
                if t < 8:
                    idx[:, t * 8 + 2 * j + hh] = ((r * 2 + hh) * 4 + j) * 1024 + t * 128 + p
                else:
                    idx[:, t * 8 + 2 * j + hh] = (hh * 4 + j) * NS + 4 * r + np.minimum(p, 3)
    return idx


def _oidx(r):
    p = np.arange(128)
    idx = np.zeros((128, 36), np.int32)
    for t in range(9):
        for j in range(4):
            if t < 8:
                idx[:, t * 4 + j] = r * 4096 + j * 1024 + t * 128 + p
            else:
                idx[:, t * 4 + j] = j * NS + 4 * r + np.minimum(p, 3)
    return idx


def make_in_maps(inp):
    maps = []
    ident = np.eye(128, dtype=np.float32)
    cs_p = rope_table(np.arange(SEQ), 64)
    cs_s = rope_table(np.tile(PAST + np.arange(4), 4), 64)
    for c in range(8):
        b, r = c // 4, c % 4
        m = {
            "xb": np.ascontiguousarray(inp["x_prompt"][b]),
            "xs": np.ascontiguousarray(inp["x_sample"][4 * b:4 * b + 4].reshape(NS, D)),
            "w_in": np.ascontiguousarray(inp["nsa_w_in"][0][:, _col_slice(r)]),
            "gmix0": np.ascontiguousarray(np.broadcast_to(inp["norm_mix"][0][None, :], (128, D))),
            "cs_p": cs_p, "cs_s": cs_s, "ident": ident,
            "cw1": inp["nsa_cmp_w1"][0], "cw2": inp["nsa_cmp_w2"][0],
            "posT": np.ascontiguousarray(inp["nsa_cmp_pos"][0].reshape(64, 128).T),
            "b1T": np.ascontiguousarray(inp["nsa_cmp_b1"][0].T),
            "x_tok": np.ascontiguousarray(np.concatenate([inp["x_prompt"][b, 1024 * r:1024 * r + 1024], inp["x_sample"][c]], axis=0)),
            "oidx": _oidx(r),
            "w_out0": inp["nsa_w_out"][0], "ffn1_0": inp["ffn_w1"][0], "ffn2_0": inp["ffn_w2"][0],
            "gffn0T": np.ascontiguousarray(inp["norm_ffn"][0].reshape(16, 128).T),
            "w_out1": inp["ret_w_out"][0], "ffn1_1": inp["ffn_w1"][1], "ffn2_1": inp["ffn_w2"][1],
            "gmix1T": np.ascontiguousarray(inp["norm_mix"][1].reshape(16, 128).T),
            "gffn1T": np.ascontiguousarray(inp["norm_ffn"][1].reshape(16, 128).T),
            "gfinT": np.ascontiguousarray(inp["norm_final"].reshape(16, 128).T),
            "wr": np.ascontiguousarray(inp["ret_w_in"][0][:, _ret_cols(r)]),
            "cs_r": CONST["cs_r"], "cs_rs": CONST["cs_rs"],
            "gnb": np.ascontiguousarray(np.broadcast_to(inp["ret_gn"][0][1024 * r:1024 * r + 1024][None, :], (128, 1024))),
            "sret": np.ascontiguousarray(inp["state_ret"][0][4 * b:4 * b + 4, 2 * r:2 * r + 2]),
            "oidx2": _oidx2(r),
            **_ret_consts(r),
            "ccache": np.ascontiguousarray(inp["cache_cmp_kv"][0][:, :, :, r, :]),
            "scache": np.ascontiguousarray(inp["cache_sel_kv"][0][:, :, :, r, :]),
            "wstate": np.ascontiguousarray(inp["state_win_kv"][0][4 * b:4 * b + 4, :, :, r, :]),
            "ptab": np.ascontiguousarray(inp["page_table"][4 * b:4 * b + 4].astype(np.int32)),
            "smask": CONST["smask"], "sel16": CONST["sel16"], "hmask": CONST["hmask"], "cover_s": CONST["cover_s"],
            "pidx": (np.arange(128) % 4).astype(np.float32)[:, None].copy(),
            "ptabT": np.ascontiguousarray(inp["page_table"][4 * b:4 * b + 4].astype(np.int32).reshape(4, 4, 32).transpose(2, 0, 1).reshape(32, 16)),
            "Rrep": (np.arange(128)[None, :] // 4 == np.arange(32)[:, None]).astype(np.float32),
            "E2": (np.arange(128)[None, :] // 2 == np.arange(64)[:, None]).astype(np.float32),
            "cover": CONST["cover"], "I0": CONST["I0"], "Jm": CONST["Jm"], "tri": CONST["tri"], "Ebig": CONST["Ebig"],
        }
        maps.append(m)
    return maps


_NC = None


def kernel(**inputs):
    global _NC
    inp = {k: np.asarray(v) for k, v in inputs.items()}
    if _NC is None:
        _NC = build()
    maps = make_in_maps(inp)
    res = run_bass_kernel_spmd(_NC, maps, core_ids=list(range(8)), **({'trace': True} if TRACE else {}))
    global LAST_RES
    LAST_RES = res
    R = res.results
    global LAST
    LAST = R
    cmp_p = np.zeros((1, 2, SEQ, 2, 4, 128), np.float32)
    sel_p = np.zeros_like(cmp_p)
    win_full = np.zeros_like(cmp_p)
    cmp_s = np.zeros((1, 8, 4, 2, 4, 128), np.float32)
    sel_s = np.zeros_like(cmp_s)
    win_new = np.zeros_like(cmp_s)
    for c in range(8):
        b, g = c // 4, c % 4
        kp = R[c]["kv_p"]
        cmp_p[0, b, :, :, g, :] = kp[:, 0]
        sel_p[0, b, :, :, g, :] = kp[:, 1]
        win_full[0, b, :, :, g, :] = kp[:, 2]
        ks = R[c]["kv_s"].reshape(4, 4, 3, 2, 128)
        cmp_s[0, 4 * b:4 * b + 4, :, :, g, :] = ks[:, :, 0]
        sel_s[0, 4 * b:4 * b + 4, :, :, g, :] = ks[:, :, 1]
        win_new[0, 4 * b:4 * b + 4, :, :, g, :] = ks[:, :, 2]
    win_p = np.ascontiguousarray(win_full[:, :, SEQ - 512:])
    y_p = np.zeros((2, SEQ, D), np.float32)
    y_s = np.zeros((8, 4, D), np.float32)
    win_s = np.zeros((1, 8, 512, 2, 4, 128), np.float32)
    ret_p = np.zeros((1, 2, 8, 256, 512), np.float32)
    ret_s = np.zeros((1, 8, 8, 256, 512), np.float32)
    for c in range(8):
        b, g = c // 4, c % 4
        win_s[0, 4 * b:4 * b + 4, :, :, g, :] = R[c]["win_s"]
        y_p[b, 1024 * g:1024 * g + 1024] = R[c]["y_tok"][:1024]
        y_s[c] = R[c]["y_tok"][1024:]
        ret_p[0, b, 2 * g:2 * g + 2] = R[c]["ret_p"]
        ret_s[0, 4 * b:4 * b + 4, 2 * g:2 * g + 2] = R[c]["ret_s"]
    return (y_p, y_s, cmp_p, cmp_s, sel_p, sel_s, win_p, win_s, ret_p, ret_s)
```

```python
import contextlib
import numpy as np
import concourse.bass as bass
import concourse.mybir as mybir
from concourse.bass_utils import run_bass_kernel_spmd

F32 = mybir.dt.float32
BF16 = mybir.dt.bfloat16
I32 = mybir.dt.int32
AF = mybir.ActivationFunctionType
ALU = mybir.AluOpType
AX = mybir.AxisListType

ENGS = ("sp", "act", "dve", "pool", "pe")


class Buf:
    __slots__ = ("name", "last_w", "readers", "cnt")

    def __init__(self, name):
        self.name = name
        self.last_w = None
        self.readers = []
        self.cnt = 0


class T:
    def __init__(self, t, name):
        self.t = t
        self.b = Buf(name)

    def __getitem__(self, k):
        return self.t[k]

    def ap(self):
        return self.t.ap()


def _b(x):
    return x.b if isinstance(x, T) else x


class Op:
    __slots__ = ("eng", "fn", "deps", "is_dma", "key", "sig", "sigidx", "cnt", "inc")

    def __init__(self, eng, fn, is_dma=False, key=None, inc=16):
        self.eng = eng
        self.fn = fn
        self.deps = []
        self.is_dma = is_dma
        self.key = key
        self.sig = False
        self.sigidx = 0
        self.cnt = 0
        self.inc = inc


class Sched:
    def __init__(self, nc):
        self.nc = nc
        self.ops = []
        self.last_real = {e: None for e in ENGS}
        self.out_dmas = []

    def _add(self, op, r, w):
        r = [_b(x) for x in r]
        w = [_b(x) for x in w]
        deps = []
        for b in r:
            if b.last_w is not None:
                deps.append(b.last_w)
        for b in w:
            if b.last_w is not None:
                deps.append(b.last_w)
            deps.extend(b.readers)
        seen = set()
        for d in deps:
            if d is op or id(d) in seen:
                continue
            seen.add(id(d))
            if (not d.is_dma) and d.eng == op.eng and op.eng == "pe" and not op.is_dma:
                continue
            op.deps.append(d)
            if not d.is_dma:
                d.sig = True
        for b in r:
            b.readers.append(op)
        for b in w:
            b.last_w = op
            b.readers = []
        self.ops.append(op)
        if not op.is_dma:
            self.last_real[op.eng] = op
        return op

    def op(self, eng, fn, r=(), w=()):
        return self._add(Op(eng, fn), r, w)

    def pe(self, fn, r=(), w=()):
        return self.op("pe", fn, r, w)

    def act(self, fn, r=(), w=()):
        return self.op("act", fn, r, w)

    def dve(self, fn, r=(), w=()):
        return self.op("dve", fn, r, w)

    def pool(self, fn, r=(), w=()):
        return self.op("pool", fn, r, w)

    def dma(self, q, out, in_, r=(), w=(), key=None, final=False, **kw):
        return self.custom_dma(q, lambda e: e.dma_start(out=out, in_=in_, **kw), r, w, key, 16, final)

    def custom_dma(self, q, fn, r=(), w=(), key=None, inc=16, final=False):
        k = _b(key)
        op = Op(q, fn, is_dma=True, key=k, inc=inc)
        self._add(op, r, w)
        if final:
            self.out_dmas.append(op)
        return op

    def barrier(self):
        pend = [o for o in self.last_real.values() if o is not None]
        latest = {}
        for o in self.ops:
            if o.is_dma:
                latest[id(o.key)] = o
        for e in ENGS:
            op = Op(e, None)
            for d in pend:
                if d.eng != e:
                    op.deps.append(d)
                    d.sig = True
            op.deps.extend(latest.values())
            self.ops.append(op)

    def emit(self):
        nc = self.nc
        fin = Op("sp", None)
        latest = {}
        for o in self.out_dmas:
            latest[id(o.key)] = o
        fin.deps = list(latest.values())
        for e in ENGS:
            o = self.last_real[e]
            if e != "sp" and o is not None:
                fin.deps.append(o)
                o.sig = True
        self.ops.append(fin)

        with contextlib.ExitStack() as st:
            esem = {e: st.enter_context(nc.semaphore("s_" + e)) for e in ENGS}
            keysems = {}
            keyvals = {}
            for o in self.ops:
                if o.is_dma:
                    kid = id(o.key)
                    if kid not in keysems:
                        keysems[kid] = st.enter_context(nc.semaphore("k%d" % len(keysems)))
                        keyvals[kid] = 0
                    keyvals[kid] += o.inc
                    o.cnt = keyvals[kid]
            cnts = {e: 0 for e in ENGS}
            for o in self.ops:
                if (not o.is_dma) and o.sig:
                    assert o.fn is not None
                    cnts[o.eng] += 1
                    o.sigidx = cnts[o.eng]
            self.n_sems = len(keysems) + 5
            per = {e: [o for o in self.ops if o.eng == e] for e in ENGS}
            block = st.enter_context(nc.Block())

            def run(eng_name):
                def body(eng):
                    waited = {}
                    for o in per[eng_name]:
                        for d in o.deps:
                            if d.is_dma:
                                s, v = keysems[id(d.key)], d.cnt
                            else:
                                s, v = esem[d.eng], d.sigidx
                            if waited.get(id(s), 0) >= v:
                                continue
                            waited[id(s)] = v
                            eng.wait_ge(s, v)
                        if o.fn is None:
                            continue
                        ins = o.fn(eng)
                        if o.is_dma:
                            ins.then_inc(keysems[id(o.key)], o.inc)
                        elif o.sig:
                            ins.then_inc(esem[eng_name], 1)

                return body

            block.sync(run("sp"))
            block.scalar(run("act"))
            block.vector(run("dve"))
            block.gpsimd(run("pool"))
            block.tensor(run("pe"))


D = 2048
SEQ = 4096
NT = SEQ // 128
NTOK = 1028
NS = 16
PAST = 16384
NCOL = 1292
RMS_EPS = 1e-6
SCALE = 128 ** -0.5
NEG = -30000.0

STAGE = 1
DEBUG = False
NTQ = NT
RUN_S = True
RUN_T0 = True
RUN_E = True
RUN_T1 = True
TRACE = False


def rope_table(pos, half):
    inv = (10000.0 ** (-(np.arange(half, dtype=np.float32)) / np.float32(half))).astype(np.float32)
    ang = (pos.astype(np.float32)[:, None] * inv[None, :]).astype(np.float32)
    return np.concatenate([np.cos(ang), np.sin(ang)], axis=1).astype(np.float32)


def build(stage=STAGE):
    nc = bass.Bass("TRN2", target_bir_lowering=False)
    S = Sched(nc)

    def din(name, shape, dt=F32):
        return nc.dram_tensor(name, list(shape), dt, kind="ExternalInput")

    def dout(name, shape, dt=F32):
        return nc.dram_tensor(name, list(shape), dt, kind="ExternalOutput")

    xb = din("xb", [SEQ, D])
    xs = din("xs", [NS, D])
    w_in = din("w_in", [D, NCOL])
    gmix0 = din("gmix0", [128, D])
    cs_p = din("cs_p", [SEQ, 128])
    cs_s = din("cs_s", [NS, 128])
    ident_d = din("ident", [128, 128])

    cw1 = din("cw1", [2, 32, 128, 128])
    cw2 = din("cw2", [2, 128, 128])
    posT = din("posT", [128, 64])
    b1T = din("b1T", [128, 2])
    cover_d = din("cover", [256, 64])
    I0_d = din("I0", [128, 128])
    Jm_d = din("Jm", [128, 64])
    tri_d = din("tri", [128, 256])
    Ebig_d = din("Ebig", [64, SEQ])

    kv_p = dout("kv_p", [SEQ, 3, 2, 128])
    o_loc = nc.dram_tensor("o_loc", [SEQ, 512], BF16)
    o_locs = nc.dram_tensor("o_locs", [NS, 512], BF16)
    og = nc.dram_tensor("og", [4 * SEQ, 512], BF16)
    ogs = nc.dram_tensor("ogs", [4 * NS, 512], BF16)
    o_locT = [Buf("o_loc%d" % i) for i in range(5)]
    ogT = Buf("og")
    x_tok = din("x_tok", [NTOK, D])
    oidx_d = din("oidx", [128, 36], I32)
    w_out0 = din("w_out0", [D, D])
    ffn1_0 = din("ffn1_0", [D, 4 * D])
    ffn2_0 = din("ffn2_0", [4 * D, D])
    gffn0T = din("gffn0T", [128, 16])
    w_out1 = din("w_out1", [2 * D, D])
    ffn1_1 = din("ffn1_1", [D, 4 * D])
    ffn2_1 = din("ffn2_1", [4 * D, D])
    gmix1T = din("gmix1T", [128, 16])
    gffn1T = din("gffn1T", [128, 16])
    gfinT = din("gfinT", [128, 16])
    wr_d = din("wr", [D, 3072])
    cs_r = din("cs_r", [SEQ, 256])
    cs_rs = din("cs_rs", [NS, 256])
    decT_d = din("decT", [128, 256])
    qdecb_d = din("qdecb", [128, 256])
    kdec_d = din("kdec", [128, 2])
    decTs_d = din("decTs", [16, 32])
    kdm_d = din("kdm", [16, 8])
    colmask_d = din("colmask", [128, 64])
    gnb_d = din("gnb", [128, 1024])
    gpow_d = din("gpow", [128, 4])
    qdecs_d = din("qdecs", [128, 32])
    sret_d = din("sret", [4, 2, 256, 512])
    oidx2_d = din("oidx2", [128, 72], I32)
    ret_p = dout("ret_p", [2, 256, 512])
    ret_s = dout("ret_s", [4, 2, 256, 512])
    y_tok = dout("y_tok", [NTOK, D])
    hspill = nc.dram_tensor("hspill", [128, 16 * NTOK], F32)
    hspT = Buf("hspill")
    xg_in = nc.dram_tensor("xg_in", [8, 128, 2 * NTOK], BF16)
    xg_all = nc.dram_tensor("xg_all", [8, 512, 2 * NTOK], BF16)
    xg_inT = [Buf("xg_in%d" % i) for i in range(8)]
    xgT = Buf("xg_all")
    o_r = nc.dram_tensor("o_r", [4, 2, 1024, 512], BF16)
    o_rs = nc.dram_tensor("o_rs", [2, NS, 512], BF16)
    og2 = nc.dram_tensor("og2", [4 * 2 * 4 * 1024, 512], BF16)
    ogs2 = nc.dram_tensor("ogs2", [2 * 4 * NS, 512], BF16)
    o_rT = [[Buf("o_r%d%d" % (c, h)) for h in range(2)] for c in range(4)]
    o_rsT = [Buf("o_rs%d" % h) for h in range(2)]
    og2T = Buf("og2")
    win_s = dout("win_s", [4, 512, 2, 128])
    ccache = din("ccache", [1280, 128, 2, 128])
    scache = din("scache", [1280, 128, 2, 128])
    wstate = din("wstate", [4, 512, 2, 128])
    ptab = din("ptab", [4, 128], I32)
    smask_d = din("smask", [16, 5, 16])
    sel16_d = din("sel16", [16, 4 + 64])
    hmask_d = din("hmask", [16, 12])
    cover_s = din("cover_s", [1024, 257])
    pidx_d = din("pidx", [128, 1])
    ptabT = din("ptabT", [32, 16], I32)
    Rrep_d = din("Rrep", [32, 128])
    E2_d = din("E2", [64, 128])
    kv_s = dout("kv_s", [NS, 3, 2, 128])

    dbg_outs = {}

    def dbg(name, t, ap, shape, dt=F32):
        if not DEBUG:
            return
        d_ = nc.dram_tensor("dbg_" + name, list(shape), dt, kind="ExternalOutput")
        S.dma("sp", d_.ap(), ap, r=[t], key=Buf("dbgk_" + name), final=True)

    with contextlib.ExitStack() as top:
        def sbuf(st, name, shape, dt):
            return T(st.enter_context(nc.sbuf_tensor(name, list(shape), dt)), name)

        def psum(st, name, shape, dt):
            return T(st.enter_context(nc.psum_tensor(name, list(shape), dt)), name)

        identb = sbuf(top, "identb", [128, 128], BF16)
        identf = sbuf(top, "identf", [128, 128], F32)
        GL = sbuf(top, "GL", [128, NT + 1, 12], F32)
        QTs = sbuf(top, "QTs", [128, 4, NS], BF16)
        KTs = sbuf(top, "KTs", [128, 3, NS], BF16)
        VAs = sbuf(top, "VAs", [NS, 2, 132], BF16)
        pp = contextlib.ExitStack()
        QT = sbuf(pp, "QT", [128, 4, SEQ + NS], BF16)
        KT = sbuf(pp, "KT", [128, 3, SEQ + NS], BF16)
        VcT = sbuf(pp, "VcT", [128, SEQ + NS], BF16)
        VA = sbuf(pp, "VA", [128, NT + 1, 2, 132], BF16)
        S.dma("sp", identf[:], ident_d.ap(), w=[identf], key=identf)
        S.dma("pool", identb[:], ident_d.ap(), w=[identb], key=identb)

        def phase_a():
            with contextlib.ExitStack() as pa:
                wsb = sbuf(pa, "wsb", [128, 16, NCOL], BF16)
                gsb = sbuf(pa, "gsb", [128, D], F32)
                xt = [sbuf(pa, "xt%d" % i, [128, D], F32) for i in range(2)]
                cst = [sbuf(pa, "cst%d" % i, [128, 128], F32) for i in range(3)]
                junk = sbuf(pa, "junk", [128, D], BF16)
                ss = sbuf(pa, "ss", [128, 2], F32)
                epsb = sbuf(pa, "epsb", [128, 1], F32)
                S.pool(lambda e: e.memset(epsb[:], RMS_EPS), w=[epsb])
                xn = sbuf(pa, "xn", [128, D], BF16)
                xnTs = [sbuf(pa, "xnT%d" % i, [128, 16, 128], BF16) for i in range(2)]
                kvf = [sbuf(pa, "kvf%d" % i, [128, 3, 2, 128], F32) for i in range(2)]
                rt = [sbuf(pa, "rt%d" % i, [128, 4, 64], F32) for i in range(4)]
                qkb = sbuf(pa, "qkb", [128, 8, 128], BF16)
                pT = [psum(pa, "pT%d" % i, [128, 1024], BF16) for i in range(2)]
                pz = [psum(pa, "pz%d" % i, [128, 512], F32) for i in range(3)]
                pq = psum(pa, "pq", [128, 1024], BF16)

                for kc in range(4):
                    S.dma("pool", wsb[:, 4 * kc:4 * kc + 4, :],
                          w_in.ap()[512 * kc:512 * kc + 512, :].rearrange("(k p) c -> p k c", p=128),
                          w=[wsb], key=wsb)
                S.dma("sp", gsb[:], gmix0.ap(), w=[gsb], key=gsb)
                S.pool(lambda e: e.memset(VA[:, :, :, 128:129], 1.0), w=[VA])

                def stageA1(j):
                    n = 128 if j < NT else NS
                    c0 = j * 128
                    X = xt[j % 2]
                    xnT = xnTs[j % 2]

                    def load(jj):
                        nn = 128 if jj < NT else NS
                        cc = jj * 128
                        S.dma("sp", xt[jj % 2][:nn, :], xb.ap()[cc:cc + 128, :] if jj < NT else xs.ap(), w=[xt[jj % 2]], key=xt[jj % 2])
                        S.dma("sp", cst[jj % 3][:nn, :], cs_p.ap()[cc:cc + 128, :] if jj < NT else cs_s.ap(), w=[cst[jj % 3]], key=cst[jj % 3])
                    if j == 0:
                        load(0)
                    if j + 1 <= NT:
                        load(j + 1)
                    S.act(lambda e, X=X, n=n: e.activation(out=junk[:n, :], in_=X[:n, :], func=AF.Square,
                                                           accum_out=ss[:n, 0:1]), r=[X], w=[junk, ss])
                    S.act(lambda e, n=n: e.activation(out=ss[:n, 1:2], in_=ss[:n, 0:1], func=AF.Sqrt, scale=1.0 / D,
                                                      bias=epsb[:n, 0:1]), r=[ss, epsb], w=[ss])
                    S.dve(lambda e, n=n: e.reciprocal(out=ss[:n, 1:2], in_=ss[:n, 1:2]), r=[ss], w=[ss])
                    S.dve(lambda e, X=X, n=n: e.scalar_tensor_tensor(out=xn[:n, :], in0=X[:n, :], scalar=ss[:n, 1:2],
                                                                     in1=gsb[:n, :], op0=ALU.mult, op1=ALU.mult),
                          r=[X, ss, gsb], w=[xn])
                    for hb in range(2):
                        def tr(e, hb=hb, n=n):
                            ins = None
                            for k in range(8):
                                ins = e.transpose(pT[hb][:, k * 128:k * 128 + n], xn[:n, (hb * 8 + k) * 128:(hb * 8 + k + 1) * 128],
                                                  identb[:n, :n])
                            return ins
                        S.pe(tr, r=[xn, identb], w=[pT[hb]])
                        S.act(lambda e, hb=hb, n=n: e.copy(out=xnT[:, hb * 8:hb * 8 + 8, :n],
                                                           in_=pT[hb][:, :].rearrange("p (k t) -> p k t", k=8)[:, :, :n]),
                              r=[pT[hb]], w=[xnT])

                def stageA2(j):
                    n = 128 if j < NT else NS
                    c0 = j * 128
                    C = cst[j % 3]
                    KV = kvf[j % 2]
                    xnT = xnTs[j % 2]
                    cbs = [(0, 512), (512, 512), (1024, NCOL - 1024)]
                    for ci, (cb, cw) in enumerate(cbs):
                        def mm(e, ci=ci, cb=cb, cw=cw, n=n):
                            ins = None
                            for k in range(16):
                                ins = e.matmul(pz[ci][:n, :cw], xnT[:, k, :n], wsb[:, k, cb:cb + cw],
                                               start=(k == 0), stop=(k == 15))
                            return ins
                        S.pe(mm, r=[xnT, wsb], w=[pz[ci]])
                    cosb = lambda h, n=n, C=C: C[:n, 0:64].unsqueeze(1).broadcast_to([n, h, 64])
                    sinb = lambda h, n=n, C=C: C[:n, 64:128].unsqueeze(1).broadcast_to([n, h, 64])

                    def rope(src3, h, out_lo, out_hi, rd, wr, n=n, cosb=cosb, sinb=sinb):
                        a, b_, c_, d_ = rt
                        S.dve(lambda e: e.tensor_tensor(out=a[:n, :h, :], in0=src3[:, :, 0:64], in1=cosb(h), op=ALU.mult), r=rd + [C], w=[a])
                        S.dve(lambda e: e.tensor_tensor(out=b_[:n, :h, :], in0=src3[:, :, 64:128], in1=sinb(h), op=ALU.mult), r=rd + [C], w=[b_])
                        S.dve(lambda e: e.tensor_tensor(out=c_[:n, :h, :], in0=src3[:, :, 64:128], in1=cosb(h), op=ALU.mult), r=rd + [C], w=[c_])
                        S.dve(lambda e: e.tensor_tensor(out=d_[:n, :h, :], in0=src3[:, :, 0:64], in1=sinb(h), op=ALU.mult), r=rd + [C], w=[d_])
                        S.pool(lambda e: e.tensor_tensor(out=out_lo, in0=a[:n, :h, :], in1=b_[:n, :h, :], op=ALU.subtract), r=[a, b_], w=wr)
                        S.pool(lambda e: e.tensor_tensor(out=out_hi, in0=c_[:n, :h, :], in1=d_[:n, :h, :], op=ALU.add), r=[c_, d_], w=wr)

                    z0 = pz[0][:n, :].rearrange("p (h d) -> p h d", h=4)
                    rope(z0, 4, qkb[:n, 0:4, 0:64], qkb[:n, 0:4, 64:128], [pz[0]], [qkb])
                    z1 = pz[1][:n, 0:384].rearrange("p (h d) -> p h d", h=3)
                    rope(z1, 3, KV[:n, :, 0, 0:64], KV[:n, :, 0, 64:128], [pz[1]], [KV])
                    S.act(lambda e, n=n, KV=KV: e.copy(out=KV[:n, 0, 1, :], in_=pz[1][:n, 384:512]), r=[pz[1]], w=[KV])
                    S.act(lambda e, n=n, KV=KV: e.copy(out=KV[:n, 1:3, 1, :],
                                                       in_=pz[2][:n, 0:256].rearrange("p (h d) -> p h d", h=2)),
                          r=[pz[2]], w=[KV])
                    S.act(lambda e, n=n, j=j: e.copy(out=GL[:n, j, :], in_=pz[2][:n, 256:268]), r=[pz[2]], w=[GL])
                    S.pool(lambda e, n=n, KV=KV: e.tensor_copy(out=qkb[:n, 4:7, :], in_=KV[:n, :, 0, :]), r=[KV], w=[qkb])
                    S.pool(lambda e, n=n, KV=KV: e.tensor_copy(out=qkb[:n, 7, :], in_=KV[:n, 0, 1, :]), r=[KV], w=[qkb])
                    S.pool(lambda e, n=n, KV=KV, j=j: e.tensor_copy(out=VA[:n, j, :, 0:128], in_=KV[:n, 1:3, 1, :]), r=[KV], w=[VA])
                    dst = kv_p.ap()[c0:c0 + 128] if j < NT else kv_s.ap()
                    S.dma("sp", dst, KV[:n], r=[KV], key=KV, final=True)
                    if j == NT:
                        for s_ in range(4):
                            S.dma("sp", win_s.ap()[s_, 508:512], KV[4 * s_:4 * s_ + 4, 2], r=[KV], key=KV, final=True)
                    def tr2(e, n=n):
                        ins = None
                        for k in range(8):
                            ins = e.transpose(pq[:, k * 128:k * 128 + n], qkb[:n, k, :], identb[:n, :n])
                        return ins
                    S.pe(tr2, r=[qkb, identb], w=[pq])
                    pq3 = pq[:, :].rearrange("p (k t) -> p k t", k=8)
                    S.act(lambda e, n=n, c0=c0, pq3=pq3: e.copy(out=QT[:, :, c0:c0 + n], in_=pq3[:, 0:4, :n]), r=[pq], w=[QT])
                    S.act(lambda e, n=n, c0=c0, pq3=pq3: e.copy(out=KT[:, :, c0:c0 + n], in_=pq3[:, 4:7, :n]), r=[pq], w=[KT])
                    S.act(lambda e, n=n, c0=c0, pq3=pq3: e.copy(out=VcT[:, c0:c0 + n], in_=pq3[:, 7, :n]), r=[pq], w=[VcT])
                stageA1(0)
                for j in range(NT + 1):
                    if j + 1 <= NT:
                        stageA1(j + 1)
                    stageA2(j)
                S.act(lambda e: e.copy(out=QTs[:, :, :], in_=QT[:, :, SEQ:SEQ + NS]), r=[QT], w=[QTs])
                S.act(lambda e: e.copy(out=KTs[:, :, :], in_=KT[:, :, SEQ:SEQ + NS]), r=[KT], w=[KTs])
                S.act(lambda e: e.copy(out=VAs[:, :, :], in_=VA[:NS, NT, :, :]), r=[VA], w=[VAs])
                S.barrier()

        phase_a()

        S.act(lambda e: e.activation(out=GL[:, :, :], in_=GL[:, :, :], func=AF.Sigmoid), r=[GL], w=[GL])

        def phase_c():
            with contextlib.ExitStack() as pc:
                w1sb = sbuf(pc, "w1sb", [128, 2, 32, 128], BF16)
                w2sb = sbuf(pc, "w2sb", [128, 2, 128], BF16)
                posb = sbuf(pc, "posb", [128, 64], BF16)
                b1sb = sbuf(pc, "b1sb", [128, 2], F32)
                biasb = sbuf(pc, "biasb", [128, 2], F32)
                I0 = sbuf(pc, "I0s", [128, 128], F32)
                Jm = sbuf(pc, "Jms", [128, 64], F32)
                trib = sbuf(pc, "trib", [128, 256], BF16)
                Ebig = sbuf(pc, "Ebigs", [64, SEQ], BF16)
                KcT = sbuf(pc, "KcT", [128, 256], BF16)
                VcA = sbuf(pc, "VcA", [128, 2, 196], BF16)
                hx = sbuf(pc, "hx", [128, 256], F32)
                ht = sbuf(pc, "ht", [128, 256], F32)
                hT = [sbuf(pc, "hT%d" % i, [128, 256], BF16) for i in range(2)]
                Eb = [sbuf(pc, "Eb%d" % i, [128, 4, 128], BF16) for i in range(3)]
                mk = sbuf(pc, "mk", [128, 128], BF16)
                rs = sbuf(pc, "rs", [128, 8], F32)
                ocats = [sbuf(pc, "ocat%d" % i, [128, 4, 128], F32) for i in range(2)]
                ocb = [sbuf(pc, "ocb%d" % i, [128, 512], BF16) for i in range(2)]
                imp = sbuf(pc, "imp", [128, 64], F32)
                vis = sbuf(pc, "vis", [128, 64], F32)
                frc = sbuf(pc, "frc", [128, 64], F32)
                col0 = sbuf(pc, "col0", [128, 64], F32)
                sc = [sbuf(pc, "sc%d" % i, [128, 64], F32) for i in range(2)]
                m8 = sbuf(pc, "m8", [128, 16], F32)
                selm = sbuf(pc, "selm", [128, 64], F32)
                nm = sbuf(pc, "nm", [128, 64], BF16)
                nmTs = [sbuf(pc, "nmT%d" % i, [64, 4, 128], BF16) for i in range(2)]
                pS = [psum(pc, "pS%d" % i, [128, 512], F32) for i in range(3)]
                pO = [psum(pc, "pO%d" % i, [128, 512], F32) for i in range(4)]
                pM = psum(pc, "pM", [128, 1024], BF16)

                S.dma("pool", w1sb[:], cw1.ap().rearrange("e s d h -> d e s h"), w=[w1sb], key=w1sb)
                S.dma("pool", w2sb[:], cw2.ap().rearrange("e h d -> h e d"), w=[w2sb], key=w2sb)
                S.dma("pool", posb[:], posT.ap(), w=[posb], key=posb)
                S.dma("sp", b1sb[:], b1T.ap(), w=[b1sb], key=b1sb)
                S.dma("sp", I0[:], I0_d.ap(), w=[I0], key=I0)
                S.dma("sp", Jm[:], Jm_d.ap(), w=[Jm], key=Jm)
                S.dma("pool", trib[:], tri_d.ap(), w=[trib], key=trib)
                S.dma("pool", Ebig[:], Ebig_d.ap(), w=[Ebig], key=Ebig)
                S.dve(lambda e: e.memset(KcT[:], 0.0), w=[KcT])
                S.dve(lambda e: e.memset(VcA[:], 0.0), w=[VcA])
                S.dve(lambda e: e.memset(VcA[:, :, 128:129], 1.0), w=[VcA])
                S.dve(lambda e: e.memset(col0[:], 0.0), w=[col0])
                S.dve(lambda e: e.memset(col0[:, 0:1], 1.0), w=[col0])
                S.dma("pool", VcA[:, :, 129:193], cover_d.ap().rearrange("(t p) j -> p t j", p=128), w=[VcA], key=VcA)

                def bias_mm(e):
                    ins = None
                    for ee in range(2):
                        for s_ in range(32):
                            ins = e.matmul(pS[0][:, ee:ee + 1], w1sb[:, ee, s_, :], posb[:, ee * 32 + s_:ee * 32 + s_ + 1],
                                           start=(ee == 0 and s_ == 0), stop=(s_ == 31), skip_group_check=True)
                    return ins
                S.pe(bias_mm, r=[w1sb, posb], w=[pS[0]])
                S.dve(lambda e: e.tensor_tensor(out=biasb[:], in0=pS[0][:, 0:2], in1=b1sb[:], op=ALU.add), r=[pS[0], b1sb], w=[biasb])

                def compress(srcT, nblk, ncols_pad, KcT_out, Vc_out_fn):
                    for ee in range(2):
                        for c0 in range(0, nblk, 512):
                            cn = min(512, nblk - c0)
                            P = pS[(c0 // 512) % 2]

                            def hmm(e, ee=ee, c0=c0, cn=cn, P=P):
                                ins = None
                                src = srcT(ee)
                                for rs_ in range(32):
                                    lo = rs_ + 16 * c0
                                    ins = e.matmul(P[:, :cn], w1sb[:, ee, rs_, :], src[:, lo:lo + 16 * (cn - 1) + 1:16],
                                                   start=(rs_ == 0), stop=(rs_ == 31))
                                return ins
                            S.pe(hmm, r=[w1sb, KT, VcT], w=[P])
                            S.act(lambda e, ee=ee, cn=cn, P=P: e.activation(out=hx[:, :cn], in_=P[:, :cn], func=AF.Identity,
                                                                             bias=biasb[:, ee:ee + 1]), r=[P, biasb], w=[hx])
                            S.dve(lambda e, cn=cn: e.tensor_tensor(out=ht[:, :cn], in0=hx[:, :cn], in1=hx[:, :cn], op=ALU.mult), r=[hx], w=[ht])
                            S.dve(lambda e, cn=cn: e.tensor_scalar(out=ht[:, :cn], in0=ht[:, :cn], scalar1=0.044715, scalar2=1.0,
                                                                   op0=ALU.mult, op1=ALU.add), r=[ht], w=[ht])
                            S.dve(lambda e, cn=cn: e.tensor_tensor(out=ht[:, :cn], in0=ht[:, :cn], in1=hx[:, :cn], op=ALU.mult), r=[ht, hx], w=[ht])
                            S.act(lambda e, cn=cn: e.activation(out=ht[:, :cn], in_=ht[:, :cn], func=AF.Sigmoid, scale=1.5957691216),
                                  r=[ht], w=[ht])
                            H = hT[ee]
                            S.dve(lambda e, cn=cn, H=H: e.tensor_tensor(out=H[:, :cn], in0=hx[:, :cn], in1=ht[:, :cn], op=ALU.mult),
                                  r=[hx, ht], w=[H])
                            if ee == 0:
                                S.pe(lambda e, cn=cn, H=H: e.matmul(pO[0][:, :cn], w2sb[:, 0, :], H[:, :cn], start=True, stop=True),
                                     r=[w2sb, H], w=[pO[0]])
                                S.act(lambda e, cn=cn, c0=c0: e.copy(out=KcT_out[:, c0:c0 + cn], in_=pO[0][:, :cn]), r=[pO[0]], w=[KcT])
                            else:
                                for t0 in range(0, cn, 128):
                                    nb = min(128, cn - t0)
                                    S.pe(lambda e, nb=nb, t0=t0, H=H: e.matmul(pO[1][:nb, 0:128], H[:, t0:t0 + nb], w2sb[:, 1, :],
                                                                               start=True, stop=True), r=[w2sb, H], w=[pO[1]])
                                    S.act(lambda e, nb=nb, ct=(c0 + t0) // 128: e.copy(out=Vc_out_fn(ct, nb), in_=pO[1][:nb, 0:128]),
                                          r=[pO[1]], w=[VcA])

                compress(lambda ee: (KT[:, 0, :] if ee == 0 else VcT[:, :]), 255, 256, KcT, lambda ct, nb: VcA[:nb, ct, 0:128])

                HB = [(0, 0), (0, 256), (1, 0), (1, 256)]

                def pv_group(Pb, E, rhs_fn, ncol, first, r):
                    def f(e):
                        ins = None
                        for h in range(4):
                            bk, co = HB[h]
                            ins = e.matmul(Pb[bk][:, co:co + ncol], E[:, h, :], rhs_fn(), start=(first and h % 2 == 0), stop=False,
                                           skip_group_check=True)
                        return ins
                    S.pe(f, r=[E] + r, w=[Pb[0], Pb[1]])

                def finish_branch(Pb, i, br, first_branch, ocat):
                    for h in range(4):
                        bk, co = HB[h]
                        S.dve(lambda e, h=h, bk=bk, co=co: e.tensor_copy(out=rs[:, h:h + 1], in_=Pb[bk][:, co + 128:co + 129]),
                              r=[Pb[bk]], w=[rs])
                    S.dve(lambda e: e.tensor_scalar(out=rs[:, 0:4], in0=rs[:, 0:4], scalar1=1e-30, scalar2=None, op0=ALU.max), r=[rs], w=[rs])
                    S.dve(lambda e: e.reciprocal(out=rs[:, 0:4], in_=rs[:, 0:4]), r=[rs], w=[rs])
                    S.dve(lambda e, i=i, br=br: e.tensor_tensor(out=rs[:, 4:8], in0=rs[:, 0:4], in1=GL[:, i, 4 * br:4 * br + 4], op=ALU.mult),
                          r=[rs, GL], w=[rs])
                    for h in range(4):
                        bk, co = HB[h]
                        if first_branch:
                            S.dve(lambda e, h=h, bk=bk, co=co: e.tensor_scalar(out=ocat[:, h, :], in0=Pb[bk][:, co:co + 128],
                                                                               scalar1=rs[:, 4 + h:5 + h], scalar2=None, op0=ALU.mult),
                                  r=[Pb[bk], rs], w=[ocat])
                        else:
                            S.dve(lambda e, h=h, bk=bk, co=co: e.scalar_tensor_tensor(out=ocat[:, h, :], in0=Pb[bk][:, co:co + 128],
                                                                                      scalar=rs[:, 4 + h:5 + h], in1=ocat[:, h, :],
                                                                                      op0=ALU.mult, op1=ALU.add),
                                  r=[Pb[bk], rs, ocat], w=[ocat])

                pair_ctr = [0]

                def qk_exp(i, lhsT_fn, lr, with_mask, nmT=None):
                    nmT = nmT if nmT is not None else nmTs[0]
                    x = pair_ctr[0] % 3
                    pair_ctr[0] += 1
                    P, E = pS[x], Eb[x]

                    def f(e):
                        ins = e.matmul(P[:, :], lhsT_fn(), QT[:, :, i * 128:i * 128 + 128], start=True, stop=not with_mask)
                        if with_mask is not False:
                            ins = e.matmul(P[:, :], Ebig[:, with_mask * 128:with_mask * 128 + 128], nmT[:, :, :], start=False, stop=True)
                        return ins
                    S.pe(f, r=[QT, Ebig, nmT] + lr, w=[P])
                    S.act(lambda e: e.activation(out=E[:, :, :], in_=P[:, :].rearrange("p (h q) -> p h q", h=4), func=AF.Exp, scale=SCALE),
                          r=[P], w=[E])
                    return E

                def mul_mask(E, m_ap, r):
                    S.dve(lambda e: e.tensor_tensor(out=E[:, :, :], in0=E[:, :, :], in1=m_ap.unsqueeze(1).broadcast_to([128, 4, 128]),
                                                    op=ALU.mult), r=[E] + r, w=[E])

                def stage1(i):
                    ocat = ocats[i % 2]
                    nmT = nmTs[i % 2]
                    Pb = pO[0:2]
                    n_ct = 1 if i < 16 else 2
                    for ct in range(n_ct):
                        E = qk_exp(i, lambda ct=ct: KcT[:, ct * 128:ct * 128 + 128], [KcT], False)
                        cval = float(128 * i - 2048 * ct - 31)
                        S.dve(lambda e, cval=cval: e.tensor_scalar(out=mk[:, :], in0=I0[:, :], scalar1=cval, scalar2=0.0,
                                                                   op0=ALU.add, op1=ALU.is_ge), r=[I0], w=[mk])
                        mul_mask(E, mk[:, :], [mk])
                        pv_group(Pb, E, lambda ct=ct: VcA[:, ct, 0:193], 193, ct == 0, [VcA])
                    for h in range(4):
                        bk, co = HB[h]
                        S.dve(lambda e, h=h, bk=bk, co=co: e.tensor_copy(out=rs[:, h:h + 1], in_=Pb[bk][:, co + 128:co + 129]),
                              r=[Pb[bk]], w=[rs])
                    S.dve(lambda e: e.tensor_scalar(out=rs[:, 0:4], in0=rs[:, 0:4], scalar1=1e-30, scalar2=None, op0=ALU.max), r=[rs], w=[rs])
                    S.dve(lambda e: e.reciprocal(out=rs[:, 0:4], in_=rs[:, 0:4]), r=[rs], w=[rs])
                    for h in range(4):
                        bk, co = HB[h]
                        if h == 0:
                            S.dve(lambda e, bk=bk, co=co: e.tensor_scalar(out=imp[:, :], in0=Pb[bk][:, co + 129:co + 193], scalar1=rs[:, 0:1],
                                                                          scalar2=None, op0=ALU.mult), r=[Pb[bk], rs], w=[imp])
                        else:
                            S.dve(lambda e, h=h, bk=bk, co=co: e.scalar_tensor_tensor(out=imp[:, :], in0=Pb[bk][:, co + 129:co + 193],
                                                                                      scalar=rs[:, h:h + 1], in1=imp[:, :],
                                                                                      op0=ALU.mult, op1=ALU.add),
                                  r=[Pb[bk], rs, imp], w=[imp])
                    finish_branch(Pb, i, 0, True, ocat)
                    if i == 0:
                        dbg("ocat_cmp", ocat, ocat[:, :, :], [128, 4, 128])
                        dbg("rs_cmp", rs, rs[:, :], [128, 8])
                        dbg("imp", imp, imp[:, :], [128, 64])
                        dbg("GL", GL, GL[:, :, :], [128, NT + 1, 12])
                    S.dve(lambda e, i=i: e.tensor_scalar(out=vis[:, :], in0=Jm[:, :], scalar1=float(2 * i), scalar2=None, op0=ALU.is_le),
                          r=[Jm], w=[vis])
                    if i >= 8:
                        S.dve(lambda e, i=i: e.tensor_scalar(out=frc[:, :], in0=Jm[:, :], scalar1=float(2 * i - 1), scalar2=None, op0=ALU.is_ge),
                              r=[Jm], w=[frc])
                        S.dve(lambda e: e.tensor_tensor(out=frc[:, :], in0=frc[:, :], in1=vis[:, :], op=ALU.mult), r=[frc, vis], w=[frc])
                        S.dve(lambda e: e.tensor_tensor(out=frc[:, :], in0=frc[:, :], in1=col0[:, :], op=ALU.max), r=[frc, col0], w=[frc])
                        S.dve(lambda e: e.tensor_tensor(out=sc[0][:, :], in0=imp[:, :], in1=vis[:, :], op=ALU.mult), r=[imp, vis], w=[sc[0]])
                        S.dve(lambda e: e.tensor_scalar(out=sc[1][:, :], in0=vis[:, :], scalar1=-1.0, scalar2=1e9, op0=ALU.add, op1=ALU.mult),
                              r=[vis], w=[sc[1]])
                        S.dve(lambda e: e.tensor_tensor(out=sc[0][:, :], in0=sc[0][:, :], in1=sc[1][:, :], op=ALU.add), r=[sc[0], sc[1]], w=[sc[0]])
                        S.dve(lambda e: e.scalar_tensor_tensor(out=sc[0][:, :], in0=frc[:, :], scalar=2e9, in1=sc[0][:, :],
                                                               op0=ALU.mult, op1=ALU.add), r=[frc, sc[0]], w=[sc[0]])
                        S.dve(lambda e: e.max(out=m8[:, 0:8], in_=sc[0][:, :]), r=[sc[0]], w=[m8])
                        S.dve(lambda e: e.match_replace(out=sc[1][:, :], in_to_replace=m8[:, 0:8], in_values=sc[0][:, :], imm_value=-3e38),
                              r=[sc[0], m8], w=[sc[1]])
                        S.dve(lambda e: e.max(out=m8[:, 8:16], in_=sc[1][:, :]), r=[sc[1]], w=[m8])
                        S.dve(lambda e: e.tensor_scalar(out=selm[:, :], in0=sc[0][:, :], scalar1=m8[:, 15:16], scalar2=None, op0=ALU.is_ge),
                              r=[sc[0], m8], w=[selm])
                        S.dve(lambda e: e.tensor_tensor(out=selm[:, :], in0=selm[:, :], in1=vis[:, :], op=ALU.mult), r=[selm, vis], w=[selm])
                        SM = selm
                    else:
                        SM = vis
                    S.dve(lambda e, SM=SM: e.tensor_scalar(out=nm[:, :], in0=SM[:, :], scalar1=-NEG, scalar2=NEG, op0=ALU.mult, op1=ALU.add),
                          r=[SM], w=[nm])

                def stage1b(i):
                    nmT = nmTs[i % 2]
                    S.pe(lambda e: e.transpose(pM[:64, 0:128], nm[:, :], identb[:, :]), r=[nm, identb], w=[pM])
                    S.act(lambda e: e.copy(out=nmT[:, :, :], in_=pM[:64, 0:128].unsqueeze(1).broadcast_to([64, 4, 128])), r=[pM], w=[nmT])

                def stage2(i):
                    ocat = ocats[i % 2]
                    nmT = nmTs[i % 2]
                    pairs = []
                    PbS = pO[2:4]
                    for kt in range(i + 1):
                        pairs.append((lambda kt=kt: KT[:, 1, kt * 128:kt * 128 + 128], kt, (trib[:, 0:128] if kt == i else None),
                                      PbS, lambda kt=kt: VA[:, kt, 0, 0:129], kt == 0, 1 if kt == i else None))
                    PbW = pO[0:2]
                    kts = [kt for kt in range(i - 4, i + 1) if kt >= 0]
                    for kt in kts:
                        m_ = trib[:, 0:128] if kt == i else (trib[:, 128:256] if kt == i - 4 else None)
                        pairs.append((lambda kt=kt: KT[:, 2, kt * 128:kt * 128 + 128], False, m_,
                                      PbW, lambda kt=kt: VA[:, kt, 1, 0:129], kt == kts[0], 2 if kt == kts[-1] else None))
                    En = qk_exp(i, pairs[0][0], [KT], pairs[0][1], nmT)
                    for k_, (lf, wm, m_, Pb_, rf, first_, fin_) in enumerate(pairs):
                        E = En
                        if k_ + 1 < len(pairs):
                            En = qk_exp(i, pairs[k_ + 1][0], [KT], pairs[k_ + 1][1], nmT)
                        if m_ is not None:
                            mul_mask(E, m_, [trib])
                        pv_group(Pb_, E, rf, 129, first_, [VA])
                        if fin_ is not None:
                            finish_branch(Pb_, i, fin_, False, ocat)
                    if i == 0:
                        pass
                    OB = ocb[i % 2]
                    S.act(lambda e, OB=OB: e.copy(out=OB[:, :], in_=ocat[:, :, :].rearrange("p h d -> p (h d)")), r=[ocat], w=[OB])
                    S.dma("sp", o_loc.ap()[i * 128:i * 128 + 128, :], OB[:, :], r=[OB], w=[o_locT[i // 8]], key=OB)
                if NTQ > 0:
                    stage1(0)
                    stage1b(0)
                for i in range(NTQ):
                    if i + 1 < NTQ:
                        stage1(i + 1)
                    stage2(i)
                    if i + 1 < NTQ:
                        stage1b(i + 1)
                S.barrier()
        phase_c()

        pp.close()

        S.dma("sp", win_s.ap()[:, 0:508], wstate.ap()[:, 4:512], key=Buf("wcopy"), final=True)
        def phase_s():
            with contextlib.ExitStack() as ps_:
                NP = PAST // 128
                NB = 257
                w1sb = sbuf(ps_, "w1sb_s", [128, 2, 32, 128], BF16)
                w2sb = sbuf(ps_, "w2sb_s", [128, 2, 128], BF16)
                posb = sbuf(ps_, "posb_s", [128, 64], BF16)
                b1sb = sbuf(ps_, "b1sb_s", [128, 2], F32)
                biasb = sbuf(ps_, "biasb_s", [128, 2], F32)
                Ebig = sbuf(ps_, "Ebig_s", [64, SEQ], BF16)
                smask = sbuf(ps_, "smask_s", [16, 5, 16], BF16)
                sel16 = sbuf(ps_, "sel16_s", [16, 68], F32)
                sel16b = sbuf(ps_, "sel16b_s", [16, 4], BF16)
                hmask = sbuf(ps_, "hmask_s", [16, 12], F32)
                big = sbuf(ps_, "bigS", [128, 2, 16512], BF16)
                XcT = big
                KsT = big
                Vs = big
                Vs3 = big[:, 1, :].rearrange("p (g c) -> p g c", c=129)
                E2 = sbuf(ps_, "E2s", [64, 128], BF16)
                Rrep = sbuf(ps_, "Rrep_s", [32, 128], F32)
                KwT = sbuf(ps_, "KwT", [128, 512], BF16)
                Vw = sbuf(ps_, "Vw", [128, 4, 130], BF16)
                stg = [sbuf(ps_, "stg%d" % i, [128, 8192], F32) for i in range(2)]
                wst = sbuf(ps_, "wst", [128, 4, 2, 128], F32)
                KcT = sbuf(ps_, "KcT_s", [128, 1024], BF16)
                Vc = sbuf(ps_, "Vc_s", [128, 8, 392], BF16)
                hx = sbuf(ps_, "hx_s", [128, 512], F32)
                ht = sbuf(ps_, "ht_s", [128, 512], F32)
                hT = [sbuf(ps_, "hT_s%d" % i, [128, 512], BF16) for i in range(2)]
                Es = [sbuf(ps_, "Es%d" % i, [128, 16], BF16) for i in range(2)]
                rs = sbuf(ps_, "rs_s", [16, 8], F32)
                U = sbuf(ps_, "U_s", [16, 260], BF16)
                gt = sbuf(ps_, "gt_s", [16, 12], F32)
                gs = sbuf(ps_, "gs_s", [16, 4], F32)
                ocat = sbuf(ps_, "ocat_s", [16, 128], F32)
                ocb = sbuf(ps_, "ocb_s", [16, 128], BF16)
                imp = sbuf(ps_, "imp_s", [4, 260], F32)
                sc = [sbuf(ps_, "sc_s%d" % i, [4, 260], F32) for i in range(2)]
                m8 = sbuf(ps_, "m8_s", [4, 16], F32)
                nm = sbuf(ps_, "nm_s", [4, 320], BF16)
                nmT = sbuf(ps_, "nmT_s", [64, 5, 4, 4], BF16)
                pS = [psum(ps_, "qS%d" % i, [128, 512], F32) for i in range(2)]
                pO = [psum(ps_, "qO%d" % i, [128, 512], F32) for i in range(3)]
                pTf = [psum(ps_, "qT%d" % i, [128, 512], F32) for i in range(2)]
                pM = psum(ps_, "qM", [128, 1024], BF16)

                S.dma("pool", w1sb[:], cw1.ap().rearrange("e s d h -> d e s h"), w=[w1sb], key=w1sb)
                S.dma("pool", w2sb[:], cw2.ap().rearrange("e h d -> h e d"), w=[w2sb], key=w2sb)
                S.dma("pool", posb[:], posT.ap(), w=[posb], key=posb)
                S.dma("sp", b1sb[:], b1T.ap(), w=[b1sb], key=b1sb)
                S.dma("pool", Ebig[:], Ebig_d.ap(), w=[Ebig], key=Ebig)
                S.dma("pool", smask[:], smask_d.ap(), w=[smask], key=smask)
                S.dma("sp", sel16[:], sel16_d.ap(), w=[sel16], key=sel16)
                S.dma("pool", sel16b[:], sel16_d.ap()[:, 0:4], w=[sel16b], key=sel16b)
                S.dma("sp", hmask[:], hmask_d.ap(), w=[hmask], key=hmask)
                S.dve(lambda e: e.memset(Vw[:, :, 128:129], 1.0), w=[Vw])
                S.dve(lambda e: e.memset(KcT[:], 0.0), w=[KcT])
                S.dve(lambda e: e.memset(Vc[:], 0.0), w=[Vc])
                S.dve(lambda e: e.memset(Vc[:, 0:7, 128:129], 1.0), w=[Vc])
                S.dve(lambda e: e.memset(Vc[:127, 7, 128:129], 1.0), w=[Vc])
                S.dma("pool", Vc[:, :, 129:386], cover_s.ap().rearrange("(t p) j -> p t j", p=128), w=[Vc], key=Vc)

                def bias_mm(e):
                    ins = None
                    for ee in range(2):
                        for s_ in range(32):
                            ins = e.matmul(pS[0][:, ee:ee + 1], w1sb[:, ee, s_, :], posb[:, ee * 32 + s_:ee * 32 + s_ + 1],
                                           start=(ee == 0 and s_ == 0), stop=(s_ == 31), skip_group_check=True)
                    return ins
                S.pe(bias_mm, r=[w1sb, posb], w=[pS[0]])
                S.dve(lambda e: e.tensor_tensor(out=biasb[:], in0=pS[0][:, 0:2], in1=b1sb[:], op=ALU.add), r=[pS[0], b1sb], w=[biasb])

                pti = sbuf(ps_, "pti", [32, 16], I32)
                ptf = sbuf(ps_, "ptf", [32, 16], F32)
                idf = sbuf(ps_, "idf", [128, 16], F32)
                idi = sbuf(ps_, "idi", [128, 16], I32)
                pix = sbuf(ps_, "pix", [128, 1], F32)
                S.dma("sp", pti[:], ptabT.ap(), w=[pti], key=pti)
                S.dma("sp", pix[:], pidx_d.ap(), w=[pix], key=pix)
                S.dma("sp", Rrep[:], Rrep_d.ap(), w=[Rrep], key=Rrep)
                S.dma("pool", E2[:], E2_d.ap(), w=[E2], key=E2)
                S.dve(lambda e: e.tensor_copy(out=ptf[:], in_=pti[:]), r=[pti], w=[ptf])
                S.pe(lambda e: e.matmul(pS[1][:, 0:16], Rrep[:, :], ptf[:, :], start=True, stop=True), r=[Rrep, ptf], w=[pS[1]])
                S.dve(lambda e: e.tensor_scalar(out=idf[:], in0=pS[1][:, 0:16], scalar1=4.0, scalar2=pix[:, 0:1], op0=ALU.mult, op1=ALU.add),
                      r=[pS[1], pix], w=[idf])
                S.dve(lambda e: e.tensor_copy(out=idi[:], in_=idf[:]), r=[idf], w=[idi])
                gctr = [0]

                def qgather(cache, s_, q):
                    G = stg[gctr[0] % 2]
                    gctr[0] += 1
                    rows = cache.ap().rearrange("g (u t) e d -> (g u) (t e d)", u=4)
                    k = s_ * 4 + q
                    S.custom_dma("pool", lambda e: e.indirect_dma_start(
                        out=G[:, :], out_offset=None, in_=rows,
                        in_offset=bass.IndirectOffsetOnAxis(ap=idi[:, k:k + 1], axis=0)), r=[idi], w=[G], key=G)
                    return G

                for s_ in range(4):
                    ev = 0
                    for q in range(4):
                        G = qgather(ccache, s_, q)
                        G3 = G[:, :].rearrange("p (t e d) -> p t e d", t=32, e=2)
                        for ee in range(2):
                            for t0 in range(0, 32, 4):
                                P = pTf[ev % 2]

                                def trp(e, G3=G3, ee=ee, t0=t0, P=P):
                                    ins = None
                                    for j in range(4):
                                        ins = e.transpose(P[:, j * 128:j * 128 + 128], G3[:, t0 + j, ee, :], identf[:, :])
                                    return ins
                                S.pe(trp, r=[G, identf], w=[P])
                                dst = XcT[:, ee, 4096 * q:4096 * q + 4096].rearrange("d (p t) -> d t p", t=32)[:, t0:t0 + 4, :]
                                src = P[:, 0:512].rearrange("d (j p) -> d j p", j=4)
                                if ev % 2 == 0:
                                    S.act(lambda e, dst=dst, src=src: e.copy(out=dst, in_=src), r=[P], w=[XcT])
                                else:
                                    S.dve(lambda e, dst=dst, src=src: e.tensor_copy(out=dst, in_=src), r=[P], w=[XcT])
                                ev += 1
                    for ee in range(2):
                        for c0 in (0, 512):
                            cn = 512 if c0 == 0 else 511
                            P = pS[(c0 // 512) % 2]

                            def hmm(e, ee=ee, c0=c0, cn=cn, P=P):
                                ins = None
                                for rs_ in range(32):
                                    lo = rs_ + 16 * c0
                                    ins = e.matmul(P[:, :cn], w1sb[:, ee, rs_, :], XcT[:, ee, lo:lo + 16 * (cn - 1) + 1:16],
                                                   start=(rs_ == 0), stop=(rs_ == 31))
                                return ins
                            S.pe(hmm, r=[w1sb, XcT], w=[P])
                            S.act(lambda e, ee=ee, cn=cn, P=P: e.activation(out=hx[:, :cn], in_=P[:, :cn], func=AF.Identity,
                                                                             bias=biasb[:, ee:ee + 1]), r=[P, biasb], w=[hx])
                            S.dve(lambda e, cn=cn: e.tensor_tensor(out=ht[:, :cn], in0=hx[:, :cn], in1=hx[:, :cn], op=ALU.mult), r=[hx], w=[ht])
                            S.dve(lambda e, cn=cn: e.tensor_scalar(out=ht[:, :cn], in0=ht[:, :cn], scalar1=0.044715, scalar2=1.0,
                                                                   op0=ALU.mult, op1=ALU.add), r=[ht], w=[ht])
                            S.dve(lambda e, cn=cn: e.tensor_tensor(out=ht[:, :cn], in0=ht[:, :cn], in1=hx[:, :cn], op=ALU.mult), r=[ht, hx], w=[ht])
                            S.act(lambda e, cn=cn: e.activation(out=ht[:, :cn], in_=ht[:, :cn], func=AF.Sigmoid, scale=1.5957691216),
                                  r=[ht], w=[ht])
                            H = hT[ee]
                            S.dve(lambda e, cn=cn, H=H: e.tensor_tensor(out=H[:, :cn], in0=hx[:, :cn], in1=ht[:, :cn], op=ALU.mult),
                                  r=[hx, ht], w=[H])
                            if ee == 0:
                                S.pe(lambda e, cn=cn, H=H: e.matmul(pO[0][:, :cn], w2sb[:, 0, :], H[:, :cn], start=True, stop=True),
                                     r=[w2sb, H], w=[pO[0]])
                                S.act(lambda e, cn=cn, c0=c0: e.copy(out=KcT[:, c0:c0 + cn], in_=pO[0][:, :cn]), r=[pO[0]], w=[KcT])
                            else:
                                for t0 in range(0, cn, 128):
                                    nb = min(128, cn - t0)
                                    S.pe(lambda e, nb=nb, t0=t0, H=H: e.matmul(pO[1][:nb, 0:128], H[:, t0:t0 + nb], w2sb[:, 1, :],
                                                                               start=True, stop=True), r=[w2sb, H], w=[pO[1]])
                                    S.act(lambda e, nb=nb, ct=(c0 + t0) // 128: e.copy(out=Vc[:nb, ct, 0:128], in_=pO[1][:nb, 0:128]),
                                          r=[pO[1]], w=[Vc])

                    pc_ = [0]
                    Qs = QTs[:, :, 4 * s_:4 * s_ + 4]

                    def qk16(lhsT, nk, lr, maskchunk=None, Qs=Qs):
                        x = pc_[0] % 2
                        pc_[0] += 1
                        P, E = pS[x], Es[x]

                        def f(e):
                            ins = e.matmul(P[:nk, 0:16], lhsT, Qs, start=True, stop=(maskchunk is None))
                            if maskchunk is not None:
                                ch, kt = maskchunk
                                ins = e.matmul(P[:nk, 0:16], E2[:, :], nmT[:, ch, :, :], start=False, stop=True)
                            return ins
                        S.pe(f, r=[QTs, E2, nmT] + lr, w=[P])
                        S.act(lambda e: e.activation(out=E[:nk, :], in_=P[:nk, 0:16], func=AF.Exp, scale=SCALE), r=[P], w=[E])
                        return E

                    def pv16(Pacc, E, nk, rhs, ncol, first, last, r):
                        S.pe(lambda e: e.matmul(Pacc[:16, 0:ncol], E[:nk, :], rhs, start=first, stop=last), r=[E] + r, w=[Pacc])

                    def fin16(Pacc, br, first_branch):
                        S.dve(lambda e: e.tensor_scalar(out=rs[:, 0:1], in0=Pacc[:16, 128:129], scalar1=1e-30, scalar2=None, op0=ALU.max),
                              r=[Pacc], w=[rs])
                        S.dve(lambda e: e.reciprocal(out=rs[:, 0:1], in_=rs[:, 0:1]), r=[rs], w=[rs])
                        S.dve(lambda e: e.tensor_tensor(out=rs[:, 1:2], in0=rs[:, 0:1], in1=gs[:, br:br + 1], op=ALU.mult), r=[rs, gs], w=[rs])
                        if first_branch:
                            S.dve(lambda e: e.tensor_scalar(out=ocat[:, :], in0=Pacc[:16, 0:128], scalar1=rs[:, 1:2], scalar2=None, op0=ALU.mult),
                                  r=[Pacc, rs], w=[ocat])
                        else:
                            S.dve(lambda e: e.scalar_tensor_tensor(out=ocat[:, :], in0=Pacc[:16, 0:128], scalar=rs[:, 1:2], in1=ocat[:, :],
                                                                   op0=ALU.mult, op1=ALU.add), r=[Pacc, rs, ocat], w=[ocat])

                    S.pe(lambda e, s_=s_: e.matmul(pO[2][:16, 0:12], sel16[:, 4 + 16 * s_:4 + 16 * s_ + 16], GL[:16, NT, :], start=True, stop=True),
                         r=[sel16, GL], w=[pO[2]])
                    S.dve(lambda e: e.tensor_tensor(out=gt[:, :], in0=pO[2][:16, 0:12], in1=hmask[:, :], op=ALU.mult), r=[pO[2], hmask], w=[gt])
                    S.dve(lambda e: e.tensor_reduce(out=gs[:, 0:3], in_=gt[:, :].rearrange("p (b h) -> p b h", b=3), axis=AX.X, op=ALU.add),
                          r=[gt], w=[gs])

                    def run_pairs(plist):
                        En = qk16(*plist[0][0][:3], **plist[0][0][3])
                        for k_, (qa, post, pa) in enumerate(plist):
                            E = En
                            if k_ + 1 < len(plist):
                                nq = plist[k_ + 1][0]
                                En = qk16(*nq[:3], **nq[3])
                            if post is not None:
                                post(E)
                            pv16(pa[0], E, *pa[1:])

                    run_pairs([((KcT[:, ct * 128:ct * 128 + 128], 128, [KcT], {}), None,
                                (pO[0], 128, Vc[:, ct, 0:386], 386, ct == 0, ct == 7, [Vc])) for ct in range(8)])
                    fin16(pO[0], 0, True)
                    S.dve(lambda e: e.tensor_scalar(out=U[:, 0:257], in0=pO[0][:16, 129:386], scalar1=rs[:, 0:1], scalar2=None, op0=ALU.mult),
                          r=[pO[0], rs], w=[U])
                    S.pe(lambda e: e.matmul(pO[2][:4, 0:257], sel16b[:, 0:4], U[:, 0:257], start=True, stop=True), r=[sel16b, U], w=[pO[2]])
                    S.dve(lambda e: e.tensor_copy(out=sc[0][:, 0:257], in_=pO[2][:4, 0:257]), r=[pO[2]], w=[sc[0]])
                    S.dve(lambda e: e.memset(sc[0][:, 0:1], 2e9), w=[sc[0]])
                    S.dve(lambda e: e.memset(sc[0][:, 255:257], 2e9), w=[sc[0]])
                    S.dve(lambda e: e.max(out=m8[:, 0:8], in_=sc[0][:, 0:257]), r=[sc[0]], w=[m8])
                    S.dve(lambda e: e.match_replace(out=sc[1][:, 0:257], in_to_replace=m8[:, 0:8], in_values=sc[0][:, 0:257], imm_value=-3e38),
                          r=[sc[0], m8], w=[sc[1]])
                    S.dve(lambda e: e.max(out=m8[:, 8:16], in_=sc[1][:, 0:257]), r=[sc[1]], w=[m8])
                    S.dve(lambda e: e.memset(nm[:, :], 0.0), w=[nm])
                    S.dve(lambda e: e.tensor_scalar(out=sc[1][:, 0:257], in0=sc[0][:, 0:257], scalar1=m8[:, 15:16], scalar2=None, op0=ALU.is_ge),
                          r=[sc[0], m8], w=[sc[1]])
                    S.dve(lambda e: e.tensor_scalar(out=nm[:, 0:257], in0=sc[1][:, 0:257], scalar1=-NEG, scalar2=NEG, op0=ALU.mult, op1=ALU.add),
                          r=[sc[1]], w=[nm])

                    def trn(e):
                        ins = None
                        for ch in range(5):
                            ins = e.transpose(pM[:64, ch * 4:ch * 4 + 4], nm[:, ch * 64:ch * 64 + 64], identb[:4, :4])
                        return ins
                    S.pe(trn, r=[nm, identb], w=[pM])
                    S.act(lambda e: e.copy(out=nmT[:, :, :, :], in_=pM[:64, 0:20].rearrange("p (c q) -> p c q", c=5).unsqueeze(2)
                                           .broadcast_to([64, 5, 4, 4])), r=[pM], w=[nmT])

                    S.dve(lambda e: e.memset(Vs3[:, :, 128:129], 1.0), w=[Vs])
                    ev = 0
                    for q in range(4):
                        G = qgather(scache, s_, q)
                        G3 = G[:, :].rearrange("p (t e d) -> p t e d", t=32, e=2)
                        for t0 in range(0, 32, 4):
                            P = pTf[ev % 2]
                            ev += 1
                            kt0 = q * 32 + t0

                            def trk(e, G3=G3, t0=t0, P=P):
                                ins = None
                                for j in range(4):
                                    ins = e.transpose(P[:, j * 128:j * 128 + 128], G3[:, t0 + j, 0, :], identf[:, :])
                                return ins
                            S.pe(trk, r=[G, identf], w=[P])
                            S.act(lambda e, P=P, kt0=kt0: e.copy(out=KsT[:, 0, kt0 * 128:kt0 * 128 + 512], in_=P[:, 0:512]), r=[P], w=[KsT])
                            S.dve(lambda e, G3=G3, t0=t0, kt0=kt0: e.tensor_copy(out=Vs3[:, kt0:kt0 + 4, 0:128], in_=G3[:, t0:t0 + 4, 1, :]),
                                  r=[G], w=[Vs])
                    def newmask(E, s_=s_):
                        S.dve(lambda e: e.tensor_tensor(out=E[:16, :], in0=E[:16, :], in1=smask[:, s_, :], op=ALU.mult), r=[E, smask], w=[E])

                    def oldmask(E):
                        S.dve(lambda e: e.tensor_tensor(out=E[:16, :], in0=E[:16, :], in1=smask[:, 4, :], op=ALU.mult), r=[E, smask], w=[E])

                    pl = [((KsT[:, 0, kt * 128:kt * 128 + 128], 128, [KsT], {"maskchunk": (kt // 32, kt)}), None,
                           (pO[1], 128, Vs3[:, kt, 0:129], 129, kt == 0, False, [Vs])) for kt in range(NP)]
                    pl.append(((KTs[:, 1, :], 16, [KTs], {}), newmask, (pO[1], 16, VAs[:, 0, 0:129], 129, False, True, [VAs])))
                    run_pairs(pl)
                    fin16(pO[1], 1, False)
                    S.dma("sp", wst[:, :, :, :], wstate.ap()[s_].rearrange("(t p) e d -> p t e d", p=128), w=[wst], key=wst)
                    for t_ in range(4):
                        P = pTf[t_ % 2]
                        S.pe(lambda e, t_=t_, P=P: e.transpose(P[:, 0:128], wst[:, t_, 0, :], identf[:, :]), r=[wst, identf], w=[P])
                        S.act(lambda e, t_=t_, P=P: e.copy(out=KwT[:, t_ * 128:t_ * 128 + 128], in_=P[:, 0:128]), r=[P], w=[KwT])
                    S.dve(lambda e: e.tensor_copy(out=Vw[:, :, 0:128], in_=wst[:, :, 1, :]), r=[wst], w=[Vw])
                    pl = [((KwT[:, t_ * 128:t_ * 128 + 128], 128, [KwT], {}), (oldmask if t_ == 0 else None),
                           (pO[0], 128, Vw[:, t_, 0:129], 129, t_ == 0, False, [Vw])) for t_ in range(4)]
                    pl.append(((KTs[:, 2, :], 16, [KTs], {}), newmask, (pO[0], 16, VAs[:, 1, 0:129], 129, False, True, [VAs])))
                    run_pairs(pl)
                    fin16(pO[0], 2, False)
                    S.act(lambda e: e.copy(out=ocb[:, :], in_=ocat[:, :]), r=[ocat], w=[ocb])
                    for h in range(4):
                        S.dma("sp", o_locs.ap()[4 * s_:4 * s_ + 4, 128 * h:128 * h + 128], ocb[4 * h:4 * h + 4, :], r=[ocb], w=[o_locT[4]], key=ocb)
                S.barrier()

        if RUN_S:
            phase_s()

        RG = [[0, 1, 2, 3], [4, 5, 6, 7]]
        def allgather(src_ap, dst_ap, rbuf, wbuf_):
            S.custom_dma("pool", lambda e: e.collective_compute("AllGather", ALU.bypass, replica_groups=RG, ins=[src_ap], outs=[dst_ap]),
                         r=[rbuf], w=[wbuf_], key=wbuf_, inc=1)
        for c in range(4):
            allgather(o_loc.ap()[c * 1024:c * 1024 + 1024, :].rearrange("(p a) f -> p (a f)", a=8),
                      og.ap()[c * 4096:c * 4096 + 4096, :].rearrange("(q a) f -> q (a f)", a=8), o_locT[c], ogT)
        allgather(o_locs.ap().rearrange("r (b x) -> (r b) x", b=8), ogs.ap().rearrange("r (b x) -> (r b) x", b=8), o_locT[4], ogT)

        TBS = [(0, 512), (512, 512), (1024, 4)]

        def token_phase(layer):
            with contextlib.ExitStack() as pd:
                hT = sbuf(pd, "hres%d" % layer, [128, 16, NTOK], F32)
                actT = sbuf(pd, "actT%d" % layer, [128, 8, NTOK], BF16)
                xnT = sbuf(pd, "xnT_d%d" % layer, [128, 16, NTOK], BF16)
                wbuf = [sbuf(pd, "wbuf%d_%d" % (i, layer), [128, 16, 512], BF16) for i in range(2)]
                xst = sbuf(pd, "xst%d" % layer, [128, D], F32)
                ost = [sbuf(pd, "ost%d_%d" % (i, layer), [128, 2, 512], BF16) for i in range(2)]
                oidx = sbuf(pd, "oidx_s%d" % layer, [128, 72], I32)
                rstd = sbuf(pd, "rstd_d%d" % layer, [128, 512], F32)
                sq = sbuf(pd, "sq_d%d" % layer, [128, 512], F32)
                rl = [sbuf(pd, "rl%d_%d" % (i, layer), [128, 512], F32) for i in range(2)]
                gT = sbuf(pd, "gT_d%d" % layer, [128, 16], F32)
                onesf = sbuf(pd, "onesf%d" % layer, [128, 128], F32)
                epsb = sbuf(pd, "epsb_d%d" % layer, [128, 1], F32)
                pz = [psum(pd, "dz%d_%d" % (i, layer), [128, 512], F32) for i in range(4)]
                pM = psum(pd, "dM%d" % layer, [128, 1024], BF16)
                pT = psum(pd, "dT%d" % layer, [128, 512], F32)
                pn = psum(pd, "dn%d" % layer, [128, 512], F32)
                ctr = {"w": 0, "z": 0, "o": 0, "r": 0}

                if layer == 0:
                    S.dma("sp", oidx[:, 0:36], oidx_d.ap(), w=[oidx], key=oidx)
                else:
                    S.dma("sp", oidx[:, :], oidx2_d.ap(), w=[oidx], key=oidx)
                S.dve(lambda e: e.memset(onesf[:], 1.0 / D), w=[onesf])
                S.dve(lambda e: e.memset(epsb[:], RMS_EPS), w=[epsb])

                if layer == 0:
                    for t in range(9):
                        n = 128 if t < 8 else 4
                        S.dma("sp", xst[:n, :], x_tok.ap()[t * 128:t * 128 + n, :], w=[xst], key=xst)
                        for q in range(4):
                            def trx(e, q=q, n=n):
                                ins = None
                                for kk in range(4):
                                    k = q * 4 + kk
                                    ins = e.transpose(pT[:, kk * 128:kk * 128 + n], xst[:n, k * 128:k * 128 + 128], identf[:n, :n])
                                return ins
                            S.pe(trx, r=[xst, identf], w=[pT])
                            S.act(lambda e, q=q, n=n, t=t: e.copy(out=hT[:, q * 4:q * 4 + 4, t * 128:t * 128 + n],
                                                                 in_=pT[:, :].rearrange("p (k t) -> p k t", k=4)[:, :, :n]), r=[pT], w=[hT])
                else:
                    S.dma("sp", hT[:, :, :], hspill.ap().rearrange("p (k t) -> p k t", k=16), r=[hspT], w=[hT], key=hT)

                def load_act(gsrc, gsrc_s, gbuf, colfn):
                    for t in range(9):
                        n = 128 if t < 8 else 4
                        O_ = ost[ctr["o"] % 2]
                        ctr["o"] += 1
                        for jj in range(2):
                            S.custom_dma("pool", lambda e, O_=O_, jj=jj, t=t, col=colfn(t, jj): e.indirect_dma_start(
                                out=O_[:, jj, :], out_offset=None, in_=(gsrc if t < 8 else gsrc_s).ap(),
                                in_offset=bass.IndirectOffsetOnAxis(ap=oidx[:, col:col + 1], axis=0)),
                                r=[oidx, gbuf], w=[O_], key=O_)

                        def tro(e, O_=O_, n=n):
                            ins = None
                            for kk in range(8):
                                jj, q = kk // 4, kk % 4
                                ins = e.transpose(pM[:, kk * 128:kk * 128 + n], O_[:n, jj, q * 128:q * 128 + 128], identb[:n, :n])
                            return ins
                        S.pe(tro, r=[O_, identb], w=[pM])
                        S.act(lambda e, n=n, t=t: e.copy(out=actT[:, :, t * 128:t * 128 + n],
                                                         in_=pM[:, :].rearrange("p (k t) -> p k t", k=8)[:, :, :n]), r=[pM], w=[actT])

                def proj_blocks(Wd, row0):
                    blks = []
                    for cb in range(4):
                        def dma(wb, cb=cb):
                            S.dma("pool", wb[:, 0:8, :], Wd.ap()[row0:row0 + 1024, cb * 512:cb * 512 + 512].rearrange("(k p) c -> p k c", p=128),
                                  w=[wb], key=wb)

                        def comp(wb, cb=cb):
                            for cc in range(4):
                                for (t0, nt) in TBS:
                                    P = pz[ctr["z"] % 4]
                                    ctr["z"] += 1

                                    def mm(e, wb=wb, cc=cc, t0=t0, nt=nt, P=P):
                                        ins = None
                                        for k in range(8):
                                            ins = e.matmul(P[:, :nt], wb[:, k, cc * 128:cc * 128 + 128], actT[:, k, t0:t0 + nt],
                                                           start=(k == 0), stop=(k == 7))
                                        return ins
                                    S.pe(mm, r=[wb, actT], w=[P])
                                    c = cb * 4 + cc
                                    S.dve(lambda e, P=P, c=c, t0=t0, nt=nt: e.tensor_tensor(out=hT[:, c, t0:t0 + nt], in0=P[:, :nt],
                                                                                             in1=hT[:, c, t0:t0 + nt], op=ALU.add),
                                          r=[P, hT], w=[hT])
                        blks.append((dma, comp))
                    return blks

                def run_blocks(blks):
                    bufs = []
                    for i, (dma, comp) in enumerate(blks):
                        if i == 0:
                            wb0 = wbuf[ctr["w"] % 2]
                            ctr["w"] += 1
                            dma(wb0)
                            bufs.append(wb0)
                        if i + 1 < len(blks):
                            wbn = wbuf[ctr["w"] % 2]
                            ctr["w"] += 1
                            blks[i + 1][0](wbn)
                            bufs.append(wbn)
                        comp(bufs[i])

                def proj_accum(Wd, row0):
                    run_blocks(proj_blocks(Wd, row0))

                def rmsnorm_T(gain_d, out_fn, wlist):
                    S.dma("sp", gT[:], gain_d.ap(), w=[gT], key=gT)
                    for (t0, nt) in TBS:
                        for k in range(16):
                            S.act(lambda e, k=k, t0=t0, nt=nt: e.activation(out=sq[:, :nt], in_=hT[:, k, t0:t0 + nt], func=AF.Square),
                                  r=[hT], w=[sq])
                            S.pe(lambda e, k=k, nt=nt: e.matmul(pn[:, :nt], onesf[:, :], sq[:, :nt], start=(k == 0), stop=(k == 15)),
                                 r=[onesf, sq], w=[pn])
                        S.act(lambda e, nt=nt: e.activation(out=rstd[:, :nt], in_=pn[:, :nt], func=AF.Sqrt, bias=epsb[:, 0:1]),
                              r=[pn, epsb], w=[rstd])
                        S.dve(lambda e, nt=nt: e.reciprocal(out=rstd[:, :nt], in_=rstd[:, :nt]), r=[rstd], w=[rstd])
                        for k in range(16):
                            S.dve(lambda e, k=k, t0=t0, nt=nt: e.scalar_tensor_tensor(out=out_fn(k, t0, nt), in0=hT[:, k, t0:t0 + nt],
                                                                                      scalar=gT[:, k:k + 1], in1=rstd[:, :nt],
                                                                                      op0=ALU.mult, op1=ALU.mult),
                                  r=[hT, gT, rstd], w=wlist)

                def ffn(W1, W2):
                    blks = []
                    for hg in range(8):
                        for sub in range(2):
                            c0 = hg * 1024 + sub * 512

                            def dma(wb, c0=c0):
                                S.dma("pool", wb[:, :, :], W1.ap()[:, c0:c0 + 512].rearrange("(k p) c -> p k c", p=128), w=[wb], key=wb)

                            def comp(wb, sub=sub):
                                for fc in range(4):
                                    for (t0, nt) in TBS:
                                        P = pz[ctr["z"] % 4]
                                        ctr["z"] += 1

                                        def mm(e, wb=wb, fc=fc, t0=t0, nt=nt, P=P):
                                            ins = None
                                            for k in range(16):
                                                ins = e.matmul(P[:, :nt], wb[:, k, fc * 128:fc * 128 + 128], xnT[:, k, t0:t0 + nt],
                                                               start=(k == 0), stop=(k == 15))
                                            return ins
                                        S.pe(mm, r=[wb, xnT], w=[P])
                                        R_ = rl[ctr["r"] % 2]
                                        ctr["r"] += 1
                                        S.act(lambda e, P=P, R_=R_, nt=nt: e.activation(out=R_[:, :nt], in_=P[:, :nt], func=AF.Relu), r=[P], w=[R_])
                                        eng = S.pool if ctr["r"] % 2 == 0 else S.dve
                                        eng(lambda e, R_=R_, nt=nt, t0=t0, kk=sub * 4 + fc: e.tensor_tensor(out=actT[:, kk, t0:t0 + nt], in0=R_[:, :nt],
                                                                                                          in1=R_[:, :nt], op=ALU.mult),
                                            r=[R_], w=[actT])
                            blks.append((dma, comp))
                        blks.extend(proj_blocks(W2, hg * 1024))
                    run_blocks(blks)

                xn_out = lambda k, t0, nt: xnT[:, k, t0:t0 + nt]
                if layer == 0:
                    for g in range(2):
                        load_act(og, ogs, ogT, lambda t, jj, g=g: t * 4 + 2 * g + jj)
                        proj_accum(w_out0, g * 1024)
                    rmsnorm_T(gffn0T, xn_out, [xnT])
                    ffn(ffn1_0, ffn2_0)
                    rmsnorm_T(gmix1T, xn_out, [xnT])
                    for cp in range(8):
                        S.dma("sp", xg_in.ap()[cp].rearrange("p (k t) -> p k t", k=2), xnT[:, 2 * cp:2 * cp + 2, :], r=[xnT],
                              w=[xg_inT[cp]], key=Buf("xgk%d" % cp))
                        allgather(xg_in.ap()[cp], xg_all.ap()[cp], xg_inT[cp], xgT)
                    S.dma("sp", hspill.ap().rearrange("p (k t) -> p k t", k=16), hT[:, :, :], r=[hT], w=[hspT], key=hT)
                else:
                    for g in range(4):
                        load_act(og2, ogs2, og2T, lambda t, jj, g=g: t * 8 + 2 * g + jj)
                        proj_accum(w_out1, g * 1024)
                    rmsnorm_T(gffn1T, xn_out, [xnT])
                    ffn(ffn1_1, ffn2_1)
                    S.dma("sp", gT[:], gfinT.ap(), w=[gT], key=gT)
                    for (t0, nt) in TBS:
                        for k in range(16):
                            S.act(lambda e, k=k, t0=t0, nt=nt: e.activation(out=sq[:, :nt], in_=hT[:, k, t0:t0 + nt], func=AF.Square),
                                  r=[hT], w=[sq])
                            S.pe(lambda e, k=k, nt=nt: e.matmul(pn[:, :nt], onesf[:, :], sq[:, :nt], start=(k == 0), stop=(k == 15)),
                                 r=[onesf, sq], w=[pn])
                        S.act(lambda e, nt=nt: e.activation(out=rstd[:, :nt], in_=pn[:, :nt], func=AF.Sqrt, bias=epsb[:, 0:1]),
                              r=[pn, epsb], w=[rstd])
                        S.dve(lambda e, nt=nt: e.reciprocal(out=rstd[:, :nt], in_=rstd[:, :nt]), r=[rstd], w=[rstd])
                        for k in range(16):
                            S.dve(lambda e, k=k, t0=t0, nt=nt: e.scalar_tensor_tensor(out=hT[:, k, t0:t0 + nt], in0=hT[:, k, t0:t0 + nt],
                                                                                      scalar=gT[:, k:k + 1], in1=rstd[:, :nt],
                                                                                      op0=ALU.mult, op1=ALU.mult),
                                  r=[hT, gT, rstd], w=[hT])
                    for t in range(9):
                        n = 128 if t < 8 else 4
                        for q in range(4):
                            def trh(e, q=q, n=n, t=t):
                                ins = None
                                for kk in range(4):
                                    ins = e.transpose(pT[:n, kk * 128:kk * 128 + 128], hT[:, q * 4 + kk, t * 128:t * 128 + n], identf[:, :])
                                return ins
                            S.pe(trh, r=[hT, identf], w=[pT])
                            S.act(lambda e, q=q, n=n: e.copy(out=xst[:n, q * 512:q * 512 + 512], in_=pT[:n, :]), r=[pT], w=[xst])
                        S.dma("sp", y_tok.ap()[t * 128:t * 128 + n, :], xst[:n, :], r=[xst], key=xst, final=True)
                S.barrier()

        def phase_e():
            with contextlib.ExitStack() as pe_:
                wr = sbuf(pe_, "wr_s", [128, 16, 3072], BF16)
                xt = [sbuf(pe_, "ext%d" % i, [128, 16, 128], BF16) for i in range(2)]
                cst = [sbuf(pe_, "ecs%d" % i, [128, 256], F32) for i in range(2)]
                St = sbuf(pe_, "St", [128, 4, 512], F32)
                Sb = sbuf(pe_, "Sb", [128, 4, 512], BF16)
                Ss = [sbuf(pe_, "Ss%d" % i, [128, 4, 512], F32) for i in range(2)]
                Ssb = [sbuf(pe_, "Ssb%d" % i, [128, 4, 512], BF16) for i in range(4)]
                qkrs = [sbuf(pe_, "qkr%d" % i, [128, 4, 256], BF16) for i in range(2)]
                Vbs = [sbuf(pe_, "Vb%d" % i, [128, 2, 512], BF16) for i in range(2)]
                sgs = [sbuf(pe_, "sg%d" % i, [128, 1024], F32) for i in range(2)]
                ob = [sbuf(pe_, "ob%d" % i, [128, 512], F32) for i in range(2)]
                jk = sbuf(pe_, "jk", [128, 512], BF16)
                rt = [sbuf(pe_, "ert%d" % i, [128, 4, 128], F32) for i in range(4)]
                qkTs = [sbuf(pe_, "qkT%d" % i, [128, 8, 128], BF16) for i in range(2)]
                QdT = sbuf(pe_, "QdT", [128, 2, 2, 128], BF16)
                QdTm = sbuf(pe_, "QdTm", [128, 4, 2, 2, 16], BF16)
                Kd = sbuf(pe_, "Kd", [128, 2, 256], BF16)
                Kdm = sbuf(pe_, "Kdm", [16, 4, 2, 256], BF16)
                AT = sbuf(pe_, "AT", [128, 2, 128], BF16)
                decT = sbuf(pe_, "decT_s", [128, 2, 128], F32)
                qdecb = sbuf(pe_, "qdecb_s", [128, 2, 128], F32)
                kdec = sbuf(pe_, "kdec_s", [128, 2], F32)
                decTs = sbuf(pe_, "decTs_s", [16, 2, 16], F32)
                kdm = sbuf(pe_, "kdm_s", [16, 4, 2], F32)
                colmask = sbuf(pe_, "colmask_s", [128, 4, 16], F32)
                gnb = sbuf(pe_, "gnb_s", [128, 2, 512], F32)
                gpow = sbuf(pe_, "gpow_s", [128, 4], F32)
                qdecs = sbuf(pe_, "qdecs_s", [128, 2, 16], F32)
                st8 = sbuf(pe_, "st8", [128, 16], F32)
                og_ = [sbuf(pe_, "og_%d" % i, [128, 2, 512], BF16) for i in range(2)]
                pz = [psum(pe_, "ez%d" % i, [128, 512], F32) for i in range(2)]
                pq = psum(pe_, "eq", [128, 1024], BF16)
                pst = psum(pe_, "est", [128, 512], F32)
                po = [psum(pe_, "eo%d" % i, [128, 512], F32) for i in range(2)]
                pu = [psum(pe_, "eu%d" % i, [128, 512], F32) for i in range(2)]

                for kc in range(4):
                    for cq in range(3):
                        S.dma("pool", wr[:, 4 * kc:4 * kc + 4, 1024 * cq:1024 * cq + 1024],
                              wr_d.ap()[512 * kc:512 * kc + 512, 1024 * cq:1024 * cq + 1024].rearrange("(k p) c -> p k c", p=128),
                              w=[wr], key=wr)
                S.dma("sp", decT[:], decT_d.ap().rearrange("p (h i) -> p h i", h=2), w=[decT], key=decT)
                S.dma("sp", qdecb[:], qdecb_d.ap().rearrange("p (h i) -> p h i", h=2), w=[qdecb], key=qdecb)
                S.dma("sp", decTs[:], decTs_d.ap().rearrange("p (h i) -> p h i", h=2), w=[decTs], key=decTs)
                S.dma("sp", kdm[:], kdm_d.ap().rearrange("p (s h) -> p s h", s=4), w=[kdm], key=kdm)
                S.dma("sp", colmask[:], colmask_d.ap().rearrange("p (s i) -> p s i", s=4), w=[colmask], key=colmask)
                S.dma("sp", gnb[:], gnb_d.ap().rearrange("p (h e) -> p h e", h=2), w=[gnb], key=gnb)
                S.dma("sp", qdecs[:], qdecs_d.ap().rearrange("p (h i) -> p h i", h=2), w=[qdecs], key=qdecs)
                S.dma("sp", kdec[:], kdec_d.ap(), w=[kdec], key=kdec)
                S.dma("sp", gpow[:], gpow_d.ap(), w=[gpow], key=gpow)
                S.dve(lambda e: e.memset(St[:], 0.0), w=[St])
                S.dve(lambda e: e.memset(Sb[:], 0.0), w=[Sb])

                xg5 = xg_all.ap().rearrange("c (j p) (k t) -> c j p k t", j=4, k=2)

                def load_tile(T_):
                    X, C = xt[T_ % 2], cst[T_ % 2]
                    if T_ < NT:
                        j, tl = T_ // 8, T_ % 8
                        for cp in range(8):
                            S.dma("sp", X[:, 2 * cp:2 * cp + 2, :], xg5[cp, j, :, :, tl * 128:tl * 128 + 128], r=[xgT], w=[X], key=X)
                        S.dma("sp", C[:, :], cs_r.ap()[T_ * 128:T_ * 128 + 128, :], w=[C], key=C)
                    else:
                        for cp in range(8):
                            for j in range(4):
                                S.dma("sp", X[:, 2 * cp:2 * cp + 2, 4 * j:4 * j + 4], xg5[cp, j, :, :, 1024:1028], r=[xgT], w=[X], key=X)
                        S.dma("sp", C[:NS, :], cs_rs.ap(), w=[C], key=C)

                load_tile(0)
                zc = [0]

                def stageE1(T_):
                    n = 128 if T_ < NT else NS
                    X, C = xt[T_ % 2], cst[T_ % 2]
                    qkr, Vb, sg, qkT = qkrs[T_ % 2], Vbs[T_ % 2], sgs[T_ % 2], qkTs[T_ % 2]
                    if T_ + 1 <= NT:
                        load_tile(T_ + 1)

                    def proj(cb, n=n, X=X):
                        P = pz[zc[0] % 2]
                        zc[0] += 1

                        def mm(e):
                            ins = None
                            for k in range(16):
                                ins = e.matmul(P[:n, :], X[:, k, :n], wr[:, k, cb * 512:cb * 512 + 512], start=(k == 0), stop=(k == 15))
                            return ins
                        S.pe(mm, r=[X, wr], w=[P])
                        return P

                    cosb = lambda n=n, C=C: C[:n, 0:128].unsqueeze(1).broadcast_to([n, 2, 128])
                    sinb = lambda n=n, C=C: C[:n, 128:256].unsqueeze(1).broadcast_to([n, 2, 128])
                    for half_ in range(2):
                        P = proj(half_)
                        z3 = P[:n, :].rearrange("p (h d) -> p h d", h=2)
                        a, b_, c_, d_ = rt

                        def emit_rope(z3=z3, P=P, half_=half_, n=n, cosb=cosb, sinb=sinb, C=C):
                            S.dve(lambda e: e.tensor_tensor(out=a[:n, :2, :], in0=z3[:, :, 0:128], in1=cosb(), op=ALU.mult), r=[P, C], w=[a])
                            S.dve(lambda e: e.tensor_tensor(out=b_[:n, :2, :], in0=z3[:, :, 128:256], in1=sinb(), op=ALU.mult), r=[P, C], w=[b_])
                            S.dve(lambda e: e.tensor_tensor(out=c_[:n, :2, :], in0=z3[:, :, 128:256], in1=cosb(), op=ALU.mult), r=[P, C], w=[c_])
                            S.dve(lambda e: e.tensor_tensor(out=d_[:n, :2, :], in0=z3[:, :, 0:128], in1=sinb(), op=ALU.mult), r=[P, C], w=[d_])
                            S.pool(lambda e: e.tensor_tensor(out=qkr[:n, 2 * half_:2 * half_ + 2, 0:128], in0=a[:n, :2, :], in1=b_[:n, :2, :],
                                                             op=ALU.subtract), r=[a, b_], w=[qkr])
                            S.pool(lambda e: e.tensor_tensor(out=qkr[:n, 2 * half_:2 * half_ + 2, 128:256], in0=c_[:n, :2, :], in1=d_[:n, :2, :],
                                                             op=ALU.add), r=[c_, d_], w=[qkr])
                        emit_rope()
                    for hh in range(2):
                        P = proj(2 + hh)
                        S.act(lambda e, P=P, hh=hh, n=n: e.copy(out=Vb[:n, hh, :], in_=P[:n, :]), r=[P], w=[Vb])
                    for hh in range(2):
                        P = proj(4 + hh)
                        S.act(lambda e, P=P, hh=hh, n=n: e.activation(out=sg[:n, hh * 512:hh * 512 + 512], in_=P[:n, :], func=AF.Silu), r=[P], w=[sg])
                    def trqk(e, n=n):
                        ins = None
                        for a_ in range(4):
                            for c in range(2):
                                ins = e.transpose(pq[:, (a_ * 2 + c) * 128:(a_ * 2 + c) * 128 + n], qkr[:n, a_, c * 128:c * 128 + 128], identb[:n, :n])
                        return ins
                    S.pe(trqk, r=[qkr, identb], w=[pq])
                    S.act(lambda e, n=n: e.copy(out=qkT[:, :, :n], in_=pq[:, :].rearrange("p (a t) -> p a t", a=8)[:, :, :n]), r=[pq], w=[qkT])

                def stageE2(T_):
                    n = 128 if T_ < NT else NS
                    qkr, Vb, sg, qkT = qkrs[T_ % 2], Vbs[T_ % 2], sgs[T_ % 2], qkTs[T_ % 2]
                    qT4 = qkT[:, 0:4, :].rearrange("p (h c) t -> p h c t", h=2)
                    kT4 = qkT[:, 4:8, :].rearrange("p (h c) t -> p h c t", h=2)

                    if T_ < NT:
                        S.dve(lambda e: e.tensor_tensor(out=QdT[:, :, :, :], in0=qT4, in1=qdecb[:, :, :].unsqueeze(2).broadcast_to([128, 2, 2, 128]),
                                                        op=ALU.mult), r=[qkT, qdecb], w=[QdT])
                        S.dve(lambda e: e.tensor_tensor(out=Kd[:, :, :], in0=qkr[:, 2:4, :], in1=kdec[:, :].unsqueeze(2).broadcast_to([128, 2, 256]),
                                                        op=ALU.mult), r=[qkr, kdec], w=[Kd])
                        def smm(e):
                            ins = None
                            for hh in range(2):
                                for c in range(2):
                                    ins = e.matmul(pst[:, hh * 128:hh * 128 + 128], kT4[:, hh, c, :], qT4[:, hh, c, :], start=(hh == 0 and c == 0),
                                                   stop=(c == 1), skip_group_check=True)
                            return ins
                        S.pe(smm, r=[qkT], w=[pst])
                        S.dve(lambda e: e.tensor_tensor(out=AT[:, :, :], in0=pst[:, 0:256].rearrange("p (h i) -> p h i", h=2), in1=decT[:, :, :],
                                                        op=ALU.mult), r=[pst, decT], w=[AT])
                        for hh in range(2):
                            def omm(e, hh=hh):
                                e.matmul(po[hh][:, :], AT[:, hh, :], Vb[:, hh, :], start=True, stop=False)
                                e.matmul(po[hh][:, :], QdT[:, hh, 0, :], Sb[:, hh * 2, :], start=False, stop=False)
                                return e.matmul(po[hh][:, :], QdT[:, hh, 1, :], Sb[:, hh * 2 + 1, :], start=False, stop=True)
                            S.pe(omm, r=[AT, Vb, QdT, Sb], w=[po[hh]])
                        for hh in range(2):
                            for c in range(2):
                                U_ = pu[c]
                                S.pe(lambda e, hh=hh, c=c, U_=U_: e.matmul(U_[:, :], Kd[:, hh, c * 128:c * 128 + 128], Vb[:, hh, :], start=True, stop=True),
                                     r=[Kd, Vb], w=[U_])
                                S.dve(lambda e, hh=hh, c=c, U_=U_: e.scalar_tensor_tensor(out=St[:, hh * 2 + c, :], in0=St[:, hh * 2 + c, :],
                                                                                         scalar=gpow[:, hh:hh + 1], in1=U_[:, :], op0=ALU.mult, op1=ALU.add),
                                      r=[St, gpow, U_], w=[St])
                                S.act(lambda e, hh=hh, c=c: e.copy(out=Sb[:, hh * 2 + c, :], in_=St[:, hh * 2 + c, :]), r=[St], w=[Sb])
                    else:
                        S.dve(lambda e: e.tensor_tensor(out=QdT[:, :, :, 0:NS], in0=qT4[:, :, :, 0:NS],
                                                        in1=qdecs[:, :, :].unsqueeze(2).broadcast_to([128, 2, 2, NS]), op=ALU.mult),
                              r=[qkT, qdecs], w=[QdT])
                        for s_ in range(4):
                            S.dve(lambda e, s_=s_: e.tensor_tensor(out=QdTm[:, s_, :, :, :], in0=QdT[:, :, :, 0:NS],
                                                                   in1=colmask[:, s_, :].unsqueeze(1).unsqueeze(2).broadcast_to([128, 2, 2, 16]),
                                                                   op=ALU.mult), r=[QdT, colmask], w=[QdTm])
                            S.dve(lambda e, s_=s_: e.tensor_tensor(out=Kdm[:, s_, :, :], in0=qkr[:NS, 2:4, :],
                                                                   in1=kdm[:, s_, :].unsqueeze(2).broadcast_to([NS, 2, 256]), op=ALU.mult),
                                  r=[qkr, kdm], w=[Kdm])

                        def smm_s(e):
                            ins = None
                            for hh in range(2):
                                for c in range(2):
                                    ins = e.matmul(pst[:NS, hh * 16:hh * 16 + 16], kT4[:, hh, c, 0:NS], qT4[:, hh, c, 0:NS], start=(hh == 0 and c == 0),
                                                   stop=(c == 1), skip_group_check=True)
                            return ins
                        S.pe(smm_s, r=[qkT], w=[pst])
                        S.dve(lambda e: e.tensor_tensor(out=AT[:NS, :, 0:NS], in0=pst[:NS, 0:32].rearrange("p (h i) -> p h i", h=2), in1=decTs[:, :, :],
                                                        op=ALU.mult), r=[pst, decTs], w=[AT])
                        for s_ in range(4):
                            SS = Ss[s_ % 2]
                            S.dma("sp", SS[:, :, :], sret_d.ap()[s_].rearrange("h (c p) e -> p (h c) e", p=128), w=[SS], key=SS)
                            S.act(lambda e, s_=s_, SS=SS: e.copy(out=Ssb[s_][:, :, :], in_=SS[:, :, :]), r=[SS], w=[Ssb[s_]])
                            for hh in range(2):
                                for c in range(2):
                                    U_ = pu[c]
                                    S.pe(lambda e, hh=hh, c=c, U_=U_, s_=s_: e.matmul(U_[:, :], Kdm[:, s_, hh, c * 128:c * 128 + 128], Vb[:NS, hh, :],
                                                                                      start=True, stop=True), r=[Kdm, Vb], w=[U_])
                                    S.dve(lambda e, hh=hh, c=c, U_=U_, SS=SS: e.scalar_tensor_tensor(out=SS[:, hh * 2 + c, :], in0=SS[:, hh * 2 + c, :],
                                                                                                    scalar=gpow[:, 2 + hh:3 + hh], in1=U_[:, :],
                                                                                                    op0=ALU.mult, op1=ALU.add),
                                          r=[SS, gpow, U_], w=[SS])
                            S.dma("sp", ret_s.ap()[s_].rearrange("h (c p) e -> p (h c) e", p=128), SS[:, :, :], r=[SS], key=SS, final=True)
                        for hh in range(2):
                            def omm_s(e, hh=hh):
                                ins = e.matmul(po[hh][:NS, :], AT[:NS, hh, 0:NS], Vb[:NS, hh, :], start=True, stop=False)
                                for s_ in range(4):
                                    for c in range(2):
                                        ins = e.matmul(po[hh][:NS, :], QdTm[:, s_, hh, c, :], Ssb[s_][:, hh * 2 + c, :], start=False,
                                                       stop=(s_ == 3 and c == 1))
                                return ins
                            S.pe(omm_s, r=[AT, Vb, QdTm] + Ssb, w=[po[hh]])

                    OG = og_[T_ % 2]
                    for hh in range(2):
                        OB = ob[hh]
                        S.act(lambda e, hh=hh, OB=OB, n=n: e.activation(out=OB[:n, :], in_=po[hh][:n, :], func=AF.Identity, accum_out=st8[:n, hh * 8:hh * 8 + 1]),
                              r=[po[hh]], w=[OB, st8])
                        S.act(lambda e, hh=hh, OB=OB, n=n: e.activation(out=jk[:n, :], in_=OB[:n, :], func=AF.Square, accum_out=st8[:n, hh * 8 + 1:hh * 8 + 2]),
                              r=[OB], w=[jk, st8])
                        o8 = hh * 8
                        S.dve(lambda e, o8=o8, n=n: e.tensor_scalar(out=st8[:n, o8 + 2:o8 + 4], in0=st8[:n, o8:o8 + 2], scalar1=1.0 / 512, scalar2=None,
                                                                    op0=ALU.mult), r=[st8], w=[st8])
                        S.dve(lambda e, o8=o8, n=n: e.tensor_tensor(out=st8[:n, o8 + 4:o8 + 5], in0=st8[:n, o8 + 2:o8 + 3], in1=st8[:n, o8 + 2:o8 + 3],
                                                                    op=ALU.mult), r=[st8], w=[st8])
                        S.dve(lambda e, o8=o8, n=n: e.tensor_tensor(out=st8[:n, o8 + 5:o8 + 6], in0=st8[:n, o8 + 3:o8 + 4], in1=st8[:n, o8 + 4:o8 + 5],
                                                                    op=ALU.subtract), r=[st8], w=[st8])
                        S.dve(lambda e, o8=o8, n=n: e.tensor_scalar(out=st8[:n, o8 + 5:o8 + 6], in0=st8[:n, o8 + 5:o8 + 6], scalar1=0.0, scalar2=1e-5,
                                                                    op0=ALU.max, op1=ALU.add), r=[st8], w=[st8])
                        S.act(lambda e, o8=o8, n=n: e.activation(out=st8[:n, o8 + 6:o8 + 7], in_=st8[:n, o8 + 5:o8 + 6], func=AF.Sqrt), r=[st8], w=[st8])
                        S.dve(lambda e, o8=o8, n=n: e.reciprocal(out=st8[:n, o8 + 6:o8 + 7], in_=st8[:n, o8 + 6:o8 + 7]), r=[st8], w=[st8])
                        S.dve(lambda e, o8=o8, OB=OB, n=n: e.tensor_scalar(out=OB[:n, :], in0=OB[:n, :], scalar1=st8[:n, o8 + 2:o8 + 3],
                                                                           scalar2=st8[:n, o8 + 6:o8 + 7], op0=ALU.subtract, op1=ALU.mult),
                              r=[OB, st8], w=[OB])
                        S.pool(lambda e, hh=hh, OB=OB, n=n: e.tensor_tensor(out=OB[:n, :], in0=OB[:n, :], in1=gnb[:n, hh, :], op=ALU.mult), r=[OB, gnb], w=[OB])
                        S.pool(lambda e, hh=hh, OB=OB, OG=OG, n=n: e.tensor_tensor(out=OG[:n, hh, :], in0=OB[:n, :], in1=sg[:n, hh * 512:hh * 512 + 512],
                                                                                   op=ALU.mult), r=[OB, sg], w=[OG])
                        if T_ < NT:
                            c_, tl = T_ // 8, T_ % 8
                            S.dma("sp", o_r.ap()[c_, hh, tl * 128:tl * 128 + 128, :], OG[:, hh, :], r=[OG], w=[o_rT[c_][hh]], key=OG)
                        else:
                            S.dma("sp", o_rs.ap()[hh], OG[:NS, hh, :], r=[OG], w=[o_rsT[hh]], key=OG)
                    if T_ < NT and T_ % 8 == 7:
                        c_ = T_ // 8
                        for hh in range(2):
                            base = ((c_ * 2 + hh) * 4) * 1024
                            allgather(o_r.ap()[c_, hh].rearrange("(p a) f -> p (a f)", a=8),
                                      og2.ap()[base:base + 4096, :].rearrange("(q a) f -> q (a f)", a=8), o_rT[c_][hh], og2T)
                    if T_ == NT - 1:
                        S.dma("sp", ret_p.ap().rearrange("h (c p) e -> p (h c) e", p=128), St[:, :, :], r=[St], key=St, final=True)
                stageE1(0)
                for T_ in range(NT + 1):
                    if T_ + 1 <= NT:
                        stageE1(T_ + 1)
                    stageE2(T_)
                for hh in range(2):
                    allgather(o_rs.ap()[hh].rearrange("r (b x) -> (r b) x", b=8),
                              ogs2.ap()[hh * 64:hh * 64 + 64, :].rearrange("r (b x) -> (r b) x", b=8), o_rsT[hh], og2T)
                S.barrier()

        if RUN_T0:
            token_phase(0)
        if RUN_E:
            phase_e()
        if RUN_T1:
            token_phase(1)

        S.emit()
    return nc


def _col_slice(g):
    cols = list(range(512 * g, 512 * g + 512))
    for e in (0, 1):
        for br in range(3):
            base = 2048 + ((br * 2 + e) * 4 + g) * 128
            cols += list(range(base, base + 128))
    for br in range(3):
        base = 2048 + 3072 + br * 16 + 4 * g
        cols += list(range(base, base + 4))
    return np.array(cols)


def _consts():
    c0 = np.arange(256)[:, None] * 16
    j0 = np.arange(64)[None, :] * 64
    cover = ((c0 < j0 + 64) & (c0 + 32 > j0)).astype(np.float32)
    cover[255] = 0.0
    p = np.arange(128)[:, None]
    col = np.arange(128)[None, :]
    I0 = (col - 16 * p).astype(np.float32)
    Jm = (np.arange(64)[None, :] - (p >= 64)).astype(np.float32)
    tri = np.concatenate([(p <= col), (col < p)], axis=1).astype(np.float32)
    Ebig = (np.arange(SEQ)[None, :] // 64 == np.arange(64)[:, None]).astype(np.float32)
    key = np.arange(16)
    hq = np.arange(16)
    smask = np.zeros((16, 5, 16), np.float32)
    for s_ in range(4):
        smask[:, s_, :] = ((key[:, None] // 4 == s_) & (key[:, None] % 4 <= hq[None, :] % 4))
    smask[:, 4, :] = (key[:, None] > hq[None, :] % 4)
    sel16 = np.zeros((16, 68), np.float32)
    sel16[:, 0:4] = (hq[:, None] % 4 == np.arange(4)[None, :])
    for s_ in range(4):
        sel16[:, 4 + 16 * s_:20 + 16 * s_] = (key[:, None] == 4 * s_ + hq[None, :] % 4)
    hmask = (np.arange(12)[None, :] % 4 == hq[:, None] // 4).astype(np.float32)
    c0s = np.arange(1024)[:, None] * 16
    j0s = np.arange(257)[None, :] * 64
    cover_s = ((c0s < j0s + 64) & (c0s + 32 > j0s)).astype(np.float32)
    cover_s[1023] = 0.0
    return {"cover": cover, "I0": I0, "Jm": Jm, "tri": tri, "Ebig": Ebig, "smask": smask, "sel16": sel16,
            "hmask": hmask, "cover_s": cover_s, "cs_r": rope_table(np.arange(SEQ), 128),
            "cs_rs": rope_table(np.tile(PAST + np.arange(4), 4), 128)}


CONST = _consts()


def _ret_cols(r):
    cols = []
    for base, w in ((0, 256), (2048, 256), (4096, 512), (8192, 512)):
        for hh in range(2):
            h = 2 * r + hh
            cols += list(range(base + w * h, base + w * h + w))
    return np.array(cols)


def _ret_consts(r):
    out = {}
    i = np.arange(128, dtype=np.float64)
    decT = np.zeros((128, 2, 128), np.float64)
    qdecb = np.zeros((128, 2, 128), np.float64)
    kdec = np.zeros((128, 2), np.float64)
    decTs = np.zeros((16, 2, 16), np.float64)
    kdm = np.zeros((16, 4, 2), np.float64)
    gpow = np.zeros((128, 4), np.float64)
    qdecs = np.zeros((128, 2, 16), np.float64)
    t16 = np.arange(16)
    for hh in range(2):
        h = 2 * r + hh
        lg = np.log(np.float32(1.0) - np.float32(2.0) ** np.float32(-5.0 - h)).astype(np.float64)
        rel = i[None, :] - i[:, None]
        decT[:, hh, :] = np.where(rel >= 0, np.exp(np.maximum(rel, 0) * lg), 0.0) / 16.0
        qdecb[:, hh, :] = np.exp((i + 1.0) * lg)[None, :]
        kdec[:, hh] = np.exp((127.0 - i) * lg) / 16.0
        rel4 = (t16[None, :] % 4) - (t16[:, None] % 4)
        same = (t16[None, :] // 4) == (t16[:, None] // 4)
        decTs[:, hh, :] = np.where(same & (rel4 >= 0), np.exp(np.maximum(rel4, 0) * lg), 0.0) / 16.0
        for s_ in range(4):
            kdm[:, s_, hh] = np.where(t16 // 4 == s_, np.exp((3.0 - t16 % 4) * lg), 0.0) / 16.0
        gpow[:, hh] = np.exp(128.0 * lg)
        gpow[:, 2 + hh] = np.exp(4.0 * lg)
        qdecs[:, hh, :] = np.exp((t16 % 4 + 1.0) * lg)[None, :]
    colmask = np.zeros((128, 4, 16), np.float32)
    for s_ in range(4):
        colmask[:, s_, :] = (t16 // 4 == s_)[None, :]
    out["decT"] = decT.reshape(128, 256).astype(np.float32)
    out["qdecb"] = qdecb.reshape(128, 256).astype(np.float32)
    out["kdec"] = kdec.astype(np.float32)
    out["decTs"] = decTs.reshape(16, 32).astype(np.float32)
    out["kdm"] = kdm.reshape(16, 8).astype(np.float32)
    out["gpow"] = gpow.astype(np.float32)
    out["qdecs"] = qdecs.reshape(128, 32).astype(np.float32)
    out["colmask"] = colmask.reshape(128, 64)
    return out


def _oidx2(r):
    p = np.arange(128)
    idx = np.zeros((128, 72), np.int32)
    for t in range(9):
        for j in range(4):
            for hh in range(2):
                if t < 8:
                    idx[:, t * 8 + 2 * j + hh] = ((r * 2 + hh) * 4 + j) * 1024 + t * 128 + p
                else:
                    idx[:, t * 8 + 2 * j + hh] = (hh * 4 + j) * NS + 4 * r + np.minimum(p, 3)
    return idx


def _oidx(r):
    p = np.arange(128)
    idx = np.zeros((128, 36), np.int32)
    for t in range(9):
        for j in range(4):
            if t < 8:
                idx[:, t * 4 + j] = r * 4096 + j * 1024 + t * 128 + p
            else:
                idx[:, t * 4 + j] = j * NS + 4 * r + np.minimum(p, 3)
    return idx


def make_in_maps(inp):
    maps = []
    ident = np.eye(128, dtype=np.float32)
    cs_p = rope_table(np.arange(SEQ), 64)
    cs_s = rope_table(np.tile(PAST + np.arange(4), 4), 64)
    for c in range(8):
        b, r = c // 4, c % 4
        m = {
            "xb": np.ascontiguousarray(inp["x_prompt"][b]),
            "xs": np.ascontiguousarray(inp["x_sample"][4 * b:4 * b + 4].reshape(NS, D)),
            "w_in": np.ascontiguousarray(inp["nsa_w_in"][0][:, _col_slice(r)]),
            "gmix0": np.ascontiguousarray(np.broadcast_to(inp["norm_mix"][0][None, :], (128, D))),
            "cs_p": cs_p, "cs_s": cs_s, "ident": ident,
            "cw1": inp["nsa_cmp_w1"][0], "cw2": inp["nsa_cmp_w2"][0],
            "posT": np.ascontiguousarray(inp["nsa_cmp_pos"][0].reshape(64, 128).T),
            "b1T": np.ascontiguousarray(inp["nsa_cmp_b1"][0].T),
            "x_tok": np.ascontiguousarray(np.concatenate([inp["x_prompt"][b, 1024 * r:1024 * r + 1024], inp["x_sample"][c]], axis=0)),
            "oidx": _oidx(r),
            "w_out0": inp["nsa_w_out"][0], "ffn1_0": inp["ffn_w1"][0], "ffn2_0": inp["ffn_w2"][0],
            "gffn0T": np.ascontiguousarray(inp["norm_ffn"][0].reshape(16, 128).T),
            "w_out1": inp["ret_w_out"][0], "ffn1_1": inp["ffn_w1"][1], "ffn2_1": inp["ffn_w2"][1],
            "gmix1T": np.ascontiguousarray(inp["norm_mix"][1].reshape(16, 128).T),
            "gffn1T": np.ascontiguousarray(inp["norm_ffn"][1].reshape(16, 128).T),
            "gfinT": np.ascontiguousarray(inp["norm_final"].reshape(16, 128).T),
            "wr": np.ascontiguousarray(inp["ret_w_in"][0][:, _ret_cols(r)]),
            "cs_r": CONST["cs_r"], "cs_rs": CONST["cs_rs"],
            "gnb": np.ascontiguousarray(np.broadcast_to(inp["ret_gn"][0][1024 * r:1024 * r + 1024][None, :], (128, 1024))),
            "sret": np.ascontiguousarray(inp["state_ret"][0][4 * b:4 * b + 4, 2 * r:2 * r + 2]),
            "oidx2": _oidx2(r),
            **_ret_consts(r),
            "ccache": np.ascontiguousarray(inp["cache_cmp_kv"][0][:, :, :, r, :]),
            "scache": np.ascontiguousarray(inp["cache_sel_kv"][0][:, :, :, r, :]),
            "wstate": np.ascontiguousarray(inp["state_win_kv"][0][4 * b:4 * b + 4, :, :, r, :]),
            "ptab": np.ascontiguousarray(inp["page_table"][4 * b:4 * b + 4].astype(np.int32)),
            "smask": CONST["smask"], "sel16": CONST["sel16"], "hmask": CONST["hmask"], "cover_s": CONST["cover_s"],
            "pidx": (np.arange(128) % 4).astype(np.float32)[:, None].copy(),
            "ptabT": np.ascontiguousarray(inp["page_table"][4 * b:4 * b + 4].astype(np.int32).reshape(4, 4, 32).transpose(2, 0, 1).reshape(32, 16)),
            "Rrep": (np.arange(128)[None, :] // 4 == np.arange(32)[:, None]).astype(np.float32),
            "E2": (np.arange(128)[None, :] // 2 == np.arange(64)[:, None]).astype(np.float32),
            "cover": CONST["cover"], "I0": CONST["I0"], "Jm": CONST["Jm"], "tri": CONST["tri"], "Ebig": CONST["Ebig"],
        }
        maps.append(m)
    return maps


_NC = None


def kernel(**inputs):
    global _NC
    inp = {k: np.asarray(v) for k, v in inputs.items()}
    if _NC is None:
        _NC = build()
    maps = make_in_maps(inp)
    res = run_bass_kernel_spmd(_NC, maps, core_ids=list(range(8)), **({'trace': True} if TRACE else {}))
    global LAST_RES
    LAST_RES = res
    R = res.results
    global LAST
    LAST = R
    cmp_p = np.zeros((1, 2, SEQ, 2, 4, 128), np.float32)
    sel_p = np.zeros_like(cmp_p)
    win_full = np.zeros_like(cmp_p)
    cmp_s = np.zeros((1, 8, 4, 2, 4, 128), np.float32)
    sel_s = np.zeros_like(cmp_s)
    win_new = np.zeros_like(cmp_s)
    for c in range(8):
        b, g = c // 4, c % 4
        kp = R[c]["kv_p"]
        cmp_p[0, b, :, :, g, :] = kp[:, 0]
        sel_p[0, b, :, :, g, :] = kp[:, 1]
        win_full[0, b, :, :, g, :] = kp[:, 2]
        ks = R[c]["kv_s"].reshape(4, 4, 3, 2, 128)
        cmp_s[0, 4 * b:4 * b + 4, :, :, g, :] = ks[:, :, 0]
        sel_s[0, 4 * b:4 * b + 4, :, :, g, :] = ks[:, :, 1]
        win_new[0, 4 * b:4 * b + 4, :, :, g, :] = ks[:, :, 2]
    win_p = np.ascontiguousarray(win_full[:, :, SEQ - 512:])
    y_p = np.zeros((2, SEQ, D), np.float32)
    y_s = np.zeros((8, 4, D), np.float32)
    win_s = np.zeros((1, 8, 512, 2, 4, 128), np.float32)
    ret_p = np.zeros((1, 2, 8, 256, 512), np.float32)
    ret_s = np.zeros((1, 8, 8, 256, 512), np.float32)
    for c in range(8):
        b, g = c // 4, c % 4
        win_s[0, 4 * b:4 * b + 4, :, :, g, :] = R[c]["win_s"]
        y_p[b, 1024 * g:1024 * g + 1024] = R[c]["y_tok"][:1024]
        y_s[c] = R[c]["y_tok"][1024:]
        ret_p[0, b, 2 * g:2 * g + 2] = R[c]["ret_p"]
        ret_s[0, 4 * b:4 * b + 4, 2 * g:2 * g + 2] = R[c]["ret_s"]
    return (y_p, y_s, cmp_p, cmp_s, sel_p, sel_s, win_p, win_s, ret_p, ret_s)
```

```python
import contextlib
import numpy as np
import concourse.bass as bass
import concourse.mybir as mybir
from concourse.bass_utils import run_bass_kernel_spmd

F32 = mybir.dt.float32
BF16 = mybir.dt.bfloat16
I32 = mybir.dt.int32
AF = mybir.ActivationFunctionType
ALU = mybir.AluOpType
AX = mybir.AxisListType

ENGS = ("sp", "act", "dve", "pool", "pe")


class Buf:
    __slots__ = ("name", "last_w", "readers", "cnt")

    def __init__(self, name):
        self.name = name
        self.last_w = None
        self.readers = []
        self.cnt = 0


class T:
    def __init__(self, t, name):
        self.t = t
        self.b = Buf(name)

    def __getitem__(self, k):
        return self.t[k]

    def ap(self):
        return self.t.ap()


def _b(x):
    return x.b if isinstance(x, T) else x


class Op:
    __slots__ = ("eng", "fn", "deps", "is_dma", "key", "sig", "sigidx", "cnt", "inc")

    def __init__(self, eng, fn, is_dma=False, key=None, inc=16):
        self.eng = eng
        self.fn = fn
        self.deps = []
        self.is_dma = is_dma
        self.key = key
        self.sig = False
        self.sigidx = 0
        self.cnt = 0
        self.inc = inc


class Sched:
    def __init__(self, nc):
        self.nc = nc
        self.ops = []
        self.last_real = {e: None for e in ENGS}
        self.out_dmas = []

    def _add(self, op, r, w):
        r = [_b(x) for x in r]
        w = [_b(x) for x in w]
        deps = []
        for b in r:
            if b.last_w is not None:
                deps.append(b.last_w)
        for b in w:
            if b.last_w is not None:
                deps.append(b.last_w)
            deps.extend(b.readers)
        seen = set()
        for d in deps:
            if d is op or id(d) in seen:
                continue
            seen.add(id(d))
            if (not d.is_dma) and d.eng == op.eng and op.eng == "pe" and not op.is_dma:
                continue
            op.deps.append(d)
            if not d.is_dma:
                d.sig = True
        for b in r:
            b.readers.append(op)
        for b in w:
            b.last_w = op
            b.readers = []
        self.ops.append(op)
        if not op.is_dma:
            self.last_real[op.eng] = op
        return op

    def op(self, eng, fn, r=(), w=()):
        return self._add(Op(eng, fn), r, w)

    def pe(self, fn, r=(), w=()):
        return self.op("pe", fn, r, w)

    def act(self, fn, r=(), w=()):
        return self.op("act", fn, r, w)

    def dve(self, fn, r=(), w=()):
        return self.op("dve", fn, r, w)

    def pool(self, fn, r=(), w=()):
        return self.op("pool", fn, r, w)

    def dma(self, q, out, in_, r=(), w=(), key=None, final=False, **kw):
        return self.custom_dma(q, lambda e: e.dma_start(out=out, in_=in_, **kw), r, w, key, 16, final)

    def custom_dma(self, q, fn, r=(), w=(), key=None, inc=16, final=False):
        k = _b(key)
        op = Op(q, fn, is_dma=True, key=k, inc=inc)
        self._add(op, r, w)
        if final:
            self.out_dmas.append(op)
        return op

    def barrier(self):
        pend = [o for o in self.last_real.values() if o is not None]
        latest = {}
        for o in self.ops:
            if o.is_dma:
                latest[id(o.key)] = o
        for e in ENGS:
            op = Op(e, None)
            for d in pend:
                if d.eng != e:
                    op.deps.append(d)
                    d.sig = True
            op.deps.extend(latest.values())
            self.ops.append(op)

    def emit(self):
        nc = self.nc
        fin = Op("sp", None)
        latest = {}
        for o in self.out_dmas:
            latest[id(o.key)] = o
        fin.deps = list(latest.values())
        for e in ENGS:
            o = self.last_real[e]
            if e != "sp" and o is not None:
                fin.deps.append(o)
                o.sig = True
        self.ops.append(fin)

        with contextlib.ExitStack() as st:
            esem = {e: st.enter_context(nc.semaphore("s_" + e)) for e in ENGS}
            keysems = {}
            keyvals = {}
            for o in self.ops:
                if o.is_dma:
                    kid = id(o.key)
                    if kid not in keysems:
                        keysems[kid] = st.enter_context(nc.semaphore("k%d" % len(keysems)))
                        keyvals[kid] = 0
                    keyvals[kid] += o.inc
                    o.cnt = keyvals[kid]
            cnts = {e: 0 for e in ENGS}
            for o in self.ops:
                if (not o.is_dma) and o.sig:
                    assert o.fn is not None
                    cnts[o.eng] += 1
                    o.sigidx = cnts[o.eng]
            self.n_sems = len(keysems) + 5
            per = {e: [o for o in self.ops if o.eng == e] for e in ENGS}
            block = st.enter_context(nc.Block())

            def run(eng_name):
                def body(eng):
                    waited = {}
                    for o in per[eng_name]:
                        for d in o.deps:
                            if d.is_dma:
                                s, v = keysems[id(d.key)], d.cnt
                            else:
                                s, v = esem[d.eng], d.sigidx
                            if waited.get(id(s), 0) >= v:
                                continue
                            waited[id(s)] = v
                            eng.wait_ge(s, v)
                        if o.fn is None:
                            continue
                        ins = o.fn(eng)
                        if o.is_dma:
                            ins.then_inc(keysems[id(o.key)], o.inc)
                        elif o.sig:
                            ins.then_inc(esem[eng_name], 1)

                return body

            block.sync(run("sp"))
            block.scalar(run("act"))
            block.vector(run("dve"))
            block.gpsimd(run("pool"))
            block.tensor(run("pe"))


D = 2048
SEQ = 4096
NT = SEQ // 128
NTOK = 1028
NS = 16
PAST = 16384
NCOL = 1292
RMS_EPS = 1e-6
SCALE = 128 ** -0.5
NEG = -30000.0

STAGE = 1
DEBUG = False
NTQ = NT
RUN_S = True
RUN_T0 = True
RUN_E = True
RUN_T1 = True
TRACE = False


def rope_table(pos, half):
    inv = (10000.0 ** (-(np.arange(half, dtype=np.float32)) / np.float32(half))).astype(np.float32)
    ang = (pos.astype(np.float32)[:, None] * inv[None, :]).astype(np.float32)
    return np.concatenate([np.cos(ang), np.sin(ang)], axis=1).astype(np.float32)


def build(stage=STAGE):
    nc = bass.Bass("TRN2", target_bir_lowering=False)
    S = Sched(nc)

    def din(name, shape, dt=F32):
        return nc.dram_tensor(name, list(shape), dt, kind="ExternalInput")

    def dout(name, shape, dt=F32):
        return nc.dram_tensor(name, list(shape), dt, kind="ExternalOutput")

    xb = din("xb", [SEQ, D])
    xs = din("xs", [NS, D])
    w_in = din("w_in", [D, NCOL])
    gmix0 = din("gmix0", [128, D])
    cs_p = din("cs_p", [SEQ, 128])
    cs_s = din("cs_s", [NS, 128])
    ident_d = din("ident", [128, 128])

    cw1 = din("cw1", [2, 32, 128, 128])
    cw2 = din("cw2", [2, 128, 128])
    posT = din("posT", [128, 64])
    b1T = din("b1T", [128, 2])
    cover_d = din("cover", [256, 64])
    I0_d = din("I0", [128, 128])
    Jm_d = din("Jm", [128, 64])
    tri_d = din("tri", [128, 256])
    Ebig_d = din("Ebig", [64, SEQ])

    kv_p = dout("kv_p", [SEQ, 3, 2, 128])
    o_loc = nc.dram_tensor("o_loc", [SEQ, 512], BF16)
    o_locs = nc.dram_tensor("o_locs", [NS, 512], BF16)
    og = nc.dram_tensor("og", [4 * SEQ, 512], BF16)
    ogs = nc.dram_tensor("ogs", [4 * NS, 512], BF16)
    o_locT = [Buf("o_loc%d" % i) for i in range(5)]
    ogT = Buf("og")
    x_tok = din("x_tok", [NTOK, D])
    oidx_d = din("oidx", [128, 36], I32)
    w_out0 = din("w_out0", [D, D])
    ffn1_0 = din("ffn1_0", [D, 4 * D])
    ffn2_0 = din("ffn2_0", [4 * D, D])
    gffn0T = din("gffn0T", [128, 16])
    w_out1 = din("w_out1", [2 * D, D])
    ffn1_1 = din("ffn1_1", [D, 4 * D])
    ffn2_1 = din("ffn2_1", [4 * D, D])
    gmix1T = din("gmix1T", [128, 16])
    gffn1T = din("gffn1T", [128, 16])
    gfinT = din("gfinT", [128, 16])
    wr_d = din("wr", [D, 3072])
    cs_r = din("cs_r", [SEQ, 256])
    cs_rs = din("cs_rs", [NS, 256])
    decT_d = din("decT", [128, 256])
    qdecb_d = din("qdecb", [128, 256])
    kdec_d = din("kdec", [128, 2])
    decTs_d = din("decTs", [16, 32])
    kdm_d = din("kdm", [16, 8])
    colmask_d = din("colmask", [128, 64])
    gnb_d = din("gnb", [128, 1024])
    gpow_d = din("gpow", [128, 4])
    qdecs_d = din("qdecs", [128, 32])
    sret_d = din("sret", [4, 2, 256, 512])
    oidx2_d = din("oidx2", [128, 72], I32)
    ret_p = dout("ret_p", [2, 256, 512])
    ret_s = dout("ret_s", [4, 2, 256, 512])
    y_tok = dout("y_tok", [NTOK, D])
    hspill = nc.dram_tensor("hspill", [128, 16 * NTOK], F32)
    hspT = Buf("hspill")
    xg_in = nc.dram_tensor("xg_in", [8, 128, 2 * NTOK], BF16)
    xg_all = nc.dram_tensor("xg_all", [8, 512, 2 * NTOK], BF16)
    xg_inT = [Buf("xg_in%d" % i) for i in range(8)]
    xgT = Buf("xg_all")
    o_r = nc.dram_tensor("o_r", [4, 2, 1024, 512], BF16)
    o_rs = nc.dram_tensor("o_rs", [2, NS, 512], BF16)
    og2 = nc.dram_tensor("og2", [4 * 2 * 4 * 1024, 512], BF16)
    ogs2 = nc.dram_tensor("ogs2", [2 * 4 * NS, 512], BF16)
    o_rT = [[Buf("o_r%d%d" % (c, h)) for h in range(2)] for c in range(4)]
    o_rsT = [Buf("o_rs%d" % h) for h in range(2)]
    og2T = Buf("og2")
    win_s = dout("win_s", [4, 512, 2, 128])
    ccache = din("ccache", [1280, 128, 2, 128])
    scache = din("scache", [1280, 128, 2, 128])
    wstate = din("wstate", [4, 512, 2, 128])
    ptab = din("ptab", [4, 128], I32)
    smask_d = din("smask", [16, 5, 16])
    sel16_d = din("sel16", [16, 4 + 64])
    hmask_d = din("hmask", [16, 12])
    cover_s = din("cover_s", [1024, 257])
    pidx_d = din("pidx", [128, 1])
    ptabT = din("ptabT", [32, 16], I32)
    Rrep_d = din("Rrep", [32, 128])
    E2_d = din("E2", [64, 128])
    kv_s = dout("kv_s", [NS, 3, 2, 128])

    dbg_outs = {}

    def dbg(name, t, ap, shape, dt=F32):
        if not DEBUG:
            return
        d_ = nc.dram_tensor("dbg_" + name, list(shape), dt, kind="ExternalOutput")
        S.dma("sp", d_.ap(), ap, r=[t], key=Buf("dbgk_" + name), final=True)

    with contextlib.ExitStack() as top:
        def sbuf(st, name, shape, dt):
            return T(st.enter_context(nc.sbuf_tensor(name, list(shape), dt)), name)

        def psum(st, name, shape, dt):
            return T(st.enter_context(nc.psum_tensor(name, list(shape), dt)), name)

        identb = sbuf(top, "identb", [128, 128], BF16)
        identf = sbuf(top, "identf", [128, 128], F32)
        GL = sbuf(top, "GL", [128, NT + 1, 12], F32)
        QTs = sbuf(top, "QTs", [128, 4, NS], BF16)
        KTs = sbuf(top, "KTs", [128, 3, NS], BF16)
        VAs = sbuf(top, "VAs", [NS, 2, 132], BF16)
        pp = contextlib.ExitStack()
        QT = sbuf(pp, "QT", [128, 4, SEQ + NS], BF16)
        KT = sbuf(pp, "KT", [128, 3, SEQ + NS], BF16)
        VcT = sbuf(pp, "VcT", [128, SEQ + NS], BF16)
        VA = sbuf(pp, "VA", [128, NT + 1, 2, 132], BF16)
        S.dma("sp", identf[:], ident_d.ap(), w=[identf], key=identf)
        S.dma("pool", identb[:], ident_d.ap(), w=[identb], key=identb)

        RG = [[0, 1, 2, 3], [4, 5, 6, 7]]

        def allgather(src_ap, dst_ap, rbuf, wbuf_):
            S.custom_dma("pool", lambda e: e.collective_compute("AllGather", ALU.bypass, replica_groups=RG, ins=[src_ap], outs=[dst_ap]),
                         r=[rbuf], w=[wbuf_], key=wbuf_, inc=1)

        def phase_a():
            with contextlib.ExitStack() as pa:
                wsb = sbuf(pa, "wsb", [128, 16, NCOL], BF16)
                gsb = sbuf(pa, "gsb", [128, D], F32)
                xt = [sbuf(pa, "xt%d" % i, [128, D], F32) for i in range(2)]
                cst = [sbuf(pa, "cst%d" % i, [128, 128], F32) for i in range(3)]
                junk = sbuf(pa, "junk", [128, D], BF16)
                ss = sbuf(pa, "ss", [128, 2], F32)
                epsb = sbuf(pa, "epsb", [128, 1], F32)
                S.pool(lambda e: e.memset(epsb[:], RMS_EPS), w=[epsb])
                xn = sbuf(pa, "xn", [128, D], BF16)
                xnTs = [sbuf(pa, "xnT%d" % i, [128, 16, 128], BF16) for i in range(2)]
                kvf = [sbuf(pa, "kvf%d" % i, [128, 3, 2, 128], F32) for i in range(2)]
                rt = [sbuf(pa, "rt%d" % i, [128, 4, 64], F32) for i in range(4)]
                qkbs = [sbuf(pa, "qkb%d" % i, [128, 8, 128], BF16) for i in range(2)]
                pT = [psum(pa, "pT%d" % i, [128, 1024], BF16) for i in range(2)]
                pz = [psum(pa, "pz%d" % i, [128, 512], F32) for i in range(3)]
                pq = psum(pa, "pq", [128, 1024], BF16)

                for kc in range(4):
                    S.dma("pool", wsb[:, 4 * kc:4 * kc + 4, :],
                          w_in.ap()[512 * kc:512 * kc + 512, :].rearrange("(k p) c -> p k c", p=128),
                          w=[wsb], key=wsb)
                S.dma("sp", gsb[:], gmix0.ap(), w=[gsb], key=gsb)
                S.pool(lambda e: e.memset(VA[:, :, :, 128:129], 1.0), w=[VA])

                def stageA1(j):
                    n = 128 if j < NT else NS
                    c0 = j * 128
                    X = xt[j % 2]
                    xnT = xnTs[j % 2]

                    def load(jj):
                        nn = 128 if jj < NT else NS
                        cc = jj * 128
                        S.dma("sp", xt[jj % 2][:nn, :], xb.ap()[cc:cc + 128, :] if jj < NT else xs.ap(), w=[xt[jj % 2]], key=xt[jj % 2])
                        S.dma("sp", cst[jj % 3][:nn, :], cs_p.ap()[cc:cc + 128, :] if jj < NT else cs_s.ap(), w=[cst[jj % 3]], key=cst[jj % 3])
                    if j == 0:
                        load(0)
                    if j + 1 <= NT:
                        load(j + 1)
                    S.act(lambda e, X=X, n=n: e.activation(out=junk[:n, :], in_=X[:n, :], func=AF.Square,
                                                           accum_out=ss[:n, 0:1]), r=[X], w=[junk, ss])
                    S.act(lambda e, n=n: e.activation(out=ss[:n, 1:2], in_=ss[:n, 0:1], func=AF.Sqrt, scale=1.0 / D,
                                                      bias=epsb[:n, 0:1]), r=[ss, epsb], w=[ss])
                    S.dve(lambda e, n=n: e.reciprocal(out=ss[:n, 1:2], in_=ss[:n, 1:2]), r=[ss], w=[ss])
                    S.dve(lambda e, X=X, n=n: e.scalar_tensor_tensor(out=xn[:n, :], in0=X[:n, :], scalar=ss[:n, 1:2],
                                                                     in1=gsb[:n, :], op0=ALU.mult, op1=ALU.mult),
                          r=[X, ss, gsb], w=[xn])
                    for hb in range(2):
                        def tr(e, hb=hb, n=n):
                            ins = None
                            for k in range(8):
                                ins = e.transpose(pT[hb][:, k * 128:k * 128 + n], xn[:n, (hb * 8 + k) * 128:(hb * 8 + k + 1) * 128],
                                                  identb[:n, :n])
                            return ins
                        S.pe(tr, r=[xn, identb], w=[pT[hb]])
                        S.act(lambda e, hb=hb, n=n: e.copy(out=xnT[:, hb * 8:hb * 8 + 8, :n],
                                                           in_=pT[hb][:, :].rearrange("p (k t) -> p k t", k=8)[:, :, :n]),
                              r=[pT[hb]], w=[xnT])

                def stageA2(j):
                    n = 128 if j < NT else NS
                    c0 = j * 128
                    C = cst[j % 3]
                    KV = kvf[j % 2]
                    xnT = xnTs[j % 2]
                    qkb = qkbs[j % 2]
                    cbs = [(0, 512), (512, 512), (1024, NCOL - 1024)]
                    for ci, (cb, cw) in enumerate(cbs):
                        def mm(e, ci=ci, cb=cb, cw=cw, n=n):
                            ins = None
                            for k in range(16):
                                ins = e.matmul(pz[ci][:n, :cw], xnT[:, k, :n], wsb[:, k, cb:cb + cw],
                                               start=(k == 0), stop=(k == 15))
                            return ins
                        S.pe(mm, r=[xnT, wsb], w=[pz[ci]])
                    cosb = lambda h, n=n, C=C: C[:n, 0:64].unsqueeze(1).broadcast_to([n, h, 64])
                    sinb = lambda h, n=n, C=C: C[:n, 64:128].unsqueeze(1).broadcast_to([n, h, 64])

                    def rope(src3, h, out_lo, out_hi, rd, wr, n=n, cosb=cosb, sinb=sinb):
                        a, b_, c_, d_ = rt
                        S.dve(lambda e: e.tensor_tensor(out=a[:n, :h, :], in0=src3[:, :, 0:64], in1=cosb(h), op=ALU.mult), r=rd + [C], w=[a])
                        S.dve(lambda e: e.tensor_tensor(out=b_[:n, :h, :], in0=src3[:, :, 64:128], in1=sinb(h), op=ALU.mult), r=rd + [C], w=[b_])
                        S.dve(lambda e: e.tensor_tensor(out=c_[:n, :h, :], in0=src3[:, :, 64:128], in1=cosb(h), op=ALU.mult), r=rd + [C], w=[c_])
                        S.dve(lambda e: e.tensor_tensor(out=d_[:n, :h, :], in0=src3[:, :, 0:64], in1=sinb(h), op=ALU.mult), r=rd + [C], w=[d_])
                        S.pool(lambda e: e.tensor_tensor(out=out_lo, in0=a[:n, :h, :], in1=b_[:n, :h, :], op=ALU.subtract), r=[a, b_], w=wr)
                        S.pool(lambda e: e.tensor_tensor(out=out_hi, in0=c_[:n, :h, :], in1=d_[:n, :h, :], op=ALU.add), r=[c_, d_], w=wr)

                    z0 = pz[0][:n, :].rearrange("p (h d) -> p h d", h=4)
                    rope(z0, 4, qkb[:n, 0:4, 0:64], qkb[:n, 0:4, 64:128], [pz[0]], [qkb])
                    z1 = pz[1][:n, 0:384].rearrange("p (h d) -> p h d", h=3)
                    rope(z1, 3, KV[:n, :, 0, 0:64], KV[:n, :, 0, 64:128], [pz[1]], [KV])
                    S.act(lambda e, n=n, KV=KV: e.copy(out=KV[:n, 0, 1, :], in_=pz[1][:n, 384:512]), r=[pz[1]], w=[KV])
                    S.act(lambda e, n=n, KV=KV: e.copy(out=KV[:n, 1:3, 1, :],
                                                       in_=pz[2][:n, 0:256].rearrange("p (h d) -> p h d", h=2)),
                          r=[pz[2]], w=[KV])
                    S.act(lambda e, n=n, j=j: e.copy(out=GL[:n, j, :], in_=pz[2][:n, 256:268]), r=[pz[2]], w=[GL])
                    S.pool(lambda e, n=n, KV=KV: e.tensor_copy(out=qkb[:n, 4:7, :], in_=KV[:n, :, 0, :]), r=[KV], w=[qkb])
                    S.pool(lambda e, n=n, KV=KV: e.tensor_copy(out=qkb[:n, 7, :], in_=KV[:n, 0, 1, :]), r=[KV], w=[qkb])
                    S.pool(lambda e, n=n, KV=KV, j=j: e.tensor_copy(out=VA[:n, j, :, 0:128], in_=KV[:n, 1:3, 1, :]), r=[KV], w=[VA])
                    dst = kv_p.ap()[c0:c0 + 128] if j < NT else kv_s.ap()
                    S.dma("sp", dst, KV[:n], r=[KV], key=KV, final=True)
                    if j == NT:
                        for s_ in range(4):
                            S.dma("sp", win_s.ap()[s_, 508:512], KV[4 * s_:4 * s_ + 4, 2], r=[KV], key=KV, final=True)

                def stageA3(j):
                    n = 128 if j < NT else NS
                    c0 = j * 128
                    qkb = qkbs[j % 2]
                    def tr2(e, n=n):
                        ins = None
                        for k in range(8):
                            ins = e.transpose(pq[:, k * 128:k * 128 + n], qkb[:n, k, :], identb[:n, :n])
                        return ins
                    S.pe(tr2, r=[qkb, identb], w=[pq])
                    pq3 = pq[:, :].rearrange("p (k t) -> p k t", k=8)
                    S.act(lambda e, n=n, c0=c0, pq3=pq3: e.copy(out=QT[:, :, c0:c0 + n], in_=pq3[:, 0:4, :n]), r=[pq], w=[QT])
                    S.act(lambda e, n=n, c0=c0, pq3=pq3: e.copy(out=KT[:, :, c0:c0 + n], in_=pq3[:, 4:7, :n]), r=[pq], w=[KT])
                    S.act(lambda e, n=n, c0=c0, pq3=pq3: e.copy(out=VcT[:, c0:c0 + n], in_=pq3[:, 7, :n]), r=[pq], w=[VcT])
                stageA1(0)
                for j in range(NT + 1):
                    if j + 1 <= NT:
                        stageA1(j + 1)
                    stageA2(j)
                    if j >= 1:
                        stageA3(j - 1)
                stageA3(NT)
                S.act(lambda e: e.copy(out=QTs[:, :, :], in_=QT[:, :, SEQ:SEQ + NS]), r=[QT], w=[QTs])
                S.act(lambda e: e.copy(out=KTs[:, :, :], in_=KT[:, :, SEQ:SEQ + NS]), r=[KT], w=[KTs])
                S.act(lambda e: e.copy(out=VAs[:, :, :], in_=VA[:NS, NT, :, :]), r=[VA], w=[VAs])
                S.barrier()

        phase_a()

        S.act(lambda e: e.activation(out=GL[:, :, :], in_=GL[:, :, :], func=AF.Sigmoid), r=[GL], w=[GL])

        def phase_c():
            with contextlib.ExitStack() as pc:
                w1sb = sbuf(pc, "w1sb", [128, 2, 32, 128], BF16)
                w2sb = sbuf(pc, "w2sb", [128, 2, 128], BF16)
                posb = sbuf(pc, "posb", [128, 64], BF16)
                b1sb = sbuf(pc, "b1sb", [128, 2], F32)
                biasb = sbuf(pc, "biasb", [128, 2], F32)
                I0 = sbuf(pc, "I0s", [128, 128], F32)
                Jm = sbuf(pc, "Jms", [128, 64], F32)
                trib = sbuf(pc, "trib", [128, 256], BF16)
                Ebig = sbuf(pc, "Ebigs", [64, SEQ], BF16)
                KcT = sbuf(pc, "KcT", [128, 256], BF16)
                VcA = sbuf(pc, "VcA", [128, 2, 196], BF16)
                hx = sbuf(pc, "hx", [128, 256], F32)
                ht = sbuf(pc, "ht", [128, 256], F32)
                hT = [sbuf(pc, "hT%d" % i, [128, 256], BF16) for i in range(2)]
                Eb = [sbuf(pc, "Eb%d" % i, [128, 4, 128], BF16) for i in range(3)]
                mk = sbuf(pc, "mk", [128, 128], BF16)
                rs = sbuf(pc, "rs", [128, 8], F32)
                ocats = [sbuf(pc, "ocat%d" % i, [128, 4, 128], F32) for i in range(2)]
                ocb = [sbuf(pc, "ocb%d" % i, [128, 512], BF16) for i in range(2)]
                imp = sbuf(pc, "imp", [128, 64], F32)
                vis = sbuf(pc, "vis", [128, 64], F32)
                frc = sbuf(pc, "frc", [128, 64], F32)
                col0 = sbuf(pc, "col0", [128, 64], F32)
                sc = [sbuf(pc, "sc%d" % i, [128, 64], F32) for i in range(2)]
                m8 = sbuf(pc, "m8", [128, 16], F32)
                selm = sbuf(pc, "selm", [128, 64], F32)
                nm = sbuf(pc, "nm", [128, 64], BF16)
                nmTs = [sbuf(pc, "nmT%d" % i, [64, 4, 128], BF16) for i in range(2)]
                pS = [psum(pc, "pS%d" % i, [128, 512], F32) for i in range(3)]
                pO = [psum(pc, "pO%d" % i, [128, 512], F32) for i in range(4)]
                pM = psum(pc, "pM", [128, 1024], BF16)

                S.dma("pool", w1sb[:], cw1.ap().rearrange("e s d h -> d e s h"), w=[w1sb], key=w1sb)
                S.dma("pool", w2sb[:], cw2.ap().rearrange("e h d -> h e d"), w=[w2sb], key=w2sb)
                S.dma("pool", posb[:], posT.ap(), w=[posb], key=posb)
                S.dma("sp", b1sb[:], b1T.ap(), w=[b1sb], key=b1sb)
                S.dma("sp", I0[:], I0_d.ap(), w=[I0], key=I0)
                S.dma("sp", Jm[:], Jm_d.ap(), w=[Jm], key=Jm)
                S.dma("pool", trib[:], tri_d.ap(), w=[trib], key=trib)
                S.dma("pool", Ebig[:], Ebig_d.ap(), w=[Ebig], key=Ebig)
                S.dve(lambda e: e.memset(KcT[:], 0.0), w=[KcT])
                S.dve(lambda e: e.memset(VcA[:], 0.0), w=[VcA])
                S.dve(lambda e: e.memset(VcA[:, :, 128:129], 1.0), w=[VcA])
                S.dve(lambda e: e.memset(col0[:], 0.0), w=[col0])
                S.dve(lambda e: e.memset(col0[:, 0:1], 1.0), w=[col0])
                S.dma("pool", VcA[:, :, 129:193], cover_d.ap().rearrange("(t p) j -> p t j", p=128), w=[VcA], key=VcA)

                def bias_mm(e):
                    ins = None
                    for ee in range(2):
                        for s_ in range(32):
                            ins = e.matmul(pS[0][:, ee:ee + 1], w1sb[:, ee, s_, :], posb[:, ee * 32 + s_:ee * 32 + s_ + 1],
                                           start=(ee == 0 and s_ == 0), stop=(s_ == 31), skip_group_check=True)
                    return ins
                S.pe(bias_mm, r=[w1sb, posb], w=[pS[0]])
                S.dve(lambda e: e.tensor_tensor(out=biasb[:], in0=pS[0][:, 0:2], in1=b1sb[:], op=ALU.add), r=[pS[0], b1sb], w=[biasb])

                def compress(srcT, nblk, ncols_pad, KcT_out, Vc_out_fn):
                    for ee in range(2):
                        for c0 in range(0, nblk, 512):
                            cn = min(512, nblk - c0)
                            P = pS[(c0 // 512) % 2]

                            def hmm(e, ee=ee, c0=c0, cn=cn, P=P):
                                ins = None
                                src = srcT(ee)
                                for rs_ in range(32):
                                    lo = rs_ + 16 * c0
                                    ins = e.matmul(P[:, :cn], w1sb[:, ee, rs_, :], src[:, lo:lo + 16 * (cn - 1) + 1:16],
                                                   start=(rs_ == 0), stop=(rs_ == 31))
                                return ins
                            S.pe(hmm, r=[w1sb, KT, VcT], w=[P])
                            S.act(lambda e, ee=ee, cn=cn, P=P: e.activation(out=hx[:, :cn], in_=P[:, :cn], func=AF.Identity,
                                                                             bias=biasb[:, ee:ee + 1]), r=[P, biasb], w=[hx])
                            S.dve(lambda e, cn=cn: e.tensor_tensor(out=ht[:, :cn], in0=hx[:, :cn], in1=hx[:, :cn], op=ALU.mult), r=[hx], w=[ht])
                            S.dve(lambda e, cn=cn: e.tensor_scalar(out=ht[:, :cn], in0=ht[:, :cn], scalar1=0.044715, scalar2=1.0,
                                                                   op0=ALU.mult, op1=ALU.add), r=[ht], w=[ht])
                            S.dve(lambda e, cn=cn: e.tensor_tensor(out=ht[:, :cn], in0=ht[:, :cn], in1=hx[:, :cn], op=ALU.mult), r=[ht, hx], w=[ht])
                            S.act(lambda e, cn=cn: e.activation(out=ht[:, :cn], in_=ht[:, :cn], func=AF.Sigmoid, scale=1.5957691216),
                                  r=[ht], w=[ht])
                            H = hT[ee]
                            S.dve(lambda e, cn=cn, H=H: e.tensor_tensor(out=H[:, :cn], in0=hx[:, :cn], in1=ht[:, :cn], op=ALU.mult),
                                  r=[hx, ht], w=[H])
                            if ee == 0:
                                S.pe(lambda e, cn=cn, H=H: e.matmul(pO[0][:, :cn], w2sb[:, 0, :], H[:, :cn], start=True, stop=True),
                                     r=[w2sb, H], w=[pO[0]])
                                S.act(lambda e, cn=cn, c0=c0: e.copy(out=KcT_out[:, c0:c0 + cn], in_=pO[0][:, :cn]), r=[pO[0]], w=[KcT])
                            else:
                                for t0 in range(0, cn, 128):
                                    nb = min(128, cn - t0)
                                    S.pe(lambda e, nb=nb, t0=t0, H=H: e.matmul(pO[1][:nb, 0:128], H[:, t0:t0 + nb], w2sb[:, 1, :],
                                                                               start=True, stop=True), r=[w2sb, H], w=[pO[1]])
                                    S.act(lambda e, nb=nb, ct=(c0 + t0) // 128: e.copy(out=Vc_out_fn(ct, nb), in_=pO[1][:nb, 0:128]),
                                          r=[pO[1]], w=[VcA])

                compress(lambda ee: (KT[:, 0, :] if ee == 0 else VcT[:, :]), 255, 256, KcT, lambda ct, nb: VcA[:nb, ct, 0:128])

                HB = [(0, 0), (0, 256), (1, 0), (1, 256)]

                def pv_group(Pb, E, rhs_fn, ncol, first, r):
                    def f(e):
                        ins = None
                        for h in range(4):
                            bk, co = HB[h]
                            ins = e.matmul(Pb[bk][:, co:co + ncol], E[:, h, :], rhs_fn(), start=(first and h % 2 == 0), stop=False,
                                           skip_group_check=True)
                        return ins
                    S.pe(f, r=[E] + r, w=[Pb[0], Pb[1]])

                def finish_branch(Pb, i, br, first_branch, ocat):
                    for h in range(4):
                        bk, co = HB[h]
                        S.dve(lambda e, h=h, bk=bk, co=co: e.tensor_copy(out=rs[:, h:h + 1], in_=Pb[bk][:, co + 128:co + 129]),
                              r=[Pb[bk]], w=[rs])
                    S.dve(lambda e: e.tensor_scalar(out=rs[:, 0:4], in0=rs[:, 0:4], scalar1=1e-30, scalar2=None, op0=ALU.max), r=[rs], w=[rs])
                    S.dve(lambda e: e.reciprocal(out=rs[:, 0:4], in_=rs[:, 0:4]), r=[rs], w=[rs])
                    S.dve(lambda e, i=i, br=br: e.tensor_tensor(out=rs[:, 4:8], in0=rs[:, 0:4], in1=GL[:, i, 4 * br:4 * br + 4], op=ALU.mult),
                          r=[rs, GL], w=[rs])
                    for h in range(4):
                        bk, co = HB[h]
                        if first_branch:
                            S.dve(lambda e, h=h, bk=bk, co=co: e.tensor_scalar(out=ocat[:, h, :], in0=Pb[bk][:, co:co + 128],
                                                                               scalar1=rs[:, 4 + h:5 + h], scalar2=None, op0=ALU.mult),
                                  r=[Pb[bk], rs], w=[ocat])
                        else:
                            S.dve(lambda e, h=h, bk=bk, co=co: e.scalar_tensor_tensor(out=ocat[:, h, :], in0=Pb[bk][:, co:co + 128],
                                                                                      scalar=rs[:, 4 + h:5 + h], in1=ocat[:, h, :],
                                                                                      op0=ALU.mult, op1=ALU.add),
                                  r=[Pb[bk], rs, ocat], w=[ocat])

                pair_ctr = [0]

                def qk_exp(i, lhsT_fn, lr, with_mask, nmT=None):
                    nmT = nmT if nmT is not None else nmTs[0]
                    x = pair_ctr[0] % 3
                    pair_ctr[0] += 1
                    P, E = pS[x], Eb[x]

                    def f(e):
                        ins = e.matmul(P[:, :], lhsT_fn(), QT[:, :, i * 128:i * 128 + 128], start=True, stop=not with_mask)
                        if with_mask is not False:
                            ins = e.matmul(P[:, :], Ebig[:, with_mask * 128:with_mask * 128 + 128], nmT[:, :, :], start=False, stop=True)
                        return ins
                    S.pe(f, r=[QT, Ebig, nmT] + lr, w=[P])
                    S.act(lambda e: e.activation(out=E[:, :, :], in_=P[:, :].rearrange("p (h q) -> p h q", h=4), func=AF.Exp, scale=SCALE),
                          r=[P], w=[E])
                    return E

                def mul_mask(E, m_ap, r):
                    S.dve(lambda e: e.tensor_tensor(out=E[:, :, :], in0=E[:, :, :], in1=m_ap.unsqueeze(1).broadcast_to([128, 4, 128]),
                                                    op=ALU.mult), r=[E] + r, w=[E])

                def stage1(i):
                    ocat = ocats[i % 2]
                    nmT = nmTs[i % 2]
                    Pb = pO[0:2]
                    n_ct = 1 if i < 16 else 2
                    for ct in range(n_ct):
                        E = qk_exp(i, lambda ct=ct: KcT[:, ct * 128:ct * 128 + 128], [KcT], False)
                        cval = float(128 * i - 2048 * ct - 31)
                        S.dve(lambda e, cval=cval: e.tensor_scalar(out=mk[:, :], in0=I0[:, :], scalar1=cval, scalar2=0.0,
                                                                   op0=ALU.add, op1=ALU.is_ge), r=[I0], w=[mk])
                        mul_mask(E, mk[:, :], [mk])
                        pv_group(Pb, E, lambda ct=ct: VcA[:, ct, 0:193], 193, ct == 0, [VcA])
                    for h in range(4):
                        bk, co = HB[h]
                        S.dve(lambda e, h=h, bk=bk, co=co: e.tensor_copy(out=rs[:, h:h + 1], in_=Pb[bk][:, co + 128:co + 129]),
                              r=[Pb[bk]], w=[rs])
                    S.dve(lambda e: e.tensor_scalar(out=rs[:, 0:4], in0=rs[:, 0:4], scalar1=1e-30, scalar2=None, op0=ALU.max), r=[rs], w=[rs])
                    S.dve(lambda e: e.reciprocal(out=rs[:, 0:4], in_=rs[:, 0:4]), r=[rs], w=[rs])
                    for h in range(4):
                        bk, co = HB[h]
                        if h == 0:
                            S.dve(lambda e, bk=bk, co=co: e.tensor_scalar(out=imp[:, :], in0=Pb[bk][:, co + 129:co + 193], scalar1=rs[:, 0:1],
                                                                          scalar2=None, op0=ALU.mult), r=[Pb[bk], rs], w=[imp])
                        else:
                            S.dve(lambda e, h=h, bk=bk, co=co: e.scalar_tensor_tensor(out=imp[:, :], in0=Pb[bk][:, co + 129:co + 193],
                                                                                      scalar=rs[:, h:h + 1], in1=imp[:, :],
                                                                                      op0=ALU.mult, op1=ALU.add),
                                  r=[Pb[bk], rs, imp], w=[imp])
                    finish_branch(Pb, i, 0, True, ocat)
                    if i == 0:
                        dbg("ocat_cmp", ocat, ocat[:, :, :], [128, 4, 128])
                        dbg("rs_cmp", rs, rs[:, :], [128, 8])
                        dbg("imp", imp, imp[:, :], [128, 64])
                        dbg("GL", GL, GL[:, :, :], [128, NT + 1, 12])
                    S.dve(lambda e, i=i: e.tensor_scalar(out=vis[:, :], in0=Jm[:, :], scalar1=float(2 * i), scalar2=None, op0=ALU.is_le),
                          r=[Jm], w=[vis])
                    if i >= 8:
                        S.dve(lambda e, i=i: e.tensor_scalar(out=frc[:, :], in0=Jm[:, :], scalar1=float(2 * i - 1), scalar2=None, op0=ALU.is_ge),
                              r=[Jm], w=[frc])
                        S.dve(lambda e: e.tensor_tensor(out=frc[:, :], in0=frc[:, :], in1=vis[:, :], op=ALU.mult), r=[frc, vis], w=[frc])
                        S.dve(lambda e: e.tensor_tensor(out=frc[:, :], in0=frc[:, :], in1=col0[:, :], op=ALU.max), r=[frc, col0], w=[frc])
                        S.dve(lambda e: e.tensor_tensor(out=sc[0][:, :], in0=imp[:, :], in1=vis[:, :], op=ALU.mult), r=[imp, vis], w=[sc[0]])
                        S.dve(lambda e: e.tensor_scalar(out=sc[1][:, :], in0=vis[:, :], scalar1=-1.0, scalar2=1e9, op0=ALU.add, op1=ALU.mult),
                              r=[vis], w=[sc[1]])
                        S.dve(lambda e: e.tensor_tensor(out=sc[0][:, :], in0=sc[0][:, :], in1=sc[1][:, :], op=ALU.add), r=[sc[0], sc[1]], w=[sc[0]])
                        S.dve(lambda e: e.scalar_tensor_tensor(out=sc[0][:, :], in0=frc[:, :], scalar=2e9, in1=sc[0][:, :],
                                                               op0=ALU.mult, op1=ALU.add), r=[frc, sc[0]], w=[sc[0]])
                        S.dve(lambda e: e.max(out=m8[:, 0:8], in_=sc[0][:, :]), r=[sc[0]], w=[m8])
                        S.dve(lambda e: e.match_replace(out=sc[1][:, :], in_to_replace=m8[:, 0:8], in_values=sc[0][:, :], imm_value=-3e38),
                              r=[sc[0], m8], w=[sc[1]])
                        S.dve(lambda e: e.max(out=m8[:, 8:16], in_=sc[1][:, :]), r=[sc[1]], w=[m8])
                        S.dve(lambda e: e.tensor_scalar(out=selm[:, :], in0=sc[0][:, :], scalar1=m8[:, 15:16], scalar2=None, op0=ALU.is_ge),
                              r=[sc[0], m8], w=[selm])
                        S.dve(lambda e: e.tensor_tensor(out=selm[:, :], in0=selm[:, :], in1=vis[:, :], op=ALU.mult), r=[selm, vis], w=[selm])
                        SM = selm
                    else:
                        SM = vis
                    S.dve(lambda e, SM=SM: e.tensor_scalar(out=nm[:, :], in0=SM[:, :], scalar1=-NEG, scalar2=NEG, op0=ALU.mult, op1=ALU.add),
                          r=[SM], w=[nm])

                def stage1b(i):
                    nmT = nmTs[i % 2]
                    S.pe(lambda e: e.transpose(pM[:64, 0:128], nm[:, :], identb[:, :]), r=[nm, identb], w=[pM])
                    S.act(lambda e: e.copy(out=nmT[:, :, :], in_=pM[:64, 0:128].unsqueeze(1).broadcast_to([64, 4, 128])), r=[pM], w=[nmT])

                def stage2(i):
                    ocat = ocats[i % 2]
                    nmT = nmTs[i % 2]
                    pairs = []
                    PbS = pO[2:4]
                    for kt in range(i + 1):
                        pairs.append((lambda kt=kt: KT[:, 1, kt * 128:kt * 128 + 128], kt, (trib[:, 0:128] if kt == i else None),
                                      PbS, lambda kt=kt: VA[:, kt, 0, 0:129], kt == 0, 1 if kt == i else None))
                    PbW = pO[0:2]
                    kts = [kt for kt in range(i - 4, i + 1) if kt >= 0]
                    for kt in kts:
                        m_ = trib[:, 0:128] if kt == i else (trib[:, 128:256] if kt == i - 4 else None)
                        pairs.append((lambda kt=kt: KT[:, 2, kt * 128:kt * 128 + 128], False, m_,
                                      PbW, lambda kt=kt: VA[:, kt, 1, 0:129], kt == kts[0], 2 if kt == kts[-1] else None))
                    En = qk_exp(i, pairs[0][0], [KT], pairs[0][1], nmT)
                    for k_, (lf, wm, m_, Pb_, rf, first_, fin_) in enumerate(pairs):
                        E = En
                        if k_ + 1 < len(pairs):
                            En = qk_exp(i, pairs[k_ + 1][0], [KT], pairs[k_ + 1][1], nmT)
                        if m_ is not None:
                            mul_mask(E, m_, [trib])
                        pv_group(Pb_, E, rf, 129, first_, [VA])
                        if fin_ is not None:
                            finish_branch(Pb_, i, fin_, False, ocat)
                    if i == 0:
                        pass
                    OB = ocb[i % 2]
                    S.act(lambda e, OB=OB: e.copy(out=OB[:, :], in_=ocat[:, :, :].rearrange("p h d -> p (h d)")), r=[ocat], w=[OB])
                    S.dma("sp", o_loc.ap()[i * 128:i * 128 + 128, :], OB[:, :], r=[OB], w=[o_locT[i // 8]], key=OB)
                    if i % 8 == 7:
                        c = i // 8
                        allgather(o_loc.ap()[c * 1024:c * 1024 + 1024, :].rearrange("(p a) f -> p (a f)", a=8),
                                  og.ap()[c * 4096:c * 4096 + 4096, :].rearrange("(q a) f -> q (a f)", a=8), o_locT[c], ogT)
                if NTQ > 0:
                    stage1(0)
                    stage1b(0)
                for i in range(NTQ):
                    if i + 1 < NTQ:
                        stage1(i + 1)
                    stage2(i)
                    if i + 1 < NTQ:
                        stage1b(i + 1)
                S.barrier()
        phase_c()

        pp.close()

        S.dma("sp", win_s.ap()[:, 0:508], wstate.ap()[:, 4:512], key=Buf("wcopy"), final=True)
        def phase_s():
            with contextlib.ExitStack() as ps_:
                NP = PAST // 128
                NB = 257
                w1sb = sbuf(ps_, "w1sb_s", [128, 2, 32, 128], BF16)
                w2sb = sbuf(ps_, "w2sb_s", [128, 2, 128], BF16)
                posb = sbuf(ps_, "posb_s", [128, 64], BF16)
                b1sb = sbuf(ps_, "b1sb_s", [128, 2], F32)
                biasb = sbuf(ps_, "biasb_s", [128, 2], F32)
                Ebig = sbuf(ps_, "Ebig_s", [64, SEQ], BF16)
                smask = sbuf(ps_, "smask_s", [16, 5, 16], BF16)
                sel16 = sbuf(ps_, "sel16_s", [16, 68], F32)
                sel16b = sbuf(ps_, "sel16b_s", [16, 4], BF16)
                hmask = sbuf(ps_, "hmask_s", [16, 12], F32)
                big = sbuf(ps_, "bigS", [128, 2, 16512], BF16)
                XcT = big
                KsT = big
                Vs = big
                Vs3 = big[:, 1, :].rearrange("p (g c) -> p g c", c=129)
                E2 = sbuf(ps_, "E2s", [64, 128], BF16)
                Rrep = sbuf(ps_, "Rrep_s", [32, 128], F32)
                KwT = sbuf(ps_, "KwT", [128, 512], BF16)
                Vw = sbuf(ps_, "Vw", [128, 4, 130], BF16)
                stg = [sbuf(ps_, "stg%d" % i, [128, 8192], F32) for i in range(2)]
                wst = sbuf(ps_, "wst", [128, 4, 2, 128], F32)
                KcT = sbuf(ps_, "KcT_s", [128, 1024], BF16)
                Vc = sbuf(ps_, "Vc_s", [128, 8, 392], BF16)
                hx = sbuf(ps_, "hx_s", [128, 512], F32)
                ht = sbuf(ps_, "ht_s", [128, 512], F32)
                hT = [sbuf(ps_, "hT_s%d" % i, [128, 512], BF16) for i in range(2)]
                Es = [sbuf(ps_, "Es%d" % i, [128, 16], BF16) for i in range(2)]
                rs = sbuf(ps_, "rs_s", [16, 8], F32)
                U = sbuf(ps_, "U_s", [16, 260], BF16)
                gt = sbuf(ps_, "gt_s", [16, 12], F32)
                gs = sbuf(ps_, "gs_s", [16, 4], F32)
                ocat = sbuf(ps_, "ocat_s", [16, 128], F32)
                ocb = sbuf(ps_, "ocb_s", [16, 128], BF16)
                imp = sbuf(ps_, "imp_s", [4, 260], F32)
                sc = [sbuf(ps_, "sc_s%d" % i, [4, 260], F32) for i in range(2)]
                m8 = sbuf(ps_, "m8_s", [4, 16], F32)
                nm = sbuf(ps_, "nm_s", [4, 320], BF16)
                nmT = sbuf(ps_, "nmT_s", [64, 5, 4, 4], BF16)
                pS = [psum(ps_, "qS%d" % i, [128, 512], F32) for i in range(2)]
                pO = [psum(ps_, "qO%d" % i, [128, 512], F32) for i in range(3)]
                pTf = [psum(ps_, "qT%d" % i, [128, 512], F32) for i in range(2)]
                pM = psum(ps_, "qM", [128, 1024], BF16)

                S.dma("pool", w1sb[:], cw1.ap().rearrange("e s d h -> d e s h"), w=[w1sb], key=w1sb)
                S.dma("pool", w2sb[:], cw2.ap().rearrange("e h d -> h e d"), w=[w2sb], key=w2sb)
                S.dma("pool", posb[:], posT.ap(), w=[posb], key=posb)
                S.dma("sp", b1sb[:], b1T.ap(), w=[b1sb], key=b1sb)
                S.dma("pool", Ebig[:], Ebig_d.ap(), w=[Ebig], key=Ebig)
                S.dma("pool", smask[:], smask_d.ap(), w=[smask], key=smask)
                S.dma("sp", sel16[:], sel16_d.ap(), w=[sel16], key=sel16)
                S.dma("pool", sel16b[:], sel16_d.ap()[:, 0:4], w=[sel16b], key=sel16b)
                S.dma("sp", hmask[:], hmask_d.ap(), w=[hmask], key=hmask)
                S.dve(lambda e: e.memset(Vw[:, :, 128:129], 1.0), w=[Vw])
                S.dve(lambda e: e.memset(KcT[:], 0.0), w=[KcT])
                S.dve(lambda e: e.memset(Vc[:], 0.0), w=[Vc])
                S.dve(lambda e: e.memset(Vc[:, 0:7, 128:129], 1.0), w=[Vc])
                S.dve(lambda e: e.memset(Vc[:127, 7, 128:129], 1.0), w=[Vc])
                S.dma("pool", Vc[:, :, 129:386], cover_s.ap().rearrange("(t p) j -> p t j", p=128), w=[Vc], key=Vc)

                def bias_mm(e):
                    ins = None
                    for ee in range(2):
                        for s_ in range(32):
                            ins = e.matmul(pS[0][:, ee:ee + 1], w1sb[:, ee, s_, :], posb[:, ee * 32 + s_:ee * 32 + s_ + 1],
                                           start=(ee == 0 and s_ == 0), stop=(s_ == 31), skip_group_check=True)
                    return ins
                S.pe(bias_mm, r=[w1sb, posb], w=[pS[0]])
                S.dve(lambda e: e.tensor_tensor(out=biasb[:], in0=pS[0][:, 0:2], in1=b1sb[:], op=ALU.add), r=[pS[0], b1sb], w=[biasb])

                pti = sbuf(ps_, "pti", [32, 16], I32)
                ptf = sbuf(ps_, "ptf", [32, 16], F32)
                idf = sbuf(ps_, "idf", [128, 16], F32)
                idi = sbuf(ps_, "idi", [128, 16], I32)
                pix = sbuf(ps_, "pix", [128, 1], F32)
                S.dma("sp", pti[:], ptabT.ap(), w=[pti], key=pti)
                S.dma("sp", pix[:], pidx_d.ap(), w=[pix], key=pix)
                S.dma("sp", Rrep[:], Rrep_d.ap(), w=[Rrep], key=Rrep)
                S.dma("pool", E2[:], E2_d.ap(), w=[E2], key=E2)
                S.dve(lambda e: e.tensor_copy(out=ptf[:], in_=pti[:]), r=[pti], w=[ptf])
                S.pe(lambda e: e.matmul(pS[1][:, 0:16], Rrep[:, :], ptf[:, :], start=True, stop=True), r=[Rrep, ptf], w=[pS[1]])
                S.dve(lambda e: e.tensor_scalar(out=idf[:], in0=pS[1][:, 0:16], scalar1=4.0, scalar2=pix[:, 0:1], op0=ALU.mult, op1=ALU.add),
                      r=[pS[1], pix], w=[idf])
                S.dve(lambda e: e.tensor_copy(out=idi[:], in_=idf[:]), r=[idf], w=[idi])
                gctr = [0]

                def qgather(cache, s_, q):
                    G = stg[gctr[0] % 2]
                    gctr[0] += 1
                    rows = cache.ap().rearrange("g (u t) e d -> (g u) (t e d)", u=4)
                    k = s_ * 4 + q
                    S.custom_dma("pool", lambda e: e.indirect_dma_start(
                        out=G[:, :], out_offset=None, in_=rows,
                        in_offset=bass.IndirectOffsetOnAxis(ap=idi[:, k:k + 1], axis=0)), r=[idi], w=[G], key=G)
                    return G

                for s_ in range(4):
                    ev = 0
                    for q in range(4):
                        G = qgather(ccache, s_, q)
                        G3 = G[:, :].rearrange("p (t e d) -> p t e d", t=32, e=2)
                        for ee in range(2):
                            for t0 in range(0, 32, 4):
                                P = pTf[ev % 2]

                                def trp(e, G3=G3, ee=ee, t0=t0, P=P):
                                    ins = None
                                    for j in range(4):
                                        ins = e.transpose(P[:, j * 128:j * 128 + 128], G3[:, t0 + j, ee, :], identf[:, :])
                                    return ins
                                S.pe(trp, r=[G, identf], w=[P])
                                dst = XcT[:, ee, 4096 * q:4096 * q + 4096].rearrange("d (p t) -> d t p", t=32)[:, t0:t0 + 4, :]
                                src = P[:, 0:512].rearrange("d (j p) -> d j p", j=4)
                                if ev % 2 == 0:
                                    S.act(lambda e, dst=dst, src=src: e.copy(out=dst, in_=src), r=[P], w=[XcT])
                                else:
                                    S.dve(lambda e, dst=dst, src=src: e.tensor_copy(out=dst, in_=src), r=[P], w=[XcT])
                                ev += 1
                    for ee in range(2):
                        for c0 in (0, 512):
                            cn = 512 if c0 == 0 else 511
                            P = pS[(c0 // 512) % 2]

                            def hmm(e, ee=ee, c0=c0, cn=cn, P=P):
                                ins = None
                                for rs_ in range(32):
                                    lo = rs_ + 16 * c0
                                    ins = e.matmul(P[:, :cn], w1sb[:, ee, rs_, :], XcT[:, ee, lo:lo + 16 * (cn - 1) + 1:16],
                                                   start=(rs_ == 0), stop=(rs_ == 31))
                                return ins
                            S.pe(hmm, r=[w1sb, XcT], w=[P])
                            S.act(lambda e, ee=ee, cn=cn, P=P: e.activation(out=hx[:, :cn], in_=P[:, :cn], func=AF.Identity,
                                                                             bias=biasb[:, ee:ee + 1]), r=[P, biasb], w=[hx])
                            S.dve(lambda e, cn=cn: e.tensor_tensor(out=ht[:, :cn], in0=hx[:, :cn], in1=hx[:, :cn], op=ALU.mult), r=[hx], w=[ht])
                            S.dve(lambda e, cn=cn: e.tensor_scalar(out=ht[:, :cn], in0=ht[:, :cn], scalar1=0.044715, scalar2=1.0,
                                                                   op0=ALU.mult, op1=ALU.add), r=[ht], w=[ht])
                            S.dve(lambda e, cn=cn: e.tensor_tensor(out=ht[:, :cn], in0=ht[:, :cn], in1=hx[:, :cn], op=ALU.mult), r=[ht, hx], w=[ht])
                            S.act(lambda e, cn=cn: e.activation(out=ht[:, :cn], in_=ht[:, :cn], func=AF.Sigmoid, scale=1.5957691216),
                                  r=[ht], w=[ht])
                            H = hT[ee]
                            S.dve(lambda e, cn=cn, H=H: e.tensor_tensor(out=H[:, :cn], in0=hx[:, :cn], in1=ht[:, :cn], op=ALU.mult),
                                  r=[hx, ht], w=[H])
                            if ee == 0:
                                S.pe(lambda e, cn=cn, H=H: e.matmul(pO[0][:, :cn], w2sb[:, 0, :], H[:, :cn], start=True, stop=True),
                                     r=[w2sb, H], w=[pO[0]])
                                S.act(lambda e, cn=cn, c0=c0: e.copy(out=KcT[:, c0:c0 + cn], in_=pO[0][:, :cn]), r=[pO[0]], w=[KcT])
                            else:
                                for t0 in range(0, cn, 128):
                                    nb = min(128, cn - t0)
                                    S.pe(lambda e, nb=nb, t0=t0, H=H: e.matmul(pO[1][:nb, 0:128], H[:, t0:t0 + nb], w2sb[:, 1, :],
                                                                               start=True, stop=True), r=[w2sb, H], w=[pO[1]])
                                    S.act(lambda e, nb=nb, ct=(c0 + t0) // 128: e.copy(out=Vc[:nb, ct, 0:128], in_=pO[1][:nb, 0:128]),
                                          r=[pO[1]], w=[Vc])

                    pc_ = [0]
                    Qs = QTs[:, :, 4 * s_:4 * s_ + 4]

                    def qk16(lhsT, nk, lr, maskchunk=None, Qs=Qs):
                        x = pc_[0] % 2
                        pc_[0] += 1
                        P, E = pS[x], Es[x]

                        def f(e):
                            ins = e.matmul(P[:nk, 0:16], lhsT, Qs, start=True, stop=(maskchunk is None))
                            if maskchunk is not None:
                                ch, kt = maskchunk
                                ins = e.matmul(P[:nk, 0:16], E2[:, :], nmT[:, ch, :, :], start=False, stop=True)
                            return ins
                        S.pe(f, r=[QTs, E2, nmT] + lr, w=[P])
                        S.act(lambda e: e.activation(out=E[:nk, :], in_=P[:nk, 0:16], func=AF.Exp, scale=SCALE), r=[P], w=[E])
                        return E

                    def pv16(Pacc, E, nk, rhs, ncol, first, last, r):
                        S.pe(lambda e: e.matmul(Pacc[:16, 0:ncol], E[:nk, :], rhs, start=first, stop=last), r=[E] + r, w=[Pacc])

                    def fin16(Pacc, br, first_branch):
                        S.dve(lambda e: e.tensor_scalar(out=rs[:, 0:1], in0=Pacc[:16, 128:129], scalar1=1e-30, scalar2=None, op0=ALU.max),
                              r=[Pacc], w=[rs])
                        S.dve(lambda e: e.reciprocal(out=rs[:, 0:1], in_=rs[:, 0:1]), r=[rs], w=[rs])
                        S.dve(lambda e: e.tensor_tensor(out=rs[:, 1:2], in0=rs[:, 0:1], in1=gs[:, br:br + 1], op=ALU.mult), r=[rs, gs], w=[rs])
                        if first_branch:
                            S.dve(lambda e: e.tensor_scalar(out=ocat[:, :], in0=Pacc[:16, 0:128], scalar1=rs[:, 1:2], scalar2=None, op0=ALU.mult),
                                  r=[Pacc, rs], w=[ocat])
                        else:
                            S.dve(lambda e: e.scalar_tensor_tensor(out=ocat[:, :], in0=Pacc[:16, 0:128], scalar=rs[:, 1:2], in1=ocat[:, :],
                                                                   op0=ALU.mult, op1=ALU.add), r=[Pacc, rs, ocat], w=[ocat])

                    S.pe(lambda e, s_=s_: e.matmul(pO[2][:16, 0:12], sel16[:, 4 + 16 * s_:4 + 16 * s_ + 16], GL[:16, NT, :], start=True, stop=True),
                         r=[sel16, GL], w=[pO[2]])
                    S.dve(lambda e: e.tensor_tensor(out=gt[:, :], in0=pO[2][:16, 0:12], in1=hmask[:, :], op=ALU.mult), r=[pO[2], hmask], w=[gt])
                    S.dve(lambda e: e.tensor_reduce(out=gs[:, 0:3], in_=gt[:, :].rearrange("p (b h) -> p b h", b=3), axis=AX.X, op=ALU.add),
                          r=[gt], w=[gs])

                    def run_pairs(plist):
                        En = qk16(*plist[0][0][:3], **plist[0][0][3])
                        for k_, (qa, post, pa) in enumerate(plist):
                            E = En
                            if k_ + 1 < len(plist):
                                nq = plist[k_ + 1][0]
                                En = qk16(*nq[:3], **nq[3])
                            if post is not None:
                                post(E)
                            pv16(pa[0], E, *pa[1:])

                    run_pairs([((KcT[:, ct * 128:ct * 128 + 128], 128, [KcT], {}), None,
                                (pO[0], 128, Vc[:, ct, 0:386], 386, ct == 0, ct == 7, [Vc])) for ct in range(8)])
                    fin16(pO[0], 0, True)
                    S.dve(lambda e: e.tensor_scalar(out=U[:, 0:257], in0=pO[0][:16, 129:386], scalar1=rs[:, 0:1], scalar2=None, op0=ALU.mult),
                          r=[pO[0], rs], w=[U])
                    S.pe(lambda e: e.matmul(pO[2][:4, 0:257], sel16b[:, 0:4], U[:, 0:257], start=True, stop=True), r=[sel16b, U], w=[pO[2]])
                    S.dve(lambda e: e.tensor_copy(out=sc[0][:, 0:257], in_=pO[2][:4, 0:257]), r=[pO[2]], w=[sc[0]])
                    S.dve(lambda e: e.memset(sc[0][:, 0:1], 2e9), w=[sc[0]])
                    S.dve(lambda e: e.memset(sc[0][:, 255:257], 2e9), w=[sc[0]])
                    S.dve(lambda e: e.max(out=m8[:, 0:8], in_=sc[0][:, 0:257]), r=[sc[0]], w=[m8])
                    S.dve(lambda e: e.match_replace(out=sc[1][:, 0:257], in_to_replace=m8[:, 0:8], in_values=sc[0][:, 0:257], imm_value=-3e38),
                          r=[sc[0], m8], w=[sc[1]])
                    S.dve(lambda e: e.max(out=m8[:, 8:16], in_=sc[1][:, 0:257]), r=[sc[1]], w=[m8])
                    S.dve(lambda e: e.memset(nm[:, :], 0.0), w=[nm])
                    S.dve(lambda e: e.tensor_scalar(out=sc[1][:, 0:257], in0=sc[0][:, 0:257], scalar1=m8[:, 15:16], scalar2=None, op0=ALU.is_ge),
                          r=[sc[0], m8], w=[sc[1]])
                    S.dve(lambda e: e.tensor_scalar(out=nm[:, 0:257], in0=sc[1][:, 0:257], scalar1=-NEG, scalar2=NEG, op0=ALU.mult, op1=ALU.add),
                          r=[sc[1]], w=[nm])

                    def trn(e):
                        ins = None
                        for ch in range(5):
                            ins = e.transpose(pM[:64, ch * 4:ch * 4 + 4], nm[:, ch * 64:ch * 64 + 64], identb[:4, :4])
                        return ins
                    S.pe(trn, r=[nm, identb], w=[pM])
                    S.act(lambda e: e.copy(out=nmT[:, :, :, :], in_=pM[:64, 0:20].rearrange("p (c q) -> p c q", c=5).unsqueeze(2)
                                           .broadcast_to([64, 5, 4, 4])), r=[pM], w=[nmT])

                    S.dve(lambda e: e.memset(Vs3[:, :, 128:129], 1.0), w=[Vs])
                    ev = 0
                    for q in range(4):
                        G = qgather(scache, s_, q)
                        G3 = G[:, :].rearrange("p (t e d) -> p t e d", t=32, e=2)
                        for t0 in range(0, 32, 4):
                            P = pTf[ev % 2]
                            ev += 1
                            kt0 = q * 32 + t0

                            def trk(e, G3=G3, t0=t0, P=P):
                                ins = None
                                for j in range(4):
                                    ins = e.transpose(P[:, j * 128:j * 128 + 128], G3[:, t0 + j, 0, :], identf[:, :])
                                return ins
                            S.pe(trk, r=[G, identf], w=[P])
                            S.act(lambda e, P=P, kt0=kt0: e.copy(out=KsT[:, 0, kt0 * 128:kt0 * 128 + 512], in_=P[:, 0:512]), r=[P], w=[KsT])
                            S.dve(lambda e, G3=G3, t0=t0, kt0=kt0: e.tensor_copy(out=Vs3[:, kt0:kt0 + 4, 0:128], in_=G3[:, t0:t0 + 4, 1, :]),
                                  r=[G], w=[Vs])
                    def newmask(E, s_=s_):
                        S.dve(lambda e: e.tensor_tensor(out=E[:16, :], in0=E[:16, :], in1=smask[:, s_, :], op=ALU.mult), r=[E, smask], w=[E])

                    def oldmask(E):
                        S.dve(lambda e: e.tensor_tensor(out=E[:16, :], in0=E[:16, :], in1=smask[:, 4, :], op=ALU.mult), r=[E, smask], w=[E])

                    pl = [((KsT[:, 0, kt * 128:kt * 128 + 128], 128, [KsT], {"maskchunk": (kt // 32, kt)}), None,
                           (pO[1], 128, Vs3[:, kt, 0:129], 129, kt == 0, False, [Vs])) for kt in range(NP)]
                    pl.append(((KTs[:, 1, :], 16, [KTs], {}), newmask, (pO[1], 16, VAs[:, 0, 0:129], 129, False, True, [VAs])))
                    run_pairs(pl)
                    fin16(pO[1], 1, False)
                    S.dma("sp", wst[:, :, :, :], wstate.ap()[s_].rearrange("(t p) e d -> p t e d", p=128), w=[wst], key=wst)
                    for t_ in range(4):
                        P = pTf[t_ % 2]
                        S.pe(lambda e, t_=t_, P=P: e.transpose(P[:, 0:128], wst[:, t_, 0, :], identf[:, :]), r=[wst, identf], w=[P])
                        S.act(lambda e, t_=t_, P=P: e.copy(out=KwT[:, t_ * 128:t_ * 128 + 128], in_=P[:, 0:128]), r=[P], w=[KwT])
                    S.dve(lambda e: e.tensor_copy(out=Vw[:, :, 0:128], in_=wst[:, :, 1, :]), r=[wst], w=[Vw])
                    pl = [((KwT[:, t_ * 128:t_ * 128 + 128], 128, [KwT], {}), (oldmask if t_ == 0 else None),
                           (pO[0], 128, Vw[:, t_, 0:129], 129, t_ == 0, False, [Vw])) for t_ in range(4)]
                    pl.append(((KTs[:, 2, :], 16, [KTs], {}), newmask, (pO[0], 16, VAs[:, 1, 0:129], 129, False, True, [VAs])))
                    run_pairs(pl)
                    fin16(pO[0], 2, False)
                    S.act(lambda e: e.copy(out=ocb[:, :], in_=ocat[:, :]), r=[ocat], w=[ocb])
                    for h in range(4):
                        S.dma("sp", o_locs.ap()[4 * s_:4 * s_ + 4, 128 * h:128 * h + 128], ocb[4 * h:4 * h + 4, :], r=[ocb], w=[o_locT[4]], key=ocb)
                S.barrier()

        if RUN_S:
            phase_s()

        allgather(o_locs.ap().rearrange("r (b x) -> (r b) x", b=8), ogs.ap().rearrange("r (b x) -> (r b) x", b=8), o_locT[4], ogT)

        TBS = [(0, 512), (512, 512), (1024, 4)]

        def token_phase(layer):
            with contextlib.ExitStack() as pd:
                hT = sbuf(pd, "hres%d" % layer, [128, 16, NTOK], F32)
                actT = sbuf(pd, "actT%d" % layer, [128, 8, NTOK], BF16)
                xnT = sbuf(pd, "xnT_d%d" % layer, [128, 16, NTOK], BF16)
                wbuf = [sbuf(pd, "wbuf%d_%d" % (i, layer), [128, 16, 512], BF16) for i in range(2)]
                xst = sbuf(pd, "xst%d" % layer, [128, D], F32)
                ost = [sbuf(pd, "ost%d_%d" % (i, layer), [128, 2, 512], BF16) for i in range(2)]
                oidx = sbuf(pd, "oidx_s%d" % layer, [128, 72], I32)
                rstd = sbuf(pd, "rstd_d%d" % layer, [128, 512], F32)
                sq = sbuf(pd, "sq_d%d" % layer, [128, 512], F32)
                rl = [sbuf(pd, "rl%d_%d" % (i, layer), [128, 512], F32) for i in range(2)]
                gT = sbuf(pd, "gT_d%d" % layer, [128, 16], F32)
                onesf = sbuf(pd, "onesf%d" % layer, [128, 128], F32)
                epsb = sbuf(pd, "epsb_d%d" % layer, [128, 1], F32)
                pz = [psum(pd, "dz%d_%d" % (i, layer), [128, 512], F32) for i in range(4)]
                pM = psum(pd, "dM%d" % layer, [128, 1024], BF16)
                pT = psum(pd, "dT%d" % layer, [128, 512], F32)
                pn = psum(pd, "dn%d" % layer, [128, 512], F32)
                ctr = {"w": 0, "z": 0, "o": 0, "r": 0}

                if layer == 0:
                    S.dma("sp", oidx[:, 0:36], oidx_d.ap(), w=[oidx], key=oidx)
                else:
                    S.dma("sp", oidx[:, :], oidx2_d.ap(), w=[oidx], key=oidx)
                S.dve(lambda e: e.memset(onesf[:], 1.0 / D), w=[onesf])
                S.dve(lambda e: e.memset(epsb[:], RMS_EPS), w=[epsb])

                if layer == 0:
                    for t in range(9):
                        n = 128 if t < 8 else 4
                        S.dma("sp", xst[:n, :], x_tok.ap()[t * 128:t * 128 + n, :], w=[xst], key=xst)
                        for q in range(4):
                            def trx(e, q=q, n=n):
                                ins = None
                                for kk in range(4):
                                    k = q * 4 + kk
                                    ins = e.transpose(pT[:, kk * 128:kk * 128 + n], xst[:n, k * 128:k * 128 + 128], identf[:n, :n])
                                return ins
                            S.pe(trx, r=[xst, identf], w=[pT])
                            S.act(lambda e, q=q, n=n, t=t: e.copy(out=hT[:, q * 4:q * 4 + 4, t * 128:t * 128 + n],
                                                                 in_=pT[:, :].rearrange("p (k t) -> p k t", k=4)[:, :, :n]), r=[pT], w=[hT])
                else:
                    S.dma("sp", hT[:, :, :], hspill.ap().rearrange("p (k t) -> p k t", k=16), r=[hspT], w=[hT], key=hT)

                def load_act(gsrc, gsrc_s, gbuf, colfn):
                    for t in range(9):
                        n = 128 if t < 8 else 4
                        O_ = ost[ctr["o"] % 2]
                        ctr["o"] += 1
                        for jj in range(2):
                            S.custom_dma("pool", lambda e, O_=O_, jj=jj, t=t, col=colfn(t, jj): e.indirect_dma_start(
                                out=O_[:, jj, :], out_offset=None, in_=(gsrc if t < 8 else gsrc_s).ap(),
                                in_offset=bass.IndirectOffsetOnAxis(ap=oidx[:, col:col + 1], axis=0)),
                                r=[oidx, gbuf], w=[O_], key=O_)

                        def tro(e, O_=O_, n=n):
                            ins = None
                            for kk in range(8):
                                jj, q = kk // 4, kk % 4
                                ins = e.transpose(pM[:, kk * 128:kk * 128 + n], O_[:n, jj, q * 128:q * 128 + 128], identb[:n, :n])
                            return ins
                        S.pe(tro, r=[O_, identb], w=[pM])
                        S.act(lambda e, n=n, t=t: e.copy(out=actT[:, :, t * 128:t * 128 + n],
                                                         in_=pM[:, :].rearrange("p (k t) -> p k t", k=8)[:, :, :n]), r=[pM], w=[actT])

                def proj_blocks(Wd, row0):
                    blks = []
                    for cb in range(4):
                        def dma(wb, cb=cb):
                            S.dma("pool", wb[:, 0:8, :], Wd.ap()[row0:row0 + 1024, cb * 512:cb * 512 + 512].rearrange("(k p) c -> p k c", p=128),
                                  w=[wb], key=wb)

                        def comp(wb, cb=cb):
                            for cc in range(4):
                                for (t0, nt) in TBS:
                                    P = pz[ctr["z"] % 4]
                                    ctr["z"] += 1

                                    def mm(e, wb=wb, cc=cc, t0=t0, nt=nt, P=P):
                                        ins = None
                                        for k in range(8):
                                            ins = e.matmul(P[:, :nt], wb[:, k, cc * 128:cc * 128 + 128], actT[:, k, t0:t0 + nt],
                                                           start=(k == 0), stop=(k == 7))
                                        return ins
                                    S.pe(mm, r=[wb, actT], w=[P])
                                    c = cb * 4 + cc
                                    S.dve(lambda e, P=P, c=c, t0=t0, nt=nt: e.tensor_tensor(out=hT[:, c, t0:t0 + nt], in0=P[:, :nt],
                                                                                             in1=hT[:, c, t0:t0 + nt], op=ALU.add),
                                          r=[P, hT], w=[hT])
                        blks.append((dma, comp))
                    return blks

                def run_blocks(blks):
                    bufs = []
                    for i, (dma, comp) in enumerate(blks):
                        if i == 0:
                            wb0 = wbuf[ctr["w"] % 2]
                            ctr["w"] += 1
                            dma(wb0)
                            bufs.append(wb0)
                        if i + 1 < len(blks):
                            wbn = wbuf[ctr["w"] % 2]
                            ctr["w"] += 1
                            blks[i + 1][0](wbn)
                            bufs.append(wbn)
                        comp(bufs[i])

                def proj_accum(Wd, row0):
                    run_blocks(proj_blocks(Wd, row0))

                def rmsnorm_T(gain_d, out_fn, wlist):
                    S.dma("sp", gT[:], gain_d.ap(), w=[gT], key=gT)
                    for (t0, nt) in TBS:
                        for k in range(16):
                            S.act(lambda e, k=k, t0=t0, nt=nt: e.activation(out=sq[:, :nt], in_=hT[:, k, t0:t0 + nt], func=AF.Square),
                                  r=[hT], w=[sq])
                            S.pe(lambda e, k=k, nt=nt: e.matmul(pn[:, :nt], onesf[:, :], sq[:, :nt], start=(k == 0), stop=(k == 15)),
                                 r=[onesf, sq], w=[pn])
                        S.act(lambda e, nt=nt: e.activation(out=rstd[:, :nt], in_=pn[:, :nt], func=AF.Sqrt, bias=epsb[:, 0:1]),
                              r=[pn, epsb], w=[rstd])
                        S.dve(lambda e, nt=nt: e.reciprocal(out=rstd[:, :nt], in_=rstd[:, :nt]), r=[rstd], w=[rstd])
                        for k in range(16):
                            S.dve(lambda e, k=k, t0=t0, nt=nt: e.scalar_tensor_tensor(out=out_fn(k, t0, nt), in0=hT[:, k, t0:t0 + nt],
                                                                                      scalar=gT[:, k:k + 1], in1=rstd[:, :nt],
                                                                                      op0=ALU.mult, op1=ALU.mult),
                                  r=[hT, gT, rstd], w=wlist)

                def ffn(W1, W2):
                    blks = []
                    for hg in range(8):
                        for sub in range(2):
                            c0 = hg * 1024 + sub * 512

                            def dma(wb, c0=c0):
                                S.dma("pool", wb[:, :, :], W1.ap()[:, c0:c0 + 512].rearrange("(k p) c -> p k c", p=128), w=[wb], key=wb)

                            def comp(wb, sub=sub):
                                for fc in range(4):
                                    for (t0, nt) in TBS:
                                        P = pz[ctr["z"] % 4]
                                        ctr["z"] += 1

                                        def mm(e, wb=wb, fc=fc, t0=t0, nt=nt, P=P):
                                            ins = None
                                            for k in range(16):
                                                ins = e.matmul(P[:, :nt], wb[:, k, fc * 128:fc * 128 + 128], xnT[:, k, t0:t0 + nt],
                                                               start=(k == 0), stop=(k == 15))
                                            return ins
                                        S.pe(mm, r=[wb, xnT], w=[P])
                                        R_ = rl[ctr["r"] % 2]
                                        ctr["r"] += 1
                                        S.act(lambda e, P=P, R_=R_, nt=nt: e.activation(out=R_[:, :nt], in_=P[:, :nt], func=AF.Relu), r=[P], w=[R_])
                                        eng = S.pool if ctr["r"] % 2 == 0 else S.dve
                                        eng(lambda e, R_=R_, nt=nt, t0=t0, kk=sub * 4 + fc: e.tensor_tensor(out=actT[:, kk, t0:t0 + nt], in0=R_[:, :nt],
                                                                                                          in1=R_[:, :nt], op=ALU.mult),
                                            r=[R_], w=[actT])
                            blks.append((dma, comp))
                        blks.extend(proj_blocks(W2, hg * 1024))
                    run_blocks(blks)

                xn_out = lambda k, t0, nt: xnT[:, k, t0:t0 + nt]
                if layer == 0:
                    for g in range(2):
                        load_act(og, ogs, ogT, lambda t, jj, g=g: t * 4 + 2 * g + jj)
                        proj_accum(w_out0, g * 1024)
                    rmsnorm_T(gffn0T, xn_out, [xnT])
                    ffn(ffn1_0, ffn2_0)
                    rmsnorm_T(gmix1T, xn_out, [xnT])
                    for cp in range(8):
                        S.dma("sp", xg_in.ap()[cp].rearrange("p (k t) -> p k t", k=2), xnT[:, 2 * cp:2 * cp + 2, :], r=[xnT],
                              w=[xg_inT[cp]], key=Buf("xgk%d" % cp))
                        allgather(xg_in.ap()[cp], xg_all.ap()[cp], xg_inT[cp], xgT)
                    S.dma("sp", hspill.ap().rearrange("p (k t) -> p k t", k=16), hT[:, :, :], r=[hT], w=[hspT], key=hT)
                else:
                    for g in range(4):
                        load_act(og2, ogs2, og2T, lambda t, jj, g=g: t * 8 + 2 * g + jj)
                        proj_accum(w_out1, g * 1024)
                    rmsnorm_T(gffn1T, xn_out, [xnT])
                    ffn(ffn1_1, ffn2_1)
                    S.dma("sp", gT[:], gfinT.ap(), w=[gT], key=gT)
                    for (t0, nt) in TBS:
                        for k in range(16):
                            S.act(lambda e, k=k, t0=t0, nt=nt: e.activation(out=sq[:, :nt], in_=hT[:, k, t0:t0 + nt], func=AF.Square),
                                  r=[hT], w=[sq])
                            S.pe(lambda e, k=k, nt=nt: e.matmul(pn[:, :nt], onesf[:, :], sq[:, :nt], start=(k == 0), stop=(k == 15)),
                                 r=[onesf, sq], w=[pn])
                        S.act(lambda e, nt=nt: e.activation(out=rstd[:, :nt], in_=pn[:, :nt], func=AF.Sqrt, bias=epsb[:, 0:1]),
                              r=[pn, epsb], w=[rstd])
                        S.dve(lambda e, nt=nt: e.reciprocal(out=rstd[:, :nt], in_=rstd[:, :nt]), r=[rstd], w=[rstd])
                        for k in range(16):
                            S.dve(lambda e, k=k, t0=t0, nt=nt: e.scalar_tensor_tensor(out=hT[:, k, t0:t0 + nt], in0=hT[:, k, t0:t0 + nt],
                                                                                      scalar=gT[:, k:k + 1], in1=rstd[:, :nt],
                                                                                      op0=ALU.mult, op1=ALU.mult),
                                  r=[hT, gT, rstd], w=[hT])
                    for t in range(9):
                        n = 128 if t < 8 else 4
                        for q in range(4):
                            def trh(e, q=q, n=n, t=t):
                                ins = None
                                for kk in range(4):
                                    ins = e.transpose(pT[:n, kk * 128:kk * 128 + 128], hT[:, q * 4 + kk, t * 128:t * 128 + n], identf[:, :])
                                return ins
                            S.pe(trh, r=[hT, identf], w=[pT])
                            S.act(lambda e, q=q, n=n: e.copy(out=xst[:n, q * 512:q * 512 + 512], in_=pT[:n, :]), r=[pT], w=[xst])
                        S.dma("sp", y_tok.ap()[t * 128:t * 128 + n, :], xst[:n, :], r=[xst], key=xst, final=True)
                S.barrier()

        def phase_e():
            with contextlib.ExitStack() as pe_:
                wr = sbuf(pe_, "wr_s", [128, 16, 3072], BF16)
                xt = [sbuf(pe_, "ext%d" % i, [128, 16, 128], BF16) for i in range(2)]
                cst = [sbuf(pe_, "ecs%d" % i, [128, 256], F32) for i in range(2)]
                St = sbuf(pe_, "St", [128, 4, 512], F32)
                Sb = sbuf(pe_, "Sb", [128, 4, 512], BF16)
                Ss = [sbuf(pe_, "Ss%d" % i, [128, 4, 512], F32) for i in range(2)]
                Ssb = [sbuf(pe_, "Ssb%d" % i, [128, 4, 512], BF16) for i in range(4)]
                qkrs = [sbuf(pe_, "qkr%d" % i, [128, 4, 256], BF16) for i in range(2)]
                Vbs = [sbuf(pe_, "Vb%d" % i, [128, 2, 512], BF16) for i in range(2)]
                sgs = [sbuf(pe_, "sg%d" % i, [128, 1024], F32) for i in range(2)]
                ob = [sbuf(pe_, "ob%d" % i, [128, 512], F32) for i in range(2)]
                jk = sbuf(pe_, "jk", [128, 512], BF16)
                rt = [sbuf(pe_, "ert%d" % i, [128, 4, 128], F32) for i in range(4)]
                qkTs = [sbuf(pe_, "qkT%d" % i, [128, 8, 128], BF16) for i in range(2)]
                QdT = sbuf(pe_, "QdT", [128, 2, 2, 128], BF16)
                QdTm = sbuf(pe_, "QdTm", [128, 4, 2, 2, 16], BF16)
                Kd = sbuf(pe_, "Kd", [128, 2, 256], BF16)
                Kdm = sbuf(pe_, "Kdm", [16, 4, 2, 256], BF16)
                AT = sbuf(pe_, "AT", [128, 2, 128], BF16)
                decT = sbuf(pe_, "decT_s", [128, 2, 128], F32)
                qdecb = sbuf(pe_, "qdecb_s", [128, 2, 128], F32)
                kdec = sbuf(pe_, "kdec_s", [128, 2], F32)
                decTs = sbuf(pe_, "decTs_s", [16, 2, 16], F32)
                kdm = sbuf(pe_, "kdm_s", [16, 4, 2], F32)
                colmask = sbuf(pe_, "colmask_s", [128, 4, 16], F32)
                gnb = sbuf(pe_, "gnb_s", [128, 2, 512], F32)
                gpow = sbuf(pe_, "gpow_s", [128, 4], F32)
                qdecs = sbuf(pe_, "qdecs_s", [128, 2, 16], F32)
                st8 = sbuf(pe_, "st8", [128, 16], F32)
                og_ = [sbuf(pe_, "og_%d" % i, [128, 2, 512], BF16) for i in range(2)]
                pz = [psum(pe_, "ez%d" % i, [128, 512], F32) for i in range(2)]
                pq = psum(pe_, "eq", [128, 1024], BF16)
                pst = psum(pe_, "est", [128, 512], F32)
                po = [psum(pe_, "eo%d" % i, [128, 512], F32) for i in range(2)]
                pu = [psum(pe_, "eu%d" % i, [128, 512], F32) for i in range(2)]

                for kc in range(4):
                    for cq in range(3):
                        S.dma("pool", wr[:, 4 * kc:4 * kc + 4, 1024 * cq:1024 * cq + 1024],
                              wr_d.ap()[512 * kc:512 * kc + 512, 1024 * cq:1024 * cq + 1024].rearrange("(k p) c -> p k c", p=128),
                              w=[wr], key=wr)
                S.dma("sp", decT[:], decT_d.ap().rearrange("p (h i) -> p h i", h=2), w=[decT], key=decT)
                S.dma("sp", qdecb[:], qdecb_d.ap().rearrange("p (h i) -> p h i", h=2), w=[qdecb], key=qdecb)
                S.dma("sp", decTs[:], decTs_d.ap().rearrange("p (h i) -> p h i", h=2), w=[decTs], key=decTs)
                S.dma("sp", kdm[:], kdm_d.ap().rearrange("p (s h) -> p s h", s=4), w=[kdm], key=kdm)
                S.dma("sp", colmask[:], colmask_d.ap().rearrange("p (s i) -> p s i", s=4), w=[colmask], key=colmask)
                S.dma("sp", gnb[:], gnb_d.ap().rearrange("p (h e) -> p h e", h=2), w=[gnb], key=gnb)
                S.dma("sp", qdecs[:], qdecs_d.ap().rearrange("p (h i) -> p h i", h=2), w=[qdecs], key=qdecs)
                S.dma("sp", kdec[:], kdec_d.ap(), w=[kdec], key=kdec)
                S.dma("sp", gpow[:], gpow_d.ap(), w=[gpow], key=gpow)
                S.dve(lambda e: e.memset(St[:], 0.0), w=[St])
                S.dve(lambda e: e.memset(Sb[:], 0.0), w=[Sb])

                xg5 = xg_all.ap().rearrange("c (j p) (k t) -> c j p k t", j=4, k=2)

                def load_tile(T_):
                    X, C = xt[T_ % 2], cst[T_ % 2]
                    if T_ < NT:
                        j, tl = T_ // 8, T_ % 8
                        for cp in range(8):
                            S.dma("sp", X[:, 2 * cp:2 * cp + 2, :], xg5[cp, j, :, :, tl * 128:tl * 128 + 128], r=[xgT], w=[X], key=X)
                        S.dma("sp", C[:, :], cs_r.ap()[T_ * 128:T_ * 128 + 128, :], w=[C], key=C)
                    else:
                        for cp in range(8):
                            for j in range(4):
                                S.dma("sp", X[:, 2 * cp:2 * cp + 2, 4 * j:4 * j + 4], xg5[cp, j, :, :, 1024:1028], r=[xgT], w=[X], key=X)
                        S.dma("sp", C[:NS, :], cs_rs.ap(), w=[C], key=C)

                load_tile(0)
                zc = [0]

                def stageE1(T_):
                    n = 128 if T_ < NT else NS
                    X, C = xt[T_ % 2], cst[T_ % 2]
                    qkr, Vb, sg, qkT = qkrs[T_ % 2], Vbs[T_ % 2], sgs[T_ % 2], qkTs[T_ % 2]
                    if T_ + 1 <= NT:
                        load_tile(T_ + 1)

                    def proj(cb, n=n, X=X):
                        P = pz[zc[0] % 2]
                        zc[0] += 1

                        def mm(e):
                            ins = None
                            for k in range(16):
                                ins = e.matmul(P[:n, :], X[:, k, :n], wr[:, k, cb * 512:cb * 512 + 512], start=(k == 0), stop=(k == 15))
                            return ins
                        S.pe(mm, r=[X, wr], w=[P])
                        return P

                    cosb = lambda n=n, C=C: C[:n, 0:128].unsqueeze(1).broadcast_to([n, 2, 128])
                    sinb = lambda n=n, C=C: C[:n, 128:256].unsqueeze(1).broadcast_to([n, 2, 128])
                    for half_ in range(2):
                        P = proj(half_)
                        z3 = P[:n, :].rearrange("p (h d) -> p h d", h=2)
                        a, b_, c_, d_ = rt

                        def emit_rope(z3=z3, P=P, half_=half_, n=n, cosb=cosb, sinb=sinb, C=C):
                            S.dve(lambda e: e.tensor_tensor(out=a[:n, :2, :], in0=z3[:, :, 0:128], in1=cosb(), op=ALU.mult), r=[P, C], w=[a])
                            S.dve(lambda e: e.tensor_tensor(out=b_[:n, :2, :], in0=z3[:, :, 128:256], in1=sinb(), op=ALU.mult), r=[P, C], w=[b_])
                            S.dve(lambda e: e.tensor_tensor(out=c_[:n, :2, :], in0=z3[:, :, 128:256], in1=cosb(), op=ALU.mult), r=[P, C], w=[c_])
                            S.dve(lambda e: e.tensor_tensor(out=d_[:n, :2, :], in0=z3[:, :, 0:128], in1=sinb(), op=ALU.mult), r=[P, C], w=[d_])
                            S.pool(lambda e: e.tensor_tensor(out=qkr[:n, 2 * half_:2 * half_ + 2, 0:128], in0=a[:n, :2, :], in1=b_[:n, :2, :],
                                                             op=ALU.subtract), r=[a, b_], w=[qkr])
                            S.pool(lambda e: e.tensor_tensor(out=qkr[:n, 2 * half_:2 * half_ + 2, 128:256], in0=c_[:n, :2, :], in1=d_[:n, :2, :],
                                                             op=ALU.add), r=[c_, d_], w=[qkr])
                        emit_rope()
                    for hh in range(2):
                        P = proj(2 + hh)
                        S.act(lambda e, P=P, hh=hh, n=n: e.copy(out=Vb[:n, hh, :], in_=P[:n, :]), r=[P], w=[Vb])
                    for hh in range(2):
                        P = proj(4 + hh)
                        S.act(lambda e, P=P, hh=hh, n=n: e.activation(out=sg[:n, hh * 512:hh * 512 + 512], in_=P[:n, :], func=AF.Silu), r=[P], w=[sg])

                def stageE1b(T_):
                    n = 128 if T_ < NT else NS
                    qkr, qkT = qkrs[T_ % 2], qkTs[T_ % 2]
                    def trqk(e, n=n):
                        ins = None
                        for a_ in range(4):
                            for c in range(2):
                                ins = e.transpose(pq[:, (a_ * 2 + c) * 128:(a_ * 2 + c) * 128 + n], qkr[:n, a_, c * 128:c * 128 + 128], identb[:n, :n])
                        return ins
                    S.pe(trqk, r=[qkr, identb], w=[pq])
                    S.act(lambda e, n=n: e.copy(out=qkT[:, :, :n], in_=pq[:, :].rearrange("p (a t) -> p a t", a=8)[:, :, :n]), r=[pq], w=[qkT])

                def stageE2(T_):
                    n = 128 if T_ < NT else NS
                    qkr, Vb, sg, qkT = qkrs[T_ % 2], Vbs[T_ % 2], sgs[T_ % 2], qkTs[T_ % 2]
                    qT4 = qkT[:, 0:4, :].rearrange("p (h c) t -> p h c t", h=2)
                    kT4 = qkT[:, 4:8, :].rearrange("p (h c) t -> p h c t", h=2)

                    if T_ < NT:
                        S.dve(lambda e: e.tensor_tensor(out=QdT[:, :, :, :], in0=qT4, in1=qdecb[:, :, :].unsqueeze(2).broadcast_to([128, 2, 2, 128]),
                                                        op=ALU.mult), r=[qkT, qdecb], w=[QdT])
                        S.dve(lambda e: e.tensor_tensor(out=Kd[:, :, :], in0=qkr[:, 2:4, :], in1=kdec[:, :].unsqueeze(2).broadcast_to([128, 2, 256]),
                                                        op=ALU.mult), r=[qkr, kdec], w=[Kd])
                        def smm(e):
                            ins = None
                            for hh in range(2):
                                for c in range(2):
                                    ins = e.matmul(pst[:, hh * 128:hh * 128 + 128], kT4[:, hh, c, :], qT4[:, hh, c, :], start=(hh == 0 and c == 0),
                                                   stop=(c == 1), skip_group_check=True)
                            return ins
                        S.pe(smm, r=[qkT], w=[pst])
                        S.dve(lambda e: e.tensor_tensor(out=AT[:, :, :], in0=pst[:, 0:256].rearrange("p (h i) -> p h i", h=2), in1=decT[:, :, :],
                                                        op=ALU.mult), r=[pst, decT], w=[AT])
                        for hh in range(2):
                            def omm(e, hh=hh):
                                e.matmul(po[hh][:, :], AT[:, hh, :], Vb[:, hh, :], start=True, stop=False)
                                e.matmul(po[hh][:, :], QdT[:, hh, 0, :], Sb[:, hh * 2, :], start=False, stop=False)
                                return e.matmul(po[hh][:, :], QdT[:, hh, 1, :], Sb[:, hh * 2 + 1, :], start=False, stop=True)
                            S.pe(omm, r=[AT, Vb, QdT, Sb], w=[po[hh]])
                        for hh in range(2):
                            for c in range(2):
                                U_ = pu[c]
                                S.pe(lambda e, hh=hh, c=c, U_=U_: e.matmul(U_[:, :], Kd[:, hh, c * 128:c * 128 + 128], Vb[:, hh, :], start=True, stop=True),
                                     r=[Kd, Vb], w=[U_])
                                S.dve(lambda e, hh=hh, c=c, U_=U_: e.scalar_tensor_tensor(out=St[:, hh * 2 + c, :], in0=St[:, hh * 2 + c, :],
                                                                                         scalar=gpow[:, hh:hh + 1], in1=U_[:, :], op0=ALU.mult, op1=ALU.add),
                                      r=[St, gpow, U_], w=[St])
                                S.act(lambda e, hh=hh, c=c: e.copy(out=Sb[:, hh * 2 + c, :], in_=St[:, hh * 2 + c, :]), r=[St], w=[Sb])
                    else:
                        S.dve(lambda e: e.tensor_tensor(out=QdT[:, :, :, 0:NS], in0=qT4[:, :, :, 0:NS],
                                                        in1=qdecs[:, :, :].unsqueeze(2).broadcast_to([128, 2, 2, NS]), op=ALU.mult),
                              r=[qkT, qdecs], w=[QdT])
                        for s_ in range(4):
                            S.dve(lambda e, s_=s_: e.tensor_tensor(out=QdTm[:, s_, :, :, :], in0=QdT[:, :, :, 0:NS],
                                                                   in1=colmask[:, s_, :].unsqueeze(1).unsqueeze(2).broadcast_to([128, 2, 2, 16]),
                                                                   op=ALU.mult), r=[QdT, colmask], w=[QdTm])
                            S.dve(lambda e, s_=s_: e.tensor_tensor(out=Kdm[:, s_, :, :], in0=qkr[:NS, 2:4, :],
                                                                   in1=kdm[:, s_, :].unsqueeze(2).broadcast_to([NS, 2, 256]), op=ALU.mult),
                                  r=[qkr, kdm], w=[Kdm])

                        def smm_s(e):
                            ins = None
                            for hh in range(2):
                                for c in range(2):
                                    ins = e.matmul(pst[:NS, hh * 16:hh * 16 + 16], kT4[:, hh, c, 0:NS], qT4[:, hh, c, 0:NS], start=(hh == 0 and c == 0),
                                                   stop=(c == 1), skip_group_check=True)
                            return ins
                        S.pe(smm_s, r=[qkT], w=[pst])
                        S.dve(lambda e: e.tensor_tensor(out=AT[:NS, :, 0:NS], in0=pst[:NS, 0:32].rearrange("p (h i) -> p h i", h=2), in1=decTs[:, :, :],
                                                        op=ALU.mult), r=[pst, decTs], w=[AT])
                        for s_ in range(4):
                            SS = Ss[s_ % 2]
                            S.dma("sp", SS[:, :, :], sret_d.ap()[s_].rearrange("h (c p) e -> p (h c) e", p=128), w=[SS], key=SS)
                            S.act(lambda e, s_=s_, SS=SS: e.copy(out=Ssb[s_][:, :, :], in_=SS[:, :, :]), r=[SS], w=[Ssb[s_]])
                            for hh in range(2):
                                for c in range(2):
                                    U_ = pu[c]
                                    S.pe(lambda e, hh=hh, c=c, U_=U_, s_=s_: e.matmul(U_[:, :], Kdm[:, s_, hh, c * 128:c * 128 + 128], Vb[:NS, hh, :],
                                                                                      start=True, stop=True), r=[Kdm, Vb], w=[U_])
                                    S.dve(lambda e, hh=hh, c=c, U_=U_, SS=SS: e.scalar_tensor_tensor(out=SS[:, hh * 2 + c, :], in0=SS[:, hh * 2 + c, :],
                                                                                                    scalar=gpow[:, 2 + hh:3 + hh], in1=U_[:, :],
                                                                                                    op0=ALU.mult, op1=ALU.add),
                                          r=[SS, gpow, U_], w=[SS])
                            S.dma("sp", ret_s.ap()[s_].rearrange("h (c p) e -> p (h c) e", p=128), SS[:, :, :], r=[SS], key=SS, final=True)
                        for hh in range(2):
                            def omm_s(e, hh=hh):
                                ins = e.matmul(po[hh][:NS, :], AT[:NS, hh, 0:NS], Vb[:NS, hh, :], start=True, stop=False)
                                for s_ in range(4):
                                    for c in range(2):
                                        ins = e.matmul(po[hh][:NS, :], QdTm[:, s_, hh, c, :], Ssb[s_][:, hh * 2 + c, :], start=False,
                                                       stop=(s_ == 3 and c == 1))
                                return ins
                            S.pe(omm_s, r=[AT, Vb, QdTm] + Ssb, w=[po[hh]])

                    OG = og_[T_ % 2]
                    for hh in range(2):
                        OB = ob[hh]
                        S.act(lambda e, hh=hh, OB=OB, n=n: e.activation(out=OB[:n, :], in_=po[hh][:n, :], func=AF.Identity, accum_out=st8[:n, hh * 8:hh * 8 + 1]),
                              r=[po[hh]], w=[OB, st8])
                        S.act(lambda e, hh=hh, OB=OB, n=n: e.activation(out=jk[:n, :], in_=OB[:n, :], func=AF.Square, accum_out=st8[:n, hh * 8 + 1:hh * 8 + 2]),
                              r=[OB], w=[jk, st8])
                        o8 = hh * 8
                        S.dve(lambda e, o8=o8, n=n: e.tensor_scalar(out=st8[:n, o8 + 2:o8 + 4], in0=st8[:n, o8:o8 + 2], scalar1=1.0 / 512, scalar2=None,
                                                                    op0=ALU.mult), r=[st8], w=[st8])
                        S.dve(lambda e, o8=o8, n=n: e.tensor_tensor(out=st8[:n, o8 + 4:o8 + 5], in0=st8[:n, o8 + 2:o8 + 3], in1=st8[:n, o8 + 2:o8 + 3],
                                                                    op=ALU.mult), r=[st8], w=[st8])
                        S.dve(lambda e, o8=o8, n=n: e.tensor_tensor(out=st8[:n, o8 + 5:o8 + 6], in0=st8[:n, o8 + 3:o8 + 4], in1=st8[:n, o8 + 4:o8 + 5],
                                                                    op=ALU.subtract), r=[st8], w=[st8])
                        S.dve(lambda e, o8=o8, n=n: e.tensor_scalar(out=st8[:n, o8 + 5:o8 + 6], in0=st8[:n, o8 + 5:o8 + 6], scalar1=0.0, scalar2=1e-5,
                                                                    op0=ALU.max, op1=ALU.add), r=[st8], w=[st8])
                        S.act(lambda e, o8=o8, n=n: e.activation(out=st8[:n, o8 + 6:o8 + 7], in_=st8[:n, o8 + 5:o8 + 6], func=AF.Sqrt), r=[st8], w=[st8])
                        S.dve(lambda e, o8=o8, n=n: e.reciprocal(out=st8[:n, o8 + 6:o8 + 7], in_=st8[:n, o8 + 6:o8 + 7]), r=[st8], w=[st8])
                        S.dve(lambda e, o8=o8, OB=OB, n=n: e.tensor_scalar(out=OB[:n, :], in0=OB[:n, :], scalar1=st8[:n, o8 + 2:o8 + 3],
                                                                           scalar2=st8[:n, o8 + 6:o8 + 7], op0=ALU.subtract, op1=ALU.mult),
                              r=[OB, st8], w=[OB])
                        S.pool(lambda e, hh=hh, OB=OB, n=n: e.tensor_tensor(out=OB[:n, :], in0=OB[:n, :], in1=gnb[:n, hh, :], op=ALU.mult), r=[OB, gnb], w=[OB])
                        S.pool(lambda e, hh=hh, OB=OB, OG=OG, n=n: e.tensor_tensor(out=OG[:n, hh, :], in0=OB[:n, :], in1=sg[:n, hh * 512:hh * 512 + 512],
                                                                                   op=ALU.mult), r=[OB, sg], w=[OG])
                        if T_ < NT:
                            c_, tl = T_ // 8, T_ % 8
                            S.dma("sp", o_r.ap()[c_, hh, tl * 128:tl * 128 + 128, :], OG[:, hh, :], r=[OG], w=[o_rT[c_][hh]], key=OG)
                        else:
                            S.dma("sp", o_rs.ap()[hh], OG[:NS, hh, :], r=[OG], w=[o_rsT[hh]], key=OG)
                    if T_ < NT and T_ % 8 == 7:
                        c_ = T_ // 8
                        for hh in range(2):
                            base = ((c_ * 2 + hh) * 4) * 1024
                            allgather(o_r.ap()[c_, hh].rearrange("(p a) f -> p (a f)", a=8),
                                      og2.ap()[base:base + 4096, :].rearrange("(q a) f -> q (a f)", a=8), o_rT[c_][hh], og2T)
                    if T_ == NT - 1:
                        S.dma("sp", ret_p.ap().rearrange("h (c p) e -> p (h c) e", p=128), St[:, :, :], r=[St], key=St, final=True)
                stageE1(0)
                stageE1b(0)
                for T_ in range(NT + 1):
                    if T_ + 1 <= NT:
                        stageE1(T_ + 1)
                    stageE2(T_)
                    if T_ + 1 <= NT:
                        stageE1b(T_ + 1)
                for hh in range(2):
                    allgather(o_rs.ap()[hh].rearrange("r (b x) -> (r b) x", b=8),
                              ogs2.ap()[hh * 64:hh * 64 + 64, :].rearrange("r (b x) -> (r b) x", b=8), o_rsT[hh], og2T)
                S.barrier()

        if RUN_T0:
            token_phase(0)
        if RUN_E:
            phase_e()
        if RUN_T1:
            token_phase(1)

        S.emit()
    return nc


def _col_slice(g):
    cols = list(range(512 * g, 512 * g + 512))
    for e in (0, 1):
        for br in range(3):
            base = 2048 + ((br * 2 + e) * 4 + g) * 128
            cols += list(range(base, base + 128))
    for br in range(3):
        base = 2048 + 3072 + br * 16 + 4 * g
        cols += list(range(base, base + 4))
    return np.array(cols)


def _consts():
    c0 = np.arange(256)[:, None] * 16
    j0 = np.arange(64)[None, :] * 64
    cover = ((c0 < j0 + 64) & (c0 + 32 > j0)).astype(np.float32)
    cover[255] = 0.0
    p = np.arange(128)[:, None]
    col = np.arange(128)[None, :]
    I0 = (col - 16 * p).astype(np.float32)
    Jm = (np.arange(64)[None, :] - (p >= 64)).astype(np.float32)
    tri = np.concatenate([(p <= col), (col < p)], axis=1).astype(np.float32)
    Ebig = (np.arange(SEQ)[None, :] // 64 == np.arange(64)[:, None]).astype(np.float32)
    key = np.arange(16)
    hq = np.arange(16)
    smask = np.zeros((16, 5, 16), np.float32)
    for s_ in range(4):
        smask[:, s_, :] = ((key[:, None] // 4 == s_) & (key[:, None] % 4 <= hq[None, :] % 4))
    smask[:, 4, :] = (key[:, None] > hq[None, :] % 4)
    sel16 = np.zeros((16, 68), np.float32)
    sel16[:, 0:4] = (hq[:, None] % 4 == np.arange(4)[None, :])
    for s_ in range(4):
        sel16[:, 4 + 16 * s_:20 + 16 * s_] = (key[:, None] == 4 * s_ + hq[None, :] % 4)
    hmask = (np.arange(12)[None, :] % 4 == hq[:, None] // 4).astype(np.float32)
    c0s = np.arange(1024)[:, None] * 16
    j0s = np.arange(257)[None, :] * 64
    cover_s = ((c0s < j0s + 64) & (c0s + 32 > j0s)).astype(np.float32)
    cover_s[1023] = 0.0
    return {"cover": cover, "I0": I0, "Jm": Jm, "tri": tri, "Ebig": Ebig, "smask": smask, "sel16": sel16,
            "hmask": hmask, "cover_s": cover_s, "cs_r": rope_table(np.arange(SEQ), 128),
            "cs_rs": rope_table(np.tile(PAST + np.arange(4), 4), 128)}


CONST = _consts()


def _ret_cols(r):
    cols = []
    for base, w in ((0, 256), (2048, 256), (4096, 512), (8192, 512)):
        for hh in range(2):
            h = 2 * r + hh
            cols += list(range(base + w * h, base + w * h + w))
    return np.array(cols)


def _ret_consts(r):
    out = {}
    i = np.arange(128, dtype=np.float64)
    decT = np.zeros((128, 2, 128), np.float64)
    qdecb = np.zeros((128, 2, 128), np.float64)
    kdec = np.zeros((128, 2), np.float64)
    decTs = np.zeros((16, 2, 16), np.float64)
    kdm = np.zeros((16, 4, 2), np.float64)
    gpow = np.zeros((128, 4), np.float64)
    qdecs = np.zeros((128, 2, 16), np.float64)
    t16 = np.arange(16)
    for hh in range(2):
        h = 2 * r + hh
        lg = np.log(np.float32(1.0) - np.float32(2.0) ** np.float32(-5.0 - h)).astype(np.float64)
        rel = i[None, :] - i[:, None]
        decT[:, hh, :] = np.where(rel >= 0, np.exp(np.maximum(rel, 0) * lg), 0.0) / 16.0
        qdecb[:, hh, :] = np.exp((i + 1.0) * lg)[None, :]
        kdec[:, hh] = np.exp((127.0 - i) * lg) / 16.0
        rel4 = (t16[None, :] % 4) - (t16[:, None] % 4)
        same = (t16[None, :] // 4) == (t16[:, None] // 4)
        decTs[:, hh, :] = np.where(same & (rel4 >= 0), np.exp(np.maximum(rel4, 0) * lg), 0.0) / 16.0
        for s_ in range(4):
            kdm[:, s_, hh] = np.where(t16 // 4 == s_, np.exp((3.0 - t16 % 4) * lg), 0.0) / 16.0
        gpow[:, hh] = np.exp(128.0 * lg)
        gpow[:, 2 + hh] = np.exp(4.0 * lg)
        qdecs[:, hh, :] = np.exp((t16 % 4 + 1.0) * lg)[None, :]
    colmask = np.zeros((128, 4, 16), np.float32)
    for s_ in range(4):
        colmask[:, s_, :] = (t16 // 4 == s_)[None, :]
    out["decT"] = decT.reshape(128, 256).astype(np.float32)
    out["qdecb"] = qdecb.reshape(128, 256).astype(np.float32)
    out["kdec"] = kdec.astype(np.float32)
    out["decTs"] = decTs.reshape(16, 32).astype(np.float32)
    out["kdm"] = kdm.reshape(16, 8).astype(np.float32)
    out["gpow"] = gpow.astype(np.float32)
    out["qdecs"] = qdecs.reshape(128, 32).astype(np.float32)
    out["colmask"] = colmask.reshape(128, 64)
    return out


def _oidx2(r):
    p = np.arange(128)
    idx = np.zeros((128, 72), np.int32)
    for t in range(9):
        for j in range(4):
            for hh in range(2):
                if t < 8:
                    idx[:, t * 8 + 2 * j + hh] = ((r * 2 + hh) * 4 + j) * 1024 + t * 128 + p
                else:
                    idx[:, t * 8 + 2 * j + hh] = (hh * 4 + j) * NS + 4 * r + np.minimum(p, 3)
    return idx


def _oidx(r):
    p = np.arange(128)
    idx = np.zeros((128, 36), np.int32)
    for t in range(9):
        for j in range(4):
            if t < 8:
                idx[:, t * 4 + j] = r * 4096 + j * 1024 + t * 128 + p
            else:
                idx[:, t * 4 + j] = j * NS + 4 * r + np.minimum(p, 3)
    return idx


def make_in_maps(inp):
    maps = []
    ident = np.eye(128, dtype=np.float32)
    cs_p = rope_table(np.arange(SEQ), 64)
    cs_s = rope_table(np.tile(PAST + np.arange(4), 4), 64)
    for c in range(8):
        b, r = c // 4, c % 4
        m = {
            "xb": np.ascontiguousarray(inp["x_prompt"][b]),
            "xs": np.ascontiguousarray(inp["x_sample"][4 * b:4 * b + 4].reshape(NS, D)),
            "w_in": np.ascontiguousarray(inp["nsa_w_in"][0][:, _col_slice(r)]),
            "gmix0": np.ascontiguousarray(np.broadcast_to(inp["norm_mix"][0][None, :], (128, D))),
            "cs_p": cs_p, "cs_s": cs_s, "ident": ident,
            "cw1": inp["nsa_cmp_w1"][0], "cw2": inp["nsa_cmp_w2"][0],
            "posT": np.ascontiguousarray(inp["nsa_cmp_pos"][0].reshape(64, 128).T),
            "b1T": np.ascontiguousarray(inp["nsa_cmp_b1"][0].T),
            "x_tok": np.ascontiguousarray(np.concatenate([inp["x_prompt"][b, 1024 * r:1024 * r + 1024], inp["x_sample"][c]], axis=0)),
            "oidx": _oidx(r),
            "w_out0": inp["nsa_w_out"][0], "ffn1_0": inp["ffn_w1"][0], "ffn2_0": inp["ffn_w2"][0],
            "gffn0T": np.ascontiguousarray(inp["norm_ffn"][0].reshape(16, 128).T),
            "w_out1": inp["ret_w_out"][0], "ffn1_1": inp["ffn_w1"][1], "ffn2_1": inp["ffn_w2"][1],
            "gmix1T": np.ascontiguousarray(inp["norm_mix"][1].reshape(16, 128).T),
            "gffn1T": np.ascontiguousarray(inp["norm_ffn"][1].reshape(16, 128).T),
            "gfinT": np.ascontiguousarray(inp["norm_final"].reshape(16, 128).T),
            "wr": np.ascontiguousarray(inp["ret_w_in"][0][:, _ret_cols(r)]),
            "cs_r": CONST["cs_r"], "cs_rs": CONST["cs_rs"],
            "gnb": np.ascontiguousarray(np.broadcast_to(inp["ret_gn"][0][1024 * r:1024 * r + 1024][None, :], (128, 1024))),
            "sret": np.ascontiguousarray(inp["state_ret"][0][4 * b:4 * b + 4, 2 * r:2 * r + 2]),
            "oidx2": _oidx2(r),
            **_ret_consts(r),
            "ccache": np.ascontiguousarray(inp["cache_cmp_kv"][0][:, :, :, r, :]),
            "scache": np.ascontiguousarray(inp["cache_sel_kv"][0][:, :, :, r, :]),
            "wstate": np.ascontiguousarray(inp["state_win_kv"][0][4 * b:4 * b + 4, :, :, r, :]),
            "ptab": np.ascontiguousarray(inp["page_table"][4 * b:4 * b + 4].astype(np.int32)),
            "smask": CONST["smask"], "sel16": CONST["sel16"], "hmask": CONST["hmask"], "cover_s": CONST["cover_s"],
            "pidx": (np.arange(128) % 4).astype(np.float32)[:, None].copy(),
            "ptabT": np.ascontiguousarray(inp["page_table"][4 * b:4 * b + 4].astype(np.int32).reshape(4, 4, 32).transpose(2, 0, 1).reshape(32, 16)),
            "Rrep": (np.arange(128)[None, :] // 4 == np.arange(32)[:, None]).astype(np.float32),
            "E2": (np.arange(128)[None, :] // 2 == np.arange(64)[:, None]).astype(np.float32),
            "cover": CONST["cover"], "I0": CONST["I0"], "Jm": CONST["Jm"], "tri": CONST["tri"], "Ebig": CONST["Ebig"],
        }
        maps.append(m)
    return maps


_NC = None


def kernel(**inputs):
    global _NC
    inp = {k: np.asarray(v) for k, v in inputs.items()}
    if _NC is None:
        _NC = build()
    maps = make_in_maps(inp)
    res = run_bass_kernel_spmd(_NC, maps, core_ids=list(range(8)), **({'trace': True} if TRACE else {}))
    global LAST_RES
    LAST_RES = res
    R = res.results
    global LAST
    LAST = R
    cmp_p = np.zeros((1, 2, SEQ, 2, 4, 128), np.float32)
    sel_p = np.zeros_like(cmp_p)
    win_full = np.zeros_like(cmp_p)
    cmp_s = np.zeros((1, 8, 4, 2, 4, 128), np.float32)
    sel_s = np.zeros_like(cmp_s)
    win_new = np.zeros_like(cmp_s)
    for c in range(8):
        b, g = c // 4, c % 4
        kp = R[c]["kv_p"]
        cmp_p[0, b, :, :, g, :] = kp[:, 0]
        sel_p[0, b, :, :, g, :] = kp[:, 1]
        win_full[0, b, :, :, g, :] = kp[:, 2]
        ks = R[c]["kv_s"].reshape(4, 4, 3, 2, 128)
        cmp_s[0, 4 * b:4 * b + 4, :, :, g, :] = ks[:, :, 0]
        sel_s[0, 4 * b:4 * b + 4, :, :, g, :] = ks[:, :, 1]
        win_new[0, 4 * b:4 * b + 4, :, :, g, :] = ks[:, :, 2]
    win_p = np.ascontiguousarray(win_full[:, :, SEQ - 512:])
    y_p = np.zeros((2, SEQ, D), np.float32)
    y_s = np.zeros((8, 4, D), np.float32)
    win_s = np.zeros((1, 8, 512, 2, 4, 128), np.float32)
    ret_p = np.zeros((1, 2, 8, 256, 512), np.float32)
    ret_s = np.zeros((1, 8, 8, 256, 512), np.float32)
    for c in range(8):
        b, g = c // 4, c % 4
        win_s[0, 4 * b:4 * b + 4, :, :, g, :] = R[c]["win_s"]
        y_p[b, 1024 * g:1024 * g + 1024] = R[c]["y_tok"][:1024]
        y_s[c] = R[c]["y_tok"][1024:]
        ret_p[0, b, 2 * g:2 * g + 2] = R[c]["ret_p"]
        ret_s[0, 4 * b:4 * b + 4, 2 * g:2 * g + 2] = R[c]["ret_s"]
    return (y_p, y_s, cmp_p, cmp_s, sel_p, sel_s, win_p, win_s, ret_p, ret_s)
```

```python
import contextlib
import numpy as np
import concourse.bass as bass
import concourse.mybir as mybir
from concourse.bass_utils import run_bass_kernel_spmd

F32 = mybir.dt.float32
BF16 = mybir.dt.bfloat16
I32 = mybir.dt.int32
AF = mybir.ActivationFunctionType
ALU = mybir.AluOpType
AX = mybir.AxisListType

ENGS = ("sp", "act", "dve", "pool", "pe")


class Buf:
    __slots__ = ("name", "last_w", "readers", "cnt")

    def __init__(self, name):
        self.name = name
        self.last_w = None
        self.readers = []
        self.cnt = 0


class T:
    def __init__(self, t, name):
        self.t = t
        self.b = Buf(name)

    def __getitem__(self, k):
        return self.t[k]

    def ap(self):
        return self.t.ap()


def _b(x):
    return x.b if isinstance(x, T) else x


class Op:
    __slots__ = ("eng", "fn", "deps", "is_dma", "key", "sig", "sigidx", "cnt", "inc")

    def __init__(self, eng, fn, is_dma=False, key=None, inc=16):
        self.eng = eng
        self.fn = fn
        self.deps = []
        self.is_dma = is_dma
        self.key = key
        self.sig = False
        self.sigidx = 0
        self.cnt = 0
        self.inc = inc


class Sched:
    def __init__(self, nc):
        self.nc = nc
        self.ops = []
        self.last_real = {e: None for e in ENGS}
        self.out_dmas = []

    def _add(self, op, r, w):
        r = [_b(x) for x in r]
        w = [_b(x) for x in w]
        deps = []
        for b in r:
            if b.last_w is not None:
                deps.append(b.last_w)
        for b in w:
            if b.last_w is not None:
                deps.append(b.last_w)
            deps.extend(b.readers)
        seen = set()
        for d in deps:
            if d is op or id(d) in seen:
                continue
            seen.add(id(d))
            if (not d.is_dma) and d.eng == op.eng and op.eng == "pe" and not op.is_dma:
                continue
            op.deps.append(d)
            if not d.is_dma:
                d.sig = True
        for b in r:
            b.readers.append(op)
        for b in w:
            b.last_w = op
            b.readers = []
        self.ops.append(op)
        if not op.is_dma:
            self.last_real[op.eng] = op
        return op

    def op(self, eng, fn, r=(), w=()):
        return self._add(Op(eng, fn), r, w)

    def pe(self, fn, r=(), w=()):
        return self.op("pe", fn, r, w)

    def act(self, fn, r=(), w=()):
        return self.op("act", fn, r, w)

    def dve(self, fn, r=(), w=()):
        return self.op("dve", fn, r, w)

    def pool(self, fn, r=(), w=()):
        return self.op("pool", fn, r, w)

    def dma(self, q, out, in_, r=(), w=(), key=None, final=False, **kw):
        return self.custom_dma(q, lambda e: e.dma_start(out=out, in_=in_, **kw), r, w, key, 16, final)

    def custom_dma(self, q, fn, r=(), w=(), key=None, inc=16, final=False):
        k = _b(key)
        op = Op(q, fn, is_dma=True, key=k, inc=inc)
        self._add(op, r, w)
        if final:
            self.out_dmas.append(op)
        return op

    def barrier(self):
        pend = [o for o in self.last_real.values() if o is not None]
        latest = {}
        for o in self.ops:
            if o.is_dma:
                latest[id(o.key)] = o
        for e in ENGS:
            op = Op(e, None)
            for d in pend:
                if d.eng != e:
                    op.deps.append(d)
                    d.sig = True
            op.deps.extend(latest.values())
            self.ops.append(op)

    def emit(self):
        nc = self.nc
        fin = Op("sp", None)
        latest = {}
        for o in self.out_dmas:
            latest[id(o.key)] = o
        fin.deps = list(latest.values())
        for e in ENGS:
            o = self.last_real[e]
            if e != "sp" and o is not None:
                fin.deps.append(o)
                o.sig = True
        self.ops.append(fin)

        with contextlib.ExitStack() as st:
            esem = {e: st.enter_context(nc.semaphore("s_" + e)) for e in ENGS}
            keysems = {}
            keyvals = {}
            for o in self.ops:
                if o.is_dma:
                    kid = id(o.key)
                    if kid not in keysems:
                        keysems[kid] = st.enter_context(nc.semaphore("k%d" % len(keysems)))
                        keyvals[kid] = 0
                    keyvals[kid] += o.inc
                    o.cnt = keyvals[kid]
            cnts = {e: 0 for e in ENGS}
            for o in self.ops:
                if (not o.is_dma) and o.sig:
                    assert o.fn is not None
                    cnts[o.eng] += 1
                    o.sigidx = cnts[o.eng]
            self.n_sems = len(keysems) + 5
            per = {e: [o for o in self.ops if o.eng == e] for e in ENGS}
            block = st.enter_context(nc.Block())

            def run(eng_name):
                def body(eng):
                    waited = {}
                    for o in per[eng_name]:
                        for d in o.deps:
                            if d.is_dma:
                                s, v = keysems[id(d.key)], d.cnt
                            else:
                                s, v = esem[d.eng], d.sigidx
                            if waited.get(id(s), 0) >= v:
                                continue
                            waited[id(s)] = v
                            eng.wait_ge(s, v)
                        if o.fn is None:
                            continue
                        ins = o.fn(eng)
                        if o.is_dma:
                            ins.then_inc(keysems[id(o.key)], o.inc)
                        elif o.sig:
                            ins.then_inc(esem[eng_name], 1)

                return body

            block.sync(run("sp"))
            block.scalar(run("act"))
            block.vector(run("dve"))
            block.gpsimd(run("pool"))
            block.tensor(run("pe"))


D = 2048
SEQ = 4096
NT = SEQ // 128
NTOK = 1028
NS = 16
PAST = 16384
NCOL = 1292
RMS_EPS = 1e-6
SCALE = 128 ** -0.5
NEG = -30000.0

STAGE = 1
DEBUG = False
NTQ = NT
RUN_S = True
RUN_T0 = True
RUN_E = True
RUN_T1 = True
TRACE = False


def rope_table(pos, half):
    inv = (10000.0 ** (-(np.arange(half, dtype=np.float32)) / np.float32(half))).astype(np.float32)
    ang = (pos.astype(np.float32)[:, None] * inv[None, :]).astype(np.float32)
    return np.concatenate([np.cos(ang), np.sin(ang)], axis=1).astype(np.float32)


def build(stage=STAGE):
    nc = bass.Bass("TRN2", target_bir_lowering=False)
    S = Sched(nc)

    def din(name, shape, dt=F32):
        return nc.dram_tensor(name, list(shape), dt, kind="ExternalInput")

    def dout(name, shape, dt=F32):
        return nc.dram_tensor(name, list(shape), dt, kind="ExternalOutput")

    xb = din("xb", [SEQ, D])
    xs = din("xs", [NS, D])
    w_in = din("w_in", [D, NCOL])
    gmix0 = din("gmix0", [128, D])
    cs_p = din("cs_p", [SEQ, 128])
    cs_s = din("cs_s", [NS, 128])
    ident_d = din("ident", [128, 128])

    cw1 = din("cw1", [2, 32, 128, 128])
    cw2 = din("cw2", [2, 128, 128])
    posT = din("posT", [128, 64])
    b1T = din("b1T", [128, 2])
    cover_d = din("cover", [256, 64])
    I0_d = din("I0", [128, 128])
    Jm_d = din("Jm", [128, 64])
    tri_d = din("tri", [128, 256])
    Ebig_d = din("Ebig", [64, SEQ])

    kv_p = dout("kv_p", [SEQ, 3, 2, 128])
    o_loc = nc.dram_tensor("o_loc", [SEQ, 512], BF16)
    o_locs = nc.dram_tensor("o_locs", [NS, 512], BF16)
    og = nc.dram_tensor("og", [4 * SEQ, 512], BF16)
    ogs = nc.dram_tensor("ogs", [4 * NS, 512], BF16)
    o_locT = [Buf("o_loc%d" % i) for i in range(5)]
    ogT = Buf("og")
    x_tok = din("x_tok", [NTOK, D])
    oidx_d = din("oidx", [128, 36], I32)
    w_out0 = din("w_out0", [D, D])
    ffn1_0 = din("ffn1_0", [D, 4 * D])
    ffn2_0 = din("ffn2_0", [4 * D, D])
    gffn0T = din("gffn0T", [128, 16])
    w_out1 = din("w_out1", [2 * D, D])
    ffn1_1 = din("ffn1_1", [D, 4 * D])
    ffn2_1 = din("ffn2_1", [4 * D, D])
    gmix1T = din("gmix1T", [128, 16])
    gffn1T = din("gffn1T", [128, 16])
    gfinT = din("gfinT", [128, 16])
    wr_d = din("wr", [D, 3072])
    cs_r = din("cs_r", [SEQ, 256])
    cs_rs = din("cs_rs", [NS, 256])
    decT_d = din("decT", [128, 256])
    qdecb_d = din("qdecb", [128, 256])
    kdec_d = din("kdec", [128, 2])
    decTs_d = din("decTs", [16, 32])
    kdm_d = din("kdm", [16, 8])
    colmask_d = din("colmask", [128, 64])
    gnb_d = din("gnb", [128, 1024])
    gpow_d = din("gpow", [128, 4])
    qdecs_d = din("qdecs", [128, 32])
    sret_d = din("sret", [4, 2, 256, 512])
    oidx2_d = din("oidx2", [128, 72], I32)
    ret_p = dout("ret_p", [2, 256, 512])
    ret_s = dout("ret_s", [4, 2, 256, 512])
    y_tok = dout("y_tok", [NTOK, D])
    hspill = nc.dram_tensor("hspill", [128, 16 * NTOK], F32)
    hspT = Buf("hspill")
    xg_in = nc.dram_tensor("xg_in", [8, 128, 2 * NTOK], BF16)
    xg_all = nc.dram_tensor("xg_all", [8, 512, 2 * NTOK], BF16)
    xg_inT = [Buf("xg_in%d" % i) for i in range(8)]
    xgT = Buf("xg_all")
    o_r = nc.dram_tensor("o_r", [4, 2, 1024, 512], BF16)
    o_rs = nc.dram_tensor("o_rs", [2, NS, 512], BF16)
    og2 = nc.dram_tensor("og2", [4 * 2 * 4 * 1024, 512], BF16)
    ogs2 = nc.dram_tensor("ogs2", [2 * 4 * NS, 512], BF16)
    o_rT = [[Buf("o_r%d%d" % (c, h)) for h in range(2)] for c in range(4)]
    o_rsT = [Buf("o_rs%d" % h) for h in range(2)]
    og2T = Buf("og2")
    win_s = dout("win_s", [4, 512, 2, 128])
    ccache = din("ccache", [1280, 128, 2, 128])
    scache = din("scache", [1280, 128, 2, 128])
    wstate = din("wstate", [4, 512, 2, 128])
    ptab = din("ptab", [4, 128], I32)
    smask_d = din("smask", [16, 5, 16])
    sel16_d = din("sel16", [16, 4 + 64])
    hmask_d = din("hmask", [16, 12])
    cover_s = din("cover_s", [1024, 257])
    pidx_d = din("pidx", [128, 1])
    ptabT = din("ptabT", [32, 16], I32)
    Rrep_d = din("Rrep", [32, 128])
    E2_d = din("E2", [64, 128])
    kv_s = dout("kv_s", [NS, 3, 2, 128])

    dbg_outs = {}

    def dbg(name, t, ap, shape, dt=F32):
        if not DEBUG:
            return
        d_ = nc.dram_tensor("dbg_" + name, list(shape), dt, kind="ExternalOutput")
        S.dma("sp", d_.ap(), ap, r=[t], key=Buf("dbgk_" + name), final=True)

    with contextlib.ExitStack() as top:
        def sbuf(st, name, shape, dt):
            return T(st.enter_context(nc.sbuf_tensor(name, list(shape), dt)), name)

        def psum(st, name, shape, dt):
            return T(st.enter_context(nc.psum_tensor(name, list(shape), dt)), name)

        identb = sbuf(top, "identb", [128, 128], BF16)
        identf = sbuf(top, "identf", [128, 128], F32)
        GL = sbuf(top, "GL", [128, NT + 1, 12], F32)
        QTs = sbuf(top, "QTs", [128, 4, NS], BF16)
        KTs = sbuf(top, "KTs", [128, 3, NS], BF16)
        VAs = sbuf(top, "VAs", [NS, 2, 132], BF16)
        pp = contextlib.ExitStack()
        QT = sbuf(pp, "QT", [128, 4, SEQ + NS], BF16)
        KT = sbuf(pp, "KT", [128, 3, SEQ + NS], BF16)
        VcT = sbuf(pp, "VcT", [128, SEQ + NS], BF16)
        VA = sbuf(pp, "VA", [128, NT + 1, 2, 132], BF16)
        S.dma("sp", identf[:], ident_d.ap(), w=[identf], key=identf)
        S.dma("pool", identb[:], ident_d.ap(), w=[identb], key=identb)

        RG = [[0, 1, 2, 3], [4, 5, 6, 7]]

        def allgather(src_ap, dst_ap, rbuf, wbuf_):
            S.custom_dma("pool", lambda e: e.collective_compute("AllGather", ALU.bypass, replica_groups=RG, ins=[src_ap], outs=[dst_ap]),
                         r=[rbuf], w=[wbuf_], key=wbuf_, inc=1)

        def phase_a():
            with contextlib.ExitStack() as pa:
                wsb = sbuf(pa, "wsb", [128, 16, NCOL], BF16)
                gsb = sbuf(pa, "gsb", [128, D], F32)
                xt = [sbuf(pa, "xt%d" % i, [128, D], F32) for i in range(2)]
                cst = [sbuf(pa, "cst%d" % i, [128, 128], F32) for i in range(3)]
                junk = sbuf(pa, "junk", [128, D], BF16)
                ss = sbuf(pa, "ss", [128, 2], F32)
                epsb = sbuf(pa, "epsb", [128, 1], F32)
                S.pool(lambda e: e.memset(epsb[:], RMS_EPS), w=[epsb])
                xn = sbuf(pa, "xn", [128, D], BF16)
                xnTs = [sbuf(pa, "xnT%d" % i, [128, 16, 128], BF16) for i in range(2)]
                kvf = [sbuf(pa, "kvf%d" % i, [128, 3, 2, 128], F32) for i in range(2)]
                rt = [sbuf(pa, "rt%d" % i, [128, 4, 64], F32) for i in range(4)]
                qkbs = [sbuf(pa, "qkb%d" % i, [128, 8, 128], BF16) for i in range(2)]
                pT = [psum(pa, "pT%d" % i, [128, 1024], BF16) for i in range(2)]
                pz = [psum(pa, "pz%d" % i, [128, 512], F32) for i in range(3)]
                pq = psum(pa, "pq", [128, 1024], BF16)

                for kc in range(4):
                    S.dma("pool", wsb[:, 4 * kc:4 * kc + 4, :],
                          w_in.ap()[512 * kc:512 * kc + 512, :].rearrange("(k p) c -> p k c", p=128),
                          w=[wsb], key=wsb)
                S.dma("sp", gsb[:], gmix0.ap(), w=[gsb], key=gsb)
                S.pool(lambda e: e.memset(VA[:, :, :, 128:129], 1.0), w=[VA])

                def stageA1(j):
                    n = 128 if j < NT else NS
                    c0 = j * 128
                    X = xt[j % 2]
                    xnT = xnTs[j % 2]

                    def load(jj):
                        nn = 128 if jj < NT else NS
                        cc = jj * 128
                        S.dma("sp", xt[jj % 2][:nn, :], xb.ap()[cc:cc + 128, :] if jj < NT else xs.ap(), w=[xt[jj % 2]], key=xt[jj % 2])
                        S.dma("sp", cst[jj % 3][:nn, :], cs_p.ap()[cc:cc + 128, :] if jj < NT else cs_s.ap(), w=[cst[jj % 3]], key=cst[jj % 3])
                    if j == 0:
                        load(0)
                    if j + 1 <= NT:
                        load(j + 1)
                    S.act(lambda e, X=X, n=n: e.activation(out=junk[:n, :], in_=X[:n, :], func=AF.Square,
                                                           accum_out=ss[:n, 0:1]), r=[X], w=[junk, ss])
                    S.act(lambda e, n=n: e.activation(out=ss[:n, 1:2], in_=ss[:n, 0:1], func=AF.Sqrt, scale=1.0 / D,
                                                      bias=epsb[:n, 0:1]), r=[ss, epsb], w=[ss])
                    S.dve(lambda e, n=n: e.reciprocal(out=ss[:n, 1:2], in_=ss[:n, 1:2]), r=[ss], w=[ss])
                    S.dve(lambda e, X=X, n=n: e.scalar_tensor_tensor(out=xn[:n, :], in0=X[:n, :], scalar=ss[:n, 1:2],
                                                                     in1=gsb[:n, :], op0=ALU.mult, op1=ALU.mult),
                          r=[X, ss, gsb], w=[xn])
                    for hb in range(2):
                        def tr(e, hb=hb, n=n):
                            ins = None
                            for k in range(8):
                                ins = e.transpose(pT[hb][:, k * 128:k * 128 + n], xn[:n, (hb * 8 + k) * 128:(hb * 8 + k + 1) * 128],
                                                  identb[:n, :n])
                            return ins
                        S.pe(tr, r=[xn, identb], w=[pT[hb]])
                        S.act(lambda e, hb=hb, n=n: e.copy(out=xnT[:, hb * 8:hb * 8 + 8, :n],
                                                           in_=pT[hb][:, :].rearrange("p (k t) -> p k t", k=8)[:, :, :n]),
                              r=[pT[hb]], w=[xnT])

                def stageA2(j):
                    n = 128 if j < NT else NS
                    c0 = j * 128
                    C = cst[j % 3]
                    KV = kvf[j % 2]
                    xnT = xnTs[j % 2]
                    qkb = qkbs[j % 2]
                    cbs = [(0, 512), (512, 512), (1024, NCOL - 1024)]
                    for ci, (cb, cw) in enumerate(cbs):
                        def mm(e, ci=ci, cb=cb, cw=cw, n=n):
                            ins = None
                            for k in range(16):
                                ins = e.matmul(pz[ci][:n, :cw], xnT[:, k, :n], wsb[:, k, cb:cb + cw],
                                               start=(k == 0), stop=(k == 15))
                            return ins
                        S.pe(mm, r=[xnT, wsb], w=[pz[ci]])
                    cosb = lambda h, n=n, C=C: C[:n, 0:64].unsqueeze(1).broadcast_to([n, h, 64])
                    sinb = lambda h, n=n, C=C: C[:n, 64:128].unsqueeze(1).broadcast_to([n, h, 64])

                    def rope(src3, h, out_lo, out_hi, rd, wr, n=n, cosb=cosb, sinb=sinb):
                        a, b_, c_, d_ = rt
                        S.dve(lambda e: e.tensor_tensor(out=a[:n, :h, :], in0=src3[:, :, 0:64], in1=cosb(h), op=ALU.mult), r=rd + [C], w=[a])
                        S.dve(lambda e: e.tensor_tensor(out=b_[:n, :h, :], in0=src3[:, :, 64:128], in1=sinb(h), op=ALU.mult), r=rd + [C], w=[b_])
                        S.dve(lambda e: e.tensor_tensor(out=c_[:n, :h, :], in0=src3[:, :, 64:128], in1=cosb(h), op=ALU.mult), r=rd + [C], w=[c_])
                        S.dve(lambda e: e.tensor_tensor(out=d_[:n, :h, :], in0=src3[:, :, 0:64], in1=sinb(h), op=ALU.mult), r=rd + [C], w=[d_])
                        S.pool(lambda e: e.tensor_tensor(out=out_lo, in0=a[:n, :h, :], in1=b_[:n, :h, :], op=ALU.subtract), r=[a, b_], w=wr)
                        S.pool(lambda e: e.tensor_tensor(out=out_hi, in0=c_[:n, :h, :], in1=d_[:n, :h, :], op=ALU.add), r=[c_, d_], w=wr)

                    z0 = pz[0][:n, :].rearrange("p (h d) -> p h d", h=4)
                    rope(z0, 4, qkb[:n, 0:4, 0:64], qkb[:n, 0:4, 64:128], [pz[0]], [qkb])
                    z1 = pz[1][:n, 0:384].rearrange("p (h d) -> p h d", h=3)
                    rope(z1, 3, KV[:n, :, 0, 0:64], KV[:n, :, 0, 64:128], [pz[1]], [KV])
                    S.act(lambda e, n=n, KV=KV: e.copy(out=KV[:n, 0, 1, :], in_=pz[1][:n, 384:512]), r=[pz[1]], w=[KV])
                    S.act(lambda e, n=n, KV=KV: e.copy(out=KV[:n, 1:3, 1, :],
                                                       in_=pz[2][:n, 0:256].rearrange("p (h d) -> p h d", h=2)),
                          r=[pz[2]], w=[KV])
                    S.act(lambda e, n=n, j=j: e.copy(out=GL[:n, j, :], in_=pz[2][:n, 256:268]), r=[pz[2]], w=[GL])
                    S.pool(lambda e, n=n, KV=KV: e.tensor_copy(out=qkb[:n, 4:7, :], in_=KV[:n, :, 0, :]), r=[KV], w=[qkb])
                    S.pool(lambda e, n=n, KV=KV: e.tensor_copy(out=qkb[:n, 7, :], in_=KV[:n, 0, 1, :]), r=[KV], w=[qkb])
                    S.pool(lambda e, n=n, KV=KV, j=j: e.tensor_copy(out=VA[:n, j, :, 0:128], in_=KV[:n, 1:3, 1, :]), r=[KV], w=[VA])
                    dst = kv_p.ap()[c0:c0 + 128] if j < NT else kv_s.ap()
                    S.dma("sp", dst, KV[:n], r=[KV], key=KV, final=True)
                    if j == NT:
                        for s_ in range(4):
                            S.dma("sp", win_s.ap()[s_, 508:512], KV[4 * s_:4 * s_ + 4, 2], r=[KV], key=KV, final=True)

                def stageA3(j):
                    n = 128 if j < NT else NS
                    c0 = j * 128
                    qkb = qkbs[j % 2]
                    def tr2(e, n=n):
                        ins = None
                        for k in range(8):
                            ins = e.transpose(pq[:, k * 128:k * 128 + n], qkb[:n, k, :], identb[:n, :n])
                        return ins
                    S.pe(tr2, r=[qkb, identb], w=[pq])
                    pq3 = pq[:, :].rearrange("p (k t) -> p k t", k=8)
                    S.act(lambda e, n=n, c0=c0, pq3=pq3: e.copy(out=QT[:, :, c0:c0 + n], in_=pq3[:, 0:4, :n]), r=[pq], w=[QT])
                    S.act(lambda e, n=n, c0=c0, pq3=pq3: e.copy(out=KT[:, :, c0:c0 + n], in_=pq3[:, 4:7, :n]), r=[pq], w=[KT])
                    S.act(lambda e, n=n, c0=c0, pq3=pq3: e.copy(out=VcT[:, c0:c0 + n], in_=pq3[:, 7, :n]), r=[pq], w=[VcT])
                stageA1(0)
                for j in range(NT + 1):
                    if j + 1 <= NT:
                        stageA1(j + 1)
                    stageA2(j)
                    if j >= 1:
                        stageA3(j - 1)
                stageA3(NT)
                S.act(lambda e: e.copy(out=QTs[:, :, :], in_=QT[:, :, SEQ:SEQ + NS]), r=[QT], w=[QTs])
                S.act(lambda e: e.copy(out=KTs[:, :, :], in_=KT[:, :, SEQ:SEQ + NS]), r=[KT], w=[KTs])
                S.act(lambda e: e.copy(out=VAs[:, :, :], in_=VA[:NS, NT, :, :]), r=[VA], w=[VAs])
                S.barrier()

        phase_a()

        S.act(lambda e: e.activation(out=GL[:, :, :], in_=GL[:, :, :], func=AF.Sigmoid), r=[GL], w=[GL])

        def phase_c():
            with contextlib.ExitStack() as pc:
                w1sb = sbuf(pc, "w1sb", [128, 2, 32, 128], BF16)
                w2sb = sbuf(pc, "w2sb", [128, 2, 128], BF16)
                posb = sbuf(pc, "posb", [128, 64], BF16)
                b1sb = sbuf(pc, "b1sb", [128, 2], F32)
                biasb = sbuf(pc, "biasb", [128, 2], F32)
                I0 = sbuf(pc, "I0s", [128, 128], F32)
                Jm = sbuf(pc, "Jms", [128, 64], F32)
                trib = sbuf(pc, "trib", [128, 256], BF16)
                Ebig = sbuf(pc, "Ebigs", [64, SEQ], BF16)
                KcT = sbuf(pc, "KcT", [128, 256], BF16)
                VcA = sbuf(pc, "VcA", [128, 2, 196], BF16)
                hx = sbuf(pc, "hx", [128, 256], F32)
                ht = sbuf(pc, "ht", [128, 256], F32)
                hT = [sbuf(pc, "hT%d" % i, [128, 256], BF16) for i in range(2)]
                Eb = [sbuf(pc, "Eb%d" % i, [128, 4, 128], BF16) for i in range(3)]
                mk = sbuf(pc, "mk", [128, 128], BF16)
                rs = sbuf(pc, "rs", [128, 8], F32)
                ocats = [sbuf(pc, "ocat%d" % i, [128, 4, 128], F32) for i in range(2)]
                ocb = [sbuf(pc, "ocb%d" % i, [128, 512], BF16) for i in range(2)]
                imp = sbuf(pc, "imp", [128, 64], F32)
                vis = sbuf(pc, "vis", [128, 64], F32)
                frc = sbuf(pc, "frc", [128, 64], F32)
                col0 = sbuf(pc, "col0", [128, 64], F32)
                sc = [sbuf(pc, "sc%d" % i, [128, 64], F32) for i in range(2)]
                m8 = sbuf(pc, "m8", [128, 16], F32)
                selm = sbuf(pc, "selm", [128, 64], F32)
                nm = sbuf(pc, "nm", [128, 64], BF16)
                nmTs = [sbuf(pc, "nmT%d" % i, [64, 4, 128], BF16) for i in range(2)]
                pS = [psum(pc, "pS%d" % i, [128, 512], F32) for i in range(3)]
                pO = [psum(pc, "pO%d" % i, [128, 512], F32) for i in range(4)]
                pM = psum(pc, "pM", [128, 1024], BF16)

                S.dma("pool", w1sb[:], cw1.ap().rearrange("e s d h -> d e s h"), w=[w1sb], key=w1sb)
                S.dma("pool", w2sb[:], cw2.ap().rearrange("e h d -> h e d"), w=[w2sb], key=w2sb)
                S.dma("pool", posb[:], posT.ap(), w=[posb], key=posb)
                S.dma("sp", b1sb[:], b1T.ap(), w=[b1sb], key=b1sb)
                S.dma("sp", I0[:], I0_d.ap(), w=[I0], key=I0)
                S.dma("sp", Jm[:], Jm_d.ap(), w=[Jm], key=Jm)
                S.dma("pool", trib[:], tri_d.ap(), w=[trib], key=trib)
                S.dma("pool", Ebig[:], Ebig_d.ap(), w=[Ebig], key=Ebig)
                S.dve(lambda e: e.memset(KcT[:], 0.0), w=[KcT])
                S.dve(lambda e: e.memset(VcA[:], 0.0), w=[VcA])
                S.dve(lambda e: e.memset(VcA[:, :, 128:129], 1.0), w=[VcA])
                S.dve(lambda e: e.memset(col0[:], 0.0), w=[col0])
                S.dve(lambda e: e.memset(col0[:, 0:1], 1.0), w=[col0])
                S.dma("pool", VcA[:, :, 129:193], cover_d.ap().rearrange("(t p) j -> p t j", p=128), w=[VcA], key=VcA)

                def bias_mm(e):
                    ins = None
                    for ee in range(2):
                        for s_ in range(32):
                            ins = e.matmul(pS[0][:, ee:ee + 1], w1sb[:, ee, s_, :], posb[:, ee * 32 + s_:ee * 32 + s_ + 1],
                                           start=(ee == 0 and s_ == 0), stop=(s_ == 31), skip_group_check=True)
                    return ins
                S.pe(bias_mm, r=[w1sb, posb], w=[pS[0]])
                S.dve(lambda e: e.tensor_tensor(out=biasb[:], in0=pS[0][:, 0:2], in1=b1sb[:], op=ALU.add), r=[pS[0], b1sb], w=[biasb])

                def compress(srcT, nblk, ncols_pad, KcT_out, Vc_out_fn):
                    for ee in range(2):
                        for c0 in range(0, nblk, 512):
                            cn = min(512, nblk - c0)
                            P = pS[(c0 // 512) % 2]

                            def hmm(e, ee=ee, c0=c0, cn=cn, P=P):
                                ins = None
                                src = srcT(ee)
                                for rs_ in range(32):
                                    lo = rs_ + 16 * c0
                                    ins = e.matmul(P[:, :cn], w1sb[:, ee, rs_, :], src[:, lo:lo + 16 * (cn - 1) + 1:16],
                                                   start=(rs_ == 0), stop=(rs_ == 31))
                                return ins
                            S.pe(hmm, r=[w1sb, KT, VcT], w=[P])
                            S.act(lambda e, ee=ee, cn=cn, P=P: e.activation(out=hx[:, :cn], in_=P[:, :cn], func=AF.Identity,
                                                                             bias=biasb[:, ee:ee + 1]), r=[P, biasb], w=[hx])
                            S.dve(lambda e, cn=cn: e.tensor_tensor(out=ht[:, :cn], in0=hx[:, :cn], in1=hx[:, :cn], op=ALU.mult), r=[hx], w=[ht])
                            S.dve(lambda e, cn=cn: e.tensor_scalar(out=ht[:, :cn], in0=ht[:, :cn], scalar1=0.044715, scalar2=1.0,
                                                                   op0=ALU.mult, op1=ALU.add), r=[ht], w=[ht])
                            S.dve(lambda e, cn=cn: e.tensor_tensor(out=ht[:, :cn], in0=ht[:, :cn], in1=hx[:, :cn], op=ALU.mult), r=[ht, hx], w=[ht])
                            S.act(lambda e, cn=cn: e.activation(out=ht[:, :cn], in_=ht[:, :cn], func=AF.Sigmoid, scale=1.5957691216),
                                  r=[ht], w=[ht])
                            H = hT[ee]
                            S.dve(lambda e, cn=cn, H=H: e.tensor_tensor(out=H[:, :cn], in0=hx[:, :cn], in1=ht[:, :cn], op=ALU.mult),
                                  r=[hx, ht], w=[H])
                            if ee == 0:
                                S.pe(lambda e, cn=cn, H=H: e.matmul(pO[0][:, :cn], w2sb[:, 0, :], H[:, :cn], start=True, stop=True),
                                     r=[w2sb, H], w=[pO[0]])
                                S.act(lambda e, cn=cn, c0=c0: e.copy(out=KcT_out[:, c0:c0 + cn], in_=pO[0][:, :cn]), r=[pO[0]], w=[KcT])
                            else:
                                for t0 in range(0, cn, 128):
                                    nb = min(128, cn - t0)
                                    S.pe(lambda e, nb=nb, t0=t0, H=H: e.matmul(pO[1][:nb, 0:128], H[:, t0:t0 + nb], w2sb[:, 1, :],
                                                                               start=True, stop=True), r=[w2sb, H], w=[pO[1]])
                                    S.act(lambda e, nb=nb, ct=(c0 + t0) // 128: e.copy(out=Vc_out_fn(ct, nb), in_=pO[1][:nb, 0:128]),
                                          r=[pO[1]], w=[VcA])

                compress(lambda ee: (KT[:, 0, :] if ee == 0 else VcT[:, :]), 255, 256, KcT, lambda ct, nb: VcA[:nb, ct, 0:128])

                HB = [(0, 0), (0, 256), (1, 0), (1, 256)]

                def pv_group(Pb, E, rhs_fn, ncol, first, r):
                    def f(e):
                        ins = None
                        for h in range(4):
                            bk, co = HB[h]
                            ins = e.matmul(Pb[bk][:, co:co + ncol], E[:, h, :], rhs_fn(), start=(first and h % 2 == 0), stop=False,
                                           skip_group_check=True)
                        return ins
                    S.pe(f, r=[E] + r, w=[Pb[0], Pb[1]])

                def finish_branch(Pb, i, br, first_branch, ocat):
                    for h in range(4):
                        bk, co = HB[h]
                        S.dve(lambda e, h=h, bk=bk, co=co: e.tensor_copy(out=rs[:, h:h + 1], in_=Pb[bk][:, co + 128:co + 129]),
                              r=[Pb[bk]], w=[rs])
                    S.dve(lambda e: e.tensor_scalar(out=rs[:, 0:4], in0=rs[:, 0:4], scalar1=1e-30, scalar2=None, op0=ALU.max), r=[rs], w=[rs])
                    S.dve(lambda e: e.reciprocal(out=rs[:, 0:4], in_=rs[:, 0:4]), r=[rs], w=[rs])
                    S.dve(lambda e, i=i, br=br: e.tensor_tensor(out=rs[:, 4:8], in0=rs[:, 0:4], in1=GL[:, i, 4 * br:4 * br + 4], op=ALU.mult),
                          r=[rs, GL], w=[rs])
                    for h in range(4):
                        bk, co = HB[h]
                        if first_branch:
                            S.dve(lambda e, h=h, bk=bk, co=co: e.tensor_scalar(out=ocat[:, h, :], in0=Pb[bk][:, co:co + 128],
                                                                               scalar1=rs[:, 4 + h:5 + h], scalar2=None, op0=ALU.mult),
                                  r=[Pb[bk], rs], w=[ocat])
                        else:
                            S.dve(lambda e, h=h, bk=bk, co=co: e.scalar_tensor_tensor(out=ocat[:, h, :], in0=Pb[bk][:, co:co + 128],
                                                                                      scalar=rs[:, 4 + h:5 + h], in1=ocat[:, h, :],
                                                                                      op0=ALU.mult, op1=ALU.add),
                                  r=[Pb[bk], rs, ocat], w=[ocat])

                pair_ctr = [0]

                def qk_exp(i, lhsT_fn, lr, with_mask, nmT=None):
                    nmT = nmT if nmT is not None else nmTs[0]
                    x = pair_ctr[0] % 3
                    pair_ctr[0] += 1
                    P, E = pS[x], Eb[x]

                    def f(e):
                        ins = e.matmul(P[:, :], lhsT_fn(), QT[:, :, i * 128:i * 128 + 128], start=True, stop=not with_mask)
                        if with_mask is not False:
                            ins = e.matmul(P[:, :], Ebig[:, with_mask * 128:with_mask * 128 + 128], nmT[:, :, :], start=False, stop=True)
                        return ins
                    S.pe(f, r=[QT, Ebig, nmT] + lr, w=[P])
                    S.act(lambda e: e.activation(out=E[:, :, :], in_=P[:, :].rearrange("p (h q) -> p h q", h=4), func=AF.Exp, scale=SCALE),
                          r=[P], w=[E])
                    return E

                def mul_mask(E, m_ap, r):
                    S.dve(lambda e: e.tensor_tensor(out=E[:, :, :], in0=E[:, :, :], in1=m_ap.unsqueeze(1).broadcast_to([128, 4, 128]),
                                                    op=ALU.mult), r=[E] + r, w=[E])

                def stage1(i):
                    ocat = ocats[i % 2]
                    nmT = nmTs[i % 2]
                    Pb = pO[0:2]
                    n_ct = 1 if i < 16 else 2
                    for ct in range(n_ct):
                        E = qk_exp(i, lambda ct=ct: KcT[:, ct * 128:ct * 128 + 128], [KcT], False)
                        cval = float(128 * i - 2048 * ct - 31)
                        S.dve(lambda e, cval=cval: e.tensor_scalar(out=mk[:, :], in0=I0[:, :], scalar1=cval, scalar2=0.0,
                                                                   op0=ALU.add, op1=ALU.is_ge), r=[I0], w=[mk])
                        mul_mask(E, mk[:, :], [mk])
                        pv_group(Pb, E, lambda ct=ct: VcA[:, ct, 0:193], 193, ct == 0, [VcA])
                    for h in range(4):
                        bk, co = HB[h]
                        S.dve(lambda e, h=h, bk=bk, co=co: e.tensor_copy(out=rs[:, h:h + 1], in_=Pb[bk][:, co + 128:co + 129]),
                              r=[Pb[bk]], w=[rs])
                    S.dve(lambda e: e.tensor_scalar(out=rs[:, 0:4], in0=rs[:, 0:4], scalar1=1e-30, scalar2=None, op0=ALU.max), r=[rs], w=[rs])
                    S.dve(lambda e: e.reciprocal(out=rs[:, 0:4], in_=rs[:, 0:4]), r=[rs], w=[rs])
                    for h in range(4):
                        bk, co = HB[h]
                        if h == 0:
                            S.dve(lambda e, bk=bk, co=co: e.tensor_scalar(out=imp[:, :], in0=Pb[bk][:, co + 129:co + 193], scalar1=rs[:, 0:1],
                                                                          scalar2=None, op0=ALU.mult), r=[Pb[bk], rs], w=[imp])
                        else:
                            S.dve(lambda e, h=h, bk=bk, co=co: e.scalar_tensor_tensor(out=imp[:, :], in0=Pb[bk][:, co + 129:co + 193],
                                                                                      scalar=rs[:, h:h + 1], in1=imp[:, :],
                                                                                      op0=ALU.mult, op1=ALU.add),
                                  r=[Pb[bk], rs, imp], w=[imp])
                    finish_branch(Pb, i, 0, True, ocat)
                    if i == 0:
                        dbg("ocat_cmp", ocat, ocat[:, :, :], [128, 4, 128])
                        dbg("rs_cmp", rs, rs[:, :], [128, 8])
                        dbg("imp", imp, imp[:, :], [128, 64])
                        dbg("GL", GL, GL[:, :, :], [128, NT + 1, 12])
                    S.dve(lambda e, i=i: e.tensor_scalar(out=vis[:, :], in0=Jm[:, :], scalar1=float(2 * i), scalar2=None, op0=ALU.is_le),
                          r=[Jm], w=[vis])
                    if i >= 8:
                        S.dve(lambda e, i=i: e.tensor_scalar(out=frc[:, :], in0=Jm[:, :], scalar1=float(2 * i - 1), scalar2=None, op0=ALU.is_ge),
                              r=[Jm], w=[frc])
                        S.dve(lambda e: e.tensor_tensor(out=frc[:, :], in0=frc[:, :], in1=vis[:, :], op=ALU.mult), r=[frc, vis], w=[frc])
                        S.dve(lambda e: e.tensor_tensor(out=frc[:, :], in0=frc[:, :], in1=col0[:, :], op=ALU.max), r=[frc, col0], w=[frc])
                        S.dve(lambda e: e.tensor_tensor(out=sc[0][:, :], in0=imp[:, :], in1=vis[:, :], op=ALU.mult), r=[imp, vis], w=[sc[0]])
                        S.dve(lambda e: e.tensor_scalar(out=sc[1][:, :], in0=vis[:, :], scalar1=-1.0, scalar2=1e9, op0=ALU.add, op1=ALU.mult),
                              r=[vis], w=[sc[1]])
                        S.dve(lambda e: e.tensor_tensor(out=sc[0][:, :], in0=sc[0][:, :], in1=sc[1][:, :], op=ALU.add), r=[sc[0], sc[1]], w=[sc[0]])
                        S.dve(lambda e: e.scalar_tensor_tensor(out=sc[0][:, :], in0=frc[:, :], scalar=2e9, in1=sc[0][:, :],
                                                               op0=ALU.mult, op1=ALU.add), r=[frc, sc[0]], w=[sc[0]])
                        S.dve(lambda e: e.max(out=m8[:, 0:8], in_=sc[0][:, :]), r=[sc[0]], w=[m8])
                        S.dve(lambda e: e.match_replace(out=sc[1][:, :], in_to_replace=m8[:, 0:8], in_values=sc[0][:, :], imm_value=-3e38),
                              r=[sc[0], m8], w=[sc[1]])
                        S.dve(lambda e: e.max(out=m8[:, 8:16], in_=sc[1][:, :]), r=[sc[1]], w=[m8])
                        S.dve(lambda e: e.tensor_scalar(out=selm[:, :], in0=sc[0][:, :], scalar1=m8[:, 15:16], scalar2=None, op0=ALU.is_ge),
                              r=[sc[0], m8], w=[selm])
                        S.dve(lambda e: e.tensor_tensor(out=selm[:, :], in0=selm[:, :], in1=vis[:, :], op=ALU.mult), r=[selm, vis], w=[selm])
                        SM = selm
                    else:
                        SM = vis
                    S.dve(lambda e, SM=SM: e.tensor_scalar(out=nm[:, :], in0=SM[:, :], scalar1=-NEG, scalar2=NEG, op0=ALU.mult, op1=ALU.add),
                          r=[SM], w=[nm])

                def stage1b(i):
                    nmT = nmTs[i % 2]
                    S.pe(lambda e: e.transpose(pM[:64, 0:128], nm[:, :], identb[:, :]), r=[nm, identb], w=[pM])
                    S.act(lambda e: e.copy(out=nmT[:, :, :], in_=pM[:64, 0:128].unsqueeze(1).broadcast_to([64, 4, 128])), r=[pM], w=[nmT])

                def stage2(i):
                    ocat = ocats[i % 2]
                    nmT = nmTs[i % 2]
                    pairs = []
                    PbS = pO[2:4]
                    for kt in range(i + 1):
                        pairs.append((lambda kt=kt: KT[:, 1, kt * 128:kt * 128 + 128], kt, (trib[:, 0:128] if kt == i else None),
                                      PbS, lambda kt=kt: VA[:, kt, 0, 0:129], kt == 0, 1 if kt == i else None))
                    PbW = pO[0:2]
                    kts = [kt for kt in range(i - 4, i + 1) if kt >= 0]
                    for kt in kts:
                        m_ = trib[:, 0:128] if kt == i else (trib[:, 128:256] if kt == i - 4 else None)
                        pairs.append((lambda kt=kt: KT[:, 2, kt * 128:kt * 128 + 128], False, m_,
                                      PbW, lambda kt=kt: VA[:, kt, 1, 0:129], kt == kts[0], 2 if kt == kts[-1] else None))
                    En = qk_exp(i, pairs[0][0], [KT], pairs[0][1], nmT)
                    for k_, (lf, wm, m_, Pb_, rf, first_, fin_) in enumerate(pairs):
                        E = En
                        if k_ + 1 < len(pairs):
                            En = qk_exp(i, pairs[k_ + 1][0], [KT], pairs[k_ + 1][1], nmT)
                        if m_ is not None:
                            mul_mask(E, m_, [trib])
                        pv_group(Pb_, E, rf, 129, first_, [VA])
                        if fin_ is not None:
                            finish_branch(Pb_, i, fin_, False, ocat)
                    if i == 0:
                        pass
                    OB = ocb[i % 2]
                    S.act(lambda e, OB=OB: e.copy(out=OB[:, :], in_=ocat[:, :, :].rearrange("p h d -> p (h d)")), r=[ocat], w=[OB])
                    S.dma("sp", o_loc.ap()[i * 128:i * 128 + 128, :], OB[:, :], r=[OB], w=[o_locT[i // 8]], key=OB)
                    if i % 8 == 7:
                        c = i // 8
                        allgather(o_loc.ap()[c * 1024:c * 1024 + 1024, :].rearrange("(p a) f -> p (a f)", a=8),
                                  og.ap()[c * 4096:c * 4096 + 4096, :].rearrange("(q a) f -> q (a f)", a=8), o_locT[c], ogT)
                if NTQ > 0:
                    stage1(0)
                    stage1b(0)
                for i in range(NTQ):
                    if i + 1 < NTQ:
                        stage1(i + 1)
                    stage2(i)
                    if i + 1 < NTQ:
                        stage1b(i + 1)
                S.barrier()
        phase_c()

        pp.close()

        S.dma("sp", win_s.ap()[:, 0:508], wstate.ap()[:, 4:512], key=Buf("wcopy"), final=True)
        def phase_s():
            with contextlib.ExitStack() as ps_:
                NP = PAST // 128
                NB = 257
                w1sb = sbuf(ps_, "w1sb_s", [128, 2, 32, 128], BF16)
                w2sb = sbuf(ps_, "w2sb_s", [128, 2, 128], BF16)
                posb = sbuf(ps_, "posb_s", [128, 64], BF16)
                b1sb = sbuf(ps_, "b1sb_s", [128, 2], F32)
                biasb = sbuf(ps_, "biasb_s", [128, 2], F32)
                Ebig = sbuf(ps_, "Ebig_s", [64, SEQ], BF16)
                smask = sbuf(ps_, "smask_s", [16, 5, 16], BF16)
                sel16 = sbuf(ps_, "sel16_s", [16, 68], F32)
                sel16b = sbuf(ps_, "sel16b_s", [16, 4], BF16)
                hmask = sbuf(ps_, "hmask_s", [16, 12], F32)
                big = sbuf(ps_, "bigS", [128, 2, 16512], BF16)
                XcT = big
                KsT = big
                Vs = big
                Vs3 = big[:, 1, :].rearrange("p (g c) -> p g c", c=129)
                E2 = sbuf(ps_, "E2s", [64, 128], BF16)
                Rrep = sbuf(ps_, "Rrep_s", [32, 128], F32)
                KwT = sbuf(ps_, "KwT", [128, 512], BF16)
                Vw = sbuf(ps_, "Vw", [128, 4, 130], BF16)
                stg = [sbuf(ps_, "stg%d" % i, [128, 8192], F32) for i in range(2)]
                wst = sbuf(ps_, "wst", [128, 4, 2, 128], F32)
                KcT = sbuf(ps_, "KcT_s", [128, 1024], BF16)
                Vc = sbuf(ps_, "Vc_s", [128, 8, 392], BF16)
                hx = sbuf(ps_, "hx_s", [128, 512], F32)
                ht = sbuf(ps_, "ht_s", [128, 512], F32)
                hT = [sbuf(ps_, "hT_s%d" % i, [128, 512], BF16) for i in range(2)]
                Es = [sbuf(ps_, "Es%d" % i, [128, 16], BF16) for i in range(2)]
                rs = sbuf(ps_, "rs_s", [16, 8], F32)
                U = sbuf(ps_, "U_s", [16, 260], BF16)
                gt = sbuf(ps_, "gt_s", [16, 12], F32)
                gs = sbuf(ps_, "gs_s", [16, 4], F32)
                ocat = sbuf(ps_, "ocat_s", [16, 128], F32)
                ocb = sbuf(ps_, "ocb_s", [16, 128], BF16)
                imp = sbuf(ps_, "imp_s", [4, 260], F32)
                sc = [sbuf(ps_, "sc_s%d" % i, [4, 260], F32) for i in range(2)]
                m8 = sbuf(ps_, "m8_s", [4, 16], F32)
                nm = sbuf(ps_, "nm_s", [4, 320], BF16)
                nmT = sbuf(ps_, "nmT_s", [64, 5, 4, 4], BF16)
                pS = [psum(ps_, "qS%d" % i, [128, 512], F32) for i in range(2)]
                pO = [psum(ps_, "qO%d" % i, [128, 512], F32) for i in range(3)]
                pTf = [psum(ps_, "qT%d" % i, [128, 512], F32) for i in range(2)]
                pM = psum(ps_, "qM", [128, 1024], BF16)

                S.dma("pool", w1sb[:], cw1.ap().rearrange("e s d h -> d e s h"), w=[w1sb], key=w1sb)
                S.dma("pool", w2sb[:], cw2.ap().rearrange("e h d -> h e d"), w=[w2sb], key=w2sb)
                S.dma("pool", posb[:], posT.ap(), w=[posb], key=posb)
                S.dma("sp", b1sb[:], b1T.ap(), w=[b1sb], key=b1sb)
                S.dma("pool", Ebig[:], Ebig_d.ap(), w=[Ebig], key=Ebig)
                S.dma("pool", smask[:], smask_d.ap(), w=[smask], key=smask)
                S.dma("sp", sel16[:], sel16_d.ap(), w=[sel16], key=sel16)
                S.dma("pool", sel16b[:], sel16_d.ap()[:, 0:4], w=[sel16b], key=sel16b)
                S.dma("sp", hmask[:], hmask_d.ap(), w=[hmask], key=hmask)
                S.dve(lambda e: e.memset(Vw[:, :, 128:129], 1.0), w=[Vw])
                S.dve(lambda e: e.memset(KcT[:], 0.0), w=[KcT])
                S.dve(lambda e: e.memset(Vc[:], 0.0), w=[Vc])
                S.dve(lambda e: e.memset(Vc[:, 0:7, 128:129], 1.0), w=[Vc])
                S.dve(lambda e: e.memset(Vc[:127, 7, 128:129], 1.0), w=[Vc])
                S.dma("pool", Vc[:, :, 129:386], cover_s.ap().rearrange("(t p) j -> p t j", p=128), w=[Vc], key=Vc)

                def bias_mm(e):
                    ins = None
                    for ee in range(2):
                        for s_ in range(32):
                            ins = e.matmul(pS[0][:, ee:ee + 1], w1sb[:, ee, s_, :], posb[:, ee * 32 + s_:ee * 32 + s_ + 1],
                                           start=(ee == 0 and s_ == 0), stop=(s_ == 31), skip_group_check=True)
                    return ins
                S.pe(bias_mm, r=[w1sb, posb], w=[pS[0]])
                S.dve(lambda e: e.tensor_tensor(out=biasb[:], in0=pS[0][:, 0:2], in1=b1sb[:], op=ALU.add), r=[pS[0], b1sb], w=[biasb])

                pti = sbuf(ps_, "pti", [32, 16], I32)
                ptf = sbuf(ps_, "ptf", [32, 16], F32)
                idf = sbuf(ps_, "idf", [128, 16], F32)
                idi = sbuf(ps_, "idi", [128, 16], I32)
                pix = sbuf(ps_, "pix", [128, 1], F32)
                S.dma("sp", pti[:], ptabT.ap(), w=[pti], key=pti)
                S.dma("sp", pix[:], pidx_d.ap(), w=[pix], key=pix)
                S.dma("sp", Rrep[:], Rrep_d.ap(), w=[Rrep], key=Rrep)
                S.dma("pool", E2[:], E2_d.ap(), w=[E2], key=E2)
                S.dve(lambda e: e.tensor_copy(out=ptf[:], in_=pti[:]), r=[pti], w=[ptf])
                S.pe(lambda e: e.matmul(pS[1][:, 0:16], Rrep[:, :], ptf[:, :], start=True, stop=True), r=[Rrep, ptf], w=[pS[1]])
                S.dve(lambda e: e.tensor_scalar(out=idf[:], in0=pS[1][:, 0:16], scalar1=4.0, scalar2=pix[:, 0:1], op0=ALU.mult, op1=ALU.add),
                      r=[pS[1], pix], w=[idf])
                S.dve(lambda e: e.tensor_copy(out=idi[:], in_=idf[:]), r=[idf], w=[idi])
                gctr = [0]

                def qgather(cache, s_, q):
                    G = stg[gctr[0] % 2]
                    gctr[0] += 1
                    rows = cache.ap().rearrange("g (u t) e d -> (g u) (t e d)", u=4)
                    k = s_ * 4 + q
                    S.custom_dma("pool", lambda e: e.indirect_dma_start(
                        out=G[:, :], out_offset=None, in_=rows,
                        in_offset=bass.IndirectOffsetOnAxis(ap=idi[:, k:k + 1], axis=0)), r=[idi], w=[G], key=G)
                    return G

                for s_ in range(4):
                    ev = 0
                    for q in range(4):
                        G = qgather(ccache, s_, q)
                        G3 = G[:, :].rearrange("p (t e d) -> p t e d", t=32, e=2)
                        for ee in range(2):
                            for t0 in range(0, 32, 4):
                                P = pTf[ev % 2]

                                def trp(e, G3=G3, ee=ee, t0=t0, P=P):
                                    ins = None
                                    for j in range(4):
                                        ins = e.transpose(P[:, j * 128:j * 128 + 128], G3[:, t0 + j, ee, :], identf[:, :])
                                    return ins
                                S.pe(trp, r=[G, identf], w=[P])
                                dst = XcT[:, ee, 4096 * q:4096 * q + 4096].rearrange("d (p t) -> d t p", t=32)[:, t0:t0 + 4, :]
                                src = P[:, 0:512].rearrange("d (j p) -> d j p", j=4)
                                if ev % 2 == 0:
                                    S.act(lambda e, dst=dst, src=src: e.copy(out=dst, in_=src), r=[P], w=[XcT])
                                else:
                                    S.dve(lambda e, dst=dst, src=src: e.tensor_copy(out=dst, in_=src), r=[P], w=[XcT])
                                ev += 1
                    for ee in range(2):
                        for c0 in (0, 512):
                            cn = 512 if c0 == 0 else 511
                            P = pS[(c0 // 512) % 2]

                            def hmm(e, ee=ee, c0=c0, cn=cn, P=P):
                                ins = None
                                for rs_ in range(32):
                                    lo = rs_ + 16 * c0
                                    ins = e.matmul(P[:, :cn], w1sb[:, ee, rs_, :], XcT[:, ee, lo:lo + 16 * (cn - 1) + 1:16],
                                                   start=(rs_ == 0), stop=(rs_ == 31))
                                return ins
                            S.pe(hmm, r=[w1sb, XcT], w=[P])
                            S.act(lambda e, ee=ee, cn=cn, P=P: e.activation(out=hx[:, :cn], in_=P[:, :cn], func=AF.Identity,
                                                                             bias=biasb[:, ee:ee + 1]), r=[P, biasb], w=[hx])
                            S.dve(lambda e, cn=cn: e.tensor_tensor(out=ht[:, :cn], in0=hx[:, :cn], in1=hx[:, :cn], op=ALU.mult), r=[hx], w=[ht])
                            S.dve(lambda e, cn=cn: e.tensor_scalar(out=ht[:, :cn], in0=ht[:, :cn], scalar1=0.044715, scalar2=1.0,
                                                                   op0=ALU.mult, op1=ALU.add), r=[ht], w=[ht])
                            S.dve(lambda e, cn=cn: e.tensor_tensor(out=ht[:, :cn], in0=ht[:, :cn], in1=hx[:, :cn], op=ALU.mult), r=[ht, hx], w=[ht])
                            S.act(lambda e, cn=cn: e.activation(out=ht[:, :cn], in_=ht[:, :cn], func=AF.Sigmoid, scale=1.5957691216),
                                  r=[ht], w=[ht])
                            H = hT[ee]
                            S.dve(lambda e, cn=cn, H=H: e.tensor_tensor(out=H[:, :cn], in0=hx[:, :cn], in1=ht[:, :cn], op=ALU.mult),
                                  r=[hx, ht], w=[H])
                            if ee == 0:
                                S.pe(lambda e, cn=cn, H=H: e.matmul(pO[0][:, :cn], w2sb[:, 0, :], H[:, :cn], start=True, stop=True),
                                     r=[w2sb, H], w=[pO[0]])
                                S.act(lambda e, cn=cn, c0=c0: e.copy(out=KcT[:, c0:c0 + cn], in_=pO[0][:, :cn]), r=[pO[0]], w=[KcT])
                            else:
                                for t0 in range(0, cn, 128):
                                    nb = min(128, cn - t0)
                                    S.pe(lambda e, nb=nb, t0=t0, H=H: e.matmul(pO[1][:nb, 0:128], H[:, t0:t0 + nb], w2sb[:, 1, :],
                                                                               start=True, stop=True), r=[w2sb, H], w=[pO[1]])
                                    S.act(lambda e, nb=nb, ct=(c0 + t0) // 128: e.copy(out=Vc[:nb, ct, 0:128], in_=pO[1][:nb, 0:128]),
                                          r=[pO[1]], w=[Vc])

                    pc_ = [0]
                    Qs = QTs[:, :, 4 * s_:4 * s_ + 4]

                    def qk16(lhsT, nk, lr, maskchunk=None, Qs=Qs):
                        x = pc_[0] % 2
                        pc_[0] += 1
                        P, E = pS[x], Es[x]

                        def f(e):
                            ins = e.matmul(P[:nk, 0:16], lhsT, Qs, start=True, stop=(maskchunk is None))
                            if maskchunk is not None:
                                ch, kt = maskchunk
                                ins = e.matmul(P[:nk, 0:16], E2[:, :], nmT[:, ch, :, :], start=False, stop=True)
                            return ins
                        S.pe(f, r=[QTs, E2, nmT] + lr, w=[P])
                        S.act(lambda e: e.activation(out=E[:nk, :], in_=P[:nk, 0:16], func=AF.Exp, scale=SCALE), r=[P], w=[E])
                        return E

                    def pv16(Pacc, E, nk, rhs, ncol, first, last, r):
                        S.pe(lambda e: e.matmul(Pacc[:16, 0:ncol], E[:nk, :], rhs, start=first, stop=last), r=[E] + r, w=[Pacc])

                    def fin16(Pacc, br, first_branch):
                        S.dve(lambda e: e.tensor_scalar(out=rs[:, 0:1], in0=Pacc[:16, 128:129], scalar1=1e-30, scalar2=None, op0=ALU.max),
                              r=[Pacc], w=[rs])
                        S.dve(lambda e: e.reciprocal(out=rs[:, 0:1], in_=rs[:, 0:1]), r=[rs], w=[rs])
                        S.dve(lambda e: e.tensor_tensor(out=rs[:, 1:2], in0=rs[:, 0:1], in1=gs[:, br:br + 1], op=ALU.mult), r=[rs, gs], w=[rs])
                        if first_branch:
                            S.dve(lambda e: e.tensor_scalar(out=ocat[:, :], in0=Pacc[:16, 0:128], scalar1=rs[:, 1:2], scalar2=None, op0=ALU.mult),
                                  r=[Pacc, rs], w=[ocat])
                        else:
                            S.dve(lambda e: e.scalar_tensor_tensor(out=ocat[:, :], in0=Pacc[:16, 0:128], scalar=rs[:, 1:2], in1=ocat[:, :],
                                                                   op0=ALU.mult, op1=ALU.add), r=[Pacc, rs, ocat], w=[ocat])

                    S.pe(lambda e, s_=s_: e.matmul(pO[2][:16, 0:12], sel16[:, 4 + 16 * s_:4 + 16 * s_ + 16], GL[:16, NT, :], start=True, stop=True),
                         r=[sel16, GL], w=[pO[2]])
                    S.dve(lambda e: e.tensor_tensor(out=gt[:, :], in0=pO[2][:16, 0:12], in1=hmask[:, :], op=ALU.mult), r=[pO[2], hmask], w=[gt])
                    S.dve(lambda e: e.tensor_reduce(out=gs[:, 0:3], in_=gt[:, :].rearrange("p (b h) -> p b h", b=3), axis=AX.X, op=ALU.add),
                          r=[gt], w=[gs])

                    def run_pairs(plist):
                        En = qk16(*plist[0][0][:3], **plist[0][0][3])
                        for k_, (qa, post, pa) in enumerate(plist):
                            E = En
                            if k_ + 1 < len(plist):
                                nq = plist[k_ + 1][0]
                                En = qk16(*nq[:3], **nq[3])
                            if post is not None:
                                post(E)
                            pv16(pa[0], E, *pa[1:])

                    run_pairs([((KcT[:, ct * 128:ct * 128 + 128], 128, [KcT], {}), None,
                                (pO[0], 128, Vc[:, ct, 0:386], 386, ct == 0, ct == 7, [Vc])) for ct in range(8)])
                    fin16(pO[0], 0, True)
                    S.dve(lambda e: e.tensor_scalar(out=U[:, 0:257], in0=pO[0][:16, 129:386], scalar1=rs[:, 0:1], scalar2=None, op0=ALU.mult),
                          r=[pO[0], rs], w=[U])
                    S.pe(lambda e: e.matmul(pO[2][:4, 0:257], sel16b[:, 0:4], U[:, 0:257], start=True, stop=True), r=[sel16b, U], w=[pO[2]])
                    S.dve(lambda e: e.tensor_copy(out=sc[0][:, 0:257], in_=pO[2][:4, 0:257]), r=[pO[2]], w=[sc[0]])
                    S.dve(lambda e: e.memset(sc[0][:, 0:1], 2e9), w=[sc[0]])
                    S.dve(lambda e: e.memset(sc[0][:, 255:257], 2e9), w=[sc[0]])
                    S.dve(lambda e: e.max(out=m8[:, 0:8], in_=sc[0][:, 0:257]), r=[sc[0]], w=[m8])
                    S.dve(lambda e: e.match_replace(out=sc[1][:, 0:257], in_to_replace=m8[:, 0:8], in_values=sc[0][:, 0:257], imm_value=-3e38),
                          r=[sc[0], m8], w=[sc[1]])
                    S.dve(lambda e: e.max(out=m8[:, 8:16], in_=sc[1][:, 0:257]), r=[sc[1]], w=[m8])
                    S.dve(lambda e: e.memset(nm[:, :], 0.0), w=[nm])
                    S.dve(lambda e: e.tensor_scalar(out=sc[1][:, 0:257], in0=sc[0][:, 0:257], scalar1=m8[:, 15:16], scalar2=None, op0=ALU.is_ge),
                          r=[sc[0], m8], w=[sc[1]])
                    S.dve(lambda e: e.tensor_scalar(out=nm[:, 0:257], in0=sc[1][:, 0:257], scalar1=-NEG, scalar2=NEG, op0=ALU.mult, op1=ALU.add),
                          r=[sc[1]], w=[nm])

                    def trn(e):
                        ins = None
                        for ch in range(5):
                            ins = e.transpose(pM[:64, ch * 4:ch * 4 + 4], nm[:, ch * 64:ch * 64 + 64], identb[:4, :4])
                        return ins
                    S.pe(trn, r=[nm, identb], w=[pM])
                    S.act(lambda e: e.copy(out=nmT[:, :, :, :], in_=pM[:64, 0:20].rearrange("p (c q) -> p c q", c=5).unsqueeze(2)
                                           .broadcast_to([64, 5, 4, 4])), r=[pM], w=[nmT])

                    S.dve(lambda e: e.memset(Vs3[:, :, 128:129], 1.0), w=[Vs])
                    ev = 0
                    for q in range(4):
                        G = qgather(scache, s_, q)
                        G3 = G[:, :].rearrange("p (t e d) -> p t e d", t=32, e=2)
                        for t0 in range(0, 32, 4):
                            P = pTf[ev % 2]
                            ev += 1
                            kt0 = q * 32 + t0

                            def trk(e, G3=G3, t0=t0, P=P):
                                ins = None
                                for j in range(4):
                                    ins = e.transpose(P[:, j * 128:j * 128 + 128], G3[:, t0 + j, 0, :], identf[:, :])
                                return ins
                            S.pe(trk, r=[G, identf], w=[P])
                            S.act(lambda e, P=P, kt0=kt0: e.copy(out=KsT[:, 0, kt0 * 128:kt0 * 128 + 512], in_=P[:, 0:512]), r=[P], w=[KsT])
                            S.dve(lambda e, G3=G3, t0=t0, kt0=kt0: e.tensor_copy(out=Vs3[:, kt0:kt0 + 4, 0:128], in_=G3[:, t0:t0 + 4, 1, :]),
                                  r=[G], w=[Vs])
                    def newmask(E, s_=s_):
                        S.dve(lambda e: e.tensor_tensor(out=E[:16, :], in0=E[:16, :], in1=smask[:, s_, :], op=ALU.mult), r=[E, smask], w=[E])

                    def oldmask(E):
                        S.dve(lambda e: e.tensor_tensor(out=E[:16, :], in0=E[:16, :], in1=smask[:, 4, :], op=ALU.mult), r=[E, smask], w=[E])

                    pl = [((KsT[:, 0, kt * 128:kt * 128 + 128], 128, [KsT], {"maskchunk": (kt // 32, kt)}), None,
                           (pO[1], 128, Vs3[:, kt, 0:129], 129, kt == 0, False, [Vs])) for kt in range(NP)]
                    pl.append(((KTs[:, 1, :], 16, [KTs], {}), newmask, (pO[1], 16, VAs[:, 0, 0:129], 129, False, True, [VAs])))
                    run_pairs(pl)
                    fin16(pO[1], 1, False)
                    S.dma("sp", wst[:, :, :, :], wstate.ap()[s_].rearrange("(t p) e d -> p t e d", p=128), w=[wst], key=wst)
                    for t_ in range(4):
                        P = pTf[t_ % 2]
                        S.pe(lambda e, t_=t_, P=P: e.transpose(P[:, 0:128], wst[:, t_, 0, :], identf[:, :]), r=[wst, identf], w=[P])
                        S.act(lambda e, t_=t_, P=P: e.copy(out=KwT[:, t_ * 128:t_ * 128 + 128], in_=P[:, 0:128]), r=[P], w=[KwT])
                    S.dve(lambda e: e.tensor_copy(out=Vw[:, :, 0:128], in_=wst[:, :, 1, :]), r=[wst], w=[Vw])
                    pl = [((KwT[:, t_ * 128:t_ * 128 + 128], 128, [KwT], {}), (oldmask if t_ == 0 else None),
                           (pO[0], 128, Vw[:, t_, 0:129], 129, t_ == 0, False, [Vw])) for t_ in range(4)]
                    pl.append(((KTs[:, 2, :], 16, [KTs], {}), newmask, (pO[0], 16, VAs[:, 1, 0:129], 129, False, True, [VAs])))
                    run_pairs(pl)
                    fin16(pO[0], 2, False)
                    S.act(lambda e: e.copy(out=ocb[:, :], in_=ocat[:, :]), r=[ocat], w=[ocb])
                    for h in range(4):
                        S.dma("sp", o_locs.ap()[4 * s_:4 * s_ + 4, 128 * h:128 * h + 128], ocb[4 * h:4 * h + 4, :], r=[ocb], w=[o_locT[4]], key=ocb)
                S.barrier()

        if RUN_S:
            phase_s()

        allgather(o_locs.ap().rearrange("r (b x) -> (r b) x", b=8), ogs.ap().rearrange("r (b x) -> (r b) x", b=8), o_locT[4], ogT)

        TBS = [(0, 512), (512, 512), (1024, 4)]

        def token_phase(layer):
            with contextlib.ExitStack() as pd:
                hT = sbuf(pd, "hres%d" % layer, [128, 16, NTOK], F32)
                actT = sbuf(pd, "actT%d" % layer, [128, 8, NTOK], BF16)
                xnT = sbuf(pd, "xnT_d%d" % layer, [128, 16, NTOK], BF16)
                wbuf = [sbuf(pd, "wbuf%d_%d" % (i, layer), [128, 16, 512], BF16) for i in range(2)]
                xst = sbuf(pd, "xst%d" % layer, [128, D], F32)
                ost = [sbuf(pd, "ost%d_%d" % (i, layer), [128, 2, 512], BF16) for i in range(4)]
                oidx = sbuf(pd, "oidx_s%d" % layer, [128, 72], I32)
                rstd = sbuf(pd, "rstd_d%d" % layer, [128, 512], F32)
                sq = sbuf(pd, "sq_d%d" % layer, [128, 512], F32)
                rl = [sbuf(pd, "rl%d_%d" % (i, layer), [128, 512], F32) for i in range(2)]
                gT = sbuf(pd, "gT_d%d" % layer, [128, 16], F32)
                onesf = sbuf(pd, "onesf%d" % layer, [128, 128], F32)
                epsb = sbuf(pd, "epsb_d%d" % layer, [128, 1], F32)
                pz = [psum(pd, "dz%d_%d" % (i, layer), [128, 512], F32) for i in range(4)]
                pM = psum(pd, "dM%d" % layer, [128, 1024], BF16)
                pT = psum(pd, "dT%d" % layer, [128, 512], F32)
                pn = psum(pd, "dn%d" % layer, [128, 512], F32)
                ctr = {"w": 0, "z": 0, "o": 0, "r": 0}

                if layer == 0:
                    S.dma("sp", oidx[:, 0:36], oidx_d.ap(), w=[oidx], key=oidx)
                else:
                    S.dma("sp", oidx[:, :], oidx2_d.ap(), w=[oidx], key=oidx)
                S.dve(lambda e: e.memset(onesf[:], 1.0 / D), w=[onesf])
                S.dve(lambda e: e.memset(epsb[:], RMS_EPS), w=[epsb])

                if layer == 0:
                    for t in range(9):
                        n = 128 if t < 8 else 4
                        S.dma("sp", xst[:n, :], x_tok.ap()[t * 128:t * 128 + n, :], w=[xst], key=xst)
                        for q in range(4):
                            def trx(e, q=q, n=n):
                                ins = None
                                for kk in range(4):
                                    k = q * 4 + kk
                                    ins = e.transpose(pT[:, kk * 128:kk * 128 + n], xst[:n, k * 128:k * 128 + 128], identf[:n, :n])
                                return ins
                            S.pe(trx, r=[xst, identf], w=[pT])
                            S.act(lambda e, q=q, n=n, t=t: e.copy(out=hT[:, q * 4:q * 4 + 4, t * 128:t * 128 + n],
                                                                 in_=pT[:, :].rearrange("p (k t) -> p k t", k=4)[:, :, :n]), r=[pT], w=[hT])
                else:
                    S.dma("sp", hT[:, :, :], hspill.ap().rearrange("p (k t) -> p k t", k=16), r=[hspT], w=[hT], key=hT)

                def load_act(gsrc, gsrc_s, gbuf, colfn):
                    for t in range(9):
                        n = 128 if t < 8 else 4
                        O_ = ost[ctr["o"] % 4]
                        ctr["o"] += 1
                        for jj in range(2):
                            S.custom_dma("pool", lambda e, O_=O_, jj=jj, t=t, col=colfn(t, jj): e.indirect_dma_start(
                                out=O_[:, jj, :], out_offset=None, in_=(gsrc if t < 8 else gsrc_s).ap(),
                                in_offset=bass.IndirectOffsetOnAxis(ap=oidx[:, col:col + 1], axis=0)),
                                r=[oidx, gbuf], w=[O_], key=O_)

                        def tro(e, O_=O_, n=n):
                            ins = None
                            for kk in range(8):
                                jj, q = kk // 4, kk % 4
                                ins = e.transpose(pM[:, kk * 128:kk * 128 + n], O_[:n, jj, q * 128:q * 128 + 128], identb[:n, :n])
                            return ins
                        S.pe(tro, r=[O_, identb], w=[pM])
                        S.act(lambda e, n=n, t=t: e.copy(out=actT[:, :, t * 128:t * 128 + n],
                                                         in_=pM[:, :].rearrange("p (k t) -> p k t", k=8)[:, :, :n]), r=[pM], w=[actT])

                def proj_blocks(Wd, row0):
                    blks = []
                    for cb in range(4):
                        def dma(wb, cb=cb):
                            S.dma("pool", wb[:, 0:8, :], Wd.ap()[row0:row0 + 1024, cb * 512:cb * 512 + 512].rearrange("(k p) c -> p k c", p=128),
                                  w=[wb], key=wb)

                        def comp(wb, cb=cb):
                            for cc in range(4):
                                for (t0, nt) in TBS:
                                    P = pz[ctr["z"] % 4]
                                    ctr["z"] += 1

                                    def mm(e, wb=wb, cc=cc, t0=t0, nt=nt, P=P):
                                        ins = None
                                        for k in range(8):
                                            ins = e.matmul(P[:, :nt], wb[:, k, cc * 128:cc * 128 + 128], actT[:, k, t0:t0 + nt],
                                                           start=(k == 0), stop=(k == 7))
                                        return ins
                                    S.pe(mm, r=[wb, actT], w=[P])
                                    c = cb * 4 + cc
                                    S.dve(lambda e, P=P, c=c, t0=t0, nt=nt: e.tensor_tensor(out=hT[:, c, t0:t0 + nt], in0=P[:, :nt],
                                                                                             in1=hT[:, c, t0:t0 + nt], op=ALU.add),
                                          r=[P, hT], w=[hT])
                        blks.append((dma, comp))
                    return blks

                def run_blocks(blks):
                    bufs = []
                    for i, (dma, comp) in enumerate(blks):
                        if i == 0:
                            wb0 = wbuf[ctr["w"] % 2]
                            ctr["w"] += 1
                            dma(wb0)
                            bufs.append(wb0)
                        if i + 1 < len(blks):
                            wbn = wbuf[ctr["w"] % 2]
                            ctr["w"] += 1
                            blks[i + 1][0](wbn)
                            bufs.append(wbn)
                        comp(bufs[i])

                def proj_accum(Wd, row0):
                    run_blocks(proj_blocks(Wd, row0))

                def rmsnorm_T(gain_d, out_fn, wlist):
                    S.dma("sp", gT[:], gain_d.ap(), w=[gT], key=gT)
                    for (t0, nt) in TBS:
                        for k in range(16):
                            S.act(lambda e, k=k, t0=t0, nt=nt: e.activation(out=sq[:, :nt], in_=hT[:, k, t0:t0 + nt], func=AF.Square),
                                  r=[hT], w=[sq])
                            S.pe(lambda e, k=k, nt=nt: e.matmul(pn[:, :nt], onesf[:, :], sq[:, :nt], start=(k == 0), stop=(k == 15)),
                                 r=[onesf, sq], w=[pn])
                        S.act(lambda e, nt=nt: e.activation(out=rstd[:, :nt], in_=pn[:, :nt], func=AF.Sqrt, bias=epsb[:, 0:1]),
                              r=[pn, epsb], w=[rstd])
                        S.dve(lambda e, nt=nt: e.reciprocal(out=rstd[:, :nt], in_=rstd[:, :nt]), r=[rstd], w=[rstd])
                        for k in range(16):
                            S.dve(lambda e, k=k, t0=t0, nt=nt: e.scalar_tensor_tensor(out=out_fn(k, t0, nt), in0=hT[:, k, t0:t0 + nt],
                                                                                      scalar=gT[:, k:k + 1], in1=rstd[:, :nt],
                                                                                      op0=ALU.mult, op1=ALU.mult),
                                  r=[hT, gT, rstd], w=wlist)

                def ffn(W1, W2):
                    blks = []
                    for hg in range(8):
                        for sub in range(2):
                            c0 = hg * 1024 + sub * 512

                            def dma(wb, c0=c0):
                                S.dma("pool", wb[:, :, :], W1.ap()[:, c0:c0 + 512].rearrange("(k p) c -> p k c", p=128), w=[wb], key=wb)

                            def comp(wb, sub=sub):
                                for fc in range(4):
                                    for (t0, nt) in TBS:
                                        P = pz[ctr["z"] % 4]
                                        ctr["z"] += 1

                                        def mm(e, wb=wb, fc=fc, t0=t0, nt=nt, P=P):
                                            ins = None
                                            for k in range(16):
                                                ins = e.matmul(P[:, :nt], wb[:, k, fc * 128:fc * 128 + 128], xnT[:, k, t0:t0 + nt],
                                                               start=(k == 0), stop=(k == 15))
                                            return ins
                                        S.pe(mm, r=[wb, xnT], w=[P])
                                        R_ = rl[ctr["r"] % 2]
                                        ctr["r"] += 1
                                        S.act(lambda e, P=P, R_=R_, nt=nt: e.activation(out=R_[:, :nt], in_=P[:, :nt], func=AF.Relu), r=[P], w=[R_])
                                        eng = S.pool if ctr["r"] % 2 == 0 else S.dve
                                        eng(lambda e, R_=R_, nt=nt, t0=t0, kk=sub * 4 + fc: e.tensor_tensor(out=actT[:, kk, t0:t0 + nt], in0=R_[:, :nt],
                                                                                                          in1=R_[:, :nt], op=ALU.mult),
                                            r=[R_], w=[actT])
                            blks.append((dma, comp))
                        blks.extend(proj_blocks(W2, hg * 1024))
                    run_blocks(blks)

                xn_out = lambda k, t0, nt: xnT[:, k, t0:t0 + nt]
                if layer == 0:
                    for g in range(2):
                        load_act(og, ogs, ogT, lambda t, jj, g=g: t * 4 + 2 * g + jj)
                        proj_accum(w_out0, g * 1024)
                    rmsnorm_T(gffn0T, xn_out, [xnT])
                    ffn(ffn1_0, ffn2_0)
                    rmsnorm_T(gmix1T, xn_out, [xnT])
                    for cp in range(8):
                        S.dma("sp", xg_in.ap()[cp].rearrange("p (k t) -> p k t", k=2), xnT[:, 2 * cp:2 * cp + 2, :], r=[xnT],
                              w=[xg_inT[cp]], key=Buf("xgk%d" % cp))
                        allgather(xg_in.ap()[cp], xg_all.ap()[cp], xg_inT[cp], xgT)
                    S.dma("sp", hspill.ap().rearrange("p (k t) -> p k t", k=16), hT[:, :, :], r=[hT], w=[hspT], key=hT)
                else:
                    for g in range(4):
                        load_act(og2, ogs2, og2T, lambda t, jj, g=g: t * 8 + 2 * g + jj)
                        proj_accum(w_out1, g * 1024)
                    rmsnorm_T(gffn1T, xn_out, [xnT])
                    ffn(ffn1_1, ffn2_1)
                    S.dma("sp", gT[:], gfinT.ap(), w=[gT], key=gT)
                    for (t0, nt) in TBS:
                        for k in range(16):
                            S.act(lambda e, k=k, t0=t0, nt=nt: e.activation(out=sq[:, :nt], in_=hT[:, k, t0:t0 + nt], func=AF.Square),
                                  r=[hT], w=[sq])
                            S.pe(lambda e, k=k, nt=nt: e.matmul(pn[:, :nt], onesf[:, :], sq[:, :nt], start=(k == 0), stop=(k == 15)),
                                 r=[onesf, sq], w=[pn])
                        S.act(lambda e, nt=nt: e.activation(out=rstd[:, :nt], in_=pn[:, :nt], func=AF.Sqrt, bias=epsb[:, 0:1]),
                              r=[pn, epsb], w=[rstd])
                        S.dve(lambda e, nt=nt: e.reciprocal(out=rstd[:, :nt], in_=rstd[:, :nt]), r=[rstd], w=[rstd])
                        for k in range(16):
                            S.dve(lambda e, k=k, t0=t0, nt=nt: e.scalar_tensor_tensor(out=hT[:, k, t0:t0 + nt], in0=hT[:, k, t0:t0 + nt],
                                                                                      scalar=gT[:, k:k + 1], in1=rstd[:, :nt],
                                                                                      op0=ALU.mult, op1=ALU.mult),
                                  r=[hT, gT, rstd], w=[hT])
                    for t in range(9):
                        n = 128 if t < 8 else 4
                        for q in range(4):
                            def trh(e, q=q, n=n, t=t):
                                ins = None
                                for kk in range(4):
                                    ins = e.transpose(pT[:n, kk * 128:kk * 128 + 128], hT[:, q * 4 + kk, t * 128:t * 128 + n], identf[:, :])
                                return ins
                            S.pe(trh, r=[hT, identf], w=[pT])
                            S.act(lambda e, q=q, n=n: e.copy(out=xst[:n, q * 512:q * 512 + 512], in_=pT[:n, :]), r=[pT], w=[xst])
                        S.dma("sp", y_tok.ap()[t * 128:t * 128 + n, :], xst[:n, :], r=[xst], key=xst, final=True)
                S.barrier()

        def phase_e():
            with contextlib.ExitStack() as pe_:
                wr = sbuf(pe_, "wr_s", [128, 16, 3072], BF16)
                xt = [sbuf(pe_, "ext%d" % i, [128, 16, 128], BF16) for i in range(2)]
                cst = [sbuf(pe_, "ecs%d" % i, [128, 256], F32) for i in range(2)]
                St = sbuf(pe_, "St", [128, 4, 512], F32)
                Sb = sbuf(pe_, "Sb", [128, 4, 512], BF16)
                Ss = [sbuf(pe_, "Ss%d" % i, [128, 4, 512], F32) for i in range(2)]
                Ssb = [sbuf(pe_, "Ssb%d" % i, [128, 4, 512], BF16) for i in range(4)]
                qkrs = [sbuf(pe_, "qkr%d" % i, [128, 4, 256], BF16) for i in range(2)]
                Vbs = [sbuf(pe_, "Vb%d" % i, [128, 2, 512], BF16) for i in range(2)]
                sgs = [sbuf(pe_, "sg%d" % i, [128, 1024], F32) for i in range(2)]
                ob = [sbuf(pe_, "ob%d" % i, [128, 512], F32) for i in range(2)]
                jk = sbuf(pe_, "jk", [128, 512], BF16)
                rt = [sbuf(pe_, "ert%d" % i, [128, 4, 128], F32) for i in range(4)]
                qkTs = [sbuf(pe_, "qkT%d" % i, [128, 8, 128], BF16) for i in range(2)]
                QdT = sbuf(pe_, "QdT", [128, 2, 2, 128], BF16)
                QdTm = sbuf(pe_, "QdTm", [128, 4, 2, 2, 16], BF16)
                Kd = sbuf(pe_, "Kd", [128, 2, 256], BF16)
                Kdm = sbuf(pe_, "Kdm", [16, 4, 2, 256], BF16)
                AT = sbuf(pe_, "AT", [128, 2, 128], BF16)
                decT = sbuf(pe_, "decT_s", [128, 2, 128], F32)
                qdecb = sbuf(pe_, "qdecb_s", [128, 2, 128], F32)
                kdec = sbuf(pe_, "kdec_s", [128, 2], F32)
                decTs = sbuf(pe_, "decTs_s", [16, 2, 16], F32)
                kdm = sbuf(pe_, "kdm_s", [16, 4, 2], F32)
                colmask = sbuf(pe_, "colmask_s", [128, 4, 16], F32)
                gnb = sbuf(pe_, "gnb_s", [128, 2, 512], F32)
                gpow = sbuf(pe_, "gpow_s", [128, 4], F32)
                qdecs = sbuf(pe_, "qdecs_s", [128, 2, 16], F32)
                st8 = sbuf(pe_, "st8", [128, 16], F32)
                og_ = [sbuf(pe_, "og_%d" % i, [128, 2, 512], BF16) for i in range(2)]
                pz = [psum(pe_, "ez%d" % i, [128, 512], F32) for i in range(2)]
                pq = psum(pe_, "eq", [128, 1024], BF16)
                pst = psum(pe_, "est", [128, 512], F32)
                po = [psum(pe_, "eo%d" % i, [128, 512], F32) for i in range(2)]
                pu = [psum(pe_, "eu%d" % i, [128, 512], F32) for i in range(2)]

                wrB = [Buf("wrB%d" % i) for i in range(3)]
                for cq in range(3):
                    for kc in range(4):
                        S.dma("pool", wr[:, 4 * kc:4 * kc + 4, 1024 * cq:1024 * cq + 1024],
                              wr_d.ap()[512 * kc:512 * kc + 512, 1024 * cq:1024 * cq + 1024].rearrange("(k p) c -> p k c", p=128),
                              w=[wrB[cq]], key=wrB[cq])
                S.dma("sp", decT[:], decT_d.ap().rearrange("p (h i) -> p h i", h=2), w=[decT], key=decT)
                S.dma("sp", qdecb[:], qdecb_d.ap().rearrange("p (h i) -> p h i", h=2), w=[qdecb], key=qdecb)
                S.dma("sp", decTs[:], decTs_d.ap().rearrange("p (h i) -> p h i", h=2), w=[decTs], key=decTs)
                S.dma("sp", kdm[:], kdm_d.ap().rearrange("p (s h) -> p s h", s=4), w=[kdm], key=kdm)
                S.dma("sp", colmask[:], colmask_d.ap().rearrange("p (s i) -> p s i", s=4), w=[colmask], key=colmask)
                S.dma("sp", gnb[:], gnb_d.ap().rearrange("p (h e) -> p h e", h=2), w=[gnb], key=gnb)
                S.dma("sp", qdecs[:], qdecs_d.ap().rearrange("p (h i) -> p h i", h=2), w=[qdecs], key=qdecs)
                S.dma("sp", kdec[:], kdec_d.ap(), w=[kdec], key=kdec)
                S.dma("sp", gpow[:], gpow_d.ap(), w=[gpow], key=gpow)
                S.dve(lambda e: e.memset(St[:], 0.0), w=[St])
                S.dve(lambda e: e.memset(Sb[:], 0.0), w=[Sb])

                xg5 = xg_all.ap().rearrange("c (j p) (k t) -> c j p k t", j=4, k=2)

                def load_tile(T_):
                    X, C = xt[T_ % 2], cst[T_ % 2]
                    if T_ < NT:
                        j, tl = T_ // 8, T_ % 8
                        for cp in range(8):
                            S.dma("sp", X[:, 2 * cp:2 * cp + 2, :], xg5[cp, j, :, :, tl * 128:tl * 128 + 128], r=[xgT], w=[X], key=X)
                        S.dma("sp", C[:, :], cs_r.ap()[T_ * 128:T_ * 128 + 128, :], w=[C], key=C)
                    else:
                        for cp in range(8):
                            for j in range(4):
                                S.dma("sp", X[:, 2 * cp:2 * cp + 2, 4 * j:4 * j + 4], xg5[cp, j, :, :, 1024:1028], r=[xgT], w=[X], key=X)
                        S.dma("sp", C[:NS, :], cs_rs.ap(), w=[C], key=C)

                load_tile(0)
                zc = [0]

                def stageE1(T_):
                    n = 128 if T_ < NT else NS
                    X, C = xt[T_ % 2], cst[T_ % 2]
                    qkr, Vb, sg, qkT = qkrs[T_ % 2], Vbs[T_ % 2], sgs[T_ % 2], qkTs[T_ % 2]
                    if T_ + 1 <= NT:
                        load_tile(T_ + 1)

                    def proj(cb, n=n, X=X):
                        P = pz[zc[0] % 2]
                        zc[0] += 1

                        def mm(e):
                            ins = None
                            for k in range(16):
                                ins = e.matmul(P[:n, :], X[:, k, :n], wr[:, k, cb * 512:cb * 512 + 512], start=(k == 0), stop=(k == 15))
                            return ins
                        S.pe(mm, r=[X, wrB[cb // 2]], w=[P])
                        return P

                    cosb = lambda n=n, C=C: C[:n, 0:128].unsqueeze(1).broadcast_to([n, 2, 128])
                    sinb = lambda n=n, C=C: C[:n, 128:256].unsqueeze(1).broadcast_to([n, 2, 128])
                    for half_ in range(2):
                        P = proj(half_)
                        z3 = P[:n, :].rearrange("p (h d) -> p h d", h=2)
                        a, b_, c_, d_ = rt

                        def emit_rope(z3=z3, P=P, half_=half_, n=n, cosb=cosb, sinb=sinb, C=C):
                            S.dve(lambda e: e.tensor_tensor(out=a[:n, :2, :], in0=z3[:, :, 0:128], in1=cosb(), op=ALU.mult), r=[P, C], w=[a])
                            S.dve(lambda e: e.tensor_tensor(out=b_[:n, :2, :], in0=z3[:, :, 128:256], in1=sinb(), op=ALU.mult), r=[P, C], w=[b_])
                            S.dve(lambda e: e.tensor_tensor(out=c_[:n, :2, :], in0=z3[:, :, 128:256], in1=cosb(), op=ALU.mult), r=[P, C], w=[c_])
                            S.dve(lambda e: e.tensor_tensor(out=d_[:n, :2, :], in0=z3[:, :, 0:128], in1=sinb(), op=ALU.mult), r=[P, C], w=[d_])
                            S.pool(lambda e: e.tensor_tensor(out=qkr[:n, 2 * half_:2 * half_ + 2, 0:128], in0=a[:n, :2, :], in1=b_[:n, :2, :],
                                                             op=ALU.subtract), r=[a, b_], w=[qkr])
                            S.pool(lambda e: e.tensor_tensor(out=qkr[:n, 2 * half_:2 * half_ + 2, 128:256], in0=c_[:n, :2, :], in1=d_[:n, :2, :],
                                                             op=ALU.add), r=[c_, d_], w=[qkr])
                        emit_rope()
                    for hh in range(2):
                        P = proj(2 + hh)
                        S.act(lambda e, P=P, hh=hh, n=n: e.copy(out=Vb[:n, hh, :], in_=P[:n, :]), r=[P], w=[Vb])
                    for hh in range(2):
                        P = proj(4 + hh)
                        S.act(lambda e, P=P, hh=hh, n=n: e.activation(out=sg[:n, hh * 512:hh * 512 + 512], in_=P[:n, :], func=AF.Silu), r=[P], w=[sg])

                def stageE1b(T_):
                    n = 128 if T_ < NT else NS
                    qkr, qkT = qkrs[T_ % 2], qkTs[T_ % 2]
                    def trqk(e, n=n):
                        ins = None
                        for a_ in range(4):
                            for c in range(2):
                                ins = e.transpose(pq[:, (a_ * 2 + c) * 128:(a_ * 2 + c) * 128 + n], qkr[:n, a_, c * 128:c * 128 + 128], identb[:n, :n])
                        return ins
                    S.pe(trqk, r=[qkr, identb], w=[pq])
                    S.act(lambda e, n=n: e.copy(out=qkT[:, :, :n], in_=pq[:, :].rearrange("p (a t) -> p a t", a=8)[:, :, :n]), r=[pq], w=[qkT])

                def stageE2(T_):
                    n = 128 if T_ < NT else NS
                    qkr, Vb, sg, qkT = qkrs[T_ % 2], Vbs[T_ % 2], sgs[T_ % 2], qkTs[T_ % 2]
                    qT4 = qkT[:, 0:4, :].rearrange("p (h c) t -> p h c t", h=2)
                    kT4 = qkT[:, 4:8, :].rearrange("p (h c) t -> p h c t", h=2)

                    if T_ < NT:
                        S.dve(lambda e: e.tensor_tensor(out=QdT[:, :, :, :], in0=qT4, in1=qdecb[:, :, :].unsqueeze(2).broadcast_to([128, 2, 2, 128]),
                                                        op=ALU.mult), r=[qkT, qdecb], w=[QdT])
                        S.dve(lambda e: e.tensor_tensor(out=Kd[:, :, :], in0=qkr[:, 2:4, :], in1=kdec[:, :].unsqueeze(2).broadcast_to([128, 2, 256]),
                                                        op=ALU.mult), r=[qkr, kdec], w=[Kd])
                        def smm(e):
                            ins = None
                            for hh in range(2):
                                for c in range(2):
                                    ins = e.matmul(pst[:, hh * 128:hh * 128 + 128], kT4[:, hh, c, :], qT4[:, hh, c, :], start=(hh == 0 and c == 0),
                                                   stop=(c == 1), skip_group_check=True)
                            return ins
                        S.pe(smm, r=[qkT], w=[pst])
                        S.dve(lambda e: e.tensor_tensor(out=AT[:, :, :], in0=pst[:, 0:256].rearrange("p (h i) -> p h i", h=2), in1=decT[:, :, :],
                                                        op=ALU.mult), r=[pst, decT], w=[AT])
                        for hh in range(2):
                            def omm(e, hh=hh):
                                e.matmul(po[hh][:, :], AT[:, hh, :], Vb[:, hh, :], start=True, stop=False)
                                e.matmul(po[hh][:, :], QdT[:, hh, 0, :], Sb[:, hh * 2, :], start=False, stop=False)
                                return e.matmul(po[hh][:, :], QdT[:, hh, 1, :], Sb[:, hh * 2 + 1, :], start=False, stop=True)
                            S.pe(omm, r=[AT, Vb, QdT, Sb], w=[po[hh]])
                        for hh in range(2):
                            for c in range(2):
                                U_ = pu[c]
                                S.pe(lambda e, hh=hh, c=c, U_=U_: e.matmul(U_[:, :], Kd[:, hh, c * 128:c * 128 + 128], Vb[:, hh, :], start=True, stop=True),
                                     r=[Kd, Vb], w=[U_])
                                S.dve(lambda e, hh=hh, c=c, U_=U_: e.scalar_tensor_tensor(out=St[:, hh * 2 + c, :], in0=St[:, hh * 2 + c, :],
                                                                                         scalar=gpow[:, hh:hh + 1], in1=U_[:, :], op0=ALU.mult, op1=ALU.add),
                                      r=[St, gpow, U_], w=[St])
                                S.act(lambda e, hh=hh, c=c: e.copy(out=Sb[:, hh * 2 + c, :], in_=St[:, hh * 2 + c, :]), r=[St], w=[Sb])
                    else:
                        S.dve(lambda e: e.tensor_tensor(out=QdT[:, :, :, 0:NS], in0=qT4[:, :, :, 0:NS],
                                                        in1=qdecs[:, :, :].unsqueeze(2).broadcast_to([128, 2, 2, NS]), op=ALU.mult),
                              r=[qkT, qdecs], w=[QdT])
                        for s_ in range(4):
                            S.dve(lambda e, s_=s_: e.tensor_tensor(out=QdTm[:, s_, :, :, :], in0=QdT[:, :, :, 0:NS],
                                                                   in1=colmask[:, s_, :].unsqueeze(1).unsqueeze(2).broadcast_to([128, 2, 2, 16]),
                                                                   op=ALU.mult), r=[QdT, colmask], w=[QdTm])
                            S.dve(lambda e, s_=s_: e.tensor_tensor(out=Kdm[:, s_, :, :], in0=qkr[:NS, 2:4, :],
                                                                   in1=kdm[:, s_, :].unsqueeze(2).broadcast_to([NS, 2, 256]), op=ALU.mult),
                                  r=[qkr, kdm], w=[Kdm])

                        def smm_s(e):
                            ins = None
                            for hh in range(2):
                                for c in range(2):
                                    ins = e.matmul(pst[:NS, hh * 16:hh * 16 + 16], kT4[:, hh, c, 0:NS], qT4[:, hh, c, 0:NS], start=(hh == 0 and c == 0),
                                                   stop=(c == 1), skip_group_check=True)
                            return ins
                        S.pe(smm_s, r=[qkT], w=[pst])
                        S.dve(lambda e: e.tensor_tensor(out=AT[:NS, :, 0:NS], in0=pst[:NS, 0:32].rearrange("p (h i) -> p h i", h=2), in1=decTs[:, :, :],
                                                        op=ALU.mult), r=[pst, decTs], w=[AT])
                        for s_ in range(4):
                            SS = Ss[s_ % 2]
                            S.dma("sp", SS[:, :, :], sret_d.ap()[s_].rearrange("h (c p) e -> p (h c) e", p=128), w=[SS], key=SS)
                            S.act(lambda e, s_=s_, SS=SS: e.copy(out=Ssb[s_][:, :, :], in_=SS[:, :, :]), r=[SS], w=[Ssb[s_]])
                            for hh in range(2):
                                for c in range(2):
                                    U_ = pu[c]
                                    S.pe(lambda e, hh=hh, c=c, U_=U_, s_=s_: e.matmul(U_[:, :], Kdm[:, s_, hh, c * 128:c * 128 + 128], Vb[:NS, hh, :],
                                                                                      start=True, stop=True), r=[Kdm, Vb], w=[U_])
                                    S.dve(lambda e, hh=hh, c=c, U_=U_, SS=SS: e.scalar_tensor_tensor(out=SS[:, hh * 2 + c, :], in0=SS[:, hh * 2 + c, :],
                                                                                                    scalar=gpow[:, 2 + hh:3 + hh], in1=U_[:, :],
                                                                                                    op0=ALU.mult, op1=ALU.add),
                                          r=[SS, gpow, U_], w=[SS])
                            S.dma("sp", ret_s.ap()[s_].rearrange("h (c p) e -> p (h c) e", p=128), SS[:, :, :], r=[SS], key=SS, final=True)
                        for hh in range(2):
                            def omm_s(e, hh=hh):
                                ins = e.matmul(po[hh][:NS, :], AT[:NS, hh, 0:NS], Vb[:NS, hh, :], start=True, stop=False)
                                for s_ in range(4):
                                    for c in range(2):
                                        ins = e.matmul(po[hh][:NS, :], QdTm[:, s_, hh, c, :], Ssb[s_][:, hh * 2 + c, :], start=False,
                                                       stop=(s_ == 3 and c == 1))
                                return ins
                            S.pe(omm_s, r=[AT, Vb, QdTm] + Ssb, w=[po[hh]])

                    OG = og_[T_ % 2]
                    for hh in range(2):
                        OB = ob[hh]
                        S.act(lambda e, hh=hh, OB=OB, n=n: e.activation(out=OB[:n, :], in_=po[hh][:n, :], func=AF.Identity, accum_out=st8[:n, hh * 8:hh * 8 + 1]),
                              r=[po[hh]], w=[OB, st8])
                        S.act(lambda e, hh=hh, OB=OB, n=n: e.activation(out=jk[:n, :], in_=OB[:n, :], func=AF.Square, accum_out=st8[:n, hh * 8 + 1:hh * 8 + 2]),
                              r=[OB], w=[jk, st8])
                        o8 = hh * 8
                        S.dve(lambda e, o8=o8, n=n: e.tensor_scalar(out=st8[:n, o8 + 2:o8 + 4], in0=st8[:n, o8:o8 + 2], scalar1=1.0 / 512, scalar2=None,
                                                                    op0=ALU.mult), r=[st8], w=[st8])
                        S.dve(lambda e, o8=o8, n=n: e.tensor_tensor(out=st8[:n, o8 + 4:o8 + 5], in0=st8[:n, o8 + 2:o8 + 3], in1=st8[:n, o8 + 2:o8 + 3],
                                                                    op=ALU.mult), r=[st8], w=[st8])
                        S.dve(lambda e, o8=o8, n=n: e.tensor_tensor(out=st8[:n, o8 + 5:o8 + 6], in0=st8[:n, o8 + 3:o8 + 4], in1=st8[:n, o8 + 4:o8 + 5],
                                                                    op=ALU.subtract), r=[st8], w=[st8])
                        S.dve(lambda e, o8=o8, n=n: e.tensor_scalar(out=st8[:n, o8 + 5:o8 + 6], in0=st8[:n, o8 + 5:o8 + 6], scalar1=0.0, scalar2=1e-5,
                                                                    op0=ALU.max, op1=ALU.add), r=[st8], w=[st8])
                        S.act(lambda e, o8=o8, n=n: e.activation(out=st8[:n, o8 + 6:o8 + 7], in_=st8[:n, o8 + 5:o8 + 6], func=AF.Sqrt), r=[st8], w=[st8])
                        S.dve(lambda e, o8=o8, n=n: e.reciprocal(out=st8[:n, o8 + 6:o8 + 7], in_=st8[:n, o8 + 6:o8 + 7]), r=[st8], w=[st8])
                        S.dve(lambda e, o8=o8, OB=OB, n=n: e.tensor_scalar(out=OB[:n, :], in0=OB[:n, :], scalar1=st8[:n, o8 + 2:o8 + 3],
                                                                           scalar2=st8[:n, o8 + 6:o8 + 7], op0=ALU.subtract, op1=ALU.mult),
                              r=[OB, st8], w=[OB])
                        S.pool(lambda e, hh=hh, OB=OB, n=n: e.tensor_tensor(out=OB[:n, :], in0=OB[:n, :], in1=gnb[:n, hh, :], op=ALU.mult), r=[OB, gnb], w=[OB])
                        S.pool(lambda e, hh=hh, OB=OB, OG=OG, n=n: e.tensor_tensor(out=OG[:n, hh, :], in0=OB[:n, :], in1=sg[:n, hh * 512:hh * 512 + 512],
                                                                                   op=ALU.mult), r=[OB, sg], w=[OG])
                        if T_ < NT:
                            c_, tl = T_ // 8, T_ % 8
                            S.dma("sp", o_r.ap()[c_, hh, tl * 128:tl * 128 + 128, :], OG[:, hh, :], r=[OG], w=[o_rT[c_][hh]], key=OG)
                        else:
                            S.dma("sp", o_rs.ap()[hh], OG[:NS, hh, :], r=[OG], w=[o_rsT[hh]], key=OG)
                    if T_ < NT and T_ % 8 == 7:
                        c_ = T_ // 8
                        for hh in range(2):
                            base = ((c_ * 2 + hh) * 4) * 1024
                            allgather(o_r.ap()[c_, hh].rearrange("(p a) f -> p (a f)", a=8),
                                      og2.ap()[base:base + 4096, :].rearrange("(q a) f -> q (a f)", a=8), o_rT[c_][hh], og2T)
                    if T_ == NT - 1:
                        S.dma("sp", ret_p.ap().rearrange("h (c p) e -> p (h c) e", p=128), St[:, :, :], r=[St], key=St, final=True)
                stageE1(0)
                stageE1b(0)
                for T_ in range(NT + 1):
                    if T_ + 1 <= NT:
                        stageE1(T_ + 1)
                    stageE2(T_)
                    if T_ + 1 <= NT:
                        stageE1b(T_ + 1)
                for hh in range(2):
                    allgather(o_rs.ap()[hh].rearrange("r (b x) -> (r b) x", b=8),
                              ogs2.ap()[hh * 64:hh * 64 + 64, :].rearrange("r (b x) -> (r b) x", b=8), o_rsT[hh], og2T)
                S.barrier()

        if RUN_T0:
            token_phase(0)
        if RUN_E:
            phase_e()
        if RUN_T1:
            token_phase(1)

        S.emit()
    return nc


def _col_slice(g):
    cols = list(range(512 * g, 512 * g + 512))
    for e in (0, 1):
        for br in range(3):
            base = 2048 + ((br * 2 + e) * 4 + g) * 128
            cols += list(range(base, base + 128))
    for br in range(3):
        base = 2048 + 3072 + br * 16 + 4 * g
        cols += list(range(base, base + 4))
    return np.array(cols)


def _consts():
    c0 = np.arange(256)[:, None] * 16
    j0 = np.arange(64)[None, :] * 64
    cover = ((c0 < j0 + 64) & (c0 + 32 > j0)).astype(np.float32)
    cover[255] = 0.0
    p = np.arange(128)[:, None]
    col = np.arange(128)[None, :]
    I0 = (col - 16 * p).astype(np.float32)
    Jm = (np.arange(64)[None, :] - (p >= 64)).astype(np.float32)
    tri = np.concatenate([(p <= col), (col < p)], axis=1).astype(np.float32)
    Ebig = (np.arange(SEQ)[None, :] // 64 == np.arange(64)[:, None]).astype(np.float32)
    key = np.arange(16)
    hq = np.arange(16)
    smask = np.zeros((16, 5, 16), np.float32)
    for s_ in range(4):
        smask[:, s_, :] = ((key[:, None] // 4 == s_) & (key[:, None] % 4 <= hq[None, :] % 4))
    smask[:, 4, :] = (key[:, None] > hq[None, :] % 4)
    sel16 = np.zeros((16, 68), np.float32)
    sel16[:, 0:4] = (hq[:, None] % 4 == np.arange(4)[None, :])
    for s_ in range(4):
        sel16[:, 4 + 16 * s_:20 + 16 * s_] = (key[:, None] == 4 * s_ + hq[None, :] % 4)
    hmask = (np.arange(12)[None, :] % 4 == hq[:, None] // 4).astype(np.float32)
    c0s = np.arange(1024)[:, None] * 16
    j0s = np.arange(257)[None, :] * 64
    cover_s = ((c0s < j0s + 64) & (c0s + 32 > j0s)).astype(np.float32)
    cover_s[1023] = 0.0
    return {"cover": cover, "I0": I0, "Jm": Jm, "tri": tri, "Ebig": Ebig, "smask": smask, "sel16": sel16,
            "hmask": hmask, "cover_s": cover_s, "cs_r": rope_table(np.arange(SEQ), 128),
            "cs_rs": rope_table(np.tile(PAST + np.arange(4), 4), 128)}


CONST = _consts()


def _ret_cols(r):
    cols = []
    for base, w in ((0, 256), (2048, 256), (4096, 512), (8192, 512)):
        for hh in range(2):
            h = 2 * r + hh
            cols += list(range(base + w * h, base + w * h + w))
    return np.array(cols)


def _ret_consts(r):
    out = {}
    i = np.arange(128, dtype=np.float64)
    decT = np.zeros((128, 2, 128), np.float64)
    qdecb = np.zeros((128, 2, 128), np.float64)
    kdec = np.zeros((128, 2), np.float64)
    decTs = np.zeros((16, 2, 16), np.float64)
    kdm = np.zeros((16, 4, 2), np.float64)
    gpow = np.zeros((128, 4), np.float64)
    qdecs = np.zeros((128, 2, 16), np.float64)
    t16 = np.arange(16)
    for hh in range(2):
        h = 2 * r + hh
        lg = np.log(np.float32(1.0) - np.float32(2.0) ** np.float32(-5.0 - h)).astype(np.float64)
        rel = i[None, :] - i[:, None]
        decT[:, hh, :] = np.where(rel >= 0, np.exp(np.maximum(rel, 0) * lg), 0.0) / 16.0
        qdecb[:, hh, :] = np.exp((i + 1.0) * lg)[None, :]
        kdec[:, hh] = np.exp((127.0 - i) * lg) / 16.0
        rel4 = (t16[None, :] % 4) - (t16[:, None] % 4)
        same = (t16[None, :] // 4) == (t16[:, None] // 4)
        decTs[:, hh, :] = np.where(same & (rel4 >= 0), np.exp(np.maximum(rel4, 0) * lg), 0.0) / 16.0
        for s_ in range(4):
            kdm[:, s_, hh] = np.where(t16 // 4 == s_, np.exp((3.0 - t16 % 4) * lg), 0.0) / 16.0
        gpow[:, hh] = np.exp(128.0 * lg)
        gpow[:, 2 + hh] = np.exp(4.0 * lg)
        qdecs[:, hh, :] = np.exp((t16 % 4 + 1.0) * lg)[None, :]
    colmask = np.zeros((128, 4, 16), np.float32)
    for s_ in range(4):
        colmask[:, s_, :] = (t16 // 4 == s_)[None, :]
    out["decT"] = decT.reshape(128, 256).astype(np.float32)
    out["qdecb"] = qdecb.reshape(128, 256).astype(np.float32)
    out["kdec"] = kdec.astype(np.float32)
    out["decTs"] = decTs.reshape(16, 32).astype(np.float32)
    out["kdm"] = kdm.reshape(16, 8).astype(np.float32)
    out["gpow"] = gpow.astype(np.float32)
    out["qdecs"] = qdecs.reshape(128, 32).astype(np.float32)
    out["colmask"] = colmask.reshape(128, 64)
    return out


def _oidx2(r):
    p = np.arange(128)
    idx = np.zeros((128, 72), np.int32)
    for t in range(9):
        for j in range(4):
            for hh in range(2):
                if t < 8:
                    idx[:, t * 8 + 2 * j + hh] = ((r * 2 + hh) * 4 + j) * 1024 + t * 128 + p
                else:
                    idx[:, t * 8 + 2 * j + hh] = (hh * 4 + j) * NS + 4 * r + np.minimum(p, 3)
    return idx


def _oidx(r):
    p = np.arange(128)
    idx = np.zeros((128, 36), np.int32)
    for t in range(9):
        for j in range(4):
            if t < 8:
                idx[:, t * 4 + j] = r * 4096 + j * 1024 + t * 128 + p
            else:
                idx[:, t * 4 + j] = j * NS + 4 * r + np.minimum(p, 3)
    return idx


def make_in_maps(inp):
    maps = []
    ident = np.eye(128, dtype=np.float32)
    cs_p = rope_table(np.arange(SEQ), 64)
    cs_s = rope_table(np.tile(PAST + np.arange(4), 4), 64)
    for c in range(8):
        b, r = c // 4, c % 4
        m = {
            "xb": np.ascontiguousarray(inp["x_prompt"][b]),
            "xs": np.ascontiguousarray(inp["x_sample"][4 * b:4 * b + 4].reshape(NS, D)),
            "w_in": np.ascontiguousarray(inp["nsa_w_in"][0][:, _col_slice(r)]),
            "gmix0": np.ascontiguousarray(np.broadcast_to(inp["norm_mix"][0][None, :], (128, D))),
            "cs_p": cs_p, "cs_s": cs_s, "ident": ident,
            "cw1": inp["nsa_cmp_w1"][0], "cw2": inp["nsa_cmp_w2"][0],
            "posT": np.ascontiguousarray(inp["nsa_cmp_pos"][0].reshape(64, 128).T),
            "b1T": np.ascontiguousarray(inp["nsa_cmp_b1"][0].T),
            "x_tok": np.ascontiguousarray(np.concatenate([inp["x_prompt"][b, 1024 * r:1024 * r + 1024], inp["x_sample"][c]], axis=0)),
            "oidx": _oidx(r),
            "w_out0": inp["nsa_w_out"][0], "ffn1_0": inp["ffn_w1"][0], "ffn2_0": inp["ffn_w2"][0],
            "gffn0T": np.ascontiguousarray(inp["norm_ffn"][0].reshape(16, 128).T),
            "w_out1": inp["ret_w_out"][0], "ffn1_1": inp["ffn_w1"][1], "ffn2_1": inp["ffn_w2"][1],
            "gmix1T": np.ascontiguousarray(inp["norm_mix"][1].reshape(16, 128).T),
            "gffn1T": np.ascontiguousarray(inp["norm_ffn"][1].reshape(16, 128).T),
            "gfinT": np.ascontiguousarray(inp["norm_final"].reshape(16, 128).T),
            "wr": np.ascontiguousarray(inp["ret_w_in"][0][:, _ret_cols(r)]),
            "cs_r": CONST["cs_r"], "cs_rs": CONST["cs_rs"],
            "gnb": np.ascontiguousarray(np.broadcast_to(inp["ret_gn"][0][1024 * r:1024 * r + 1024][None, :], (128, 1024))),
            "sret": np.ascontiguousarray(inp["state_ret"][0][4 * b:4 * b + 4, 2 * r:2 * r + 2]),
            "oidx2": _oidx2(r),
            **_ret_consts(r),
            "ccache": np.ascontiguousarray(inp["cache_cmp_kv"][0][:, :, :, r, :]),
            "scache": np.ascontiguousarray(inp["cache_sel_kv"][0][:, :, :, r, :]),
            "wstate": np.ascontiguousarray(inp["state_win_kv"][0][4 * b:4 * b + 4, :, :, r, :]),
            "ptab": np.ascontiguousarray(inp["page_table"][4 * b:4 * b + 4].astype(np.int32)),
            "smask": CONST["smask"], "sel16": CONST["sel16"], "hmask": CONST["hmask"], "cover_s": CONST["cover_s"],
            "pidx": (np.arange(128) % 4).astype(np.float32)[:, None].copy(),
            "ptabT": np.ascontiguousarray(inp["page_table"][4 * b:4 * b + 4].astype(np.int32).reshape(4, 4, 32).transpose(2, 0, 1).reshape(32, 16)),
            "Rrep": (np.arange(128)[None, :] // 4 == np.arange(32)[:, None]).astype(np.float32),
            "E2": (np.arange(128)[None, :] // 2 == np.arange(64)[:, None]).astype(np.float32),
            "cover": CONST["cover"], "I0": CONST["I0"], "Jm": CONST["Jm"], "tri": CONST["tri"], "Ebig": CONST["Ebig"],
        }
        maps.append(m)
    return maps


_NC = None


def kernel(**inputs):
    global _NC
    inp = {k: np.asarray(v) for k, v in inputs.items()}
    if _NC is None:
        _NC = build()
    maps = make_in_maps(inp)
    res = run_bass_kernel_spmd(_NC, maps, core_ids=list(range(8)), **({'trace': True} if TRACE else {}))
    global LAST_RES
    LAST_RES = res
    R = res.results
    global LAST
    LAST = R
    cmp_p = np.zeros((1, 2, SEQ, 2, 4, 128), np.float32)
    sel_p = np.zeros_like(cmp_p)
    win_full = np.zeros_like(cmp_p)
    cmp_s = np.zeros((1, 8, 4, 2, 4, 128), np.float32)
    sel_s = np.zeros_like(cmp_s)
    win_new = np.zeros_like(cmp_s)
    for c in range(8):
        b, g = c // 4, c % 4
        kp = R[c]["kv_p"]
        cmp_p[0, b, :, :, g, :] = kp[:, 0]
        sel_p[0, b, :, :, g, :] = kp[:, 1]
        win_full[0, b, :, :, g, :] = kp[:, 2]
        ks = R[c]["kv_s"].reshape(4, 4, 3, 2, 128)
        cmp_s[0, 4 * b:4 * b + 4, :, :, g, :] = ks[:, :, 0]
        sel_s[0, 4 * b:4 * b + 4, :, :, g, :] = ks[:, :, 1]
        win_new[0, 4 * b:4 * b + 4, :, :, g, :] = ks[:, :, 2]
    win_p = np.ascontiguousarray(win_full[:, :, SEQ - 512:])
    y_p = np.zeros((2, SEQ, D), np.float32)
    y_s = np.zeros((8, 4, D), np.float32)
    win_s = np.zeros((1, 8, 512, 2, 4, 128), np.float32)
    ret_p = np.zeros((1, 2, 8, 256, 512), np.float32)
    ret_s = np.zeros((1, 8, 8, 256, 512), np.float32)
    for c in range(8):
        b, g = c // 4, c % 4
        win_s[0, 4 * b:4 * b + 4, :, :, g, :] = R[c]["win_s"]
        y_p[b, 1024 * g:1024 * g + 1024] = R[c]["y_tok"][:1024]
        y_s[c] = R[c]["y_tok"][1024:]
        ret_p[0, b, 2 * g:2 * g + 2] = R[c]["ret_p"]
        ret_s[0, 4 * b:4 * b + 4, 2 * g:2 * g + 2] = R[c]["ret_s"]
    return (y_p, y_s, cmp_p, cmp_s, sel_p, sel_s, win_p, win_s, ret_p, ret_s)
```

```python
import contextlib
import numpy as np
import concourse.bass as bass
import concourse.mybir as mybir
from concourse.bass_utils import run_bass_kernel_spmd

F32 = mybir.dt.float32
BF16 = mybir.dt.bfloat16
I32 = mybir.dt.int32
AF = mybir.ActivationFunctionType
ALU = mybir.AluOpType
AX = mybir.AxisListType

ENGS = ("sp", "act", "dve", "pool", "pe")


class Buf:
    __slots__ = ("name", "last_w", "readers", "cnt")

    def __init__(self, name):
        self.name = name
        self.last_w = None
        self.readers = []
        self.cnt = 0


class T:
    def __init__(self, t, name):
        self.t = t
        self.b = Buf(name)

    def __getitem__(self, k):
        return self.t[k]

    def ap(self):
        return self.t.ap()


def _b(x):
    return x.b if isinstance(x, T) else x


class Op:
    __slots__ = ("eng", "fn", "deps", "is_dma", "key", "sig", "sigidx", "cnt", "inc")

    def __init__(self, eng, fn, is_dma=False, key=None, inc=16):
        self.eng = eng
        self.fn = fn
        self.deps = []
        self.is_dma = is_dma
        self.key = key
        self.sig = False
        self.sigidx = 0
        self.cnt = 0
        self.inc = inc


class Sched:
    def __init__(self, nc):
        self.nc = nc
        self.ops = []
        self.last_real = {e: None for e in ENGS}
        self.out_dmas = []

    def _add(self, op, r, w):
        r = [_b(x) for x in r]
        w = [_b(x) for x in w]
        deps = []
        for b in r:
            if b.last_w is not None:
                deps.append(b.last_w)
        for b in w:
            if b.last_w is not None:
                deps.append(b.last_w)
            deps.extend(b.readers)
        seen = set()
        for d in deps:
            if d is op or id(d) in seen:
                continue
            seen.add(id(d))
            if (not d.is_dma) and d.eng == op.eng and op.eng == "pe" and not op.is_dma:
                continue
            op.deps.append(d)
            if not d.is_dma:
                d.sig = True
        for b in r:
            b.readers.append(op)
        for b in w:
            b.last_w = op
            b.readers = []
        self.ops.append(op)
        if not op.is_dma:
            self.last_real[op.eng] = op
        return op

    def op(self, eng, fn, r=(), w=()):
        return self._add(Op(eng, fn), r, w)

    def pe(self, fn, r=(), w=()):
        return self.op("pe", fn, r, w)

    def act(self, fn, r=(), w=()):
        return self.op("act", fn, r, w)

    def dve(self, fn, r=(), w=()):
        return self.op("dve", fn, r, w)

    def pool(self, fn, r=(), w=()):
        return self.op("pool", fn, r, w)

    def dma(self, q, out, in_, r=(), w=(), key=None, final=False, **kw):
        return self.custom_dma(q, lambda e: e.dma_start(out=out, in_=in_, **kw), r, w, key, 16, final)

    def custom_dma(self, q, fn, r=(), w=(), key=None, inc=16, final=False):
        k = _b(key)
        op = Op(q, fn, is_dma=True, key=k, inc=inc)
        self._add(op, r, w)
        if final:
            self.out_dmas.append(op)
        return op

    def barrier(self):
        pend = [o for o in self.last_real.values() if o is not None]
        latest = {}
        for o in self.ops:
            if o.is_dma:
                latest[id(o.key)] = o
        for e in ENGS:
            op = Op(e, None)
            for d in pend:
                if d.eng != e:
                    op.deps.append(d)
                    d.sig = True
            op.deps.extend(latest.values())
            self.ops.append(op)

    def emit(self):
        nc = self.nc
        fin = Op("sp", None)
        latest = {}
        for o in self.out_dmas:
            latest[id(o.key)] = o
        fin.deps = list(latest.values())
        for e in ENGS:
            o = self.last_real[e]
            if e != "sp" and o is not None:
                fin.deps.append(o)
                o.sig = True
        self.ops.append(fin)

        with contextlib.ExitStack() as st:
            esem = {e: st.enter_context(nc.semaphore("s_" + e)) for e in ENGS}
            keysems = {}
            keyvals = {}
            for o in self.ops:
                if o.is_dma:
                    kid = id(o.key)
                    if kid not in keysems:
                        keysems[kid] = st.enter_context(nc.semaphore("k%d" % len(keysems)))
                        keyvals[kid] = 0
                    keyvals[kid] += o.inc
                    o.cnt = keyvals[kid]
            cnts = {e: 0 for e in ENGS}
            for o in self.ops:
                if (not o.is_dma) and o.sig:
                    assert o.fn is not None
                    cnts[o.eng] += 1
                    o.sigidx = cnts[o.eng]
            self.n_sems = len(keysems) + 5
            per = {e: [o for o in self.ops if o.eng == e] for e in ENGS}
            block = st.enter_context(nc.Block())

            def run(eng_name):
                def body(eng):
                    waited = {}
                    for o in per[eng_name]:
                        for d in o.deps:
                            if d.is_dma:
                                s, v = keysems[id(d.key)], d.cnt
                            else:
                                s, v = esem[d.eng], d.sigidx
                            if waited.get(id(s), 0) >= v:
                                continue
                            waited[id(s)] = v
                            eng.wait_ge(s, v)
                        if o.fn is None:
                            continue
                        ins = o.fn(eng)
                        if o.is_dma:
                            ins.then_inc(keysems[id(o.key)], o.inc)
                        elif o.sig:
                            ins.then_inc(esem[eng_name], 1)

                return body

            block.sync(run("sp"))
            block.scalar(run("act"))
            block.vector(run("dve"))
            block.gpsimd(run("pool"))
            block.tensor(run("pe"))


D = 2048
SEQ = 4096
NT = SEQ // 128
NTOK = 1028
NS = 16
PAST = 16384
NCOL = 1292
RMS_EPS = 1e-6
SCALE = 128 ** -0.5
NEG = -30000.0

STAGE = 1
DEBUG = False
NTQ = NT
RUN_S = True
RUN_T0 = True
RUN_E = True
RUN_T1 = True
TRACE = False


def rope_table(pos, half):
    inv = (10000.0 ** (-(np.arange(half, dtype=np.float32)) / np.float32(half))).astype(np.float32)
    ang = (pos.astype(np.float32)[:, None] * inv[None, :]).astype(np.float32)
    return np.concatenate([np.cos(ang), np.sin(ang)], axis=1).astype(np.float32)


def build(stage=STAGE):
    nc = bass.Bass("TRN2", target_bir_lowering=False)
    S = Sched(nc)

    def din(name, shape, dt=F32):
        return nc.dram_tensor(name, list(shape), dt, kind="ExternalInput")

    def dout(name, shape, dt=F32):
        return nc.dram_tensor(name, list(shape), dt, kind="ExternalOutput")

    xb = din("xb", [SEQ, D])
    xs = din("xs", [NS, D])
    w_in = din("w_in", [D, NCOL])
    gmix0 = din("gmix0", [128, D])
    cs_p = din("cs_p", [SEQ, 128])
    cs_s = din("cs_s", [NS, 128])
    ident_d = din("ident", [128, 128])

    cw1 = din("cw1", [2, 32, 128, 128])
    cw2 = din("cw2", [2, 128, 128])
    posT = din("posT", [128, 64])
    b1T = din("b1T", [128, 2])
    cover_d = din("cover", [256, 64])
    I0_d = din("I0", [128, 128])
    Jm_d = din("Jm", [128, 64])
    tri_d = din("tri", [128, 256])
    Ebig_d = din("Ebig", [64, SEQ])

    kv_p = dout("kv_p", [SEQ, 3, 2, 128])
    o_loc = nc.dram_tensor("o_loc", [SEQ, 512], BF16)
    o_locs = nc.dram_tensor("o_locs", [NS, 512], BF16)
    og = nc.dram_tensor("og", [4 * SEQ, 512], BF16)
    ogs = nc.dram_tensor("ogs", [4 * NS, 512], BF16)
    o_locT = [Buf("o_loc%d" % i) for i in range(5)]
    ogT = Buf("og")
    x_tok = din("x_tok", [NTOK, D])
    oidx_d = din("oidx", [128, 36], I32)
    w_out0 = din("w_out0", [D, D])
    ffn1_0 = din("ffn1_0", [D, 4 * D])
    ffn2_0 = din("ffn2_0", [4 * D, D])
    gffn0T = din("gffn0T", [128, 16])
    w_out1 = din("w_out1", [2 * D, D])
    ffn1_1 = din("ffn1_1", [D, 4 * D])
    ffn2_1 = din("ffn2_1", [4 * D, D])
    gmix1T = din("gmix1T", [128, 16])
    gffn1T = din("gffn1T", [128, 16])
    gfinT = din("gfinT", [128, 16])
    wr_d = din("wr", [D, 3072])
    cs_r = din("cs_r", [SEQ, 256])
    cs_rs = din("cs_rs", [NS, 256])
    decT_d = din("decT", [128, 256])
    qdecb_d = din("qdecb", [128, 256])
    kdec_d = din("kdec", [128, 2])
    decTs_d = din("decTs", [16, 32])
    kdm_d = din("kdm", [16, 8])
    colmask_d = din("colmask", [128, 64])
    gnb_d = din("gnb", [128, 1024])
    gpow_d = din("gpow", [128, 4])
    qdecs_d = din("qdecs", [128, 32])
    sret_d = din("sret", [4, 2, 256, 512])
    oidx2_d = din("oidx2", [128, 72], I32)
    ret_p = dout("ret_p", [2, 256, 512])
    ret_s = dout("ret_s", [4, 2, 256, 512])
    y_tok = dout("y_tok", [NTOK, D])
    hspill = nc.dram_tensor("hspill", [128, 16 * NTOK], F32)
    hspT = Buf("hspill")
    xg_in = nc.dram_tensor("xg_in", [8, 128, 2 * NTOK], BF16)
    xg_all = nc.dram_tensor("xg_all", [8, 512, 2 * NTOK], BF16)
    xg_inT = [Buf("xg_in%d" % i) for i in range(8)]
    xgT = Buf("xg_all")
    o_r = nc.dram_tensor("o_r", [4, 2, 1024, 512], BF16)
    o_rs = nc.dram_tensor("o_rs", [2, NS, 512], BF16)
    og2 = nc.dram_tensor("og2", [4 * 2 * 4 * 1024, 512], BF16)
    ogs2 = nc.dram_tensor("ogs2", [2 * 4 * NS, 512], BF16)
    o_rT = [[Buf("o_r%d%d" % (c, h)) for h in range(2)] for c in range(4)]
    o_rsT = [Buf("o_rs%d" % h) for h in range(2)]
    og2T = Buf("og2")
    win_s = dout("win_s", [4, 512, 2, 128])
    ccache = din("ccache", [1280, 128, 2, 128])
    scache = din("scache", [1280, 128, 2, 128])
    wstate = din("wstate", [4, 512, 2, 128])
    ptab = din("ptab", [4, 128], I32)
    smask_d = din("smask", [16, 5, 16])
    sel16_d = din("sel16", [16, 4 + 64])
    hmask_d = din("hmask", [16, 12])
    cover_s = din("cover_s", [1024, 257])
    pidx_d = din("pidx", [128, 1])
    ptabT = din("ptabT", [32, 16], I32)
    Rrep_d = din("Rrep", [32, 128])
    E2_d = din("E2", [64, 128])
    kv_s = dout("kv_s", [NS, 3, 2, 128])

    dbg_outs = {}

    def dbg(name, t, ap, shape, dt=F32):
        if not DEBUG:
            return
        d_ = nc.dram_tensor("dbg_" + name, list(shape), dt, kind="ExternalOutput")
        S.dma("sp", d_.ap(), ap, r=[t], key=Buf("dbgk_" + name), final=True)

    with contextlib.ExitStack() as top:
        def sbuf(st, name, shape, dt):
            return T(st.enter_context(nc.sbuf_tensor(name, list(shape), dt)), name)

        def psum(st, name, shape, dt):
            return T(st.enter_context(nc.psum_tensor(name, list(shape), dt)), name)

        identb = sbuf(top, "identb", [128, 128], BF16)
        identf = sbuf(top, "identf", [128, 128], F32)
        GL = sbuf(top, "GL", [128, NT + 1, 12], F32)
        QTs = sbuf(top, "QTs", [128, 4, NS], BF16)
        KTs = sbuf(top, "KTs", [128, 3, NS], BF16)
        VAs = sbuf(top, "VAs", [NS, 2, 132], BF16)
        pp = contextlib.ExitStack()
        QT = sbuf(pp, "QT", [128, 4, SEQ + NS], BF16)
        KT = sbuf(pp, "KT", [128, 3, SEQ + NS], BF16)
        VcT = sbuf(pp, "VcT", [128, SEQ + NS], BF16)
        VA = sbuf(pp, "VA", [128, NT + 1, 2, 132], BF16)
        S.dma("sp", identf[:], ident_d.ap(), w=[identf], key=identf)
        S.dma("pool", identb[:], ident_d.ap(), w=[identb], key=identb)

        RG = [[0, 1, 2, 3], [4, 5, 6, 7]]

        def allgather(src_ap, dst_ap, rbuf, wbuf_):
            S.custom_dma("pool", lambda e: e.collective_compute("AllGather", ALU.bypass, replica_groups=RG, ins=[src_ap], outs=[dst_ap]),
                         r=[rbuf], w=[wbuf_], key=wbuf_, inc=1)

        def phase_a():
            with contextlib.ExitStack() as pa:
                wsb = sbuf(pa, "wsb", [128, 16, NCOL], BF16)
                gsb = sbuf(pa, "gsb", [128, D], F32)
                xt = [sbuf(pa, "xt%d" % i, [128, D], F32) for i in range(2)]
                cst = [sbuf(pa, "cst%d" % i, [128, 128], F32) for i in range(3)]
                junk = sbuf(pa, "junk", [128, D], BF16)
                ss = sbuf(pa, "ss", [128, 2], F32)
                epsb = sbuf(pa, "epsb", [128, 1], F32)
                S.pool(lambda e: e.memset(epsb[:], RMS_EPS), w=[epsb])
                xn = sbuf(pa, "xn", [128, D], BF16)
                xnTs = [sbuf(pa, "xnT%d" % i, [128, 16, 128], BF16) for i in range(2)]
                kvf = [sbuf(pa, "kvf%d" % i, [128, 3, 2, 128], F32) for i in range(2)]
                rt = [sbuf(pa, "rt%d" % i, [128, 4, 64], F32) for i in range(4)]
                qkbs = [sbuf(pa, "qkb%d" % i, [128, 8, 128], BF16) for i in range(2)]
                pT = [psum(pa, "pT%d" % i, [128, 1024], BF16) for i in range(2)]
                pz = [psum(pa, "pz%d" % i, [128, 512], F32) for i in range(3)]
                pq = psum(pa, "pq", [128, 1024], BF16)

                for kc in range(4):
                    S.dma("pool", wsb[:, 4 * kc:4 * kc + 4, :],
                          w_in.ap()[512 * kc:512 * kc + 512, :].rearrange("(k p) c -> p k c", p=128),
                          w=[wsb], key=wsb)
                S.dma("sp", gsb[:], gmix0.ap(), w=[gsb], key=gsb)
                S.pool(lambda e: e.memset(VA[:, :, :, 128:129], 1.0), w=[VA])

                def stageA1(j):
                    n = 128 if j < NT else NS
                    c0 = j * 128
                    X = xt[j % 2]
                    xnT = xnTs[j % 2]

                    def load(jj):
                        nn = 128 if jj < NT else NS
                        cc = jj * 128
                        S.dma("sp", xt[jj % 2][:nn, :], xb.ap()[cc:cc + 128, :] if jj < NT else xs.ap(), w=[xt[jj % 2]], key=xt[jj % 2])
                        S.dma("sp", cst[jj % 3][:nn, :], cs_p.ap()[cc:cc + 128, :] if jj < NT else cs_s.ap(), w=[cst[jj % 3]], key=cst[jj % 3])
                    if j == 0:
                        load(0)
                    if j + 1 <= NT:
                        load(j + 1)
                    S.act(lambda e, X=X, n=n: e.activation(out=junk[:n, :], in_=X[:n, :], func=AF.Square,
                                                           accum_out=ss[:n, 0:1]), r=[X], w=[junk, ss])
                    S.act(lambda e, n=n: e.activation(out=ss[:n, 1:2], in_=ss[:n, 0:1], func=AF.Sqrt, scale=1.0 / D,
                                                      bias=epsb[:n, 0:1]), r=[ss, epsb], w=[ss])
                    S.dve(lambda e, n=n: e.reciprocal(out=ss[:n, 1:2], in_=ss[:n, 1:2]), r=[ss], w=[ss])
                    S.dve(lambda e, X=X, n=n: e.scalar_tensor_tensor(out=xn[:n, :], in0=X[:n, :], scalar=ss[:n, 1:2],
                                                                     in1=gsb[:n, :], op0=ALU.mult, op1=ALU.mult),
                          r=[X, ss, gsb], w=[xn])
                    for hb in range(2):
                        def tr(e, hb=hb, n=n):
                            ins = None
                            for k in range(8):
                                ins = e.transpose(pT[hb][:, k * 128:k * 128 + n], xn[:n, (hb * 8 + k) * 128:(hb * 8 + k + 1) * 128],
                                                  identb[:n, :n])
                            return ins
                        S.pe(tr, r=[xn, identb], w=[pT[hb]])
                        S.act(lambda e, hb=hb, n=n: e.copy(out=xnT[:, hb * 8:hb * 8 + 8, :n],
                                                           in_=pT[hb][:, :].rearrange("p (k t) -> p k t", k=8)[:, :, :n]),
                              r=[pT[hb]], w=[xnT])

                def stageA2(j):
                    n = 128 if j < NT else NS
                    c0 = j * 128
                    C = cst[j % 3]
                    KV = kvf[j % 2]
                    xnT = xnTs[j % 2]
                    qkb = qkbs[j % 2]
                    cbs = [(0, 512), (512, 512), (1024, NCOL - 1024)]
                    for ci, (cb, cw) in enumerate(cbs):
                        def mm(e, ci=ci, cb=cb, cw=cw, n=n):
                            ins = None
                            for k in range(16):
                                ins = e.matmul(pz[ci][:n, :cw], xnT[:, k, :n], wsb[:, k, cb:cb + cw],
                                               start=(k == 0), stop=(k == 15))
                            return ins
                        S.pe(mm, r=[xnT, wsb], w=[pz[ci]])
                    cosb = lambda h, n=n, C=C: C[:n, 0:64].unsqueeze(1).broadcast_to([n, h, 64])
                    sinb = lambda h, n=n, C=C: C[:n, 64:128].unsqueeze(1).broadcast_to([n, h, 64])

                    def rope(src3, h, out_lo, out_hi, rd, wr, n=n, cosb=cosb, sinb=sinb):
                        a, b_, c_, d_ = rt
                        S.dve(lambda e: e.tensor_tensor(out=a[:n, :h, :], in0=src3[:, :, 0:64], in1=cosb(h), op=ALU.mult), r=rd + [C], w=[a])
                        S.dve(lambda e: e.tensor_tensor(out=b_[:n, :h, :], in0=src3[:, :, 64:128], in1=sinb(h), op=ALU.mult), r=rd + [C], w=[b_])
                        S.dve(lambda e: e.tensor_tensor(out=c_[:n, :h, :], in0=src3[:, :, 64:128], in1=cosb(h), op=ALU.mult), r=rd + [C], w=[c_])
                        S.dve(lambda e: e.tensor_tensor(out=d_[:n, :h, :], in0=src3[:, :, 0:64], in1=sinb(h), op=ALU.mult), r=rd + [C], w=[d_])
                        S.pool(lambda e: e.tensor_tensor(out=out_lo, in0=a[:n, :h, :], in1=b_[:n, :h, :], op=ALU.subtract), r=[a, b_], w=wr)
                        S.pool(lambda e: e.tensor_tensor(out=out_hi, in0=c_[:n, :h, :], in1=d_[:n, :h, :], op=ALU.add), r=[c_, d_], w=wr)

                    z0 = pz[0][:n, :].rearrange("p (h d) -> p h d", h=4)
                    rope(z0, 4, qkb[:n, 0:4, 0:64], qkb[:n, 0:4, 64:128], [pz[0]], [qkb])
                    z1 = pz[1][:n, 0:384].rearrange("p (h d) -> p h d", h=3)
                    rope(z1, 3, KV[:n, :, 0, 0:64], KV[:n, :, 0, 64:128], [pz[1]], [KV])
                    S.act(lambda e, n=n, KV=KV: e.copy(out=KV[:n, 0, 1, :], in_=pz[1][:n, 384:512]), r=[pz[1]], w=[KV])
                    S.act(lambda e, n=n, KV=KV: e.copy(out=KV[:n, 1:3, 1, :],
                                                       in_=pz[2][:n, 0:256].rearrange("p (h d) -> p h d", h=2)),
                          r=[pz[2]], w=[KV])
                    S.act(lambda e, n=n, j=j: e.copy(out=GL[:n, j, :], in_=pz[2][:n, 256:268]), r=[pz[2]], w=[GL])
                    S.pool(lambda e, n=n, KV=KV: e.tensor_copy(out=qkb[:n, 4:7, :], in_=KV[:n, :, 0, :]), r=[KV], w=[qkb])
                    S.pool(lambda e, n=n, KV=KV: e.tensor_copy(out=qkb[:n, 7, :], in_=KV[:n, 0, 1, :]), r=[KV], w=[qkb])
                    S.pool(lambda e, n=n, KV=KV, j=j: e.tensor_copy(out=VA[:n, j, :, 0:128], in_=KV[:n, 1:3, 1, :]), r=[KV], w=[VA])
                    dst = kv_p.ap()[c0:c0 + 128] if j < NT else kv_s.ap()
                    S.dma("sp", dst, KV[:n], r=[KV], key=KV, final=True)
                    if j == NT:
                        for s_ in range(4):
                            S.dma("sp", win_s.ap()[s_, 508:512], KV[4 * s_:4 * s_ + 4, 2], r=[KV], key=KV, final=True)

                def stageA3(j):
                    n = 128 if j < NT else NS
                    c0 = j * 128
                    qkb = qkbs[j % 2]
                    def tr2(e, n=n):
                        ins = None
                        for k in range(8):
                            ins = e.transpose(pq[:, k * 128:k * 128 + n], qkb[:n, k, :], identb[:n, :n])
                        return ins
                    S.pe(tr2, r=[qkb, identb], w=[pq])
                    pq3 = pq[:, :].rearrange("p (k t) -> p k t", k=8)
                    S.act(lambda e, n=n, c0=c0, pq3=pq3: e.copy(out=QT[:, :, c0:c0 + n], in_=pq3[:, 0:4, :n]), r=[pq], w=[QT])
                    S.act(lambda e, n=n, c0=c0, pq3=pq3: e.copy(out=KT[:, :, c0:c0 + n], in_=pq3[:, 4:7, :n]), r=[pq], w=[KT])
                    S.act(lambda e, n=n, c0=c0, pq3=pq3: e.copy(out=VcT[:, c0:c0 + n], in_=pq3[:, 7, :n]), r=[pq], w=[VcT])
                stageA1(0)
                for j in range(NT + 1):
                    if j + 1 <= NT:
                        stageA1(j + 1)
                    stageA2(j)
                    if j >= 1:
                        stageA3(j - 1)
                stageA3(NT)
                S.act(lambda e: e.copy(out=QTs[:, :, :], in_=QT[:, :, SEQ:SEQ + NS]), r=[QT], w=[QTs])
                S.act(lambda e: e.copy(out=KTs[:, :, :], in_=KT[:, :, SEQ:SEQ + NS]), r=[KT], w=[KTs])
                S.act(lambda e: e.copy(out=VAs[:, :, :], in_=VA[:NS, NT, :, :]), r=[VA], w=[VAs])
                S.barrier()

        phase_a()

        S.act(lambda e: e.activation(out=GL[:, :, :], in_=GL[:, :, :], func=AF.Sigmoid), r=[GL], w=[GL])

        def phase_c():
            with contextlib.ExitStack() as pc:
                w1sb = sbuf(pc, "w1sb", [128, 2, 32, 128], BF16)
                w2sb = sbuf(pc, "w2sb", [128, 2, 128], BF16)
                posb = sbuf(pc, "posb", [128, 64], BF16)
                b1sb = sbuf(pc, "b1sb", [128, 2], F32)
                biasb = sbuf(pc, "biasb", [128, 2], F32)
                I0 = sbuf(pc, "I0s", [128, 128], F32)
                Jm = sbuf(pc, "Jms", [128, 64], F32)
                trib = sbuf(pc, "trib", [128, 256], BF16)
                Ebig = sbuf(pc, "Ebigs", [64, SEQ], BF16)
                KcT = sbuf(pc, "KcT", [128, 256], BF16)
                VcA = sbuf(pc, "VcA", [128, 2, 196], BF16)
                hx = sbuf(pc, "hx", [128, 256], F32)
                ht = sbuf(pc, "ht", [128, 256], F32)
                hT = [sbuf(pc, "hT%d" % i, [128, 256], BF16) for i in range(2)]
                Eb = [sbuf(pc, "Eb%d" % i, [128, 4, 128], BF16) for i in range(3)]
                mk = sbuf(pc, "mk", [128, 128], BF16)
                rs = sbuf(pc, "rs", [128, 8], F32)
                ocats = [sbuf(pc, "ocat%d" % i, [128, 4, 128], F32) for i in range(2)]
                ocb = [sbuf(pc, "ocb%d" % i, [128, 512], BF16) for i in range(2)]
                imp = sbuf(pc, "imp", [128, 64], F32)
                vis = sbuf(pc, "vis", [128, 64], F32)
                frc = sbuf(pc, "frc", [128, 64], F32)
                col0 = sbuf(pc, "col0", [128, 64], F32)
                sc = [sbuf(pc, "sc%d" % i, [128, 64], F32) for i in range(2)]
                m8 = sbuf(pc, "m8", [128, 16], F32)
                selm = sbuf(pc, "selm", [128, 64], F32)
                nm = sbuf(pc, "nm", [128, 64], BF16)
                nmTs = [sbuf(pc, "nmT%d" % i, [64, 4, 128], BF16) for i in range(2)]
                pS = [psum(pc, "pS%d" % i, [128, 512], F32) for i in range(3)]
                pO = [psum(pc, "pO%d" % i, [128, 512], F32) for i in range(4)]
                pM = psum(pc, "pM", [128, 1024], BF16)

                S.dma("pool", w1sb[:], cw1.ap().rearrange("e s d h -> d e s h"), w=[w1sb], key=w1sb)
                S.dma("pool", w2sb[:], cw2.ap().rearrange("e h d -> h e d"), w=[w2sb], key=w2sb)
                S.dma("pool", posb[:], posT.ap(), w=[posb], key=posb)
                S.dma("sp", b1sb[:], b1T.ap(), w=[b1sb], key=b1sb)
                S.dma("sp", I0[:], I0_d.ap(), w=[I0], key=I0)
                S.dma("sp", Jm[:], Jm_d.ap(), w=[Jm], key=Jm)
                S.dma("pool", trib[:], tri_d.ap(), w=[trib], key=trib)
                S.dma("pool", Ebig[:], Ebig_d.ap(), w=[Ebig], key=Ebig)
                S.dve(lambda e: e.memset(KcT[:], 0.0), w=[KcT])
                S.dve(lambda e: e.memset(VcA[:], 0.0), w=[VcA])
                S.dve(lambda e: e.memset(VcA[:, :, 128:129], 1.0), w=[VcA])
                S.dve(lambda e: e.memset(col0[:], 0.0), w=[col0])
                S.dve(lambda e: e.memset(col0[:, 0:1], 1.0), w=[col0])
                S.dma("pool", VcA[:, :, 129:193], cover_d.ap().rearrange("(t p) j -> p t j", p=128), w=[VcA], key=VcA)

                def bias_mm(e):
                    ins = None
                    for ee in range(2):
                        for s_ in range(32):
                            ins = e.matmul(pS[0][:, ee:ee + 1], w1sb[:, ee, s_, :], posb[:, ee * 32 + s_:ee * 32 + s_ + 1],
                                           start=(ee == 0 and s_ == 0), stop=(s_ == 31), skip_group_check=True)
                    return ins
                S.pe(bias_mm, r=[w1sb, posb], w=[pS[0]])
                S.dve(lambda e: e.tensor_tensor(out=biasb[:], in0=pS[0][:, 0:2], in1=b1sb[:], op=ALU.add), r=[pS[0], b1sb], w=[biasb])

                def compress(srcT, nblk, ncols_pad, KcT_out, Vc_out_fn):
                    for ee in range(2):
                        for c0 in range(0, nblk, 512):
                            cn = min(512, nblk - c0)
                            P = pS[(c0 // 512) % 2]

                            def hmm(e, ee=ee, c0=c0, cn=cn, P=P):
                                ins = None
                                src = srcT(ee)
                                for rs_ in range(32):
                                    lo = rs_ + 16 * c0
                                    ins = e.matmul(P[:, :cn], w1sb[:, ee, rs_, :], src[:, lo:lo + 16 * (cn - 1) + 1:16],
                                                   start=(rs_ == 0), stop=(rs_ == 31))
                                return ins
                            S.pe(hmm, r=[w1sb, KT, VcT], w=[P])
                            S.act(lambda e, ee=ee, cn=cn, P=P: e.activation(out=hx[:, :cn], in_=P[:, :cn], func=AF.Identity,
                                                                             bias=biasb[:, ee:ee + 1]), r=[P, biasb], w=[hx])
                            S.dve(lambda e, cn=cn: e.tensor_tensor(out=ht[:, :cn], in0=hx[:, :cn], in1=hx[:, :cn], op=ALU.mult), r=[hx], w=[ht])
                            S.dve(lambda e, cn=cn: e.tensor_scalar(out=ht[:, :cn], in0=ht[:, :cn], scalar1=0.044715, scalar2=1.0,
                                                                   op0=ALU.mult, op1=ALU.add), r=[ht], w=[ht])
                            S.dve(lambda e, cn=cn: e.tensor_tensor(out=ht[:, :cn], in0=ht[:, :cn], in1=hx[:, :cn], op=ALU.mult), r=[ht, hx], w=[ht])
                            S.act(lambda e, cn=cn: e.activation(out=ht[:, :cn], in_=ht[:, :cn], func=AF.Sigmoid, scale=1.5957691216),
                                  r=[ht], w=[ht])
                            H = hT[ee]
                            S.dve(lambda e, cn=cn, H=H: e.tensor_tensor(out=H[:, :cn], in0=hx[:, :cn], in1=ht[:, :cn], op=ALU.mult),
                                  r=[hx, ht], w=[H])
                            if ee == 0:
                                S.pe(lambda e, cn=cn, H=H: e.matmul(pO[0][:, :cn], w2sb[:, 0, :], H[:, :cn], start=True, stop=True),
                                     r=[w2sb, H], w=[pO[0]])
                                S.act(lambda e, cn=cn, c0=c0: e.copy(out=KcT_out[:, c0:c0 + cn], in_=pO[0][:, :cn]), r=[pO[0]], w=[KcT])
                            else:
                                for t0 in range(0, cn, 128):
                                    nb = min(128, cn - t0)
                                    S.pe(lambda e, nb=nb, t0=t0, H=H: e.matmul(pO[1][:nb, 0:128], H[:, t0:t0 + nb], w2sb[:, 1, :],
                                                                               start=True, stop=True), r=[w2sb, H], w=[pO[1]])
                                    S.act(lambda e, nb=nb, ct=(c0 + t0) // 128: e.copy(out=Vc_out_fn(ct, nb), in_=pO[1][:nb, 0:128]),
                                          r=[pO[1]], w=[VcA])

                compress(lambda ee: (KT[:, 0, :] if ee == 0 else VcT[:, :]), 255, 256, KcT, lambda ct, nb: VcA[:nb, ct, 0:128])

                HB = [(0, 0), (0, 256), (1, 0), (1, 256)]

                def pv_group(Pb, E, rhs_fn, ncol, first, r):
                    def f(e):
                        ins = None
                        for h in range(4):
                            bk, co = HB[h]
                            ins = e.matmul(Pb[bk][:, co:co + ncol], E[:, h, :], rhs_fn(), start=(first and h % 2 == 0), stop=False,
                                           skip_group_check=True)
                        return ins
                    S.pe(f, r=[E] + r, w=[Pb[0], Pb[1]])

                def finish_branch(Pb, i, br, first_branch, ocat):
                    for h in range(4):
                        bk, co = HB[h]
                        S.dve(lambda e, h=h, bk=bk, co=co: e.tensor_copy(out=rs[:, h:h + 1], in_=Pb[bk][:, co + 128:co + 129]),
                              r=[Pb[bk]], w=[rs])
                    S.dve(lambda e: e.tensor_scalar(out=rs[:, 0:4], in0=rs[:, 0:4], scalar1=1e-30, scalar2=None, op0=ALU.max), r=[rs], w=[rs])
                    S.dve(lambda e: e.reciprocal(out=rs[:, 0:4], in_=rs[:, 0:4]), r=[rs], w=[rs])
                    S.dve(lambda e, i=i, br=br: e.tensor_tensor(out=rs[:, 4:8], in0=rs[:, 0:4], in1=GL[:, i, 4 * br:4 * br + 4], op=ALU.mult),
                          r=[rs, GL], w=[rs])
                    for h in range(4):
                        bk, co = HB[h]
                        if first_branch:
                            S.dve(lambda e, h=h, bk=bk, co=co: e.tensor_scalar(out=ocat[:, h, :], in0=Pb[bk][:, co:co + 128],
                                                                               scalar1=rs[:, 4 + h:5 + h], scalar2=None, op0=ALU.mult),
                                  r=[Pb[bk], rs], w=[ocat])
                        else:
                            S.dve(lambda e, h=h, bk=bk, co=co: e.scalar_tensor_tensor(out=ocat[:, h, :], in0=Pb[bk][:, co:co + 128],
                                                                                      scalar=rs[:, 4 + h:5 + h], in1=ocat[:, h, :],
                                                                                      op0=ALU.mult, op1=ALU.add),
                                  r=[Pb[bk], rs, ocat], w=[ocat])

                pair_ctr = [0]

                def qk_exp(i, lhsT_fn, lr, with_mask, nmT=None):
                    nmT = nmT if nmT is not None else nmTs[0]
                    x = pair_ctr[0] % 3
                    pair_ctr[0] += 1
                    P, E = pS[x], Eb[x]

                    def f(e):
                        ins = e.matmul(P[:, :], lhsT_fn(), QT[:, :, i * 128:i * 128 + 128], start=True, stop=not with_mask)
                        if with_mask is not False:
                            ins = e.matmul(P[:, :], Ebig[:, with_mask * 128:with_mask * 128 + 128], nmT[:, :, :], start=False, stop=True)
                        return ins
                    S.pe(f, r=[QT, Ebig, nmT] + lr, w=[P])
                    S.act(lambda e: e.activation(out=E[:, :, :], in_=P[:, :].rearrange("p (h q) -> p h q", h=4), func=AF.Exp, scale=SCALE),
                          r=[P], w=[E])
                    return E

                def mul_mask(E, m_ap, r):
                    S.dve(lambda e: e.tensor_tensor(out=E[:, :, :], in0=E[:, :, :], in1=m_ap.unsqueeze(1).broadcast_to([128, 4, 128]),
                                                    op=ALU.mult), r=[E] + r, w=[E])

                def stage1(i):
                    ocat = ocats[i % 2]
                    nmT = nmTs[i % 2]
                    Pb = pO[0:2]
                    n_ct = 1 if i < 16 else 2
                    for ct in range(n_ct):
                        E = qk_exp(i, lambda ct=ct: KcT[:, ct * 128:ct * 128 + 128], [KcT], False)
                        cval = float(128 * i - 2048 * ct - 31)
                        S.dve(lambda e, cval=cval: e.tensor_scalar(out=mk[:, :], in0=I0[:, :], scalar1=cval, scalar2=0.0,
                                                                   op0=ALU.add, op1=ALU.is_ge), r=[I0], w=[mk])
                        mul_mask(E, mk[:, :], [mk])
                        pv_group(Pb, E, lambda ct=ct: VcA[:, ct, 0:193], 193, ct == 0, [VcA])
                    for h in range(4):
                        bk, co = HB[h]
                        S.dve(lambda e, h=h, bk=bk, co=co: e.tensor_copy(out=rs[:, h:h + 1], in_=Pb[bk][:, co + 128:co + 129]),
                              r=[Pb[bk]], w=[rs])
                    S.dve(lambda e: e.tensor_scalar(out=rs[:, 0:4], in0=rs[:, 0:4], scalar1=1e-30, scalar2=None, op0=ALU.max), r=[rs], w=[rs])
                    S.dve(lambda e: e.reciprocal(out=rs[:, 0:4], in_=rs[:, 0:4]), r=[rs], w=[rs])
                    for h in range(4):
                        bk, co = HB[h]
                        if h == 0:
                            S.dve(lambda e, bk=bk, co=co: e.tensor_scalar(out=imp[:, :], in0=Pb[bk][:, co + 129:co + 193], scalar1=rs[:, 0:1],
                                                                          scalar2=None, op0=ALU.mult), r=[Pb[bk], rs], w=[imp])
                        else:
                            S.dve(lambda e, h=h, bk=bk, co=co: e.scalar_tensor_tensor(out=imp[:, :], in0=Pb[bk][:, co + 129:co + 193],
                                                                                      scalar=rs[:, h:h + 1], in1=imp[:, :],
                                                                                      op0=ALU.mult, op1=ALU.add),
                                  r=[Pb[bk], rs, imp], w=[imp])
                    finish_branch(Pb, i, 0, True, ocat)
                    if i == 0:
                        dbg("ocat_cmp", ocat, ocat[:, :, :], [128, 4, 128])
                        dbg("rs_cmp", rs, rs[:, :], [128, 8])
                        dbg("imp", imp, imp[:, :], [128, 64])
                        dbg("GL", GL, GL[:, :, :], [128, NT + 1, 12])
                    S.dve(lambda e, i=i: e.tensor_scalar(out=vis[:, :], in0=Jm[:, :], scalar1=float(2 * i), scalar2=None, op0=ALU.is_le),
                          r=[Jm], w=[vis])
                    if i >= 8:
                        S.dve(lambda e, i=i: e.tensor_scalar(out=frc[:, :], in0=Jm[:, :], scalar1=float(2 * i - 1), scalar2=None, op0=ALU.is_ge),
                              r=[Jm], w=[frc])
                        S.dve(lambda e: e.tensor_tensor(out=frc[:, :], in0=frc[:, :], in1=vis[:, :], op=ALU.mult), r=[frc, vis], w=[frc])
                        S.dve(lambda e: e.tensor_tensor(out=frc[:, :], in0=frc[:, :], in1=col0[:, :], op=ALU.max), r=[frc, col0], w=[frc])
                        S.dve(lambda e: e.tensor_tensor(out=sc[0][:, :], in0=imp[:, :], in1=vis[:, :], op=ALU.mult), r=[imp, vis], w=[sc[0]])
                        S.dve(lambda e: e.tensor_scalar(out=sc[1][:, :], in0=vis[:, :], scalar1=-1.0, scalar2=1e9, op0=ALU.add, op1=ALU.mult),
                              r=[vis], w=[sc[1]])
                        S.dve(lambda e: e.tensor_tensor(out=sc[0][:, :], in0=sc[0][:, :], in1=sc[1][:, :], op=ALU.add), r=[sc[0], sc[1]], w=[sc[0]])
                        S.dve(lambda e: e.scalar_tensor_tensor(out=sc[0][:, :], in0=frc[:, :], scalar=2e9, in1=sc[0][:, :],
                                                               op0=ALU.mult, op1=ALU.add), r=[frc, sc[0]], w=[sc[0]])
                        S.dve(lambda e: e.max(out=m8[:, 0:8], in_=sc[0][:, :]), r=[sc[0]], w=[m8])
                        S.dve(lambda e: e.match_replace(out=sc[1][:, :], in_to_replace=m8[:, 0:8], in_values=sc[0][:, :], imm_value=-3e38),
                              r=[sc[0], m8], w=[sc[1]])
                        S.dve(lambda e: e.max(out=m8[:, 8:16], in_=sc[1][:, :]), r=[sc[1]], w=[m8])
                        S.dve(lambda e: e.tensor_scalar(out=selm[:, :], in0=sc[0][:, :], scalar1=m8[:, 15:16], scalar2=None, op0=ALU.is_ge),
                              r=[sc[0], m8], w=[selm])
                        S.dve(lambda e: e.tensor_tensor(out=selm[:, :], in0=selm[:, :], in1=vis[:, :], op=ALU.mult), r=[selm, vis], w=[selm])
                        SM = selm
                    else:
                        SM = vis
                    S.dve(lambda e, SM=SM: e.tensor_scalar(out=nm[:, :], in0=SM[:, :], scalar1=-NEG, scalar2=NEG, op0=ALU.mult, op1=ALU.add),
                          r=[SM], w=[nm])

                def stage1b(i):
                    nmT = nmTs[i % 2]
                    S.pe(lambda e: e.transpose(pM[:64, 0:128], nm[:, :], identb[:, :]), r=[nm, identb], w=[pM])
                    S.act(lambda e: e.copy(out=nmT[:, :, :], in_=pM[:64, 0:128].unsqueeze(1).broadcast_to([64, 4, 128])), r=[pM], w=[nmT])

                def stage2(i):
                    ocat = ocats[i % 2]
                    nmT = nmTs[i % 2]
                    pairs = []
                    PbS = pO[2:4]
                    for kt in range(i + 1):
                        pairs.append((lambda kt=kt: KT[:, 1, kt * 128:kt * 128 + 128], kt, (trib[:, 0:128] if kt == i else None),
                                      PbS, lambda kt=kt: VA[:, kt, 0, 0:129], kt == 0, 1 if kt == i else None))
                    PbW = pO[0:2]
                    kts = [kt for kt in range(i - 4, i + 1) if kt >= 0]
                    for kt in kts:
                        m_ = trib[:, 0:128] if kt == i else (trib[:, 128:256] if kt == i - 4 else None)
                        pairs.append((lambda kt=kt: KT[:, 2, kt * 128:kt * 128 + 128], False, m_,
                                      PbW, lambda kt=kt: VA[:, kt, 1, 0:129], kt == kts[0], 2 if kt == kts[-1] else None))
                    En = qk_exp(i, pairs[0][0], [KT], pairs[0][1], nmT)
                    for k_, (lf, wm, m_, Pb_, rf, first_, fin_) in enumerate(pairs):
                        E = En
                        if k_ + 1 < len(pairs):
                            En = qk_exp(i, pairs[k_ + 1][0], [KT], pairs[k_ + 1][1], nmT)
                        if m_ is not None:
                            mul_mask(E, m_, [trib])
                        pv_group(Pb_, E, rf, 129, first_, [VA])
                        if fin_ is not None:
                            finish_branch(Pb_, i, fin_, False, ocat)
                    if i == 0:
                        pass
                    OB = ocb[i % 2]
                    S.act(lambda e, OB=OB: e.copy(out=OB[:, :], in_=ocat[:, :, :].rearrange("p h d -> p (h d)")), r=[ocat], w=[OB])
                    S.dma("sp", o_loc.ap()[i * 128:i * 128 + 128, :], OB[:, :], r=[OB], w=[o_locT[i // 8]], key=OB)
                    if i % 8 == 7:
                        c = i // 8
                        allgather(o_loc.ap()[c * 1024:c * 1024 + 1024, :].rearrange("(p a) f -> p (a f)", a=8),
                                  og.ap()[c * 4096:c * 4096 + 4096, :].rearrange("(q a) f -> q (a f)", a=8), o_locT[c], ogT)
                if NTQ > 0:
                    stage1(0)
                    stage1b(0)
                for i in range(NTQ):
                    if i + 1 < NTQ:
                        stage1(i + 1)
                    stage2(i)
                    if i + 1 < NTQ:
                        stage1b(i + 1)
                S.barrier()
        phase_c()

        pp.close()

        S.dma("sp", win_s.ap()[:, 0:508], wstate.ap()[:, 4:512], key=Buf("wcopy"), final=True)
        def phase_s():
            with contextlib.ExitStack() as ps_:
                NP = PAST // 128
                NB = 257
                w1sb = sbuf(ps_, "w1sb_s", [128, 2, 32, 128], BF16)
                w2sb = sbuf(ps_, "w2sb_s", [128, 2, 128], BF16)
                posb = sbuf(ps_, "posb_s", [128, 64], BF16)
                b1sb = sbuf(ps_, "b1sb_s", [128, 2], F32)
                biasb = sbuf(ps_, "biasb_s", [128, 2], F32)
                Ebig = sbuf(ps_, "Ebig_s", [64, SEQ], BF16)
                smask = sbuf(ps_, "smask_s", [16, 5, 16], BF16)
                sel16 = sbuf(ps_, "sel16_s", [16, 68], F32)
                sel16b = sbuf(ps_, "sel16b_s", [16, 4], BF16)
                hmask = sbuf(ps_, "hmask_s", [16, 12], F32)
                big = sbuf(ps_, "bigS", [128, 2, 16512], BF16)
                XcT = big
                KsT = big
                Vs = big
                Vs3 = big[:, 1, :].rearrange("p (g c) -> p g c", c=129)
                E2 = sbuf(ps_, "E2s", [64, 128], BF16)
                Rrep = sbuf(ps_, "Rrep_s", [32, 128], F32)
                KwT = sbuf(ps_, "KwT", [128, 512], BF16)
                Vw = sbuf(ps_, "Vw", [128, 4, 130], BF16)
                stg = [sbuf(ps_, "stg%d" % i, [128, 8192], F32) for i in range(2)]
                wst = sbuf(ps_, "wst", [128, 4, 2, 128], F32)
                KcT = sbuf(ps_, "KcT_s", [128, 1024], BF16)
                Vc = sbuf(ps_, "Vc_s", [128, 8, 392], BF16)
                hx = sbuf(ps_, "hx_s", [128, 512], F32)
                ht = sbuf(ps_, "ht_s", [128, 512], F32)
                hT = [sbuf(ps_, "hT_s%d" % i, [128, 512], BF16) for i in range(2)]
                Es = [sbuf(ps_, "Es%d" % i, [128, 16], BF16) for i in range(2)]
                rs = sbuf(ps_, "rs_s", [16, 8], F32)
                U = sbuf(ps_, "U_s", [16, 260], BF16)
                gt = sbuf(ps_, "gt_s", [16, 12], F32)
                gs = sbuf(ps_, "gs_s", [16, 4], F32)
                ocat = sbuf(ps_, "ocat_s", [16, 128], F32)
                ocb = sbuf(ps_, "ocb_s", [16, 128], BF16)
                imp = sbuf(ps_, "imp_s", [4, 260], F32)
                sc = [sbuf(ps_, "sc_s%d" % i, [4, 260], F32) for i in range(2)]
                m8 = sbuf(ps_, "m8_s", [4, 16], F32)
                nm = sbuf(ps_, "nm_s", [4, 320], BF16)
                nmT = sbuf(ps_, "nmT_s", [64, 5, 4, 4], BF16)
                pS = [psum(ps_, "qS%d" % i, [128, 512], F32) for i in range(2)]
                pO = [psum(ps_, "qO%d" % i, [128, 512], F32) for i in range(3)]
                pTf = [psum(ps_, "qT%d" % i, [128, 512], F32) for i in range(2)]
                pM = psum(ps_, "qM", [128, 1024], BF16)

                S.dma("pool", w1sb[:], cw1.ap().rearrange("e s d h -> d e s h"), w=[w1sb], key=w1sb)
                S.dma("pool", w2sb[:], cw2.ap().rearrange("e h d -> h e d"), w=[w2sb], key=w2sb)
                S.dma("pool", posb[:], posT.ap(), w=[posb], key=posb)
                S.dma("sp", b1sb[:], b1T.ap(), w=[b1sb], key=b1sb)
                S.dma("pool", Ebig[:], Ebig_d.ap(), w=[Ebig], key=Ebig)
                S.dma("pool", smask[:], smask_d.ap(), w=[smask], key=smask)
                S.dma("sp", sel16[:], sel16_d.ap(), w=[sel16], key=sel16)
                S.dma("pool", sel16b[:], sel16_d.ap()[:, 0:4], w=[sel16b], key=sel16b)
                S.dma("sp", hmask[:], hmask_d.ap(), w=[hmask], key=hmask)
                S.dve(lambda e: e.memset(Vw[:, :, 128:129], 1.0), w=[Vw])
                S.dve(lambda e: e.memset(KcT[:], 0.0), w=[KcT])
                S.dve(lambda e: e.memset(Vc[:], 0.0), w=[Vc])
                S.dve(lambda e: e.memset(Vc[:, 0:7, 128:129], 1.0), w=[Vc])
                S.dve(lambda e: e.memset(Vc[:127, 7, 128:129], 1.0), w=[Vc])
                S.dma("pool", Vc[:, :, 129:386], cover_s.ap().rearrange("(t p) j -> p t j", p=128), w=[Vc], key=Vc)

                def bias_mm(e):
                    ins = None
                    for ee in range(2):
                        for s_ in range(32):
                            ins = e.matmul(pS[0][:, ee:ee + 1], w1sb[:, ee, s_, :], posb[:, ee * 32 + s_:ee * 32 + s_ + 1],
                                           start=(ee == 0 and s_ == 0), stop=(s_ == 31), skip_group_check=True)
                    return ins
                S.pe(bias_mm, r=[w1sb, posb], w=[pS[0]])
                S.dve(lambda e: e.tensor_tensor(out=biasb[:], in0=pS[0][:, 0:2], in1=b1sb[:], op=ALU.add), r=[pS[0], b1sb], w=[biasb])

                pti = sbuf(ps_, "pti", [32, 16], I32)
                ptf = sbuf(ps_, "ptf", [32, 16], F32)
                idf = sbuf(ps_, "idf", [128, 16], F32)
                idi = sbuf(ps_, "idi", [128, 16], I32)
                pix = sbuf(ps_, "pix", [128, 1], F32)
                S.dma("sp", pti[:], ptabT.ap(), w=[pti], key=pti)
                S.dma("sp", pix[:], pidx_d.ap(), w=[pix], key=pix)
                S.dma("sp", Rrep[:], Rrep_d.ap(), w=[Rrep], key=Rrep)
                S.dma("pool", E2[:], E2_d.ap(), w=[E2], key=E2)
                S.dve(lambda e: e.tensor_copy(out=ptf[:], in_=pti[:]), r=[pti], w=[ptf])
                S.pe(lambda e: e.matmul(pS[1][:, 0:16], Rrep[:, :], ptf[:, :], start=True, stop=True), r=[Rrep, ptf], w=[pS[1]])
                S.dve(lambda e: e.tensor_scalar(out=idf[:], in0=pS[1][:, 0:16], scalar1=4.0, scalar2=pix[:, 0:1], op0=ALU.mult, op1=ALU.add),
                      r=[pS[1], pix], w=[idf])
                S.dve(lambda e: e.tensor_copy(out=idi[:], in_=idf[:]), r=[idf], w=[idi])
                gctr = [0]

                def qgather(cache, s_, q):
                    G = stg[gctr[0] % 2]
                    gctr[0] += 1
                    rows = cache.ap().rearrange("g (u t) e d -> (g u) (t e d)", u=4)
                    k = s_ * 4 + q
                    S.custom_dma("pool", lambda e: e.indirect_dma_start(
                        out=G[:, :], out_offset=None, in_=rows,
                        in_offset=bass.IndirectOffsetOnAxis(ap=idi[:, k:k + 1], axis=0)), r=[idi], w=[G], key=G)
                    return G

                for s_ in range(4):
                    ev = 0
                    for q in range(4):
                        G = qgather(ccache, s_, q)
                        G3 = G[:, :].rearrange("p (t e d) -> p t e d", t=32, e=2)
                        for ee in range(2):
                            for t0 in range(0, 32, 4):
                                P = pTf[ev % 2]

                                def trp(e, G3=G3, ee=ee, t0=t0, P=P):
                                    ins = None
                                    for j in range(4):
                                        ins = e.transpose(P[:, j * 128:j * 128 + 128], G3[:, t0 + j, ee, :], identf[:, :])
                                    return ins
                                S.pe(trp, r=[G, identf], w=[P])
                                dst = XcT[:, ee, 4096 * q:4096 * q + 4096].rearrange("d (p t) -> d t p", t=32)[:, t0:t0 + 4, :]
                                src = P[:, 0:512].rearrange("d (j p) -> d j p", j=4)
                                if ev % 2 == 0:
                                    S.act(lambda e, dst=dst, src=src: e.copy(out=dst, in_=src), r=[P], w=[XcT])
                                else:
                                    S.dve(lambda e, dst=dst, src=src: e.tensor_copy(out=dst, in_=src), r=[P], w=[XcT])
                                ev += 1
                    for ee in range(2):
                        for c0 in (0, 512):
                            cn = 512 if c0 == 0 else 511
                            P = pS[(c0 // 512) % 2]

                            def hmm(e, ee=ee, c0=c0, cn=cn, P=P):
                                ins = None
                                for rs_ in range(32):
                                    lo = rs_ + 16 * c0
                                    ins = e.matmul(P[:, :cn], w1sb[:, ee, rs_, :], XcT[:, ee, lo:lo + 16 * (cn - 1) + 1:16],
                                                   start=(rs_ == 0), stop=(rs_ == 31))
                                return ins
                            S.pe(hmm, r=[w1sb, XcT], w=[P])
                            S.act(lambda e, ee=ee, cn=cn, P=P: e.activation(out=hx[:, :cn], in_=P[:, :cn], func=AF.Identity,
                                                                             bias=biasb[:, ee:ee + 1]), r=[P, biasb], w=[hx])
                            S.dve(lambda e, cn=cn: e.tensor_tensor(out=ht[:, :cn], in0=hx[:, :cn], in1=hx[:, :cn], op=ALU.mult), r=[hx], w=[ht])
                            S.dve(lambda e, cn=cn: e.tensor_scalar(out=ht[:, :cn], in0=ht[:, :cn], scalar1=0.044715, scalar2=1.0,
                                                                   op0=ALU.mult, op1=ALU.add), r=[ht], w=[ht])
                            S.dve(lambda e, cn=cn: e.tensor_tensor(out=ht[:, :cn], in0=ht[:, :cn], in1=hx[:, :cn], op=ALU.mult), r=[ht, hx], w=[ht])
                            S.act(lambda e, cn=cn: e.activation(out=ht[:, :cn], in_=ht[:, :cn], func=AF.Sigmoid, scale=1.5957691216),
                                  r=[ht], w=[ht])
                            H = hT[ee]
                            S.dve(lambda e, cn=cn, H=H: e.tensor_tensor(out=H[:, :cn], in0=hx[:, :cn], in1=ht[:, :cn], op=ALU.mult),
                                  r=[hx, ht], w=[H])
                            if ee == 0:
                                S.pe(lambda e, cn=cn, H=H: e.matmul(pO[0][:, :cn], w2sb[:, 0, :], H[:, :cn], start=True, stop=True),
                                     r=[w2sb, H], w=[pO[0]])
                                S.act(lambda e, cn=cn, c0=c0: e.copy(out=KcT[:, c0:c0 + cn], in_=pO[0][:, :cn]), r=[pO[0]], w=[KcT])
                            else:
                                for t0 in range(0, cn, 128):
                                    nb = min(128, cn - t0)
                                    S.pe(lambda e, nb=nb, t0=t0, H=H: e.matmul(pO[1][:nb, 0:128], H[:, t0:t0 + nb], w2sb[:, 1, :],
                                                                               start=True, stop=True), r=[w2sb, H], w=[pO[1]])
                                    S.act(lambda e, nb=nb, ct=(c0 + t0) // 128: e.copy(out=Vc[:nb, ct, 0:128], in_=pO[1][:nb, 0:128]),
                                          r=[pO[1]], w=[Vc])

                    pc_ = [0]
                    Qs = QTs[:, :, 4 * s_:4 * s_ + 4]

                    def qk16(lhsT, nk, lr, maskchunk=None, Qs=Qs):
                        x = pc_[0] % 2
                        pc_[0] += 1
                        P, E = pS[x], Es[x]

                        def f(e):
                            ins = e.matmul(P[:nk, 0:16], lhsT, Qs, start=True, stop=(maskchunk is None))
                            if maskchunk is not None:
                                ch, kt = maskchunk
                                ins = e.matmul(P[:nk, 0:16], E2[:, :], nmT[:, ch, :, :], start=False, stop=True)
                            return ins
                        S.pe(f, r=[QTs, E2, nmT] + lr, w=[P])
                        S.act(lambda e: e.activation(out=E[:nk, :], in_=P[:nk, 0:16], func=AF.Exp, scale=SCALE), r=[P], w=[E])
                        return E

                    def pv16(Pacc, E, nk, rhs, ncol, first, last, r):
                        S.pe(lambda e: e.matmul(Pacc[:16, 0:ncol], E[:nk, :], rhs, start=first, stop=last), r=[E] + r, w=[Pacc])

                    def fin16(Pacc, br, first_branch):
                        S.dve(lambda e: e.tensor_scalar(out=rs[:, 0:1], in0=Pacc[:16, 128:129], scalar1=1e-30, scalar2=None, op0=ALU.max),
                              r=[Pacc], w=[rs])
                        S.dve(lambda e: e.reciprocal(out=rs[:, 0:1], in_=rs[:, 0:1]), r=[rs], w=[rs])
                        S.dve(lambda e: e.tensor_tensor(out=rs[:, 1:2], in0=rs[:, 0:1], in1=gs[:, br:br + 1], op=ALU.mult), r=[rs, gs], w=[rs])
                        if first_branch:
                            S.dve(lambda e: e.tensor_scalar(out=ocat[:, :], in0=Pacc[:16, 0:128], scalar1=rs[:, 1:2], scalar2=None, op0=ALU.mult),
                                  r=[Pacc, rs], w=[ocat])
                        else:
                            S.dve(lambda e: e.scalar_tensor_tensor(out=ocat[:, :], in0=Pacc[:16, 0:128], scalar=rs[:, 1:2], in1=ocat[:, :],
                                                                   op0=ALU.mult, op1=ALU.add), r=[Pacc, rs, ocat], w=[ocat])

                    S.pe(lambda e, s_=s_: e.matmul(pO[2][:16, 0:12], sel16[:, 4 + 16 * s_:4 + 16 * s_ + 16], GL[:16, NT, :], start=True, stop=True),
                         r=[sel16, GL], w=[pO[2]])
                    S.dve(lambda e: e.tensor_tensor(out=gt[:, :], in0=pO[2][:16, 0:12], in1=hmask[:, :], op=ALU.mult), r=[pO[2], hmask], w=[gt])
                    S.dve(lambda e: e.tensor_reduce(out=gs[:, 0:3], in_=gt[:, :].rearrange("p (b h) -> p b h", b=3), axis=AX.X, op=ALU.add),
                          r=[gt], w=[gs])

                    def run_pairs(plist):
                        En = qk16(*plist[0][0][:3], **plist[0][0][3])
                        for k_, (qa, post, pa) in enumerate(plist):
                            E = En
                            if k_ + 1 < len(plist):
                                nq = plist[k_ + 1][0]
                                En = qk16(*nq[:3], **nq[3])
                            if post is not None:
                                post(E)
                            pv16(pa[0], E, *pa[1:])

                    run_pairs([((KcT[:, ct * 128:ct * 128 + 128], 128, [KcT], {}), None,
                                (pO[0], 128, Vc[:, ct, 0:386], 386, ct == 0, ct == 7, [Vc])) for ct in range(8)])
                    fin16(pO[0], 0, True)
                    S.dve(lambda e: e.tensor_scalar(out=U[:, 0:257], in0=pO[0][:16, 129:386], scalar1=rs[:, 0:1], scalar2=None, op0=ALU.mult),
                          r=[pO[0], rs], w=[U])
                    S.pe(lambda e: e.matmul(pO[2][:4, 0:257], sel16b[:, 0:4], U[:, 0:257], start=True, stop=True), r=[sel16b, U], w=[pO[2]])
                    S.dve(lambda e: e.tensor_copy(out=sc[0][:, 0:257], in_=pO[2][:4, 0:257]), r=[pO[2]], w=[sc[0]])
                    S.dve(lambda e: e.memset(sc[0][:, 0:1], 2e9), w=[sc[0]])
                    S.dve(lambda e: e.memset(sc[0][:, 255:257], 2e9), w=[sc[0]])
                    S.dve(lambda e: e.max(out=m8[:, 0:8], in_=sc[0][:, 0:257]), r=[sc[0]], w=[m8])
                    S.dve(lambda e: e.match_replace(out=sc[1][:, 0:257], in_to_replace=m8[:, 0:8], in_values=sc[0][:, 0:257], imm_value=-3e38),
                          r=[sc[0], m8], w=[sc[1]])
                    S.dve(lambda e: e.max(out=m8[:, 8:16], in_=sc[1][:, 0:257]), r=[sc[1]], w=[m8])
                    S.dve(lambda e: e.memset(nm[:, :], 0.0), w=[nm])
                    S.dve(lambda e: e.tensor_scalar(out=sc[1][:, 0:257], in0=sc[0][:, 0:257], scalar1=m8[:, 15:16], scalar2=None, op0=ALU.is_ge),
                          r=[sc[0], m8], w=[sc[1]])
                    S.dve(lambda e: e.tensor_scalar(out=nm[:, 0:257], in0=sc[1][:, 0:257], scalar1=-NEG, scalar2=NEG, op0=ALU.mult, op1=ALU.add),
                          r=[sc[1]], w=[nm])

                    def trn(e):
                        ins = None
                        for ch in range(5):
                            ins = e.transpose(pM[:64, ch * 4:ch * 4 + 4], nm[:, ch * 64:ch * 64 + 64], identb[:4, :4])
                        return ins
                    S.pe(trn, r=[nm, identb], w=[pM])
                    S.act(lambda e: e.copy(out=nmT[:, :, :, :], in_=pM[:64, 0:20].rearrange("p (c q) -> p c q", c=5).unsqueeze(2)
                                           .broadcast_to([64, 5, 4, 4])), r=[pM], w=[nmT])

                    S.dve(lambda e: e.memset(Vs3[:, :, 128:129], 1.0), w=[Vs])
                    ev = 0
                    for q in range(4):
                        G = qgather(scache, s_, q)
                        G3 = G[:, :].rearrange("p (t e d) -> p t e d", t=32, e=2)
                        for t0 in range(0, 32, 4):
                            P = pTf[ev % 2]
                            ev += 1
                            kt0 = q * 32 + t0

                            def trk(e, G3=G3, t0=t0, P=P):
                                ins = None
                                for j in range(4):
                                    ins = e.transpose(P[:, j * 128:j * 128 + 128], G3[:, t0 + j, 0, :], identf[:, :])
                                return ins
                            S.pe(trk, r=[G, identf], w=[P])
                            S.act(lambda e, P=P, kt0=kt0: e.copy(out=KsT[:, 0, kt0 * 128:kt0 * 128 + 512], in_=P[:, 0:512]), r=[P], w=[KsT])
                            S.dve(lambda e, G3=G3, t0=t0, kt0=kt0: e.tensor_copy(out=Vs3[:, kt0:kt0 + 4, 0:128], in_=G3[:, t0:t0 + 4, 1, :]),
                                  r=[G], w=[Vs])
                    def newmask(E, s_=s_):
                        S.dve(lambda e: e.tensor_tensor(out=E[:16, :], in0=E[:16, :], in1=smask[:, s_, :], op=ALU.mult), r=[E, smask], w=[E])

                    def oldmask(E):
                        S.dve(lambda e: e.tensor_tensor(out=E[:16, :], in0=E[:16, :], in1=smask[:, 4, :], op=ALU.mult), r=[E, smask], w=[E])

                    pl = [((KsT[:, 0, kt * 128:kt * 128 + 128], 128, [KsT], {"maskchunk": (kt // 32, kt)}), None,
                           (pO[1], 128, Vs3[:, kt, 0:129], 129, kt == 0, False, [Vs])) for kt in range(NP)]
                    pl.append(((KTs[:, 1, :], 16, [KTs], {}), newmask, (pO[1], 16, VAs[:, 0, 0:129], 129, False, True, [VAs])))
                    run_pairs(pl)
                    fin16(pO[1], 1, False)
                    S.dma("sp", wst[:, :, :, :], wstate.ap()[s_].rearrange("(t p) e d -> p t e d", p=128), w=[wst], key=wst)
                    for t_ in range(4):
                        P = pTf[t_ % 2]
                        S.pe(lambda e, t_=t_, P=P: e.transpose(P[:, 0:128], wst[:, t_, 0, :], identf[:, :]), r=[wst, identf], w=[P])
                        S.act(lambda e, t_=t_, P=P: e.copy(out=KwT[:, t_ * 128:t_ * 128 + 128], in_=P[:, 0:128]), r=[P], w=[KwT])
                    S.dve(lambda e: e.tensor_copy(out=Vw[:, :, 0:128], in_=wst[:, :, 1, :]), r=[wst], w=[Vw])
                    pl = [((KwT[:, t_ * 128:t_ * 128 + 128], 128, [KwT], {}), (oldmask if t_ == 0 else None),
                           (pO[0], 128, Vw[:, t_, 0:129], 129, t_ == 0, False, [Vw])) for t_ in range(4)]
                    pl.append(((KTs[:, 2, :], 16, [KTs], {}), newmask, (pO[0], 16, VAs[:, 1, 0:129], 129, False, True, [VAs])))
                    run_pairs(pl)
                    fin16(pO[0], 2, False)
                    S.act(lambda e: e.copy(out=ocb[:, :], in_=ocat[:, :]), r=[ocat], w=[ocb])
                    for h in range(4):
                        S.dma("sp", o_locs.ap()[4 * s_:4 * s_ + 4, 128 * h:128 * h + 128], ocb[4 * h:4 * h + 4, :], r=[ocb], w=[o_locT[4]], key=ocb)
                S.barrier()

        if RUN_S:
            phase_s()

        allgather(o_locs.ap().rearrange("r (b x) -> (r b) x", b=8), ogs.ap().rearrange("r (b x) -> (r b) x", b=8), o_locT[4], ogT)

        TBS = [(0, 512), (512, 512), (1024, 4)]

        def token_phase(layer):
            with contextlib.ExitStack() as pd:
                hT = sbuf(pd, "hres%d" % layer, [128, 16, NTOK], F32)
                actT = sbuf(pd, "actT%d" % layer, [128, 8, NTOK], BF16)
                xnT = sbuf(pd, "xnT_d%d" % layer, [128, 16, NTOK], BF16)
                wbuf = [sbuf(pd, "wbuf%d_%d" % (i, layer), [128, 16, 512], BF16) for i in range(2)]
                xsts = [sbuf(pd, "xst%d_%d" % (layer, i), [128, D], F32) for i in range(2)]
                ost = [sbuf(pd, "ost%d_%d" % (i, layer), [128, 2, 512], BF16) for i in range(4)]
                oidx = sbuf(pd, "oidx_s%d" % layer, [128, 72], I32)
                rstd = sbuf(pd, "rstd_d%d" % layer, [128, 512], F32)
                sq = sbuf(pd, "sq_d%d" % layer, [128, 512], F32)
                rl = [sbuf(pd, "rl%d_%d" % (i, layer), [128, 512], F32) for i in range(2)]
                gT = sbuf(pd, "gT_d%d" % layer, [128, 16], F32)
                onesf = sbuf(pd, "onesf%d" % layer, [128, 128], F32)
                epsb = sbuf(pd, "epsb_d%d" % layer, [128, 1], F32)
                pz = [psum(pd, "dz%d_%d" % (i, layer), [128, 512], F32) for i in range(4)]
                pM = psum(pd, "dM%d" % layer, [128, 1024], BF16)
                pT = psum(pd, "dT%d" % layer, [128, 512], F32)
                pn = psum(pd, "dn%d" % layer, [128, 512], F32)
                ctr = {"w": 0, "z": 0, "o": 0, "r": 0}

                if layer == 0:
                    S.dma("sp", oidx[:, 0:36], oidx_d.ap(), w=[oidx], key=oidx)
                else:
                    S.dma("sp", oidx[:, :], oidx2_d.ap(), w=[oidx], key=oidx)
                S.dve(lambda e: e.memset(onesf[:], 1.0 / D), w=[onesf])
                S.dve(lambda e: e.memset(epsb[:], RMS_EPS), w=[epsb])

                if layer == 0:
                    for t in range(9):
                        n = 128 if t < 8 else 4
                        xst = xsts[t % 2]
                        S.dma("sp", xst[:n, :], x_tok.ap()[t * 128:t * 128 + n, :], w=[xst], key=xst)
                        for q in range(4):
                            def trx(e, q=q, n=n, xst=xst):
                                ins = None
                                for kk in range(4):
                                    k = q * 4 + kk
                                    ins = e.transpose(pT[:, kk * 128:kk * 128 + n], xst[:n, k * 128:k * 128 + 128], identf[:n, :n])
                                return ins
                            S.pe(trx, r=[xst, identf], w=[pT])
                            S.act(lambda e, q=q, n=n, t=t: e.copy(out=hT[:, q * 4:q * 4 + 4, t * 128:t * 128 + n],
                                                                 in_=pT[:, :].rearrange("p (k t) -> p k t", k=4)[:, :, :n]), r=[pT], w=[hT])
                else:
                    S.dma("sp", hT[:, :, :], hspill.ap().rearrange("p (k t) -> p k t", k=16), r=[hspT], w=[hT], key=hT)

                def load_act(gsrc, gsrc_s, gbuf, colfn):
                    for t in range(9):
                        n = 128 if t < 8 else 4
                        O_ = ost[ctr["o"] % 4]
                        ctr["o"] += 1
                        for jj in range(2):
                            S.custom_dma("pool", lambda e, O_=O_, jj=jj, t=t, col=colfn(t, jj): e.indirect_dma_start(
                                out=O_[:, jj, :], out_offset=None, in_=(gsrc if t < 8 else gsrc_s).ap(),
                                in_offset=bass.IndirectOffsetOnAxis(ap=oidx[:, col:col + 1], axis=0)),
                                r=[oidx, gbuf], w=[O_], key=O_)

                        def tro(e, O_=O_, n=n):
                            ins = None
                            for kk in range(8):
                                jj, q = kk // 4, kk % 4
                                ins = e.transpose(pM[:, kk * 128:kk * 128 + n], O_[:n, jj, q * 128:q * 128 + 128], identb[:n, :n])
                            return ins
                        S.pe(tro, r=[O_, identb], w=[pM])
                        S.act(lambda e, n=n, t=t: e.copy(out=actT[:, :, t * 128:t * 128 + n],
                                                         in_=pM[:, :].rearrange("p (k t) -> p k t", k=8)[:, :, :n]), r=[pM], w=[actT])

                def proj_blocks(Wd, row0):
                    blks = []
                    for cb in range(4):
                        def dma(wb, cb=cb):
                            S.dma("pool", wb[:, 0:8, :], Wd.ap()[row0:row0 + 1024, cb * 512:cb * 512 + 512].rearrange("(k p) c -> p k c", p=128),
                                  w=[wb], key=wb)

                        def comp(wb, cb=cb):
                            for cc in range(4):
                                for (t0, nt) in TBS:
                                    P = pz[ctr["z"] % 4]
                                    ctr["z"] += 1

                                    def mm(e, wb=wb, cc=cc, t0=t0, nt=nt, P=P):
                                        ins = None
                                        for k in range(8):
                                            ins = e.matmul(P[:, :nt], wb[:, k, cc * 128:cc * 128 + 128], actT[:, k, t0:t0 + nt],
                                                           start=(k == 0), stop=(k == 7))
                                        return ins
                                    S.pe(mm, r=[wb, actT], w=[P])
                                    c = cb * 4 + cc
                                    S.dve(lambda e, P=P, c=c, t0=t0, nt=nt: e.tensor_tensor(out=hT[:, c, t0:t0 + nt], in0=P[:, :nt],
                                                                                             in1=hT[:, c, t0:t0 + nt], op=ALU.add),
                                          r=[P, hT], w=[hT])
                        blks.append((dma, comp))
                    return blks

                def run_blocks(blks):
                    bufs = []
                    for i, (dma, comp) in enumerate(blks):
                        if i == 0:
                            wb0 = wbuf[ctr["w"] % 2]
                            ctr["w"] += 1
                            dma(wb0)
                            bufs.append(wb0)
                        if i + 1 < len(blks):
                            wbn = wbuf[ctr["w"] % 2]
                            ctr["w"] += 1
                            blks[i + 1][0](wbn)
                            bufs.append(wbn)
                        comp(bufs[i])

                def proj_accum(Wd, row0):
                    run_blocks(proj_blocks(Wd, row0))

                def rmsnorm_T(gain_d, out_fn, wlist):
                    S.dma("sp", gT[:], gain_d.ap(), w=[gT], key=gT)
                    for (t0, nt) in TBS:
                        for k in range(16):
                            S.act(lambda e, k=k, t0=t0, nt=nt: e.activation(out=sq[:, :nt], in_=hT[:, k, t0:t0 + nt], func=AF.Square),
                                  r=[hT], w=[sq])
                            S.pe(lambda e, k=k, nt=nt: e.matmul(pn[:, :nt], onesf[:, :], sq[:, :nt], start=(k == 0), stop=(k == 15)),
                                 r=[onesf, sq], w=[pn])
                        S.act(lambda e, nt=nt: e.activation(out=rstd[:, :nt], in_=pn[:, :nt], func=AF.Sqrt, bias=epsb[:, 0:1]),
                              r=[pn, epsb], w=[rstd])
                        S.dve(lambda e, nt=nt: e.reciprocal(out=rstd[:, :nt], in_=rstd[:, :nt]), r=[rstd], w=[rstd])
                        for k in range(16):
                            S.dve(lambda e, k=k, t0=t0, nt=nt: e.scalar_tensor_tensor(out=out_fn(k, t0, nt), in0=hT[:, k, t0:t0 + nt],
                                                                                      scalar=gT[:, k:k + 1], in1=rstd[:, :nt],
                                                                                      op0=ALU.mult, op1=ALU.mult),
                                  r=[hT, gT, rstd], w=wlist)

                def ffn(W1, W2):
                    blks = []
                    for hg in range(8):
                        for sub in range(2):
                            c0 = hg * 1024 + sub * 512

                            def dma(wb, c0=c0):
                                S.dma("pool", wb[:, :, :], W1.ap()[:, c0:c0 + 512].rearrange("(k p) c -> p k c", p=128), w=[wb], key=wb)

                            def comp(wb, sub=sub):
                                for fc in range(4):
                                    for (t0, nt) in TBS:
                                        P = pz[ctr["z"] % 4]
                                        ctr["z"] += 1

                                        def mm(e, wb=wb, fc=fc, t0=t0, nt=nt, P=P):
                                            ins = None
                                            for k in range(16):
                                                ins = e.matmul(P[:, :nt], wb[:, k, fc * 128:fc * 128 + 128], xnT[:, k, t0:t0 + nt],
                                                               start=(k == 0), stop=(k == 15))
                                            return ins
                                        S.pe(mm, r=[wb, xnT], w=[P])
                                        R_ = rl[ctr["r"] % 2]
                                        ctr["r"] += 1
                                        S.act(lambda e, P=P, R_=R_, nt=nt: e.activation(out=R_[:, :nt], in_=P[:, :nt], func=AF.Relu), r=[P], w=[R_])
                                        eng = S.pool if ctr["r"] % 2 == 0 else S.dve
                                        eng(lambda e, R_=R_, nt=nt, t0=t0, kk=sub * 4 + fc: e.tensor_tensor(out=actT[:, kk, t0:t0 + nt], in0=R_[:, :nt],
                                                                                                          in1=R_[:, :nt], op=ALU.mult),
                                            r=[R_], w=[actT])
                            blks.append((dma, comp))
                        blks.extend(proj_blocks(W2, hg * 1024))
                    run_blocks(blks)

                xn_out = lambda k, t0, nt: xnT[:, k, t0:t0 + nt]
                if layer == 0:
                    for g in range(2):
                        load_act(og, ogs, ogT, lambda t, jj, g=g: t * 4 + 2 * g + jj)
                        proj_accum(w_out0, g * 1024)
                    rmsnorm_T(gffn0T, xn_out, [xnT])
                    ffn(ffn1_0, ffn2_0)
                    rmsnorm_T(gmix1T, xn_out, [xnT])
                    for cp in range(8):
                        S.dma("sp", xg_in.ap()[cp].rearrange("p (k t) -> p k t", k=2), xnT[:, 2 * cp:2 * cp + 2, :], r=[xnT],
                              w=[xg_inT[cp]], key=Buf("xgk%d" % cp))
                        allgather(xg_in.ap()[cp], xg_all.ap()[cp], xg_inT[cp], xgT)
                    S.dma("sp", hspill.ap().rearrange("p (k t) -> p k t", k=16), hT[:, :, :], r=[hT], w=[hspT], key=hT)
                else:
                    for g in range(4):
                        load_act(og2, ogs2, og2T, lambda t, jj, g=g: t * 8 + 2 * g + jj)
                        proj_accum(w_out1, g * 1024)
                    rmsnorm_T(gffn1T, xn_out, [xnT])
                    ffn(ffn1_1, ffn2_1)
                    S.dma("sp", gT[:], gfinT.ap(), w=[gT], key=gT)
                    for (t0, nt) in TBS:
                        for k in range(16):
                            S.act(lambda e, k=k, t0=t0, nt=nt: e.activation(out=sq[:, :nt], in_=hT[:, k, t0:t0 + nt], func=AF.Square),
                                  r=[hT], w=[sq])
                            S.pe(lambda e, k=k, nt=nt: e.matmul(pn[:, :nt], onesf[:, :], sq[:, :nt], start=(k == 0), stop=(k == 15)),
                                 r=[onesf, sq], w=[pn])
                        S.act(lambda e, nt=nt: e.activation(out=rstd[:, :nt], in_=pn[:, :nt], func=AF.Sqrt, bias=epsb[:, 0:1]),
                              r=[pn, epsb], w=[rstd])
                        S.dve(lambda e, nt=nt: e.reciprocal(out=rstd[:, :nt], in_=rstd[:, :nt]), r=[rstd], w=[rstd])
                        for k in range(16):
                            S.dve(lambda e, k=k, t0=t0, nt=nt: e.scalar_tensor_tensor(out=hT[:, k, t0:t0 + nt], in0=hT[:, k, t0:t0 + nt],
                                                                                      scalar=gT[:, k:k + 1], in1=rstd[:, :nt],
                                                                                      op0=ALU.mult, op1=ALU.mult),
                                  r=[hT, gT, rstd], w=[hT])
                    for t in range(9):
                        n = 128 if t < 8 else 4
                        xst = xsts[t % 2]
                        for q in range(4):
                            def trh(e, q=q, n=n, t=t):
                                ins = None
                                for kk in range(4):
                                    ins = e.transpose(pT[:n, kk * 128:kk * 128 + 128], hT[:, q * 4 + kk, t * 128:t * 128 + n], identf[:, :])
                                return ins
                            S.pe(trh, r=[hT, identf], w=[pT])
                            S.act(lambda e, q=q, n=n, xst=xst: e.copy(out=xst[:n, q * 512:q * 512 + 512], in_=pT[:n, :]), r=[pT], w=[xst])
                        S.dma("sp", y_tok.ap()[t * 128:t * 128 + n, :], xst[:n, :], r=[xst], key=xst, final=True)
                S.barrier()

        def phase_e():
            with contextlib.ExitStack() as pe_:
                wr = sbuf(pe_, "wr_s", [128, 16, 3072], BF16)
                xt = [sbuf(pe_, "ext%d" % i, [128, 16, 128], BF16) for i in range(2)]
                cst = [sbuf(pe_, "ecs%d" % i, [128, 256], F32) for i in range(2)]
                St = sbuf(pe_, "St", [128, 4, 512], F32)
                Sb = sbuf(pe_, "Sb", [128, 4, 512], BF16)
                Ss = [sbuf(pe_, "Ss%d" % i, [128, 4, 512], F32) for i in range(2)]
                Ssb = [sbuf(pe_, "Ssb%d" % i, [128, 4, 512], BF16) for i in range(4)]
                qkrs = [sbuf(pe_, "qkr%d" % i, [128, 4, 256], BF16) for i in range(2)]
                Vbs = [sbuf(pe_, "Vb%d" % i, [128, 2, 512], BF16) for i in range(2)]
                sgs = [sbuf(pe_, "sg%d" % i, [128, 1024], F32) for i in range(2)]
                ob = [sbuf(pe_, "ob%d" % i, [128, 512], F32) for i in range(2)]
                jk = sbuf(pe_, "jk", [128, 512], BF16)
                rt = [sbuf(pe_, "ert%d" % i, [128, 4, 128], F32) for i in range(4)]
                qkTs = [sbuf(pe_, "qkT%d" % i, [128, 8, 128], BF16) for i in range(2)]
                QdT = sbuf(pe_, "QdT", [128, 2, 2, 128], BF16)
                QdTm = sbuf(pe_, "QdTm", [128, 4, 2, 2, 16], BF16)
                Kd = sbuf(pe_, "Kd", [128, 2, 256], BF16)
                Kdm = sbuf(pe_, "Kdm", [16, 4, 2, 256], BF16)
                AT = sbuf(pe_, "AT", [128, 2, 128], BF16)
                decT = sbuf(pe_, "decT_s", [128, 2, 128], F32)
                qdecb = sbuf(pe_, "qdecb_s", [128, 2, 128], F32)
                kdec = sbuf(pe_, "kdec_s", [128, 2], F32)
                decTs = sbuf(pe_, "decTs_s", [16, 2, 16], F32)
                kdm = sbuf(pe_, "kdm_s", [16, 4, 2], F32)
                colmask = sbuf(pe_, "colmask_s", [128, 4, 16], F32)
                gnb = sbuf(pe_, "gnb_s", [128, 2, 512], F32)
                gpow = sbuf(pe_, "gpow_s", [128, 4], F32)
                qdecs = sbuf(pe_, "qdecs_s", [128, 2, 16], F32)
                st8 = sbuf(pe_, "st8", [128, 16], F32)
                og_ = [sbuf(pe_, "og_%d" % i, [128, 2, 512], BF16) for i in range(2)]
                pz = [psum(pe_, "ez%d" % i, [128, 512], F32) for i in range(2)]
                pq = psum(pe_, "eq", [128, 1024], BF16)
                pst = psum(pe_, "est", [128, 512], F32)
                po = [psum(pe_, "eo%d" % i, [128, 512], F32) for i in range(2)]
                pu = [psum(pe_, "eu%d" % i, [128, 512], F32) for i in range(2)]

                wrB = [Buf("wrB%d" % i) for i in range(3)]
                for cq in range(3):
                    for kc in range(4):
                        S.dma("pool", wr[:, 4 * kc:4 * kc + 4, 1024 * cq:1024 * cq + 1024],
                              wr_d.ap()[512 * kc:512 * kc + 512, 1024 * cq:1024 * cq + 1024].rearrange("(k p) c -> p k c", p=128),
                              w=[wrB[cq]], key=wrB[cq])
                S.dma("sp", decT[:], decT_d.ap().rearrange("p (h i) -> p h i", h=2), w=[decT], key=decT)
                S.dma("sp", qdecb[:], qdecb_d.ap().rearrange("p (h i) -> p h i", h=2), w=[qdecb], key=qdecb)
                S.dma("sp", decTs[:], decTs_d.ap().rearrange("p (h i) -> p h i", h=2), w=[decTs], key=decTs)
                S.dma("sp", kdm[:], kdm_d.ap().rearrange("p (s h) -> p s h", s=4), w=[kdm], key=kdm)
                S.dma("sp", colmask[:], colmask_d.ap().rearrange("p (s i) -> p s i", s=4), w=[colmask], key=colmask)
                S.dma("sp", gnb[:], gnb_d.ap().rearrange("p (h e) -> p h e", h=2), w=[gnb], key=gnb)
                S.dma("sp", qdecs[:], qdecs_d.ap().rearrange("p (h i) -> p h i", h=2), w=[qdecs], key=qdecs)
                S.dma("sp", kdec[:], kdec_d.ap(), w=[kdec], key=kdec)
                S.dma("sp", gpow[:], gpow_d.ap(), w=[gpow], key=gpow)
                S.dve(lambda e: e.memset(St[:], 0.0), w=[St])
                S.dve(lambda e: e.memset(Sb[:], 0.0), w=[Sb])

                xg5 = xg_all.ap().rearrange("c (j p) (k t) -> c j p k t", j=4, k=2)

                def load_tile(T_):
                    X, C = xt[T_ % 2], cst[T_ % 2]
                    if T_ < NT:
                        j, tl = T_ // 8, T_ % 8
                        for cp in range(8):
                            S.dma("sp", X[:, 2 * cp:2 * cp + 2, :], xg5[cp, j, :, :, tl * 128:tl * 128 + 128], r=[xgT], w=[X], key=X)
                        S.dma("sp", C[:, :], cs_r.ap()[T_ * 128:T_ * 128 + 128, :], w=[C], key=C)
                    else:
                        for cp in range(8):
                            for j in range(4):
                                S.dma("sp", X[:, 2 * cp:2 * cp + 2, 4 * j:4 * j + 4], xg5[cp, j, :, :, 1024:1028], r=[xgT], w=[X], key=X)
                        S.dma("sp", C[:NS, :], cs_rs.ap(), w=[C], key=C)

                load_tile(0)
                zc = [0]

                def stageE1(T_):
                    n = 128 if T_ < NT else NS
                    X, C = xt[T_ % 2], cst[T_ % 2]
                    qkr, Vb, sg, qkT = qkrs[T_ % 2], Vbs[T_ % 2], sgs[T_ % 2], qkTs[T_ % 2]
                    if T_ + 1 <= NT:
                        load_tile(T_ + 1)

                    def proj(cb, n=n, X=X):
                        P = pz[zc[0] % 2]
                        zc[0] += 1

                        def mm(e):
                            ins = None
                            for k in range(16):
                                ins = e.matmul(P[:n, :], X[:, k, :n], wr[:, k, cb * 512:cb * 512 + 512], start=(k == 0), stop=(k == 15))
                            return ins
                        S.pe(mm, r=[X, wrB[cb // 2]], w=[P])
                        return P

                    cosb = lambda n=n, C=C: C[:n, 0:128].unsqueeze(1).broadcast_to([n, 2, 128])
                    sinb = lambda n=n, C=C: C[:n, 128:256].unsqueeze(1).broadcast_to([n, 2, 128])
                    for half_ in range(2):
                        P = proj(half_)
                        z3 = P[:n, :].rearrange("p (h d) -> p h d", h=2)
                        a, b_, c_, d_ = rt

                        def emit_rope(z3=z3, P=P, half_=half_, n=n, cosb=cosb, sinb=sinb, C=C):
                            S.dve(lambda e: e.tensor_tensor(out=a[:n, :2, :], in0=z3[:, :, 0:128], in1=cosb(), op=ALU.mult), r=[P, C], w=[a])
                            S.dve(lambda e: e.tensor_tensor(out=b_[:n, :2, :], in0=z3[:, :, 128:256], in1=sinb(), op=ALU.mult), r=[P, C], w=[b_])
                            S.dve(lambda e: e.tensor_tensor(out=c_[:n, :2, :], in0=z3[:, :, 128:256], in1=cosb(), op=ALU.mult), r=[P, C], w=[c_])
                            S.dve(lambda e: e.tensor_tensor(out=d_[:n, :2, :], in0=z3[:, :, 0:128], in1=sinb(), op=ALU.mult), r=[P, C], w=[d_])
                            S.pool(lambda e: e.tensor_tensor(out=qkr[:n, 2 * half_:2 * half_ + 2, 0:128], in0=a[:n, :2, :], in1=b_[:n, :2, :],
                                                             op=ALU.subtract), r=[a, b_], w=[qkr])
                            S.pool(lambda e: e.tensor_tensor(out=qkr[:n, 2 * half_:2 * half_ + 2, 128:256], in0=c_[:n, :2, :], in1=d_[:n, :2, :],
                                                             op=ALU.add), r=[c_, d_], w=[qkr])
                        emit_rope()
                    for hh in range(2):
                        P = proj(2 + hh)
                        S.act(lambda e, P=P, hh=hh, n=n: e.copy(out=Vb[:n, hh, :], in_=P[:n, :]), r=[P], w=[Vb])
                    for hh in range(2):
                        P = proj(4 + hh)
                        S.act(lambda e, P=P, hh=hh, n=n: e.activation(out=sg[:n, hh * 512:hh * 512 + 512], in_=P[:n, :], func=AF.Silu), r=[P], w=[sg])

                def stageE1b(T_):
                    n = 128 if T_ < NT else NS
                    qkr, qkT = qkrs[T_ % 2], qkTs[T_ % 2]
                    def trqk(e, n=n):
                        ins = None
                        for a_ in range(4):
                            for c in range(2):
                                ins = e.transpose(pq[:, (a_ * 2 + c) * 128:(a_ * 2 + c) * 128 + n], qkr[:n, a_, c * 128:c * 128 + 128], identb[:n, :n])
                        return ins
                    S.pe(trqk, r=[qkr, identb], w=[pq])
                    S.act(lambda e, n=n: e.copy(out=qkT[:, :, :n], in_=pq[:, :].rearrange("p (a t) -> p a t", a=8)[:, :, :n]), r=[pq], w=[qkT])

                def stageE2(T_):
                    n = 128 if T_ < NT else NS
                    qkr, Vb, sg, qkT = qkrs[T_ % 2], Vbs[T_ % 2], sgs[T_ % 2], qkTs[T_ % 2]
                    qT4 = qkT[:, 0:4, :].rearrange("p (h c) t -> p h c t", h=2)
                    kT4 = qkT[:, 4:8, :].rearrange("p (h c) t -> p h c t", h=2)

                    if T_ < NT:
                        S.dve(lambda e: e.tensor_tensor(out=QdT[:, :, :, :], in0=qT4, in1=qdecb[:, :, :].unsqueeze(2).broadcast_to([128, 2, 2, 128]),
                                                        op=ALU.mult), r=[qkT, qdecb], w=[QdT])
                        S.dve(lambda e: e.tensor_tensor(out=Kd[:, :, :], in0=qkr[:, 2:4, :], in1=kdec[:, :].unsqueeze(2).broadcast_to([128, 2, 256]),
                                                        op=ALU.mult), r=[qkr, kdec], w=[Kd])
                        def smm(e):
                            ins = None
                            for hh in range(2):
                                for c in range(2):
                                    ins = e.matmul(pst[:, hh * 128:hh * 128 + 128], kT4[:, hh, c, :], qT4[:, hh, c, :], start=(hh == 0 and c == 0),
                                                   stop=(c == 1), skip_group_check=True)
                            return ins
                        S.pe(smm, r=[qkT], w=[pst])
                        S.dve(lambda e: e.tensor_tensor(out=AT[:, :, :], in0=pst[:, 0:256].rearrange("p (h i) -> p h i", h=2), in1=decT[:, :, :],
                                                        op=ALU.mult), r=[pst, decT], w=[AT])
                        for hh in range(2):
                            def omm(e, hh=hh):
                                e.matmul(po[hh][:, :], AT[:, hh, :], Vb[:, hh, :], start=True, stop=False)
                                e.matmul(po[hh][:, :], QdT[:, hh, 0, :], Sb[:, hh * 2, :], start=False, stop=False)
                                return e.matmul(po[hh][:, :], QdT[:, hh, 1, :], Sb[:, hh * 2 + 1, :], start=False, stop=True)
                            S.pe(omm, r=[AT, Vb, QdT, Sb], w=[po[hh]])
                        for hh in range(2):
                            for c in range(2):
                                U_ = pu[c]
                                S.pe(lambda e, hh=hh, c=c, U_=U_: e.matmul(U_[:, :], Kd[:, hh, c * 128:c * 128 + 128], Vb[:, hh, :], start=True, stop=True),
                                     r=[Kd, Vb], w=[U_])
                                S.dve(lambda e, hh=hh, c=c, U_=U_: e.scalar_tensor_tensor(out=St[:, hh * 2 + c, :], in0=St[:, hh * 2 + c, :],
                                                                                         scalar=gpow[:, hh:hh + 1], in1=U_[:, :], op0=ALU.mult, op1=ALU.add),
                                      r=[St, gpow, U_], w=[St])
                                S.act(lambda e, hh=hh, c=c: e.copy(out=Sb[:, hh * 2 + c, :], in_=St[:, hh * 2 + c, :]), r=[St], w=[Sb])
                    else:
                        S.dve(lambda e: e.tensor_tensor(out=QdT[:, :, :, 0:NS], in0=qT4[:, :, :, 0:NS],
                                                        in1=qdecs[:, :, :].unsqueeze(2).broadcast_to([128, 2, 2, NS]), op=ALU.mult),
                              r=[qkT, qdecs], w=[QdT])
                        for s_ in range(4):
                            S.dve(lambda e, s_=s_: e.tensor_tensor(out=QdTm[:, s_, :, :, :], in0=QdT[:, :, :, 0:NS],
                                                                   in1=colmask[:, s_, :].unsqueeze(1).unsqueeze(2).broadcast_to([128, 2, 2, 16]),
                                                                   op=ALU.mult), r=[QdT, colmask], w=[QdTm])
                            S.dve(lambda e, s_=s_: e.tensor_tensor(out=Kdm[:, s_, :, :], in0=qkr[:NS, 2:4, :],
                                                                   in1=kdm[:, s_, :].unsqueeze(2).broadcast_to([NS, 2, 256]), op=ALU.mult),
                                  r=[qkr, kdm], w=[Kdm])

                        def smm_s(e):
                            ins = None
                            for hh in range(2):
                                for c in range(2):
                                    ins = e.matmul(pst[:NS, hh * 16:hh * 16 + 16], kT4[:, hh, c, 0:NS], qT4[:, hh, c, 0:NS], start=(hh == 0 and c == 0),
                                                   stop=(c == 1), skip_group_check=True)
                            return ins
                        S.pe(smm_s, r=[qkT], w=[pst])
                        S.dve(lambda e: e.tensor_tensor(out=AT[:NS, :, 0:NS], in0=pst[:NS, 0:32].rearrange("p (h i) -> p h i", h=2), in1=decTs[:, :, :],
                                                        op=ALU.mult), r=[pst, decTs], w=[AT])
                        for s_ in range(4):
                            SS = Ss[s_ % 2]
                            S.dma("sp", SS[:, :, :], sret_d.ap()[s_].rearrange("h (c p) e -> p (h c) e", p=128), w=[SS], key=SS)
                            S.act(lambda e, s_=s_, SS=SS: e.copy(out=Ssb[s_][:, :, :], in_=SS[:, :, :]), r=[SS], w=[Ssb[s_]])
                            for hh in range(2):
                                for c in range(2):
                                    U_ = pu[c]
                                    S.pe(lambda e, hh=hh, c=c, U_=U_, s_=s_: e.matmul(U_[:, :], Kdm[:, s_, hh, c * 128:c * 128 + 128], Vb[:NS, hh, :],
                                                                                      start=True, stop=True), r=[Kdm, Vb], w=[U_])
                                    S.dve(lambda e, hh=hh, c=c, U_=U_, SS=SS: e.scalar_tensor_tensor(out=SS[:, hh * 2 + c, :], in0=SS[:, hh * 2 + c, :],
                                                                                                    scalar=gpow[:, 2 + hh:3 + hh], in1=U_[:, :],
                                                                                                    op0=ALU.mult, op1=ALU.add),
                                          r=[SS, gpow, U_], w=[SS])
                            S.dma("sp", ret_s.ap()[s_].rearrange("h (c p) e -> p (h c) e", p=128), SS[:, :, :], r=[SS], key=SS, final=True)
                        for hh in range(2):
                            def omm_s(e, hh=hh):
                                ins = e.matmul(po[hh][:NS, :], AT[:NS, hh, 0:NS], Vb[:NS, hh, :], start=True, stop=False)
                                for s_ in range(4):
                                    for c in range(2):
                                        ins = e.matmul(po[hh][:NS, :], QdTm[:, s_, hh, c, :], Ssb[s_][:, hh * 2 + c, :], start=False,
                                                       stop=(s_ == 3 and c == 1))
                                return ins
                            S.pe(omm_s, r=[AT, Vb, QdTm] + Ssb, w=[po[hh]])

                    OG = og_[T_ % 2]
                    for hh in range(2):
                        OB = ob[hh]
                        S.act(lambda e, hh=hh, OB=OB, n=n: e.activation(out=OB[:n, :], in_=po[hh][:n, :], func=AF.Identity, accum_out=st8[:n, hh * 8:hh * 8 + 1]),
                              r=[po[hh]], w=[OB, st8])
                        S.act(lambda e, hh=hh, OB=OB, n=n: e.activation(out=jk[:n, :], in_=OB[:n, :], func=AF.Square, accum_out=st8[:n, hh * 8 + 1:hh * 8 + 2]),
                              r=[OB], w=[jk, st8])
                        o8 = hh * 8
                        S.dve(lambda e, o8=o8, n=n: e.tensor_scalar(out=st8[:n, o8 + 2:o8 + 4], in0=st8[:n, o8:o8 + 2], scalar1=1.0 / 512, scalar2=None,
                                                                    op0=ALU.mult), r=[st8], w=[st8])
                        S.dve(lambda e, o8=o8, n=n: e.tensor_tensor(out=st8[:n, o8 + 4:o8 + 5], in0=st8[:n, o8 + 2:o8 + 3], in1=st8[:n, o8 + 2:o8 + 3],
                                                                    op=ALU.mult), r=[st8], w=[st8])
                        S.dve(lambda e, o8=o8, n=n: e.tensor_tensor(out=st8[:n, o8 + 5:o8 + 6], in0=st8[:n, o8 + 3:o8 + 4], in1=st8[:n, o8 + 4:o8 + 5],
                                                                    op=ALU.subtract), r=[st8], w=[st8])
                        S.dve(lambda e, o8=o8, n=n: e.tensor_scalar(out=st8[:n, o8 + 5:o8 + 6], in0=st8[:n, o8 + 5:o8 + 6], scalar1=0.0, scalar2=1e-5,
                                                                    op0=ALU.max, op1=ALU.add), r=[st8], w=[st8])
                        S.act(lambda e, o8=o8, n=n: e.activation(out=st8[:n, o8 + 6:o8 + 7], in_=st8[:n, o8 + 5:o8 + 6], func=AF.Sqrt), r=[st8], w=[st8])
                        S.dve(lambda e, o8=o8, n=n: e.reciprocal(out=st8[:n, o8 + 6:o8 + 7], in_=st8[:n, o8 + 6:o8 + 7]), r=[st8], w=[st8])
                        S.dve(lambda e, o8=o8, OB=OB, n=n: e.tensor_scalar(out=OB[:n, :], in0=OB[:n, :], scalar1=st8[:n, o8 + 2:o8 + 3],
                                                                           scalar2=st8[:n, o8 + 6:o8 + 7], op0=ALU.subtract, op1=ALU.mult),
                              r=[OB, st8], w=[OB])
                        S.pool(lambda e, hh=hh, OB=OB, n=n: e.tensor_tensor(out=OB[:n, :], in0=OB[:n, :], in1=gnb[:n, hh, :], op=ALU.mult), r=[OB, gnb], w=[OB])
                        S.pool(lambda e, hh=hh, OB=OB, OG=OG, n=n: e.tensor_tensor(out=OG[:n, hh, :], in0=OB[:n, :], in1=sg[:n, hh * 512:hh * 512 + 512],
                                                                                   op=ALU.mult), r=[OB, sg], w=[OG])
                        if T_ < NT:
                            c_, tl = T_ // 8, T_ % 8
                            S.dma("sp", o_r.ap()[c_, hh, tl * 128:tl * 128 + 128, :], OG[:, hh, :], r=[OG], w=[o_rT[c_][hh]], key=OG)
                        else:
                            S.dma("sp", o_rs.ap()[hh], OG[:NS, hh, :], r=[OG], w=[o_rsT[hh]], key=OG)
                    if T_ < NT and T_ % 8 == 7:
                        c_ = T_ // 8
                        for hh in range(2):
                            base = ((c_ * 2 + hh) * 4) * 1024
                            allgather(o_r.ap()[c_, hh].rearrange("(p a) f -> p (a f)", a=8),
                                      og2.ap()[base:base + 4096, :].rearrange("(q a) f -> q (a f)", a=8), o_rT[c_][hh], og2T)
                    if T_ == NT - 1:
                        S.dma("sp", ret_p.ap().rearrange("h (c p) e -> p (h c) e", p=128), St[:, :, :], r=[St], key=St, final=True)
                stageE1(0)
                stageE1b(0)
                for T_ in range(NT + 1):
                    if T_ + 1 <= NT:
                        stageE1(T_ + 1)
                    stageE2(T_)
                    if T_ + 1 <= NT:
                        stageE1b(T_ + 1)
                for hh in range(2):
                    allgather(o_rs.ap()[hh].rearrange("r (b x) -> (r b) x", b=8),
                              ogs2.ap()[hh * 64:hh * 64 + 64, :].rearrange("r (b x) -> (r b) x", b=8), o_rsT[hh], og2T)
                S.barrier()

        if RUN_T0:
            token_phase(0)
        if RUN_E:
            phase_e()
        if RUN_T1:
            token_phase(1)

        S.emit()
    return nc


def _col_slice(g):
    cols = list(range(512 * g, 512 * g + 512))
    for e in (0, 1):
        for br in range(3):
            base = 2048 + ((br * 2 + e) * 4 + g) * 128
            cols += list(range(base, base + 128))
    for br in range(3):
        base = 2048 + 3072 + br * 16 + 4 * g
        cols += list(range(base, base + 4))
    return np.array(cols)


def _consts():
    c0 = np.arange(256)[:, None] * 16
    j0 = np.arange(64)[None, :] * 64
    cover = ((c0 < j0 + 64) & (c0 + 32 > j0)).astype(np.float32)
    cover[255] = 0.0
    p = np.arange(128)[:, None]
    col = np.arange(128)[None, :]
    I0 = (col - 16 * p).astype(np.float32)
    Jm = (np.arange(64)[None, :] - (p >= 64)).astype(np.float32)
    tri = np.concatenate([(p <= col), (col < p)], axis=1).astype(np.float32)
    Ebig = (np.arange(SEQ)[None, :] // 64 == np.arange(64)[:, None]).astype(np.float32)
    key = np.arange(16)
    hq = np.arange(16)
    smask = np.zeros((16, 5, 16), np.float32)
    for s_ in range(4):
        smask[:, s_, :] = ((key[:, None] // 4 == s_) & (key[:, None] % 4 <= hq[None, :] % 4))
    smask[:, 4, :] = (key[:, None] > hq[None, :] % 4)
    sel16 = np.zeros((16, 68), np.float32)
    sel16[:, 0:4] = (hq[:, None] % 4 == np.arange(4)[None, :])
    for s_ in range(4):
        sel16[:, 4 + 16 * s_:20 + 16 * s_] = (key[:, None] == 4 * s_ + hq[None, :] % 4)
    hmask = (np.arange(12)[None, :] % 4 == hq[:, None] // 4).astype(np.float32)
    c0s = np.arange(1024)[:, None] * 16
    j0s = np.arange(257)[None, :] * 64
    cover_s = ((c0s < j0s + 64) & (c0s + 32 > j0s)).astype(np.float32)
    cover_s[1023] = 0.0
    return {"cover": cover, "I0": I0, "Jm": Jm, "tri": tri, "Ebig": Ebig, "smask": smask, "sel16": sel16,
            "hmask": hmask, "cover_s": cover_s, "cs_r": rope_table(np.arange(SEQ), 128),
            "cs_rs": rope_table(np.tile(PAST + np.arange(4), 4), 128)}


CONST = _consts()


def _ret_cols(r):
    cols = []
    for base, w in ((0, 256), (2048, 256), (4096, 512), (8192, 512)):
        for hh in range(2):
            h = 2 * r + hh
            cols += list(range(base + w * h, base + w * h + w))
    return np.array(cols)


def _ret_consts(r):
    out = {}
    i = np.arange(128, dtype=np.float64)
    decT = np.zeros((128, 2, 128), np.float64)
    qdecb = np.zeros((128, 2, 128), np.float64)
    kdec = np.zeros((128, 2), np.float64)
    decTs = np.zeros((16, 2, 16), np.float64)
    kdm = np.zeros((16, 4, 2), np.float64)
    gpow = np.zeros((128, 4), np.float64)
    qdecs = np.zeros((128, 2, 16), np.float64)
    t16 = np.arange(16)
    for hh in range(2):
        h = 2 * r + hh
        lg = np.log(np.float32(1.0) - np.float32(2.0) ** np.float32(-5.0 - h)).astype(np.float64)
        rel = i[None, :] - i[:, None]
        decT[:, hh, :] = np.where(rel >= 0, np.exp(np.maximum(rel, 0) * lg), 0.0) / 16.0
        qdecb[:, hh, :] = np.exp((i + 1.0) * lg)[None, :]
        kdec[:, hh] = np.exp((127.0 - i) * lg) / 16.0
        rel4 = (t16[None, :] % 4) - (t16[:, None] % 4)
        same = (t16[None, :] // 4) == (t16[:, None] // 4)
        decTs[:, hh, :] = np.where(same & (rel4 >= 0), np.exp(np.maximum(rel4, 0) * lg), 0.0) / 16.0
        for s_ in range(4):
            kdm[:, s_, hh] = np.where(t16 // 4 == s_, np.exp((3.0 - t16 % 4) * lg), 0.0) / 16.0
        gpow[:, hh] = np.exp(128.0 * lg)
        gpow[:, 2 + hh] = np.exp(4.0 * lg)
        qdecs[:, hh, :] = np.exp((t16 % 4 + 1.0) * lg)[None, :]
    colmask = np.zeros((128, 4, 16), np.float32)
    for s_ in range(4):
        colmask[:, s_, :] = (t16 // 4 == s_)[None, :]
    out["decT"] = decT.reshape(128, 256).astype(np.float32)
    out["qdecb"] = qdecb.reshape(128, 256).astype(np.float32)
    out["kdec"] = kdec.astype(np.float32)
    out["decTs"] = decTs.reshape(16, 32).astype(np.float32)
    out["kdm"] = kdm.reshape(16, 8).astype(np.float32)
    out["gpow"] = gpow.astype(np.float32)
    out["qdecs"] = qdecs.reshape(128, 32).astype(np.float32)
    out["colmask"] = colmask.reshape(128, 64)
    return out


def _oidx2(r):
    p = np.arange(128)
    idx = np.zeros((128, 72), np.int32)
    for t in range(9):
        for j in range(4):
            for hh in range(2):
                if t < 8:
                    idx[:, t * 8 + 2 * j + hh] = ((r * 2 + hh) * 4 + j) * 1024 + t * 128 + p
                else:
                    idx[:, t * 8 + 2 * j + hh] = (hh * 4 + j) * NS + 4 * r + np.minimum(p, 3)
    return idx


def _oidx(r):
    p = np.arange(128)
    idx = np.zeros((128, 36), np.int32)
    for t in range(9):
        for j in range(4):
            if t < 8:
                idx[:, t * 4 + j] = r * 4096 + j * 1024 + t * 128 + p
            else:
                idx[:, t * 4 + j] = j * NS + 4 * r + np.minimum(p, 3)
    return idx


def make_in_maps(inp):
    maps = []
    ident = np.eye(128, dtype=np.float32)
    cs_p = rope_table(np.arange(SEQ), 64)
    cs_s = rope_table(np.tile(PAST + np.arange(4), 4), 64)
    for c in range(8):
        b, r = c // 4, c % 4
        m = {
            "xb": np.ascontiguousarray(inp["x_prompt"][b]),
            "xs": np.ascontiguousarray(inp["x_sample"][4 * b:4 * b + 4].reshape(NS, D)),
            "w_in": np.ascontiguousarray(inp["nsa_w_in"][0][:, _col_slice(r)]),
            "gmix0": np.ascontiguousarray(np.broadcast_to(inp["norm_mix"][0][None, :], (128, D))),
            "cs_p": cs_p, "cs_s": cs_s, "ident": ident,
            "cw1": inp["nsa_cmp_w1"][0], "cw2": inp["nsa_cmp_w2"][0],
            "posT": np.ascontiguousarray(inp["nsa_cmp_pos"][0].reshape(64, 128).T),
            "b1T": np.ascontiguousarray(inp["nsa_cmp_b1"][0].T),
            "x_tok": np.ascontiguousarray(np.concatenate([inp["x_prompt"][b, 1024 * r:1024 * r + 1024], inp["x_sample"][c]], axis=0)),
            "oidx": _oidx(r),
            "w_out0": inp["nsa_w_out"][0], "ffn1_0": inp["ffn_w1"][0], "ffn2_0": inp["ffn_w2"][0],
            "gffn0T": np.ascontiguousarray(inp["norm_ffn"][0].reshape(16, 128).T),
            "w_out1": inp["ret_w_out"][0], "ffn1_1": inp["ffn_w1"][1], "ffn2_1": inp["ffn_w2"][1],
            "gmix1T": np.ascontiguousarray(inp["norm_mix"][1].reshape(16, 128).T),
            "gffn1T": np.ascontiguousarray(inp["norm_ffn"][1].reshape(16, 128).T),
            "gfinT": np.ascontiguousarray(inp["norm_final"].reshape(16, 128).T),
            "wr": np.ascontiguousarray(inp["ret_w_in"][0][:, _ret_cols(r)]),
            "cs_r": CONST["cs_r"], "cs_rs": CONST["cs_rs"],
            "gnb": np.ascontiguousarray(np.broadcast_to(inp["ret_gn"][0][1024 * r:1024 * r + 1024][None, :], (128, 1024))),
            "sret": np.ascontiguousarray(inp["state_ret"][0][4 * b:4 * b + 4, 2 * r:2 * r + 2]),
            "oidx2": _oidx2(r),
            **_ret_consts(r),
            "ccache": np.ascontiguousarray(inp["cache_cmp_kv"][0][:, :, :, r, :]),
            "scache": np.ascontiguousarray(inp["cache_sel_kv"][0][:, :, :, r, :]),
            "wstate": np.ascontiguousarray(inp["state_win_kv"][0][4 * b:4 * b + 4, :, :, r, :]),
            "ptab": np.ascontiguousarray(inp["page_table"][4 * b:4 * b + 4].astype(np.int32)),
            "smask": CONST["smask"], "sel16": CONST["sel16"], "hmask": CONST["hmask"], "cover_s": CONST["cover_s"],
            "pidx": (np.arange(128) % 4).astype(np.float32)[:, None].copy(),
            "ptabT": np.ascontiguousarray(inp["page_table"][4 * b:4 * b + 4].astype(np.int32).reshape(4, 4, 32).transpose(2, 0, 1).reshape(32, 16)),
            "Rrep": (np.arange(128)[None, :] // 4 == np.arange(32)[:, None]).astype(np.float32),
            "E2": (np.arange(128)[None, :] // 2 == np.arange(64)[:, None]).astype(np.float32),
            "cover": CONST["cover"], "I0": CONST["I0"], "Jm": CONST["Jm"], "tri": CONST["tri"], "Ebig": CONST["Ebig"],
        }
        maps.append(m)
    return maps


_NC = None


def kernel(**inputs):
    global _NC
    inp = {k: np.asarray(v) for k, v in inputs.items()}
    if _NC is None:
        _NC = build()
    maps = make_in_maps(inp)
    res = run_bass_kernel_spmd(_NC, maps, core_ids=list(range(8)), **({'trace': True} if TRACE else {}))
    global LAST_RES
    LAST_RES = res
    R = res.results
    global LAST
    LAST = R
    cmp_p = np.zeros((1, 2, SEQ, 2, 4, 128), np.float32)
    sel_p = np.zeros_like(cmp_p)
    win_full = np.zeros_like(cmp_p)
    cmp_s = np.zeros((1, 8, 4, 2, 4, 128), np.float32)
    sel_s = np.zeros_like(cmp_s)
    win_new = np.zeros_like(cmp_s)
    for c in range(8):
        b, g = c // 4, c % 4
        kp = R[c]["kv_p"]
        cmp_p[0, b, :, :, g, :] = kp[:, 0]
        sel_p[0, b, :, :, g, :] = kp[:, 1]
        win_full[0, b, :, :, g, :] = kp[:, 2]
        ks = R[c]["kv_s"].reshape(4, 4, 3, 2, 128)
        cmp_s[0, 4 * b:4 * b + 4, :, :, g, :] = ks[:, :, 0]
        sel_s[0, 4 * b:4 * b + 4, :, :, g, :] = ks[:, :, 1]
        win_new[0, 4 * b:4 * b + 4, :, :, g, :] = ks[:, :, 2]
    win_p = np.ascontiguousarray(win_full[:, :, SEQ - 512:])
    y_p = np.zeros((2, SEQ, D), np.float32)
    y_s = np.zeros((8, 4, D), np.float32)
    win_s = np.zeros((1, 8, 512, 2, 4, 128), np.float32)
    ret_p = np.zeros((1, 2, 8, 256, 512), np.float32)
    ret_s = np.zeros((1, 8, 8, 256, 512), np.float32)
    for c in range(8):
        b, g = c // 4, c % 4
        win_s[0, 4 * b:4 * b + 4, :, :, g, :] = R[c]["win_s"]
        y_p[b, 1024 * g:1024 * g + 1024] = R[c]["y_tok"][:1024]
        y_s[c] = R[c]["y_tok"][1024:]
        ret_p[0, b, 2 * g:2 * g + 2] = R[c]["ret_p"]
        ret_s[0, 4 * b:4 * b + 4, 2 * g:2 * g + 2] = R[c]["ret_s"]
    return (y_p, y_s, cmp_p, cmp_s, sel_p, sel_s, win_p, win_s, ret_p, ret_s)
```
